# Optimizing a Trainium2 kernel written in Bass

```python
import math
import numpy as np
import jax
import jax.numpy as jnp
from jax import lax

D_MODEL = 1024
BATCH = 8
SEQ = 8192
DEPTH = 1
DEC_BATCH = 1
DEC_SEQ = 16384
PAST_LEN = 128

GRID_W = 64
HEAD_DIM = 128
N_Q_HEADS = 8
N_KV_HEADS = 2
ATTN_W = N_Q_HEADS * HEAD_DIM
KV_W = N_KV_HEADS * HEAD_DIM
Q_BLOCK = 128
ROPE_THETA = 10000.0
SSM_W = 512
SSM_GROUP = 16
SSM_GROUPS = SSM_W // SSM_GROUP
SSM_STATE = 64
STEP_MIN = 0.001
STEP_MAX = 0.1
N_MEM = 256
X_HEADS = 4
X_HEAD_DIM = 128
X_W = X_HEADS * X_HEAD_DIM
N_BRANCH = 3
IN_SIZES = (ATTN_W, KV_W, KV_W, ATTN_W, SSM_W, SSM_W, X_W, X_W, N_BRANCH * D_MODEL)
IN_W = ATTN_W + 2 * KV_W + ATTN_W + 2 * SSM_W + 2 * X_W + N_BRANCH * D_MODEL
EPS = 1e-6
F32 = jnp.float32

kernel_name = 'hybrid_gqa_s5_memory_encoder'


def rmsnorm(x, g):
    xf = x.astype(F32)
    y = xf * lax.rsqrt(jnp.mean(xf * xf, axis=-1, keepdims=True) + EPS) * g.astype(F32)
    return y.astype(x.dtype)


def axial_rope(seq_len):
    rows = seq_len // GRID_W
    row = jnp.broadcast_to(jnp.arange(rows, dtype=F32)[:, None], (rows, GRID_W)).reshape(-1)
    col = jnp.broadcast_to(jnp.arange(GRID_W, dtype=F32)[None, :], (rows, GRID_W)).reshape(-1)
    n_pairs = HEAD_DIM // 4
    freqs = ROPE_THETA ** (-jnp.arange(n_pairs, dtype=F32) / n_pairs)
    ang = jnp.concatenate([row[:, None] * freqs, col[:, None] * freqs], axis=-1)
    return jnp.cos(ang), jnp.sin(ang)


def apply_rope(x, cos, sin):
    xr = x.reshape(x.shape[:-1] + (HEAD_DIM // 2, 2))
    x0, x1 = xr[..., 0], xr[..., 1]
    c = cos[None, :, None, :]
    s = sin[None, :, None, :]
    return jnp.stack([x0 * c - x1 * s, x0 * s + x1 * c], axis=-1).reshape(x.shape)


def blocked_gqa(q, k, v):
    bsz, seq_len = q.shape[:2]
    grp = N_Q_HEADS // N_KV_HEADS
    n_blocks = seq_len // Q_BLOCK
    qb = q.reshape(bsz, n_blocks, Q_BLOCK, N_KV_HEADS, grp, HEAD_DIM).transpose(1, 0, 2, 3, 4, 5)
    scale = HEAD_DIM ** -0.5

    def one_block(qi):
        s = jnp.einsum('bqkgd,bskd->bkgqs', qi, k) * scale
        p = jax.nn.softmax(s.astype(F32), axis=-1)
        return jnp.einsum('bkgqs,bskd->bqkgd', p, v)

    o = lax.map(one_block, qb)
    return o.transpose(1, 0, 2, 3, 4, 5).reshape(bsz, seq_len, ATTN_W)


def s5_scan(u_g, a_re, a_im, b_re, b_im, c_re, c_im, log_step, reverse):
    lam = lax.complex(jnp.minimum(a_re.astype(F32), -1e-4), a_im.astype(F32))
    step = jnp.exp(log_step.astype(F32))[:, None]
    lam_bar = jnp.exp(lam * step)
    b_mat = lax.complex(b_re.astype(F32), b_im.astype(F32))
    b_bar = ((lam_bar - 1.0) / lam)[..., None] * b_mat
    bu = jnp.einsum('gnc,blgc->blgn', b_bar, u_g.astype(jnp.complex64))
    a = jnp.broadcast_to(lam_bar, bu.shape)

    def combine(left, right):
        a1, b1 = left
        a2, b2 = right
        return a2 * a1, a2 * b1 + b2

    _, states = lax.associative_scan(combine, (a, bu), axis=1, reverse=reverse)
    c_mat = lax.complex(c_re.astype(F32), c_im.astype(F32))
    return jnp.einsum('gcn,blgn->blgc', c_mat, states).real


def memory_cross_attention(q_x, mem, norm_mem, w_mem_kv):
    bsz, seq_len, _ = q_x.shape
    n_mem = mem.shape[1]
    kv = (rmsnorm(mem, norm_mem) @ w_mem_kv).astype(F32)
    k_m, v_m = jnp.split(kv, 2, axis=-1)
    k_m = k_m.reshape(bsz, n_mem, X_HEADS, X_HEAD_DIM)
    v_m = v_m.reshape(bsz, n_mem, X_HEADS, X_HEAD_DIM)
    qh = q_x.astype(F32).reshape(bsz, seq_len, X_HEADS, X_HEAD_DIM)
    s = jnp.einsum('blhd,bmhd->bhlm', qh, k_m) * (X_HEAD_DIM ** -0.5)
    p = jax.nn.softmax(s, axis=-1)
    return jnp.einsum('bhlm,bmhd->blhd', p, v_m).reshape(bsz, seq_len, X_W)


def hybrid_layer(x, mem, norm_in, w_in, q_norm, k_norm, s5_a_re, s5_a_im, s5_b_re, s5_b_im,
                 s5_c_re, s5_c_im, s5_log_step, s5_d, w_glu, b_glu, norm_mem, w_mem_kv,
                 w_proj_attn, w_proj_ssm, w_proj_cross, w_out):
    bsz, seq_len, _ = x.shape
    h = rmsnorm(x, norm_in)
    z = h @ w_in
    split_at = np.cumsum(IN_SIZES)[:-1].tolist()
    q, k, v, gate_a, u_s, gate_s, q_x, gate_x, merge_logits = jnp.split(z, split_at, axis=-1)

    cos, sin = axial_rope(seq_len)
    qh = rmsnorm(q.astype(F32).reshape(bsz, seq_len, N_Q_HEADS, HEAD_DIM), q_norm)
    kh = rmsnorm(k.astype(F32).reshape(bsz, seq_len, N_KV_HEADS, HEAD_DIM), k_norm)
    qh = apply_rope(qh, cos, sin)
    kh = apply_rope(kh, cos, sin)
    vh = v.astype(F32).reshape(bsz, seq_len, N_KV_HEADS, HEAD_DIM)
    y_a = blocked_gqa(qh, kh, vh) * jax.nn.silu(gate_a.astype(F32))

    u = u_s.astype(F32)
    u_g = u.reshape(bsz, seq_len, SSM_GROUPS, SSM_GROUP)
    y_s = (s5_scan(u_g, s5_a_re[0], s5_a_im[0], s5_b_re[0], s5_b_im[0], s5_c_re[0], s5_c_im[0], s5_log_step[0], False)
           + s5_scan(u_g, s5_a_re[1], s5_a_im[1], s5_b_re[1], s5_b_im[1], s5_c_re[1], s5_c_im[1], s5_log_step[1], True))
    y_s = y_s.reshape(bsz, seq_len, SSM_W) + s5_d.astype(F32) * u
    y_s = jax.nn.gelu(y_s)
    y_s = y_s * jax.nn.sigmoid(y_s @ w_glu.astype(F32) + b_glu.astype(F32))
    y_s = y_s * jax.nn.silu(gate_s.astype(F32))

    y_x = memory_cross_attention(q_x, mem, norm_mem, w_mem_kv) * jax.nn.silu(gate_x.astype(F32))

    g = jax.nn.sigmoid(merge_logits.astype(F32)).reshape(bsz, seq_len, N_BRANCH, D_MODEL)
    merged = (g[:, :, 0] * (y_a.astype(x.dtype) @ w_proj_attn)
              + g[:, :, 1] * (y_s.astype(x.dtype) @ w_proj_ssm)
              + g[:, :, 2] * (y_x.astype(x.dtype) @ w_proj_cross))
    return x + (merged.astype(x.dtype) @ w_out)


def trunk(x, mem, norm_in, w_in, q_norm, k_norm, s5_a_re, s5_a_im, s5_b_re, s5_b_im, s5_c_re,
          s5_c_im, s5_log_step, s5_d, w_glu, b_glu, norm_mem, w_mem_kv, w_proj_attn, w_proj_ssm,
          w_proj_cross, w_out, norm_final):
    for l in range(DEPTH):
        x = hybrid_layer(x, mem, norm_in[l], w_in[l], q_norm[l], k_norm[l], s5_a_re[l], s5_a_im[l],
                         s5_b_re[l], s5_b_im[l], s5_c_re[l], s5_c_im[l], s5_log_step[l], s5_d[l],
                         w_glu[l], b_glu[l], norm_mem[l], w_mem_kv[l], w_proj_attn[l], w_proj_ssm[l],
                         w_proj_cross[l], w_out[l])
    return rmsnorm(x, norm_final)


def setup_inputs(seed: int = 0) -> dict:
    key = jax.random.key(seed)
    ks = jax.random.split(key, 28)
    nrm = lambda k, shape, scale: jax.random.normal(k, shape, F32) * scale
    n_idx = jnp.arange(SSM_STATE, dtype=F32)
    s5_shape = (DEPTH, 2, SSM_GROUPS, SSM_STATE)
    return {
        'x_prompt': nrm(ks[0], (BATCH, SEQ, D_MODEL), 1.0),
        'x_sample': nrm(ks[1], (DEC_BATCH, DEC_SEQ, D_MODEL), 1.0),
        'mem_prompt': nrm(ks[2], (BATCH, N_MEM, D_MODEL), 1.0),
        'mem_sample': nrm(ks[3], (DEC_BATCH, N_MEM, D_MODEL), 1.0),
        'norm_in': 1.0 + nrm(ks[4], (DEPTH, D_MODEL), 0.02),
        'w_in': nrm(ks[5], (DEPTH, D_MODEL, IN_W), D_MODEL ** -0.5),
        'q_norm': 1.0 + nrm(ks[6], (DEPTH, HEAD_DIM), 0.02),
        'k_norm': 1.0 + nrm(ks[7], (DEPTH, HEAD_DIM), 0.02),
        's5_a_re': -0.5 + nrm(ks[8], s5_shape, 0.01),
        's5_a_im': math.pi * n_idx + nrm(ks[9], s5_shape, 0.01),
        's5_b_re': nrm(ks[10], (DEPTH, 2, SSM_GROUPS, SSM_STATE, SSM_GROUP), (2.0 * SSM_GROUP) ** -0.5),
        's5_b_im': nrm(ks[11], (DEPTH, 2, SSM_GROUPS, SSM_STATE, SSM_GROUP), (2.0 * SSM_GROUP) ** -0.5),
        's5_c_re': nrm(ks[12], (DEPTH, 2, SSM_GROUPS, SSM_GROUP, SSM_STATE), (2.0 * SSM_STATE) ** -0.5),
        's5_c_im': nrm(ks[13], (DEPTH, 2, SSM_GROUPS, SSM_GROUP, SSM_STATE), (2.0 * SSM_STATE) ** -0.5),
        's5_log_step': jax.random.uniform(ks[14], (DEPTH, 2, SSM_GROUPS), F32, math.log(STEP_MIN), math.log(STEP_MAX)),
        's5_d': nrm(ks[15], (DEPTH, SSM_W), 1.0),
        'w_glu': nrm(ks[16], (DEPTH, SSM_W, SSM_W), SSM_W ** -0.5),
        'b_glu': nrm(ks[17], (DEPTH, SSM_W), 0.01),
        'norm_mem': 1.0 + nrm(ks[18], (DEPTH, D_MODEL), 0.02),
        'w_mem_kv': nrm(ks[19], (DEPTH, D_MODEL, 2 * X_W), D_MODEL ** -0.5),
        'w_proj_attn': nrm(ks[20], (DEPTH, ATTN_W, D_MODEL), ATTN_W ** -0.5),
        'w_proj_ssm': nrm(ks[21], (DEPTH, SSM_W, D_MODEL), SSM_W ** -0.5),
        'w_proj_cross': nrm(ks[22], (DEPTH, X_W, D_MODEL), X_W ** -0.5),
        'w_out': nrm(ks[23], (DEPTH, D_MODEL, D_MODEL), D_MODEL ** -0.5),
        'norm_final': 1.0 + nrm(ks[24], (D_MODEL,), 0.02),
    }


def reference(x_prompt, x_sample, mem_prompt, mem_sample, norm_in, w_in, q_norm, k_norm,
              s5_a_re, s5_a_im, s5_b_re, s5_b_im, s5_c_re, s5_c_im, s5_log_step, s5_d,
              w_glu, b_glu, norm_mem, w_mem_kv, w_proj_attn, w_proj_ssm, w_proj_cross,
              w_out, norm_final):
    weights = (norm_in, w_in, q_norm, k_norm, s5_a_re, s5_a_im, s5_b_re, s5_b_im, s5_c_re,
               s5_c_im, s5_log_step, s5_d, w_glu, b_glu, norm_mem, w_mem_kv, w_proj_attn,
               w_proj_ssm, w_proj_cross, w_out, norm_final)
    y_prompt = trunk(x_prompt, mem_prompt, *weights)
    y_sample = trunk(x_sample, mem_sample, *weights)
    return (y_prompt, y_sample)
```

```python
import math
from contextlib import ExitStack

import numpy as np
import concourse.bass as bass
import concourse.mybir as mybir
from concourse.bass_utils import run_bass_kernel_spmd

F32 = mybir.dt.float32
BF16 = mybir.dt.bfloat16
AF = mybir.ActivationFunctionType
ALU = mybir.AluOpType
AX = mybir.AxisListType

D = 1024
IN_W = 7680
C_Q, C_K, C_V, C_GA, C_U, C_GS, C_QX, C_GX, C_MG = 0, 1024, 1280, 1536, 2560, 3072, 3584, 4096, 4608
EPS = 1e-6
NCORES = 8
TS = 256
MAGIC = 12582912.0
TWO_PI = 2.0 * math.pi
CW1 = 6.28125
CW2 = TWO_PI - CW1
SEM_CH = 20000
N_DMA_SEMS = 40


class Buf:
    __slots__ = ("lw", "rd", "ex")

    def __init__(self):
        self.lw = None
        self.rd = []
        self.ex = False


class T:
    def __init__(self, h, b=None):
        self.h = h
        self.b = b if b is not None else Buf()

    def __getitem__(self, k):
        return self.h[k]


def _bufs(xs):
    out = []
    for x in xs:
        if x is None:
            continue
        out.append(x.b if isinstance(x, T) else x)
    return out


class Prog:
    ENGS = ("pe", "act", "dve", "pool", "sp")

    def __init__(self, nc, stack):
        self.nc = nc
        self.stack = stack
        self.q = {e: [] for e in self.ENGS}
        self.cnt = {e: 0 for e in self.ENGS}
        self.sems = {e: [] for e in self.ENGS}
        self.dsems = []
        self.dcnt = []
        self.drr = 0
        self.n_dma = 0
        self.pend = {e: [] for e in self.ENGS}

    def _sem(self, e, idx):
        k = idx // SEM_CH
        while len(self.sems[e]) <= k:
            self.sems[e].append(self.stack.enter_context(
                self.nc.semaphore(f"s_{e}_{len(self.sems[e])}")))
        return self.sems[e][k], idx % SEM_CH + 1

    def op(self, e, fn, reads=(), writes=(), dma=False):
        reads = _bufs(reads)
        writes = _bufs(writes)
        exr = [b for b in reads if b.ex]
        if exr:
            reads = [b for b in reads if not b.ex]
            writes = writes + [b for b in exr if b not in writes]
        deps = {}

        def add(d):
            if d is None:
                return
            if d[0] not in deps or deps[d[0]][1] < d[1]:
                deps[d[0]] = d
        for b in reads:
            add(b.lw)
        for b in writes:
            add(b.lw)
            for r in b.rd:
                add(r)
        idx = self.cnt[e]
        if not dma:
            self.cnt[e] += 1
        waits = list(self.pend[e])
        self.pend[e] = []
        for key, d in deps.items():
            if key == "pe" and e == "pe":
                continue
            waits.append((d[2], d[3]))
        if dma:
            if len(self.dsems) < N_DMA_SEMS:
                self.dsems.append(self.stack.enter_context(
                    self.nc.semaphore(f"s_dma_{len(self.dsems)}")))
                self.dcnt.append(0)
                i = len(self.dsems) - 1
            else:
                i = self.drr
                self.drr = (self.drr + 1) % N_DMA_SEMS
            if self.dcnt[i] > 0:
                waits.append((self.dsems[i], self.dcnt[i]))
            self.dcnt[i] += 16
            me = ("dma%d" % self.n_dma, 0, self.dsems[i], self.dcnt[i])
            self.n_dma += 1
            self.q[e].append((waits, fn, self.dsems[i], 16))
        else:
            s, v = self._sem(e, idx)
            me = (e, idx, s, v)
            self.q[e].append((waits, fn, s, 1))
        for b in reads:
            b.rd.append(me)
        for b in writes:
            b.lw = me
            b.rd = []
        return me

    def all_done_waits(self):
        final = [(s, c) for s, c in zip(self.dsems, self.dcnt) if c > 0]
        for e in self.ENGS:
            if self.cnt[e] > 0:
                final.append(self._sem(e, self.cnt[e] - 1))
        return final

    def barrier(self):
        w = self.all_done_waits()
        for e in self.ENGS:
            self.pend[e] = list(w)

    def emit(self, last=False):
        nc = self.nc
        prog = self
        final = self.all_done_waits() if last else []
        with nc.Block() as block:
            def run(eng, name):
                for waits, fn, s, inc in prog.q[name]:
                    for (ws, wv) in waits:
                        eng.wait_ge(ws, wv)
                    fn(eng).then_inc(s, inc)
                prog.q[name] = []

            @block.tensor
            def _(eng):
                run(eng, "pe")

            @block.scalar
            def _(eng):
                run(eng, "act")

            @block.vector
            def _(eng):
                run(eng, "dve")

            @block.gpsimd
            def _(eng):
                run(eng, "pool")

            @block.sync
            def _(eng):
                run(eng, "sp")
                for (ws, wv) in final:
                    eng.wait_ge(ws, wv)


def rev_ap(ap2d):
    a = ap2d.ap
    assert len(a) == 2, a
    n = a[1][1]
    st = a[1][0]
    return bass.AP(ap2d.tensor, ap2d.offset + st * (n - 1), [list(a[0]), [-st, n]])


class Builder:
    def __init__(self, Lp, S, dbg=False):
        self.Lp, self.S = Lp, S
        self.Ls = 8 * S
        self.Lo = Lp + S
        self.dbg = dbg
        self.nc = bass.Bass("TRN2", target_bir_lowering=False)

    def dram_in(self, name, shape, dt=F32):
        return self.nc.dram_tensor(name, list(shape), dt, kind="ExternalInput").ap()

    def dram_out(self, name, shape, dt=F32):
        return self.nc.dram_tensor(name, list(shape), dt, kind="ExternalOutput").ap()

    def dram_scr(self, name, shape, dt):
        kind = "ExternalOutput" if (self.dbg and name.split("_")[0] in str(self.dbg)) else "Internal"
        return T(self.nc.dram_tensor(name, list(shape), dt, kind=kind).ap())

    _uid = 0

    def sb(self, st, name, shape, dt):
        Builder._uid += 1
        return T(st.enter_context(self.nc.sbuf_tensor(f"sb{Builder._uid}_{name}", list(shape), dt)))

    def ps(self, st, name, shape, dt):
        Builder._uid += 1
        nbytes = int(np.prod(shape[1:])) * (4 if dt == F32 else 2)
        assert nbytes == 2048, (name, shape)
        t = T(st.enter_context(self.nc.psum_tensor(f"ps{Builder._uid}_{name}", list(shape), dt)))
        t.b.ex = True
        return t

    def load(self, out, in_, r=(), w=()):
        self.P.op("sp", lambda e: e.dma_start(out=out, in_=in_), r, w, dma=True)

    def store(self, out, in_, r=(), w=()):
        self.P.op("pool", lambda e: e.dma_start(out=out, in_=in_), r, w, dma=True)

    def mm(self, out, lhsT, rhs, start, stop, r=(), w=()):
        self.P.op("pe", lambda e: e.matmul(out, lhsT=lhsT, rhs=rhs, start=start, stop=stop), r, w)

    def tr(self, out, in_, ident, r=(), w=()):
        self.P.op("pe", lambda e: e.transpose(out, in_, ident), r, w)

    def act(self, out, in_, func, r=(), w=(), eng="act", **kw):
        self.P.op(eng, lambda e: e.activation(out=out, in_=in_, func=func, **kw), r, w)

    def tt(self, out, in0, in1, op, r=(), w=(), eng="dve"):
        self.P.op(eng, lambda e: e.tensor_tensor(out=out, in0=in0, in1=in1, op=op), r, w)

    def ts(self, out, in0, s1, s2, op0, op1=None, r=(), w=(), eng="dve"):
        if op1 is None:
            self.P.op(eng, lambda e: e.tensor_scalar(out=out, in0=in0, scalar1=s1, scalar2=None, op0=op0), r, w)
        else:
            self.P.op(eng, lambda e: e.tensor_scalar(out=out, in0=in0, scalar1=s1, scalar2=s2, op0=op0, op1=op1), r, w)

    def stt(self, out, in0, scalar, in1, op0, op1, r=(), w=()):
        self.P.op("dve", lambda e: e.scalar_tensor_tensor(out=out, in0=in0, scalar=scalar, in1=in1, op0=op0, op1=op1), r, w)

    def cp(self, out, in_, r=(), w=(), eng="dve"):
        if eng == "act":
            self.P.op("act", lambda e: e.activation(out=out, in_=in_, func=AF.Copy), r, w)
        elif eng == "dve":
            self.P.op("dve", lambda e: e.tensor_scalar(out=out, in0=in_, scalar1=1.0, scalar2=None, op0=ALU.mult), r, w)
        else:
            self.P.op(eng, lambda e: e.tensor_copy(out=out, in_=in_), r, w)

    def recip(self, out, in_, r=(), w=()):
        self.P.op("dve", lambda e: e.reciprocal(out=out, in_=in_), r, w)

    def memset(self, ap, val, r=(), w=(), eng="dve"):
        self.P.op(eng, lambda e: e.memset(ap, val), r, w)

    def rstd(self, v, n, inv_n, r=(), w=()):
        self.ts(v[:, 0:n], v[:, 0:n], inv_n, EPS, ALU.mult, ALU.add, r=list(r) + [v], w=[v])
        self.act(v[:, 0:n], v[:, 0:n], AF.Sqrt, r=[v], w=[v])
        self.recip(v[:, 0:n], v[:, 0:n], r=[v], w=list(w) + [v])

    def build(self):
        nc = self.nc
        Lp, S, Ls, Lo = self.Lp, self.S, self.Ls, self.Lo
        I = {}
        I["xp"] = self.dram_in("xp", [Lp, D])
        I["xs"] = self.dram_in("xs", [Ls, D])
        I["memp"] = self.dram_in("memp", [256, D])
        I["mems"] = self.dram_in("mems", [256, D])
        I["csp"] = self.dram_in("csp", [Lp, 128])
        I["css"] = self.dram_in("css", [Ls, 128])
        I["mf"] = self.dram_in("mf", [1, 7 * S])
        I["mb"] = self.dram_in("mb", [1, 7 * S])
        I["w_in"] = self.dram_in("w_in", [D, IN_W])
        I["w_glu"] = self.dram_in("w_glu", [512, 512])
        I["w_mem_kv"] = self.dram_in("w_mem_kv", [D, 1024])
        I["w_pa"] = self.dram_in("w_pa", [1024, D])
        I["w_ps"] = self.dram_in("w_ps", [512, D])
        I["w_px"] = self.dram_in("w_px", [512, D])
        I["w_out"] = self.dram_in("w_out", [D, D])
        I["g_in"] = self.dram_in("g_in", [128, 8])
        I["g_mem"] = self.dram_in("g_mem", [128, 8])
        I["g_q"] = self.dram_in("g_q", [128, 128])
        I["g_k"] = self.dram_in("g_k", [128, 128])
        I["g_f"] = self.dram_in("g_f", [128, D])
        I["s5d"] = self.dram_in("s5d", [128, 4])
        I["bglu"] = self.dram_in("bglu", [128, 4])
        I["are"] = self.dram_in("are", [128, 64])
        I["aim"] = self.dram_in("aim", [128, 64])
        I["lst"] = self.dram_in("lst", [128, 64])
        I["B1"] = self.dram_in("B1", [64, 128, 128])
        I["B2"] = self.dram_in("B2", [64, 128, 128])
        I["C1"] = self.dram_in("C1", [64, 128, 16])
        I["C2"] = self.dram_in("C2", [64, 128, 16])
        I["ident"] = self.dram_in("ident", [128, 128])
        I["swap"] = self.dram_in("swap", [128, 128])
        I["iota1"] = self.dram_in("iota1", [128, TS])
        I["sgn"] = self.dram_in("sgn", [128, 2])
        self.I = I
        self.y_out = self.dram_out("y", [Lo, D])

        self.W = {
            "w_in": self.dram_scr("wb_in", [D, IN_W], BF16),
            "w_glu": self.dram_scr("wb_glu", [512, 512], BF16),
            "w_mem_kv": self.dram_scr("wb_mkv", [D, 1024], BF16),
            "w_pa": self.dram_scr("wb_pa", [1024, D], BF16),
            "w_ps": self.dram_scr("wb_ps", [512, D], BF16),
            "w_px": self.dram_scr("wb_px", [512, D], BF16),
            "w_out": self.dram_scr("wb_out", [D, D], BF16),
        }
        self.KT = {"p": self.dram_scr("KT_p", [2, 128, Lp], BF16),
                   "s": self.dram_scr("KT_s", [2, 128, Ls], BF16)}
        self.VA = {"p": self.dram_scr("VA_p", [2, 128, Lp // 128, 129], BF16),
                   "s": self.dram_scr("VA_s", [2, 128, Ls // 128, 129], BF16)}
        self.UT = {"p": self.dram_scr("UT_p", [512, Lp], BF16),
                   "sf": self.dram_scr("UT_sf", [512, Ls], BF16),
                   "sb": self.dram_scr("UT_sb", [512, Ls], BF16)}
        self.YS = [self.dram_scr("YS_f", [512, Lo], F32), self.dram_scr("YS_b", [512, Lo], F32)]

        with ExitStack() as gst:
            self.P = Prog(nc, gst)
            self.gst = gst
            self.consts(gst)
            phases = [self.phase0, self.phase1, self.phase2, self.phase3]
            stop = getattr(self, "stop", 3)
            for i, ph in enumerate(phases):
                with ExitStack() as st:
                    ph(st)
                    self.P.emit(last=(i == stop))
                if i == stop:
                    break
                self.P.barrier()
        return nc

    def consts(self, st):
        I = self.I
        self.ident_f = self.sb(st, "ident_f", [128, 128], F32)
        self.ident_b = self.sb(st, "ident_b", [128, 128], BF16)
        self.swap_f = self.sb(st, "swap_f", [128, 128], F32)
        self.ones_b = self.sb(st, "ones_b", [128, 128], BF16)
        self.g_in = self.sb(st, "g_in", [128, 8], F32)
        self.g_mem = self.sb(st, "g_mem", [128, 8], F32)
        self.g_q = self.sb(st, "g_q", [128, 128], F32)
        self.g_k = self.sb(st, "g_k", [128, 128], F32)
        self.s5d = self.sb(st, "s5d", [128, 4], F32)
        self.bglu = self.sb(st, "bglu", [128, 4], F32)
        self.sgn = self.sb(st, "sgn", [128, 2], F32)
        self.halfpi = self.sb(st, "halfpi", [128, 1], F32)
        self.KmT = {k: self.sb(st, "KmT" + k, [128, 4, 256], BF16) for k in "ps"}
        self.Vm = {k: self.sb(st, "Vm" + k, [128, 2, 512], BF16) for k in "ps"}
        for t, n in ((self.ident_f, "ident"), (self.swap_f, "swap"), (self.g_in, "g_in"),
                     (self.g_mem, "g_mem"), (self.g_q, "g_q"), (self.g_k, "g_k"),
                     (self.s5d, "s5d"), (self.bglu, "bglu"), (self.sgn, "sgn")):
            self.load(t[:], I[n][:, :], w=[t])
        self.cp(self.ident_b[:], self.ident_f[:], r=[self.ident_f], w=[self.ident_b])
        self.memset(self.ones_b[:], 1.0, w=[self.ones_b])
        self.memset(self.halfpi[:], math.pi / 2.0, w=[self.halfpi])

    def make_hT(self, x_ap_rows, xt, ss, xn, ptr, hT, gain, nblk=4, mask=None):
        self.load(xt[:, 0:nblk, :], x_ap_rows.rearrange("(b p) d -> p b d", p=128), w=[xt])
        for b in range(nblk):
            self.act(xn[:, b, :], xt[:, b, :], AF.Square, r=[xt], w=[xn, ss],
                     accum_out=ss[:, b:b + 1])
        import os
        if os.environ.get("DBG_H") == "1":
            return
        self.rstd(ss, nblk, 1.0 / D)
        if os.environ.get("DBG_H") == "2":
            return
        for b in range(nblk):
            if b % 2 == 0:
                self.act(xn[:, b, :], xt[:, b, :], AF.Copy, r=[xt, ss], w=[xn], scale=ss[:, b:b + 1])
            else:
                self.ts(xn[:, b, :], xt[:, b, :], ss[:, b:b + 1], None, ALU.mult, r=[xt, ss], w=[xn])
        if os.environ.get("DBG_H") == "3":
            return
        for j in range(8):
            pt = ptr[j % len(ptr)]
            for b in range(nblk):
                self.tr(pt[:, b * 128:(b + 1) * 128], xn[:, b, j * 128:(j + 1) * 128], self.ident_b[:],
                        r=[xn, self.ident_b], w=[pt])
            if j % 2 == 0:
                self.ts(hT[:, j, 0:nblk * 128], pt[:, 0:nblk * 128], gain[:, j:j + 1], None, ALU.mult,
                        r=[pt, gain], w=[hT])
            else:
                self.act(hT[:, j, 0:nblk * 128], pt[:, 0:nblk * 128], AF.Copy, r=[pt, gain], w=[hT],
                         scale=gain[:, j:j + 1])

    def phase0(self, st):
        I = self.I
        import os
        if os.environ.get("DBG_P0") == "none":
            return
        stg = [self.sb(st, f"wstg{i}", [128, 2048], F32) for i in range(2)]
        stb = [self.sb(st, f"wstb{i}", [128, 2048], BF16) for i in range(2)]
        k = 0
        for name, rows, cols in (("w_in", D, IN_W), ("w_glu", 512, 512), ("w_mem_kv", D, 1024),
                                 ("w_pa", 1024, D), ("w_ps", 512, D), ("w_px", 512, D), ("w_out", D, D)):
            cw = 1920 if cols == IN_W else cols
            for r0 in range(0, rows, 128):
                for c0 in range(0, cols, cw):
                    a, b = stg[k % 2], stb[k % 2]
                    self.load(a[:, 0:cw], I[name][r0:r0 + 128, c0:c0 + cw], w=[a])
                    self.cp(b[:, 0:cw], a[:, 0:cw], r=[a], w=[b], eng="dve" if k % 2 == 0 else "act")
                    self.store(self.W[name][r0:r0 + 128, c0:c0 + cw], b[:, 0:cw], r=[b], w=[self.W[name]])
                    k += 1
        import os
        if os.environ.get("DBG_P0") == "a":
            return
        wm = self.sb(st, "wm", [128, 8, 1024], BF16)
        self.load(wm[:], self.W["w_mem_kv"].h.rearrange("(j p) c -> p j c", p=128), r=[self.W["w_mem_kv"]], w=[wm])
        xt = self.sb(st, "m_xt", [128, 2, D], F32)
        xn = self.sb(st, "m_xn", [128, 2, D], BF16)
        ss = self.sb(st, "m_ss", [128, 4], F32)
        hT = self.sb(st, "m_hT", [128, 8, 256], BF16)
        vtmp = self.sb(st, "m_v", [128, 512], BF16)
        ptr = [self.ps(st, f"m_ptr{i}", [128, 1024], BF16) for i in range(2)]
        pk = [self.ps(st, f"m_pk{i}", [128, 512], F32) for i in range(2)]
        for key, src in (("p", I["memp"]), ("s", I["mems"])):
            self.make_hT(src[:, :], xt, ss, xn, ptr, hT, self.g_mem, nblk=2)
            if os.environ.get("DBG_P0") == "b1":
                continue
            for hx in range(4):
                p = pk[hx % 2]
                for j in range(8):
                    self.mm(p[:, 0:256], wm[:, j, hx * 128:(hx + 1) * 128], hT[:, j, :], j == 0, j == 7,
                            r=[wm, hT], w=[p])
                if os.environ.get("DBG_P0") == "b2":
                    continue
                self.cp(self.KmT[key][:, hx, :], p[:, 0:256], r=[p], w=[self.KmT[key]], eng="act")
            if os.environ.get("DBG_P0") in ("b2", "b3"):
                continue
            for m in range(2):
                p = pk[m % 2]
                for j in range(8):
                    self.mm(p[:, :], hT[:, j, m * 128:(m + 1) * 128], wm[:, j, 512:1024], j == 0, j == 7,
                            r=[wm, hT], w=[p])
                self.cp(self.Vm[key][:, m, :], p[:, :], r=[p], w=[self.Vm[key]], eng=os.environ.get("DBG_VE", "dve"))

    def rope_tables(self, cs, b, gain, tabs, r_extra=()):
        c = cs[:, b, 0:64]
        s = cs[:, b, 64:128]
        g0 = gain[:, 0:64]
        g1 = gain[:, 64:128]
        rr = [cs, gain] + list(r_extra)
        self.tt(tabs[:, 0, :], c, g0, ALU.mult, r=rr, w=[tabs])
        self.tt(tabs[:, 1, :], s, g1, ALU.mult, r=rr, w=[tabs])
        self.tt(tabs[:, 2, :], s, g0, ALU.mult, r=rr, w=[tabs])
        self.tt(tabs[:, 3, :], c, g1, ALU.mult, r=rr, w=[tabs])

    def norm_rope(self, psrc, nh, sq, ssv, xa, t4, tabs, out_bf, rsrc):
        n = nh * 128
        self.act(sq[:, 0:n], psrc, AF.Square, r=rsrc, w=[sq])
        self.P.op("dve", lambda e: e.tensor_reduce(out=ssv[:, 0:nh], in_=sq[:, 0:n].rearrange("p (h d) -> p h d", h=nh),
                                                   axis=AX.X, op=ALU.add), _bufs([sq]), _bufs([ssv]))
        self.rstd(ssv, nh, 1.0 / 128.0)
        xa3 = xa[:, 0:n].rearrange("p (h d) -> p h d", h=nh)
        self.tt(xa3, psrc.rearrange("p (h d) -> p h d", h=nh),
                ssv[:, 0:nh].unsqueeze(2).to_broadcast([128, nh, 128]), ALU.mult, r=list(rsrc) + [ssv], w=[xa])
        x0 = xa[:, 0:n].rearrange("p (h i two) -> p h i two", h=nh, two=2)[:, :, :, 0]
        x1 = xa[:, 0:n].rearrange("p (h i two) -> p h i two", h=nh, two=2)[:, :, :, 1]
        o0 = out_bf[:, 0:nh, :].rearrange("p h (i two) -> p h i two", two=2)[:, :, :, 0]
        o1 = out_bf[:, 0:nh, :].rearrange("p h (i two) -> p h i two", two=2)[:, :, :, 1]

        def tb(i):
            return tabs[:, i, :].unsqueeze(1).to_broadcast([128, nh, 64])
        tv = [t4[:, i, 0:nh * 64].rearrange("p (h i) -> p h i", h=nh) for i in range(4)]
        self.tt(tv[0], x0, tb(0), ALU.mult, r=[xa, tabs], w=[t4])
        self.tt(tv[1], x1, tb(1), ALU.mult, r=[xa, tabs], w=[t4])
        self.tt(tv[2], x0, tb(2), ALU.mult, r=[xa, tabs], w=[t4])
        self.tt(tv[3], x1, tb(3), ALU.mult, r=[xa, tabs], w=[t4])
        self.tt(o0, tv[0], tv[1], ALU.subtract, r=[t4], w=[out_bf])
        self.tt(o1, tv[2], tv[3], ALU.add, r=[t4], w=[out_bf])

    def phase1(self, st):
        I = self.I
        Lp, S, Ls = self.Lp, self.S, self.Ls
        wkv = self.sb(st, "wkv", [128, 8, 512], BF16)
        wu = self.sb(st, "wu", [128, 8, 512], BF16)
        wv = self.W["w_in"].h.rearrange("(j p) c -> p j c", p=128)
        self.load(wkv[:], wv[:, :, C_K:C_K + 512], r=[self.W["w_in"]], w=[wkv])
        self.load(wu[:], wv[:, :, C_U:C_U + 512], r=[self.W["w_in"]], w=[wu])
        xt = [self.sb(st, f"xt{i}", [128, 4, D], F32) for i in range(2)]
        xn = [self.sb(st, f"xn{i}", [128, 4, D], BF16) for i in range(2)]
        ss = [self.sb(st, f"ss{i}", [128, 4], F32) for i in range(2)]
        hT = [self.sb(st, f"hT{i}", [128, 8, 512], BF16) for i in range(2)]
        cs = [self.sb(st, f"cs{i}", [128, 4, 128], F32) for i in range(2)]
        tabs = [self.sb(st, f"tabs{i}", [128, 4, 64], F32) for i in range(2)]
        sq = self.sb(st, "sq", [128, 256], F32)
        kss = [self.sb(st, f"kss{i}", [128, 2], F32) for i in range(2)]
        ka = self.sb(st, "ka", [128, 256], F32)
        t4 = self.sb(st, "t4", [128, 4, 128], F32)
        krot = [self.sb(st, f"krot{i}", [128, 2, 128], BF16) for i in range(2)]
        KTt = [self.sb(st, f"KTt{i}", [128, 2, 512], BF16) for i in range(2)]
        VAt = [self.sb(st, f"VAt{i}", [128, 2, 4, 129], BF16) for i in range(2)]
        UTt = [self.sb(st, f"UTt{i}", [128, 4, 512], BF16) for i in range(2)]
        UTb = [self.sb(st, f"UTb{i}", [128, 4, 512], BF16) for i in range(2)]
        mrow = [self.sb(st, f"mrow{i}", [128, 2, 512], F32) for i in range(2)]
        ptr = [self.ps(st, f"ptr{i}", [128, 1024], BF16) for i in range(2)]
        pkv = [self.ps(st, f"pkv{i}", [128, 512], F32) for i in range(2)]
        pkt = self.ps(st, "pkt", [128, 2, 512], BF16)
        pu = [self.ps(st, f"pu{i}", [128, 512], F32) for i in range(2)]
        for v in VAt:
            self.memset(v[:, :, :, 128:129], 1.0, w=[v])
        it = 0
        for key, xsrc, cssrc, L in (("p", I["xp"], I["csp"], Lp), ("s", I["xs"], I["css"], Ls)):
            for t in range(L // 512):
                t0 = t * 512
                sl = it % 2
                it += 1
                prefix = (key == "s" and t0 < 7 * S)
                self.make_hT(xsrc[t0:t0 + 512, :], xt[sl], ss[sl], xn[sl], ptr, hT[sl], self.g_in)
                self.load(cs[sl][:], cssrc[t0:t0 + 512, :].rearrange("(b p) c -> p b c", p=128), w=[cs[sl]])
                if prefix:
                    self.load(mrow[sl][:, 0, :], I["mf"][0:1, t0:t0 + 512].partition_broadcast(128), w=[mrow[sl]])
                    self.load(mrow[sl][:, 1, :], I["mb"][0:1, t0:t0 + 512].partition_broadcast(128), w=[mrow[sl]])
                for b in range(4):
                    p = pkv[b % 2]
                    for j in range(8):
                        self.mm(p[:, :], hT[sl][:, j, b * 128:(b + 1) * 128], wkv[:, j, :], j == 0, j == 7,
                                r=[hT[sl], wkv], w=[p])
                    self.cp(VAt[sl][:, :, b, 0:128], p[:, 256:512].rearrange("p (h d) -> p h d", h=2),
                            r=[p], w=[VAt[sl]], eng="act")
                    tb_ = tabs[b % 2]
                    self.rope_tables(cs[sl], b, self.g_k, tb_)
                    kr = krot[b % 2]
                    self.norm_rope(p[:, 0:256], 2, sq, kss[b % 2], ka, t4, tb_, kr, [p])
                    for h in range(2):
                        self.tr(pkt[:, h, b * 128:(b + 1) * 128], kr[:, h, :], self.ident_b[:],
                                r=[kr, self.ident_b], w=[pkt])
                self.cp(KTt[sl][:], pkt[:], r=[pkt], w=[KTt[sl]])
                self.store(self.KT[key].h[:, :, t0:t0 + 512].rearrange("h p l -> p h l"), KTt[sl][:],
                           r=[KTt[sl]], w=[self.KT[key]])
                self.store(self.VA[key].h[:, :, t0 // 128:t0 // 128 + 4, :].rearrange("h p b c -> p h b c"),
                           VAt[sl][:], r=[VAt[sl]], w=[self.VA[key]])
                for i in range(4):
                    p = pu[i % 2]
                    for j in range(8):
                        self.mm(p[:, :], wu[:, j, i * 128:(i + 1) * 128], hT[sl][:, j, :], j == 0, j == 7,
                                r=[hT[sl], wu], w=[p])
                    if prefix:
                        self.tt(UTt[sl][:, i, :], p[:, :], mrow[sl][:, 0, :], ALU.mult, r=[p, mrow[sl]], w=[UTt[sl]])
                        self.tt(UTb[sl][:, i, :], p[:, :], mrow[sl][:, 1, :], ALU.mult, r=[p, mrow[sl]], w=[UTb[sl]])
                    else:
                        self.cp(UTt[sl][:, i, :], p[:, :], r=[p], w=[UTt[sl]], eng="act" if i % 2 else "dve")
                if key == "p":
                    self.store(self.UT["p"].h[:, t0:t0 + 512].rearrange("(i p) l -> p i l", p=128), UTt[sl][:],
                               r=[UTt[sl]], w=[self.UT["p"]])
                else:
                    self.store(self.UT["sf"].h[:, t0:t0 + 512].rearrange("(i p) l -> p i l", p=128), UTt[sl][:],
                               r=[UTt[sl]], w=[self.UT["sf"]])
                    self.store(self.UT["sb"].h[:, t0:t0 + 512].rearrange("(i p) l -> p i l", p=128),
                               (UTb if prefix else UTt)[sl][:], r=[(UTb if prefix else UTt)[sl]], w=[self.UT["sb"]])

    def phase2(self, st):
        I = self.I
        Lp, S, Ls = self.Lp, self.S, self.Ls
        def gt(name):
            return self.sb(st, name, [128, 64], F32)
        are, aim, lst = gt("are"), gt("aim"), gt("lst")
        for t, n in ((are, "are"), (aim, "aim"), (lst, "lst")):
            self.load(t[:], I[n][:, :], w=[t])
        step, Rv, th, kk, thr, sn, cs_, ab = gt("step"), gt("Rv"), gt("th"), gt("kk"), gt("thr"), gt("sn"), gt("cs_"), gt("ab")
        nr, ni, den, CR, CI, SCI, NSCR, tmp = gt("nr"), gt("ni"), gt("den"), gt("CR"), gt("CI"), gt("SCI"), gt("NSCR"), gt("tmp")
        self.act(step[:], lst[:], AF.Exp, r=[lst], w=[step])
        self.ts(are[:], are[:], -1e-4, None, ALU.min, r=[are], w=[are])
        self.tt(Rv[:], are[:], step[:], ALU.mult, r=[are, step], w=[Rv])
        self.act(Rv[:], Rv[:], AF.Exp, r=[Rv], w=[Rv])
        self.tt(th[:], aim[:], step[:], ALU.mult, r=[aim, step], w=[th])
        self.reduce_angle(th, kk, thr)
        self.sincos(thr, ab, sn, cs_)
        self.tt(nr[:], Rv[:], cs_[:], ALU.mult, r=[Rv, cs_], w=[nr])
        self.ts(nr[:], nr[:], -1.0, None, ALU.add, r=[nr], w=[nr])
        self.tt(ni[:], Rv[:], sn[:], ALU.mult, r=[Rv, sn], w=[ni])
        self.tt(den[:], are[:], are[:], ALU.mult, r=[are], w=[den])
        self.tt(tmp[:], aim[:], aim[:], ALU.mult, r=[aim], w=[tmp])
        self.tt(den[:], den[:], tmp[:], ALU.add, r=[den, tmp], w=[den])
        self.recip(den[:], den[:], r=[den], w=[den])
        self.tt(CR[:], nr[:], are[:], ALU.mult, r=[nr, are], w=[CR])
        self.tt(tmp[:], ni[:], aim[:], ALU.mult, r=[ni, aim], w=[tmp])
        self.tt(CR[:], CR[:], tmp[:], ALU.add, r=[CR, tmp], w=[CR])
        self.tt(CR[:], CR[:], den[:], ALU.mult, r=[CR, den], w=[CR])
        self.tt(CI[:], ni[:], are[:], ALU.mult, r=[ni, are], w=[CI])
        self.tt(tmp[:], nr[:], aim[:], ALU.mult, r=[nr, aim], w=[tmp])
        self.tt(CI[:], CI[:], tmp[:], ALU.subtract, r=[CI, tmp], w=[CI])
        self.tt(CI[:], CI[:], den[:], ALU.mult, r=[CI, den], w=[CI])
        self.ts(SCI[:], CI[:], self.sgn[:, 0:1], None, ALU.mult, r=[CI, self.sgn], w=[SCI])
        self.ts(NSCR[:], CR[:], self.sgn[:, 1:2], None, ALU.mult, r=[CR, self.sgn], w=[NSCR])

        iota1 = self.sb(st, "iota1", [128, TS], F32)
        self.load(iota1[:], I["iota1"][:, :], w=[iota1])
        ones_f = self.sb(st, "ones_f", [128, TS], F32)
        self.memset(ones_f[:], 1.0, w=[ones_f])

        up = self.sb(st, "up", [128, Lp], BF16)
        usf = self.sb(st, "usf", [128, Ls], BF16)
        usb = self.sb(st, "usb", [128, Ls], BF16)

        class Stream:
            pass
        strs = []
        for d in range(2):
            s_ = Stream()
            n = f"s{d}_"
            s_.phi = self.sb(st, n + "phi", [128, TS], F32)
            s_.k2 = self.sb(st, n + "k2", [128, TS], F32)
            s_.sinp = self.sb(st, n + "sinp", [128, TS], F32)
            s_.cosp = self.sb(st, n + "cosp", [128, TS], F32)
            s_.TA = self.sb(st, n + "TA", [128, TS], F32)
            s_.TB = self.sb(st, n + "TB", [128, TS], F32)
            s_.RC = self.sb(st, n + "RC", [128, TS], F32)
            s_.RS = self.sb(st, n + "RS", [128, TS], F32)
            s_.Rd = self.sb(st, n + "Rd", [128, TS], F32)
            s_.rot = self.sb(st, n + "rot", [128, 128], F32)
            s_.rb = self.sb(st, n + "rb", [128, 1], F32)
            s_.Bf = self.sb(st, n + "Bf", [128, 2, 128], F32)
            s_.Bb = self.sb(st, n + "Bb", [128, 2, 128], BF16)
            s_.Cf = self.sb(st, n + "Cf", [128, 2, 16], F32)
            s_.Cb = self.sb(st, n + "Cb", [128, 2, 16], BF16)
            s_.t2 = [self.sb(st, n + f"t2{i}", [128, TS], F32) for i in range(2)]
            s_.dd = [self.sb(st, n + f"dd{i}", [128, TS], F32) for i in range(2)]
            s_.e1 = [self.sb(st, n + f"e1{i}", [128, TS], BF16) for i in range(2)]
            s_.e2 = [self.sb(st, n + f"e2{i}", [128, TS], BF16) for i in range(2)]
            s_.wl = self.sb(st, n + "wl", [128, 1], F32)
            s_.carry = self.sb(st, n + "carry", [128, 1], F32)
            s_.ystg = [self.sb(st, n + f"ystg{i}", [16, 512], F32) for i in range(2)]
            s_.pp = [self.ps(st, n + f"pp{i}", [128, 2 * TS], F32) for i in range(2)]
            s_.pw = self.ps(st, n + "pw", [128, 512], F32)
            s_.prot = self.ps(st, n + "prot", [128, 512], F32)
            s_.pwY = s_.prot
            s_.nchunk = 0
            s_.nstg = 0
            strs.append(s_)

        import os
        DS = os.environ.get("DBG_S5", "")
        for j in range(4):
            if DS and j > 0:
                break
            self.load(up[:], self.UT["p"].h[j * 128:(j + 1) * 128, :], r=[self.UT["p"]], w=[up])
            self.load(usf[:], self.UT["sf"].h[j * 128:(j + 1) * 128, :], r=[self.UT["sf"]], w=[usf])
            self.load(usb[:], self.UT["sb"].h[j * 128:(j + 1) * 128, :], r=[self.UT["sb"]], w=[usb])
            for gl in range(8):
                if DS and gl > 0:
                    break
                g = 8 * j + gl
                for d in range(2):
                    s_ = strs[d]
                    col = d * 32 + g
                    self.s5_prep(s_, col, iota1, ones_f, Rv, thr, CR, CI, SCI, NSCR)
                if DS == "prep":
                    continue
                nT = TS
                sched = [[], []]
                for i in range(Lp // nT):
                    sched[0].append((up, i * nT, False, i * nT))
                    sched[1].append((up, Lp - (i + 1) * nT, True, Lp - (i + 1) * nT))
                seqs = [(sched[0], sched[1])]
                s0, s1 = [], []
                for i in range(Ls // nT):
                    t0 = i * nT
                    s0.append((usf, t0, False, (Lp + t0 - 7 * S) if t0 >= 7 * S else None))
                for i in range(7 * S // nT):
                    s1.append((usb, 7 * S - (i + 1) * nT, True, None))
                for i in range(S // nT):
                    t0 = Ls - (i + 1) * nT
                    s1.append((usb, t0, True, Lp + t0 - 7 * S))
                seqs.append((s0, s1))
                for sq0, sq1 in seqs:
                    for d in range(2):
                        self.memset(strs[d].carry[:], 0.0, w=[strs[d].carry])
                    n = max(len(sq0), len(sq1))
                    for i in range(n):
                        for d, sq_ in ((0, sq0), (1, sq1)):
                            if i < len(sq_):
                                ut, t0, rev, ypos = sq_[i]
                                last = (i == len(sq_) - 1)
                                self.s5_chunk(strs[d], d, g, ut, t0, rev, ypos, last)

    def reduce_angle(self, th, kk, thr):
        self.ts(kk[:], th[:], 1.0 / TWO_PI, None, ALU.mult, r=[th], w=[kk])
        self.ts(kk[:], kk[:], MAGIC, None, ALU.add, r=[kk], w=[kk])
        self.ts(kk[:], kk[:], -MAGIC, None, ALU.add, r=[kk], w=[kk])
        self.stt(thr[:], kk[:], -CW1, th[:], ALU.mult, ALU.add, r=[kk, th], w=[thr])
        self.stt(thr[:], kk[:], -CW2, thr[:], ALU.mult, ALU.add, r=[kk, thr], w=[thr])
        self.ts(thr[:], thr[:], math.pi, -math.pi, ALU.min, ALU.max, r=[thr], w=[thr])

    def sincos(self, thr, ab, sn, cs_):
        self.act(sn[:], thr[:], AF.Sin, r=[thr], w=[sn])
        self.act(ab[:], thr[:], AF.Sin, r=[thr], w=[ab], scale=0.5)
        self.tt(cs_[:], ab[:], ab[:], ALU.mult, r=[ab], w=[cs_])
        self.ts(cs_[:], cs_[:], -2.0, 1.0, ALU.mult, ALU.add, r=[cs_], w=[cs_])

    def s5_prep(self, s_, col, iota1, ones_f, Rv, thr, CR, CI, SCI, NSCR):
        I = self.I
        c1 = slice(col, col + 1)
        self.load(s_.Bf[:, 0, :], I["B1"][col, :, :], w=[s_.Bf])
        self.load(s_.Bf[:, 1, :], I["B2"][col, :, :], w=[s_.Bf])
        self.load(s_.Cf[:, 0, :], I["C1"][col, :, :], w=[s_.Cf])
        self.load(s_.Cf[:, 1, :], I["C2"][col, :, :], w=[s_.Cf])
        self.cp(s_.Bb[:], s_.Bf[:], r=[s_.Bf], w=[s_.Bb], eng="act")
        self.cp(s_.Cb[:], s_.Cf[:], r=[s_.Cf], w=[s_.Cb], eng="act")
        self.ts(s_.phi[:], iota1[:], thr[:, c1], None, ALU.mult, r=[iota1, thr], w=[s_.phi])
        self.reduce_angle(s_.phi, s_.k2, s_.phi)
        self.sincos(s_.phi, s_.k2, s_.sinp, s_.cosp)
        self.ts(s_.TA[:], s_.cosp[:], CR[:, c1], None, ALU.mult, r=[s_.cosp, CR], w=[s_.TA])
        self.stt(s_.TA[:], s_.sinp[:], CI[:, c1], s_.TA[:], ALU.mult, ALU.add, r=[s_.sinp, CI, s_.TA], w=[s_.TA])
        self.ts(s_.TB[:], s_.cosp[:], SCI[:, c1], None, ALU.mult, r=[s_.cosp, SCI], w=[s_.TB])
        self.stt(s_.TB[:], s_.sinp[:], NSCR[:, c1], s_.TB[:], ALU.mult, ALU.add, r=[s_.sinp, NSCR, s_.TB], w=[s_.TB])
        self.ts(s_.RC[:], s_.cosp[:], self.sgn[:, 1:2], None, ALU.mult, r=[s_.cosp, self.sgn], w=[s_.RC])
        self.act(s_.RS[:], s_.sinp[:], AF.Copy, r=[s_.sinp], w=[s_.RS], scale=-1.0)
        self.act(s_.Rd[:], ones_f[:], AF.Copy, r=[ones_f, Rv], w=[s_.Rd], scale=Rv[:, c1])
        self.ts(s_.rb[:], s_.sinp[:, TS - 1:TS], self.sgn[:, 1:2], None, ALU.mult, r=[s_.sinp, self.sgn], w=[s_.rb])
        self.ts(s_.rot[:], self.ident_f[:], s_.cosp[:, TS - 1:TS], None, ALU.mult, r=[self.ident_f, s_.cosp], w=[s_.rot])
        self.stt(s_.rot[:], self.swap_f[:], s_.rb[:, 0:1], s_.rot[:], ALU.mult, ALU.add,
                 r=[self.swap_f, s_.rb, s_.rot], w=[s_.rot])

    def s5_chunk(self, s_, d, g, ut, t0, rev, ypos, last):
        k = s_.nchunk % 2
        s_.nchunk += 1
        pp, t2, dd = s_.pp[k], s_.t2[k], s_.dd[k]
        rhs = ut[:, t0:t0 + TS]
        if rev:
            rhs = rev_ap(rhs)
        self.mm(pp[:, 0:TS], s_.Bb[:, 0, :], rhs, True, True, r=[s_.Bb, ut], w=[pp])
        self.mm(pp[:, TS:2 * TS], s_.Bb[:, 1, :], rhs, True, True, r=[s_.Bb, ut], w=[pp])
        import os
        DS = os.environ.get("DBG_S5", "")
        if DS == "mm":
            return
        self.tt(pp[:, 0:TS], pp[:, 0:TS], s_.TA[:], ALU.mult, r=[pp, s_.TA], w=[pp])
        self.tt(t2[:], pp[:, TS:2 * TS], s_.TB[:], ALU.mult, r=[pp, s_.TB], w=[t2])
        self.tt(dd[:], pp[:, 0:TS], t2[:], ALU.add, r=[pp, t2], w=[dd])
        if DS == "demod":
            return
        self.P.op("dve", lambda e: e.tensor_tensor_scan(out=s_.pw[:, 0:TS], data0=s_.Rd[:], data1=dd[:],
                                                        initial=s_.carry[:, 0:1], op0=ALU.mult, op1=ALU.add),
                  _bufs([s_.Rd, dd, s_.carry]), _bufs([s_.pw]))
        if DS == "scan":
            return
        if not last:
            self.cp(s_.wl[:], s_.pw[:, TS - 1:TS], r=[s_.pw], w=[s_.wl], eng="act")
            self.mm(s_.prot[:, 0:1], s_.rot[:], s_.wl[:], True, True, r=[s_.rot, s_.wl], w=[s_.prot])
            self.cp(s_.carry[:], s_.prot[:, 0:1], r=[s_.prot], w=[s_.carry], eng="act")
        if DS == "carry":
            return
        if ypos is not None:
            e1, e2 = s_.e1[k], s_.e2[k]
            self.tt(e1[:], s_.pw[:, 0:TS], s_.RC[:], ALU.mult, r=[s_.pw, s_.RC], w=[e1])
            self.tt(e2[:], s_.pw[:, 0:TS], s_.RS[:], ALU.mult, r=[s_.pw, s_.RS], w=[e2])
            py = s_.prot[0:16, TS:2 * TS]
            self.mm(py, s_.Cb[:, 0, :], e1[:], True, False, r=[s_.Cb, e1], w=[s_.pwY])
            self.mm(py, s_.Cb[:, 1, :], e2[:], False, True, r=[s_.Cb, e2], w=[s_.pwY])
            nper = 512 // TS
            sidx = s_.nstg // nper
            stg = s_.ystg[sidx % 2]
            q = s_.nstg % nper
            s_.nstg += 1
            base = (ypos // 512) * 512
            off = ypos - base
            dst = stg[0:16, off:off + TS]
            if rev:
                dst = rev_ap(dst)
            self.cp(dst, py, r=[s_.pwY], w=[stg], eng="act")
            if q == nper - 1:
                self.store(self.YS[d].h[g * 16:(g + 1) * 16, base:base + 512], stg[0:16, :], r=[stg], w=[self.YS[d]])

    def phase3(self, st):
        I = self.I
        Lp, S, Ls = self.Lp, self.S, self.Ls
        SC = 128.0 ** -0.5
        wv = self.W["w_in"].h.rearrange("(j p) c -> p j c", p=128)
        ws = [self.sb(st, f"ws{i}", [128, 8, 512], BF16) for i in range(3)]
        self.wsi = 0

        def wload(src_ap, rd):
            t = ws[self.wsi % 3]
            self.wsi += 1
            self.load(t[:], src_ap, r=[rd], w=[t])
            return t

        xt = [self.sb(st, f"xt{i}", [128, D], F32) for i in range(2)]
        xn = self.sb(st, "xn", [128, 4, D], BF16)
        ss = self.sb(st, "ss", [128, 4], F32)
        hT = self.sb(st, "hT", [128, 8, 512], BF16)
        cs = self.sb(st, "cs", [128, 4, 128], F32)
        tabs = [self.sb(st, f"tabs{i}", [128, 4, 64], F32) for i in range(2)]
        sq = self.sb(st, "sq", [128, 512], F32)
        qss = [self.sb(st, f"qss{i}", [128, 8], F32) for i in range(2)]
        qa = self.sb(st, "qa", [128, 512], F32)
        t4 = self.sb(st, "t4", [128, 4, 256], F32)
        qrot = [self.sb(st, f"qrot{i}", [128, 8, 128], BF16) for i in range(2)]
        QT = self.sb(st, "QT", [128, 8, 512], BF16)
        GA = self.sb(st, "GA", [128, 8, 512], BF16)
        YA = self.sb(st, "YA", [128, 8, 512], BF16)
        KC = 512
        kts = [self.sb(st, f"kts{i}", [128, KC], BF16) for i in range(3)]
        vas = [self.sb(st, f"vas{i}", [128, KC // 128, 129], BF16) for i in range(3)]
        PT = [self.sb(st, f"PT{i}", [128, 512], BF16) for i in range(2)]
        rcp = self.sb(st, "rcp", [128, 4], F32)
        yn = [self.sb(st, f"yn{i}", [128, 128], BF16) for i in range(2)]
        y0 = [self.sb(st, f"y0_{i}", [128, 512], F32) for i in range(1)] * 2
        y1 = [self.sb(st, f"y1_{i}", [128, 512], F32) for i in range(1)] * 2
        uu = [self.sb(st, f"uu{i}", [128, 512], BF16) for i in range(2)]
        gx = [self.sb(st, f"gx{i}", [128, 512], F32) for i in range(2)]
        g2 = [self.sb(st, f"g2{i}", [128, 512], F32) for i in range(2)]
        YG = self.sb(st, "YG", [128, 4, 512], F32)
        YGb = self.sb(st, "YGb", [128, 4, 512], BF16)
        GS = self.sb(st, "GS", [128, 4, 512], BF16)
        sgl = [self.sb(st, f"sgl{i}", [128, 512], F32) for i in range(2)]
        YSb = self.sb(st, "YSb", [128, 4, 512], BF16)
        wglu = self.sb(st, "wglu", [128, 4, 512], BF16)
        self.load(wglu[:], self.W["w_glu"].h.rearrange("(k p) c -> p k c", p=128), r=[self.W["w_glu"]], w=[wglu])
        QX = self.sb(st, "QX", [128, 4, 512], BF16)
        GX = self.sb(st, "GX", [128, 4, 512], BF16)
        PX = [self.sb(st, f"PX{i}", [128, 512], BF16) for i in range(2)]
        rd = self.sb(st, "rd", [128, 512], F32)
        yx = self.sb(st, "yx", [128, 512], F32)
        YX = self.sb(st, "YX", [128, 4, 512], BF16)
        G3 = self.sb(st, "G3", [128, 3, 4, 512], BF16)
        m = [self.sb(st, f"m{i}", [128, 512], F32) for i in range(3)]
        M = self.sb(st, "M", [128, 8, 512], BF16)
        yres = [self.sb(st, f"yres{i}", [128, D], F32) for i in range(1)] * 2
        fss = [self.sb(st, f"fss{i}", [128, 1], F32) for i in range(2)]
        gf = self.sb(st, "gf", [128, D], F32)
        self.load(gf[:], I["g_f"][:, :], w=[gf])
        bk = [self.ps(st, f"bk{i}", [128, 512], F32) for i in range(8)]

        def bf(b):
            return b[:].bitcast(BF16)

        seqs = [("p", I["xp"], I["csp"], 0, Lp, Lp, 0), ("s", I["xs"], I["css"], 7 * S, S, Ls, Lp)]
        for key, xsrc, cssrc, xoff, nown, Lk, yoff in seqs:
            for t in range(nown // 512):
                t0 = xoff + t * 512
                yo = yoff + t * 512
                self.make_hT_bank(xsrc[t0:t0 + 512, :], xt, ss, xn, [bk[6], bk[7]], hT, self.g_in)
                self.load(cs[:], cssrc[t0:t0 + 512, :].rearrange("(b p) c -> p b c", p=128), w=[cs])
                wq0 = wload(wv[:, :, C_Q:C_Q + 512], self.W["w_in"])
                wq1 = wload(wv[:, :, C_Q + 512:C_Q + 1024], self.W["w_in"])
                for b in range(4):
                    pa, pb, pT = bk[(2 * b) % 4], bk[(2 * b + 1) % 4], bk[4 + b % 2]
                    for j in range(8):
                        self.mm(pa[:, :], hT[:, j, b * 128:(b + 1) * 128], wq0[:, j, :], j == 0, j == 7, r=[hT, wq0], w=[pa])
                    for j in range(8):
                        self.mm(pb[:, :], hT[:, j, b * 128:(b + 1) * 128], wq1[:, j, :], j == 0, j == 7, r=[hT, wq1], w=[pb])
                    tb_ = tabs[b % 2]
                    self.rope_tables(cs, b, self.g_q, tb_)
                    qr = qrot[b % 2]
                    for half, pbank in ((0, pa), (1, pb)):
                        self.norm_rope_q(pbank, half, sq, qss[b % 2], qa, t4, tb_, qr)
                    for h in range(8):
                        self.tr(bf(pT)[:, h * 128:(h + 1) * 128], qr[:, h, :], self.ident_b[:],
                                r=[qr, self.ident_b], w=[pT])
                    self.cp(QT[:, :, b * 128:(b + 1) * 128], bf(pT).rearrange("p (h q) -> p h q", h=8),
                            r=[pT], w=[QT], eng="act")
                for half in range(2):
                    wg = wload(wv[:, :, C_GA + half * 512:C_GA + (half + 1) * 512], self.W["w_in"])
                    for o in range(4):
                        p = bk[o % 2]
                        for j in range(8):
                            self.mm(p[:, :], wg[:, j, o * 128:(o + 1) * 128], hT[:, j, :], j == 0, j == 7, r=[wg, hT], w=[p])
                        self.act(GA[:, half * 4 + o, :], p[:, :], AF.Silu, r=[p], w=[GA])
                nkc = Lk // KC
                for h in range(8):
                    hk = h // 4
                    po = [bk[2 + jj] for jj in range(4)]
                    for c in range(nkc):
                        sl = (h * nkc + c) % 3
                        kt_, va_ = kts[sl], vas[sl]
                        self.load(kt_[:], self.KT[key].h[hk, :, c * KC:(c + 1) * KC], r=[self.KT[key]], w=[kt_])
                        self.load(va_[:], self.VA[key].h[hk, :, c * (KC // 128):(c + 1) * (KC // 128), :],
                                  r=[self.VA[key]], w=[va_])
                        for kk in range(KC // 128):
                            kidx = c * (KC // 128) + kk
                            psb = bk[kidx % 2]
                            pt = PT[kidx % 2]
                            self.mm(psb[:, :], kt_[:, kk * 128:(kk + 1) * 128], QT[:, h, :], True, True,
                                    r=[kt_, QT], w=[psb])
                            self.act(pt[:], psb[:, :], AF.Exp, r=[psb], w=[pt], scale=SC)
                            first, lastk = (kidx == 0), (kidx == Lk // 128 - 1)
                            for jj in range(4):
                                self.mm(po[jj][:, 0:129], pt[:, jj * 128:(jj + 1) * 128], va_[:, kk, :], first, lastk,
                                        r=[pt, va_], w=[po[jj]])
                    pT = bk[6 + h % 2]
                    for jj in range(4):
                        self.recip(rcp[:, jj:jj + 1], po[jj][:, 128:129], r=[po[jj]], w=[rcp])
                        y_ = yn[jj % 2]
                        self.ts(y_[:], po[jj][:, 0:128], rcp[:, jj:jj + 1], None, ALU.mult, r=[po[jj], rcp], w=[y_])
                        self.tr(bf(pT)[:, jj * 128:(jj + 1) * 128], y_[:], self.ident_b[:], r=[y_, self.ident_b], w=[pT])
                    self.tt(YA[:, h, :], bf(pT)[:, 0:512], GA[:, h, :], ALU.mult, r=[pT, GA], w=[YA])
                wgs = wload(wv[:, :, C_GS:C_GS + 512], self.W["w_in"])
                for i in range(4):
                    k2 = i % 2
                    self.load(y0[k2][:], self.YS[0].h[i * 128:(i + 1) * 128, yo:yo + 512], r=[self.YS[0]], w=[y0[k2]])
                    self.load(y1[k2][:], self.YS[1].h[i * 128:(i + 1) * 128, yo:yo + 512], r=[self.YS[1]], w=[y1[k2]])
                    usrc = self.UT["p"] if key == "p" else self.UT["sf"]
                    self.load(uu[k2][:], usrc.h[i * 128:(i + 1) * 128, t0:t0 + 512], r=[usrc], w=[uu[k2]])
                    a, bq = gx[k2], g2[k2]
                    self.tt(a[:], y0[k2][:], y1[k2][:], ALU.add, r=[y0[k2], y1[k2]], w=[a])
                    self.stt(a[:], uu[k2][:], self.s5d[:, i:i + 1], a[:], ALU.mult, ALU.add, r=[uu[k2], self.s5d, a], w=[a])
                    self.tt(bq[:], a[:], a[:], ALU.mult, r=[a], w=[bq])
                    self.ts(bq[:], bq[:], 0.044715, 1.0, ALU.mult, ALU.add, r=[bq], w=[bq])
                    self.tt(bq[:], bq[:], a[:], ALU.mult, r=[bq, a], w=[bq])
                    self.act(bq[:], bq[:], AF.Sigmoid, r=[bq], w=[bq], scale=2.0 * math.sqrt(2.0 / math.pi))
                    self.tt(YG[:, i, :], a[:], bq[:], ALU.mult, r=[a, bq], w=[YG])
                    self.cp(YGb[:, i, :], YG[:, i, :], r=[YG], w=[YGb], eng="act")
                    p = bk[i % 2]
                    for j in range(8):
                        self.mm(p[:, :], wgs[:, j, i * 128:(i + 1) * 128], hT[:, j, :], j == 0, j == 7, r=[wgs, hT], w=[p])
                    self.act(GS[:, i, :], p[:, :], AF.Silu, r=[p], w=[GS])
                for o in range(4):
                    p = bk[2 + o % 2]
                    for k_ in range(4):
                        self.mm(p[:, :], wglu[:, k_, o * 128:(o + 1) * 128], YGb[:, k_, :], k_ == 0, k_ == 3,
                                r=[wglu, YGb], w=[p])
                    s_ = sgl[o % 2]
                    self.act(s_[:], p[:, :], AF.Sigmoid, r=[p, self.bglu], w=[s_], bias=self.bglu[:, o:o + 1])
                    self.tt(s_[:], s_[:], YG[:, o, :], ALU.mult, r=[s_, YG], w=[s_])
                    self.tt(YSb[:, o, :], s_[:], GS[:, o, :], ALU.mult, r=[s_, GS], w=[YSb])
                wqx = wload(wv[:, :, C_QX:C_QX + 512], self.W["w_in"])
                wgx = wload(wv[:, :, C_GX:C_GX + 512], self.W["w_in"])
                for o in range(4):
                    p = bk[o % 2]
                    for j in range(8):
                        self.mm(p[:, :], wqx[:, j, o * 128:(o + 1) * 128], hT[:, j, :], j == 0, j == 7, r=[wqx, hT], w=[p])
                    self.cp(QX[:, o, :], p[:, :], r=[p], w=[QX], eng="act")
                    p2 = bk[2 + o % 2]
                    for j in range(8):
                        self.mm(p2[:, :], wgx[:, j, o * 128:(o + 1) * 128], hT[:, j, :], j == 0, j == 7, r=[wgx, hT], w=[p2])
                    self.act(GX[:, o, :], p2[:, :], AF.Silu, r=[p2], w=[GX])
                KmT, Vm = self.KmT[key], self.Vm[key]
                for hx in range(4):
                    for mt in range(2):
                        p = bk[mt]
                        self.mm(p[:, :], KmT[:, hx, mt * 128:(mt + 1) * 128], QX[:, hx, :], True, True, r=[KmT, QX], w=[p])
                        self.act(PX[mt][:], p[:, :], AF.Exp, r=[p], w=[PX[mt]], scale=SC)
                    po_, pd_ = bk[4], bk[5]
                    for mt in range(2):
                        self.mm(po_[:, :], Vm[:, mt, hx * 128:(hx + 1) * 128], PX[mt][:], mt == 0, mt == 1, r=[Vm, PX[mt]], w=[po_])
                    for mt in range(2):
                        self.mm(pd_[:, :], self.ones_b[:], PX[mt][:], mt == 0, mt == 1, r=[self.ones_b, PX[mt]], w=[pd_])
                    self.recip(rd[:], pd_[:, :], r=[pd_], w=[rd])
                    self.tt(yx[:], po_[:, :], rd[:], ALU.mult, r=[po_, rd], w=[yx])
                    self.tt(YX[:, hx, :], yx[:], GX[:, hx, :], ALU.mult, r=[yx, GX], w=[YX])
                wpa = self.W["w_pa"].h.rearrange("(k p) c -> p k c", p=128)
                wps = self.W["w_ps"].h.rearrange("(k p) c -> p k c", p=128)
                wpx = self.W["w_px"].h.rearrange("(k p) c -> p k c", p=128)
                for og in range(2):
                    for br in range(3):
                        wm_ = wload(wv[:, :, C_MG + br * 1024 + og * 512:C_MG + br * 1024 + (og + 1) * 512], self.W["w_in"])
                        for o in range(4):
                            p = bk[o % 2]
                            for j in range(8):
                                self.mm(p[:, :], wm_[:, j, o * 128:(o + 1) * 128], hT[:, j, :], j == 0, j == 7, r=[wm_, hT], w=[p])
                            self.act(G3[:, br, o, :], p[:, :], AF.Sigmoid, r=[p], w=[G3])
                    wa = wload(wpa[:, :, og * 512:(og + 1) * 512], self.W["w_pa"])
                    wsx = ws[self.wsi % 3]
                    self.wsi += 1
                    self.load(wsx[:, 0:4, :], wps[:, :, og * 512:(og + 1) * 512], r=[self.W["w_ps"]], w=[wsx])
                    self.load(wsx[:, 4:8, :], wpx[:, :, og * 512:(og + 1) * 512], r=[self.W["w_px"]], w=[wsx])
                    for o in range(4):
                        pa_, ps_, px_ = bk[2 + (o % 2) * 3], bk[3 + (o % 2) * 3], bk[4 + (o % 2) * 3]
                        for k_ in range(8):
                            self.mm(pa_[:, :], wa[:, k_, o * 128:(o + 1) * 128], YA[:, k_, :], k_ == 0, k_ == 7, r=[wa, YA], w=[pa_])
                        for k_ in range(4):
                            self.mm(ps_[:, :], wsx[:, k_, o * 128:(o + 1) * 128], YSb[:, k_, :], k_ == 0, k_ == 3, r=[wsx, YSb], w=[ps_])
                        for k_ in range(4):
                            self.mm(px_[:, :], wsx[:, 4 + k_, o * 128:(o + 1) * 128], YX[:, k_, :], k_ == 0, k_ == 3, r=[wsx, YX], w=[px_])
                        self.tt(m[0][:], pa_[:, :], G3[:, 0, o, :], ALU.mult, r=[pa_, G3], w=[m[0]])
                        self.tt(m[1][:], ps_[:, :], G3[:, 1, o, :], ALU.mult, r=[ps_, G3], w=[m[1]])
                        self.tt(m[2][:], px_[:, :], G3[:, 2, o, :], ALU.mult, r=[px_, G3], w=[m[2]])
                        self.tt(m[0][:], m[0][:], m[1][:], ALU.add, r=[m[0], m[1]], w=[m[0]])
                        self.tt(M[:, og * 4 + o, :], m[0][:], m[2][:], ALU.add, r=[m[0], m[2]], w=[M])
                wo_ = self.W["w_out"].h.rearrange("(k p) c -> p k c", p=128)
                wo0 = wload(wo_[:, :, 0:512], self.W["w_out"])
                wo1 = wload(wo_[:, :, 512:1024], self.W["w_out"])
                for b in range(4):
                    pa_, pb_ = bk[(2 * b) % 4], bk[(2 * b + 1) % 4]
                    for k_ in range(8):
                        self.mm(pa_[:, :], M[:, k_, b * 128:(b + 1) * 128], wo0[:, k_, :], k_ == 0, k_ == 7, r=[M, wo0], w=[pa_])
                    for k_ in range(8):
                        self.mm(pb_[:, :], M[:, k_, b * 128:(b + 1) * 128], wo1[:, k_, :], k_ == 0, k_ == 7, r=[M, wo1], w=[pb_])
                    yr, fs = yres[b % 2], fss[b % 2]
                    xb = xt[b % 2]
                    self.load(xb[:], xsrc[t0 + b * 128:t0 + (b + 1) * 128, :], w=[xb])
                    self.tt(yr[:, 0:512], pa_[:, :], xb[:, 0:512], ALU.add, r=[pa_, xb], w=[yr])
                    self.tt(yr[:, 512:1024], pb_[:, :], xb[:, 512:1024], ALU.add, r=[pb_, xb], w=[yr])
                    self.act(xn[:, 0, :], yr[:], AF.Square, r=[yr], w=[xn, fs], accum_out=fs[:, 0:1])
                    self.rstd(fs, 1, 1.0 / D)
                    self.stt(yr[:], yr[:], fs[:, 0:1], gf[:], ALU.mult, ALU.mult, r=[yr, fs, gf], w=[yr])
                    self.store(self.y_out[yo + b * 128:yo + (b + 1) * 128, :], yr[:], r=[yr])

    def make_hT_bank(self, x_rows, xt, ss, xn, banks, hT, gain):
        for b in range(4):
            xb = xt[b % 2]
            sb_ = ss[b % 2] if isinstance(ss, list) else ss
            self.load(xb[:], x_rows[b * 128:(b + 1) * 128, :], w=[xb])
            self.act(xn[:, b, :], xb[:], AF.Square, r=[xb], w=[xn, ss], accum_out=ss[:, b:b + 1])
            v = ss[:, b:b + 1]
            self.ts(v, v, 1.0 / D, EPS, ALU.mult, ALU.add, r=[ss], w=[ss])
            self.act(v, v, AF.Sqrt, r=[ss], w=[ss])
            self.recip(v, v, r=[ss], w=[ss])
            if b % 2 == 0:
                self.act(xn[:, b, :], xb[:], AF.Copy, r=[xb, ss], w=[xn], scale=ss[:, b:b + 1])
            else:
                self.ts(xn[:, b, :], xb[:], ss[:, b:b + 1], None, ALU.mult, r=[xb, ss], w=[xn])
        for j in range(8):
            bank = banks[j % len(banks)]
            pv = bank[:].bitcast(BF16)
            for b in range(4):
                self.tr(pv[:, b * 128:(b + 1) * 128], xn[:, b, j * 128:(j + 1) * 128], self.ident_b[:],
                        r=[xn, self.ident_b], w=[bank])
            if j % 2 == 0:
                self.ts(hT[:, j, :], pv[:, 0:512], gain[:, j:j + 1], None, ALU.mult, r=[bank, gain], w=[hT])
            else:
                self.act(hT[:, j, :], pv[:, 0:512], AF.Copy, r=[bank, gain], w=[hT], scale=gain[:, j:j + 1])

    def norm_rope_q(self, pbank, half, sq, ssv, xa, t4, tabs, out_bf):
        nh = 4
        o = 0
        h0 = half * 4
        psrc = pbank[:, :]
        self.act(sq[:, o:o + 512], psrc, AF.Square, r=[pbank], w=[sq])
        self.P.op("dve", lambda e: e.tensor_reduce(out=ssv[:, h0:h0 + 4], in_=sq[:, o:o + 512].rearrange("p (h d) -> p h d", h=nh),
                                                   axis=AX.X, op=ALU.add), _bufs([sq]), _bufs([ssv]))
        v = ssv[:, h0:h0 + 4]
        self.ts(v, v, 1.0 / 128.0, EPS, ALU.mult, ALU.add, r=[ssv], w=[ssv])
        self.act(v, v, AF.Sqrt, r=[ssv], w=[ssv])
        self.recip(v, v, r=[ssv], w=[ssv])
        xa3 = xa[:, o:o + 512].rearrange("p (h d) -> p h d", h=nh)
        self.tt(xa3, psrc.rearrange("p (h d) -> p h d", h=nh), v.unsqueeze(2).to_broadcast([128, nh, 128]),
                ALU.mult, r=[pbank, ssv], w=[xa])
        x0 = xa[:, o:o + 512].rearrange("p (h i two) -> p h i two", h=nh, two=2)[:, :, :, 0]
        x1 = xa[:, o:o + 512].rearrange("p (h i two) -> p h i two", h=nh, two=2)[:, :, :, 1]
        ob = out_bf[:, h0:h0 + 4, :].rearrange("p h (i two) -> p h i two", two=2)
        o0, o1 = ob[:, :, :, 0], ob[:, :, :, 1]

        def tb(i):
            return tabs[:, i, :].unsqueeze(1).to_broadcast([128, nh, 64])
        tv = [t4[:, i, 0:256].rearrange("p (h i) -> p h i", h=nh) for i in range(4)]
        self.tt(tv[0], x0, tb(0), ALU.mult, r=[xa, tabs], w=[t4])
        self.tt(tv[1], x1, tb(1), ALU.mult, r=[xa, tabs], w=[t4])
        self.tt(tv[2], x0, tb(2), ALU.mult, r=[xa, tabs], w=[t4])
        self.tt(tv[3], x1, tb(3), ALU.mult, r=[xa, tabs], w=[t4])
        self.tt(o0, tv[0], tv[1], ALU.subtract, r=[t4], w=[out_bf])
        self.tt(o1, tv[2], tv[3], ALU.add, r=[t4], w=[out_bf])


def rope_table(pos):
    pos = np.asarray(pos)
    row = (pos // 64).astype(np.float32)
    col = (pos % 64).astype(np.float32)
    freqs = (np.float32(10000.0) ** (-np.arange(32, dtype=np.float32) / np.float32(32))).astype(np.float32)
    ang = np.concatenate([row[:, None] * freqs, col[:, None] * freqs], axis=-1).astype(np.float32)
    return np.concatenate([np.cos(ang), np.sin(ang)], axis=-1).astype(np.float32)


def host_inputs(inp, Lp, S, ncores=NCORES):
    f = lambda a: np.ascontiguousarray(np.asarray(a, dtype=np.float32))
    Ls = 8 * S
    xs_all = f(inp["x_sample"])[0]
    shared = {}
    shared["w_in"] = f(inp["w_in"])[0]
    shared["w_glu"] = f(inp["w_glu"])[0]
    shared["w_mem_kv"] = f(inp["w_mem_kv"])[0]
    shared["w_pa"] = f(inp["w_proj_attn"])[0]
    shared["w_ps"] = f(inp["w_proj_ssm"])[0]
    shared["w_px"] = f(inp["w_proj_cross"])[0]
    shared["w_out"] = f(inp["w_out"])[0]
    shared["g_in"] = f(f(inp["norm_in"])[0].reshape(8, 128).T)
    shared["g_mem"] = f(f(inp["norm_mem"])[0].reshape(8, 128).T)
    qn, kn = f(inp["q_norm"])[0], f(inp["k_norm"])[0]
    shared["g_q"] = f(np.tile(np.concatenate([qn[0::2], qn[1::2]])[None, :], (128, 1)))
    shared["g_k"] = f(np.tile(np.concatenate([kn[0::2], kn[1::2]])[None, :], (128, 1)))
    shared["g_f"] = f(np.tile(f(inp["norm_final"])[None, :], (128, 1)))
    shared["s5d"] = f(f(inp["s5_d"])[0].reshape(4, 128).T)
    shared["bglu"] = f(f(inp["b_glu"])[0].reshape(4, 128).T)
    a_re, a_im = f(inp["s5_a_re"])[0], f(inp["s5_a_im"])[0]
    dup = lambda a: f(np.concatenate([a.reshape(64, 64).T, a.reshape(64, 64).T], axis=0))
    shared["are"] = dup(a_re)
    shared["aim"] = dup(a_im)
    shared["lst"] = f(np.tile(f(inp["s5_log_step"])[0].reshape(1, 64), (128, 1)))
    b_re, b_im = f(inp["s5_b_re"])[0], f(inp["s5_b_im"])[0]
    c_re, c_im = f(inp["s5_c_re"])[0], f(inp["s5_c_im"])[0]
    B1 = np.zeros((64, 128, 128), np.float32)
    B2 = np.zeros((64, 128, 128), np.float32)
    C1 = np.zeros((64, 128, 16), np.float32)
    C2 = np.zeros((64, 128, 16), np.float32)
    for d in range(2):
        for g in range(32):
            col = d * 32 + g
            r0 = (g % 8) * 16
            B1[col, r0:r0 + 16, 0:64] = b_re[d, g].T
            B1[col, r0:r0 + 16, 64:128] = b_im[d, g].T
            B2[col, r0:r0 + 16, 0:64] = b_im[d, g].T
            B2[col, r0:r0 + 16, 64:128] = b_re[d, g].T
            C1[col, 0:64, :] = c_re[d, g].T
            C1[col, 64:128, :] = c_im[d, g].T
            C2[col, 0:64, :] = c_im[d, g].T
            C2[col, 64:128, :] = c_re[d, g].T
    shared.update(B1=B1, B2=B2, C1=C1, C2=C2)
    shared["ident"] = np.eye(128, dtype=np.float32)
    sw = np.zeros((128, 128), np.float32)
    sw[np.arange(128), (np.arange(128) + 64) % 128] = 1.0
    shared["swap"] = sw
    shared["iota1"] = f(np.tile(np.arange(1, TS + 1, dtype=np.float32)[None, :], (128, 1)))
    sg = np.ones((128, 2), np.float32)
    sg[0:64, 0] = -1.0
    sg[64:128, 1] = -1.0
    shared["sgn"] = sg
    shared["csp"] = rope_table(np.arange(Lp))
    maps = []
    for c in range(ncores):
        m = dict(shared)
        m["xp"] = f(inp["x_prompt"])[c]
        order = np.concatenate([np.arange((c + 1) * S, Ls), np.arange(0, c * S), np.arange(c * S, (c + 1) * S)])
        m["xs"] = np.ascontiguousarray(xs_all[order])
        m["css"] = rope_table(order)
        m["memp"] = f(inp["mem_prompt"])[c]
        m["mems"] = f(inp["mem_sample"])[0]
        mf = np.zeros((1, 7 * S), np.float32)
        mf[0, (7 - c) * S:] = 1.0
        m["mf"] = mf
        m["mb"] = (1.0 - mf).astype(np.float32)
        maps.append(m)
    return maps


_NC_CACHE = {}


def run(inp, Lp, S):
    key = (Lp, S)
    if key not in _NC_CACHE:
        _NC_CACHE[key] = Builder(Lp, S).build()
    nc = _NC_CACHE[key]
    maps = host_inputs(inp, Lp, S)
    res = run_bass_kernel_spmd(nc, maps, core_ids=list(range(NCORES)))
    ys = [np.asarray(r["y"]) for r in res.results]
    y_prompt = np.stack([y[:Lp] for y in ys], axis=0).astype(np.float32)
    y_sample = np.concatenate([y[Lp:Lp + S] for y in ys], axis=0)[None].astype(np.float32)
    return y_prompt, y_sample


def kernel(**inputs):
    Lp = int(np.asarray(inputs["x_prompt"]).shape[1])
    Ls = int(np.asarray(inputs["x_sample"]).shape[1])
    return run(inputs, Lp, Ls // 8)
```

```python
import math
from contextlib import ExitStack

import numpy as np
import concourse.bass as bass
import concourse.mybir as mybir
from concourse.bass_utils import run_bass_kernel_spmd

F32 = mybir.dt.float32
BF16 = mybir.dt.bfloat16
AF = mybir.ActivationFunctionType
ALU = mybir.AluOpType
AX = mybir.AxisListType

D = 1024
IN_W = 7680
C_Q, C_K, C_V, C_GA, C_U, C_GS, C_QX, C_GX, C_MG = 0, 1024, 1280, 1536, 2560, 3072, 3584, 4096, 4608
EPS = 1e-6
NCORES = 8
TS = 256
MAGIC = 12582912.0
TWO_PI = 2.0 * math.pi
CW1 = 6.28125
CW2 = TWO_PI - CW1
SEM_CH = 20000
N_DMA_SEMS = 40


class Buf:
    __slots__ = ("lw", "rd", "ex")

    def __init__(self):
        self.lw = None
        self.rd = []
        self.ex = False


class T:
    def __init__(self, h, b=None):
        self.h = h
        self.b = b if b is not None else Buf()

    def __getitem__(self, k):
        return self.h[k]


def _bufs(xs):
    out = []
    for x in xs:
        if x is None:
            continue
        out.append(x.b if isinstance(x, T) else x)
    return out


class Prog:
    ENGS = ("pe", "act", "dve", "pool", "sp")

    def __init__(self, nc, stack):
        self.nc = nc
        self.stack = stack
        self.q = {e: [] for e in self.ENGS}
        self.cnt = {e: 0 for e in self.ENGS}
        self.sems = {e: [] for e in self.ENGS}
        self.dsems = []
        self.dcnt = []
        self.drr = 0
        self.n_dma = 0
        self.pend = {e: [] for e in self.ENGS}
        self.waited = {e: {} for e in self.ENGS}

    def _sem(self, e, idx):
        k = idx // SEM_CH
        while len(self.sems[e]) <= k:
            self.sems[e].append(self.stack.enter_context(
                self.nc.semaphore(f"s_{e}_{len(self.sems[e])}")))
        return self.sems[e][k], idx % SEM_CH + 1

    def op(self, e, fn, reads=(), writes=(), dma=False):
        reads = _bufs(reads)
        writes = _bufs(writes)
        exr = [b for b in reads if b.ex]
        if exr:
            reads = [b for b in reads if not b.ex]
            writes = writes + [b for b in exr if b not in writes]
        deps = {}

        def add(d):
            if d is None:
                return
            if d[0] not in deps or deps[d[0]][1] < d[1]:
                deps[d[0]] = d
        for b in reads:
            add(b.lw)
        for b in writes:
            add(b.lw)
            for r in b.rd:
                add(r)
        idx = self.cnt[e]
        if not dma:
            self.cnt[e] += 1
        waits = list(self.pend[e])
        self.pend[e] = []
        for key, d in deps.items():
            if key == "pe" and e == "pe":
                continue
            waits.append((d[2], d[3]))
        wd = self.waited[e]
        ww = []
        for (ws, wv) in waits:
            if wd.get(id(ws), 0) >= wv:
                continue
            wd[id(ws)] = wv
            ww.append((ws, wv))
        waits = ww
        if dma:
            if len(self.dsems) < N_DMA_SEMS:
                self.dsems.append(self.stack.enter_context(
                    self.nc.semaphore(f"s_dma_{len(self.dsems)}")))
                self.dcnt.append(0)
                i = len(self.dsems) - 1
            else:
                i = self.drr
                self.drr = (self.drr + 1) % N_DMA_SEMS
            if self.dcnt[i] > 0 and wd.get(id(self.dsems[i]), 0) < self.dcnt[i]:
                wd[id(self.dsems[i])] = self.dcnt[i]
                waits.append((self.dsems[i], self.dcnt[i]))
            self.dcnt[i] += 16
            me = ("dma%d" % self.n_dma, 0, self.dsems[i], self.dcnt[i])
            self.n_dma += 1
            self.q[e].append((waits, fn, self.dsems[i], 16))
        else:
            s, v = self._sem(e, idx)
            me = (e, idx, s, v)
            self.q[e].append((waits, fn, s, 1))
        for b in reads:
            b.rd.append(me)
        for b in writes:
            b.lw = me
            b.rd = []
        return me

    def all_done_waits(self):
        final = [(s, c) for s, c in zip(self.dsems, self.dcnt) if c > 0]
        for e in self.ENGS:
            if self.cnt[e] > 0:
                final.append(self._sem(e, self.cnt[e] - 1))
        return final

    def barrier(self):
        w = self.all_done_waits()
        for e in self.ENGS:
            self.pend[e] = list(w)

    def emit(self, last=False):
        nc = self.nc
        prog = self
        final = self.all_done_waits() if last else []
        with nc.Block() as block:
            def run(eng, name):
                for waits, fn, s, inc in prog.q[name]:
                    for (ws, wv) in waits:
                        eng.wait_ge(ws, wv)
                    fn(eng).then_inc(s, inc)
                prog.q[name] = []

            @block.tensor
            def _(eng):
                run(eng, "pe")

            @block.scalar
            def _(eng):
                run(eng, "act")

            @block.vector
            def _(eng):
                run(eng, "dve")

            @block.gpsimd
            def _(eng):
                run(eng, "pool")

            @block.sync
            def _(eng):
                run(eng, "sp")
                for (ws, wv) in final:
                    eng.wait_ge(ws, wv)


def rev_ap(ap2d):
    a = ap2d.ap
    assert len(a) == 2, a
    n = a[1][1]
    st = a[1][0]
    return bass.AP(ap2d.tensor, ap2d.offset + st * (n - 1), [list(a[0]), [-st, n]])


class Builder:
    def __init__(self, Lp, S, dbg=False):
        self.Lp, self.S = Lp, S
        self.Ls = 8 * S
        self.Lo = Lp + S
        self.dbg = dbg
        self.nc = bass.Bass("TRN2", target_bir_lowering=False)

    def dram_in(self, name, shape, dt=F32):
        return self.nc.dram_tensor(name, list(shape), dt, kind="ExternalInput").ap()

    def dram_out(self, name, shape, dt=F32):
        return self.nc.dram_tensor(name, list(shape), dt, kind="ExternalOutput").ap()

    def dram_scr(self, name, shape, dt):
        kind = "ExternalOutput" if (self.dbg and name.split("_")[0] in str(self.dbg)) else "Internal"
        return T(self.nc.dram_tensor(name, list(shape), dt, kind=kind).ap())

    _uid = 0

    def sb(self, st, name, shape, dt):
        Builder._uid += 1
        return T(st.enter_context(self.nc.sbuf_tensor(f"sb{Builder._uid}_{name}", list(shape), dt)))

    def ps(self, st, name, shape, dt):
        Builder._uid += 1
        nbytes = int(np.prod(shape[1:])) * (4 if dt == F32 else 2)
        assert nbytes == 2048, (name, shape)
        t = T(st.enter_context(self.nc.psum_tensor(f"ps{Builder._uid}_{name}", list(shape), dt)))
        t.b.ex = True
        return t

    def load(self, out, in_, r=(), w=()):
        self.P.op("sp", lambda e: e.dma_start(out=out, in_=in_), r, w, dma=True)

    def store(self, out, in_, r=(), w=()):
        self.P.op("pool", lambda e: e.dma_start(out=out, in_=in_), r, w, dma=True)

    def mm(self, out, lhsT, rhs, start, stop, r=(), w=()):
        self.P.op("pe", lambda e: e.matmul(out, lhsT=lhsT, rhs=rhs, start=start, stop=stop), r, w)

    def tr(self, out, in_, ident, r=(), w=()):
        self.P.op("pe", lambda e: e.transpose(out, in_, ident), r, w)

    def act(self, out, in_, func, r=(), w=(), eng="act", **kw):
        self.P.op(eng, lambda e: e.activation(out=out, in_=in_, func=func, **kw), r, w)

    def tt(self, out, in0, in1, op, r=(), w=(), eng="dve"):
        self.P.op(eng, lambda e: e.tensor_tensor(out=out, in0=in0, in1=in1, op=op), r, w)

    def ts(self, out, in0, s1, s2, op0, op1=None, r=(), w=(), eng="dve"):
        if op1 is None:
            self.P.op(eng, lambda e: e.tensor_scalar(out=out, in0=in0, scalar1=s1, scalar2=None, op0=op0), r, w)
        else:
            self.P.op(eng, lambda e: e.tensor_scalar(out=out, in0=in0, scalar1=s1, scalar2=s2, op0=op0, op1=op1), r, w)

    def stt(self, out, in0, scalar, in1, op0, op1, r=(), w=()):
        self.P.op("dve", lambda e: e.scalar_tensor_tensor(out=out, in0=in0, scalar=scalar, in1=in1, op0=op0, op1=op1), r, w)

    def cp(self, out, in_, r=(), w=(), eng="dve"):
        if eng == "act":
            self.P.op("act", lambda e: e.activation(out=out, in_=in_, func=AF.Copy), r, w)
        elif eng == "dve":
            self.P.op("dve", lambda e: e.tensor_scalar(out=out, in0=in_, scalar1=1.0, scalar2=None, op0=ALU.mult), r, w)
        else:
            self.P.op(eng, lambda e: e.tensor_copy(out=out, in_=in_), r, w)

    def recip(self, out, in_, r=(), w=()):
        self.P.op("dve", lambda e: e.reciprocal(out=out, in_=in_), r, w)

    def memset(self, ap, val, r=(), w=(), eng="dve"):
        self.P.op(eng, lambda e: e.memset(ap, val), r, w)

    def rstd(self, v, n, inv_n, r=(), w=()):
        self.ts(v[:, 0:n], v[:, 0:n], inv_n, EPS, ALU.mult, ALU.add, r=list(r) + [v], w=[v])
        self.act(v[:, 0:n], v[:, 0:n], AF.Sqrt, r=[v], w=[v])
        self.recip(v[:, 0:n], v[:, 0:n], r=[v], w=list(w) + [v])

    def build(self):
        nc = self.nc
        Lp, S, Ls, Lo = self.Lp, self.S, self.Ls, self.Lo
        I = {}
        I["xp"] = self.dram_in("xp", [Lp, D])
        I["xs"] = self.dram_in("xs", [Ls, D])
        I["memp"] = self.dram_in("memp", [256, D])
        I["mems"] = self.dram_in("mems", [256, D])
        I["csp"] = self.dram_in("csp", [Lp, 128])
        I["css"] = self.dram_in("css", [Ls, 128])
        I["mf"] = self.dram_in("mf", [1, 7 * S])
        I["mb"] = self.dram_in("mb", [1, 7 * S])
        I["w_in"] = self.dram_in("w_in", [D, IN_W])
        I["w_glu"] = self.dram_in("w_glu", [512, 512])
        I["w_mem_kv"] = self.dram_in("w_mem_kv", [D, 1024])
        I["w_pa"] = self.dram_in("w_pa", [1024, D])
        I["w_ps"] = self.dram_in("w_ps", [512, D])
        I["w_px"] = self.dram_in("w_px", [512, D])
        I["w_out"] = self.dram_in("w_out", [D, D])
        I["g_in"] = self.dram_in("g_in", [128, 8])
        I["g_mem"] = self.dram_in("g_mem", [128, 8])
        I["g_q"] = self.dram_in("g_q", [128, 128])
        I["g_k"] = self.dram_in("g_k", [128, 128])
        I["g_f"] = self.dram_in("g_f", [128, D])
        I["s5d"] = self.dram_in("s5d", [128, 4])
        I["bglu"] = self.dram_in("bglu", [128, 4])
        I["are"] = self.dram_in("are", [128, 64])
        I["aim"] = self.dram_in("aim", [128, 64])
        I["lst"] = self.dram_in("lst", [128, 64])
        I["B1"] = self.dram_in("B1", [64, 128, 128])
        I["B2"] = self.dram_in("B2", [64, 128, 128])
        I["C1"] = self.dram_in("C1", [64, 128, 16])
        I["C2"] = self.dram_in("C2", [64, 128, 16])
        I["ident"] = self.dram_in("ident", [128, 128])
        I["swap"] = self.dram_in("swap", [128, 128])
        I["iota1"] = self.dram_in("iota1", [128, TS])
        I["sgn"] = self.dram_in("sgn", [128, 2])
        self.I = I
        self.y_out = self.dram_out("y", [Lo, D])

        self.W = {
            "w_in": self.dram_scr("wb_in", [D, IN_W], BF16),
            "w_glu": self.dram_scr("wb_glu", [512, 512], BF16),
            "w_mem_kv": self.dram_scr("wb_mkv", [D, 1024], BF16),
            "w_pa": self.dram_scr("wb_pa", [1024, D], BF16),
            "w_ps": self.dram_scr("wb_ps", [512, D], BF16),
            "w_px": self.dram_scr("wb_px", [512, D], BF16),
            "w_out": self.dram_scr("wb_out", [D, D], BF16),
        }
        self.KT = {"p": self.dram_scr("KT_p", [2, 128, Lp], BF16),
                   "s": self.dram_scr("KT_s", [2, 128, Ls], BF16)}
        self.VA = {"p": self.dram_scr("VA_p", [2, 128, Lp // 128, 129], BF16),
                   "s": self.dram_scr("VA_s", [2, 128, Ls // 128, 129], BF16)}
        self.UT = {"p": self.dram_scr("UT_p", [512, Lp], BF16),
                   "sf": self.dram_scr("UT_sf", [512, Ls], BF16),
                   "sb": self.dram_scr("UT_sb", [512, Ls], BF16)}
        self.YS = [self.dram_scr("YS_f", [512, Lo], F32), self.dram_scr("YS_b", [512, Lo], F32)]

        with ExitStack() as gst:
            self.P = Prog(nc, gst)
            self.gst = gst
            self.consts(gst)
            phases = [self.phase0, self.phase1, self.phase2, self.phase3]
            stop = getattr(self, "stop", 3)
            for i, ph in enumerate(phases):
                with ExitStack() as st:
                    ph(st)
                    self.P.emit(last=(i == stop))
                if i == stop:
                    break
                self.P.barrier()
        return nc

    def consts(self, st):
        I = self.I
        self.ident_f = self.sb(st, "ident_f", [128, 128], F32)
        self.ident_b = self.sb(st, "ident_b", [128, 128], BF16)
        self.swap_f = self.sb(st, "swap_f", [128, 128], F32)
        self.ones_b = self.sb(st, "ones_b", [128, 128], BF16)
        self.g_in = self.sb(st, "g_in", [128, 8], F32)
        self.g_mem = self.sb(st, "g_mem", [128, 8], F32)
        self.g_q = self.sb(st, "g_q", [128, 128], F32)
        self.g_k = self.sb(st, "g_k", [128, 128], F32)
        self.s5d = self.sb(st, "s5d", [128, 4], F32)
        self.bglu = self.sb(st, "bglu", [128, 4], F32)
        self.sgn = self.sb(st, "sgn", [128, 2], F32)
        self.halfpi = self.sb(st, "halfpi", [128, 1], F32)
        self.KmT = {k: self.sb(st, "KmT" + k, [128, 4, 256], BF16) for k in "ps"}
        self.Vm = {k: self.sb(st, "Vm" + k, [128, 2, 512], BF16) for k in "ps"}
        for t, n in ((self.ident_f, "ident"), (self.swap_f, "swap"), (self.g_in, "g_in"),
                     (self.g_mem, "g_mem"), (self.g_q, "g_q"), (self.g_k, "g_k"),
                     (self.s5d, "s5d"), (self.bglu, "bglu"), (self.sgn, "sgn")):
            self.load(t[:], I[n][:, :], w=[t])
        self.cp(self.ident_b[:], self.ident_f[:], r=[self.ident_f], w=[self.ident_b])
        self.memset(self.ones_b[:], 1.0, w=[self.ones_b])
        self.memset(self.halfpi[:], math.pi / 2.0, w=[self.halfpi])

    def make_hT(self, x_ap_rows, xt, ss, xn, ptr, hT, gain, nblk=4, mask=None):
        self.load(xt[:, 0:nblk, :], x_ap_rows.rearrange("(b p) d -> p b d", p=128), w=[xt])
        for b in range(nblk):
            self.act(xn[:, b, :], xt[:, b, :], AF.Square, r=[xt], w=[xn, ss],
                     accum_out=ss[:, b:b + 1])
        import os
        if os.environ.get("DBG_H") == "1":
            return
        self.rstd(ss, nblk, 1.0 / D)
        if os.environ.get("DBG_H") == "2":
            return
        for b in range(nblk):
            if b % 2 == 0:
                self.act(xn[:, b, :], xt[:, b, :], AF.Copy, r=[xt, ss], w=[xn], scale=ss[:, b:b + 1])
            else:
                self.ts(xn[:, b, :], xt[:, b, :], ss[:, b:b + 1], None, ALU.mult, r=[xt, ss], w=[xn])
        if os.environ.get("DBG_H") == "3":
            return
        for j in range(8):
            pt = ptr[j % len(ptr)]
            for b in range(nblk):
                self.tr(pt[:, b * 128:(b + 1) * 128], xn[:, b, j * 128:(j + 1) * 128], self.ident_b[:],
                        r=[xn, self.ident_b], w=[pt])
            if j % 2 == 0:
                self.ts(hT[:, j, 0:nblk * 128], pt[:, 0:nblk * 128], gain[:, j:j + 1], None, ALU.mult,
                        r=[pt, gain], w=[hT])
            else:
                self.act(hT[:, j, 0:nblk * 128], pt[:, 0:nblk * 128], AF.Copy, r=[pt, gain], w=[hT],
                         scale=gain[:, j:j + 1])

    def phase0(self, st):
        I = self.I
        import os
        if os.environ.get("DBG_P0") == "none":
            return
        stg = [self.sb(st, f"wstg{i}", [128, 2048], F32) for i in range(2)]
        stb = [self.sb(st, f"wstb{i}", [128, 2048], BF16) for i in range(2)]
        k = 0
        for name, rows, cols in (("w_in", D, IN_W), ("w_glu", 512, 512), ("w_mem_kv", D, 1024),
                                 ("w_pa", 1024, D), ("w_ps", 512, D), ("w_px", 512, D), ("w_out", D, D)):
            cw = 1920 if cols == IN_W else cols
            for r0 in range(0, rows, 128):
                for c0 in range(0, cols, cw):
                    a, b = stg[k % 2], stb[k % 2]
                    self.load(a[:, 0:cw], I[name][r0:r0 + 128, c0:c0 + cw], w=[a])
                    self.cp(b[:, 0:cw], a[:, 0:cw], r=[a], w=[b], eng="dve" if k % 2 == 0 else "act")
                    self.store(self.W[name][r0:r0 + 128, c0:c0 + cw], b[:, 0:cw], r=[b], w=[self.W[name]])
                    k += 1
        import os
        if os.environ.get("DBG_P0") == "a":
            return
        wm = self.sb(st, "wm", [128, 8, 1024], BF16)
        self.load(wm[:], self.W["w_mem_kv"].h.rearrange("(j p) c -> p j c", p=128), r=[self.W["w_mem_kv"]], w=[wm])
        xt = self.sb(st, "m_xt", [128, 2, D], F32)
        xn = self.sb(st, "m_xn", [128, 2, D], BF16)
        ss = self.sb(st, "m_ss", [128, 4], F32)
        hT = self.sb(st, "m_hT", [128, 8, 256], BF16)
        vtmp = self.sb(st, "m_v", [128, 512], BF16)
        ptr = [self.ps(st, f"m_ptr{i}", [128, 1024], BF16) for i in range(2)]
        pk = [self.ps(st, f"m_pk{i}", [128, 512], F32) for i in range(2)]
        for key, src in (("p", I["memp"]), ("s", I["mems"])):
            self.make_hT(src[:, :], xt, ss, xn, ptr, hT, self.g_mem, nblk=2)
            if os.environ.get("DBG_P0") == "b1":
                continue
            for hx in range(4):
                p = pk[hx % 2]
                for j in range(8):
                    self.mm(p[:, 0:256], wm[:, j, hx * 128:(hx + 1) * 128], hT[:, j, :], j == 0, j == 7,
                            r=[wm, hT], w=[p])
                if os.environ.get("DBG_P0") == "b2":
                    continue
                self.cp(self.KmT[key][:, hx, :], p[:, 0:256], r=[p], w=[self.KmT[key]], eng="act")
            if os.environ.get("DBG_P0") in ("b2", "b3"):
                continue
            for m in range(2):
                p = pk[m % 2]
                for j in range(8):
                    self.mm(p[:, :], hT[:, j, m * 128:(m + 1) * 128], wm[:, j, 512:1024], j == 0, j == 7,
                            r=[wm, hT], w=[p])
                self.cp(self.Vm[key][:, m, :], p[:, :], r=[p], w=[self.Vm[key]], eng=os.environ.get("DBG_VE", "dve"))

    def rope_tables(self, cs, b, gain, tabs, r_extra=()):
        c = cs[:, b, 0:64]
        s = cs[:, b, 64:128]
        g0 = gain[:, 0:64]
        g1 = gain[:, 64:128]
        rr = [cs, gain] + list(r_extra)
        self.tt(tabs[:, 0, :], c, g0, ALU.mult, r=rr, w=[tabs])
        self.tt(tabs[:, 1, :], s, g1, ALU.mult, r=rr, w=[tabs])
        self.tt(tabs[:, 2, :], s, g0, ALU.mult, r=rr, w=[tabs])
        self.tt(tabs[:, 3, :], c, g1, ALU.mult, r=rr, w=[tabs])

    def norm_rope(self, psrc, nh, sq, ssv, xa, t4, tabs, out_bf, rsrc):
        n = nh * 128
        self.act(sq[:, 0:n], psrc, AF.Square, r=rsrc, w=[sq])
        self.P.op("dve", lambda e: e.tensor_reduce(out=ssv[:, 0:nh], in_=sq[:, 0:n].rearrange("p (h d) -> p h d", h=nh),
                                                   axis=AX.X, op=ALU.add), _bufs([sq]), _bufs([ssv]))
        self.rstd(ssv, nh, 1.0 / 128.0)
        xa3 = xa[:, 0:n].rearrange("p (h d) -> p h d", h=nh)
        self.tt(xa3, psrc.rearrange("p (h d) -> p h d", h=nh),
                ssv[:, 0:nh].unsqueeze(2).to_broadcast([128, nh, 128]), ALU.mult, r=list(rsrc) + [ssv], w=[xa])
        x0 = xa[:, 0:n].rearrange("p (h i two) -> p h i two", h=nh, two=2)[:, :, :, 0]
        x1 = xa[:, 0:n].rearrange("p (h i two) -> p h i two", h=nh, two=2)[:, :, :, 1]
        o0 = out_bf[:, 0:nh, :].rearrange("p h (i two) -> p h i two", two=2)[:, :, :, 0]
        o1 = out_bf[:, 0:nh, :].rearrange("p h (i two) -> p h i two", two=2)[:, :, :, 1]

        def tb(i):
            return tabs[:, i, :].unsqueeze(1).to_broadcast([128, nh, 64])
        tv = [t4[:, i, 0:nh * 64].rearrange("p (h i) -> p h i", h=nh) for i in range(4)]
        self.tt(tv[0], x0, tb(0), ALU.mult, r=[xa, tabs], w=[t4])
        self.tt(tv[1], x1, tb(1), ALU.mult, r=[xa, tabs], w=[t4])
        self.tt(tv[2], x0, tb(2), ALU.mult, r=[xa, tabs], w=[t4])
        self.tt(tv[3], x1, tb(3), ALU.mult, r=[xa, tabs], w=[t4])
        self.tt(o0, tv[0], tv[1], ALU.subtract, r=[t4], w=[out_bf])
        self.tt(o1, tv[2], tv[3], ALU.add, r=[t4], w=[out_bf])

    def phase1(self, st):
        I = self.I
        Lp, S, Ls = self.Lp, self.S, self.Ls
        wkv = self.sb(st, "wkv", [128, 8, 512], BF16)
        wu = self.sb(st, "wu", [128, 8, 512], BF16)
        wv = self.W["w_in"].h.rearrange("(j p) c -> p j c", p=128)
        self.load(wkv[:], wv[:, :, C_K:C_K + 512], r=[self.W["w_in"]], w=[wkv])
        self.load(wu[:], wv[:, :, C_U:C_U + 512], r=[self.W["w_in"]], w=[wu])
        xt = [self.sb(st, f"xt{i}", [128, 4, D], F32) for i in range(2)]
        xn = [self.sb(st, f"xn{i}", [128, 4, D], BF16) for i in range(2)]
        ss = [self.sb(st, f"ss{i}", [128, 4], F32) for i in range(2)]
        hT = [self.sb(st, f"hT{i}", [128, 8, 512], BF16) for i in range(2)]
        cs = [self.sb(st, f"cs{i}", [128, 4, 128], F32) for i in range(2)]
        tabs = [self.sb(st, f"tabs{i}", [128, 4, 64], F32) for i in range(2)]
        sq = self.sb(st, "sq", [128, 256], F32)
        kss = [self.sb(st, f"kss{i}", [128, 2], F32) for i in range(2)]
        ka = self.sb(st, "ka", [128, 256], F32)
        t4 = self.sb(st, "t4", [128, 4, 128], F32)
        krot = [self.sb(st, f"krot{i}", [128, 2, 128], BF16) for i in range(2)]
        KTt = [self.sb(st, f"KTt{i}", [128, 2, 512], BF16) for i in range(2)]
        VAt = [self.sb(st, f"VAt{i}", [128, 2, 4, 129], BF16) for i in range(2)]
        UTt = [self.sb(st, f"UTt{i}", [128, 4, 512], BF16) for i in range(2)]
        UTb = [self.sb(st, f"UTb{i}", [128, 4, 512], BF16) for i in range(2)]
        mrow = [self.sb(st, f"mrow{i}", [128, 2, 512], F32) for i in range(2)]
        ptr = [self.ps(st, f"ptr{i}", [128, 1024], BF16) for i in range(2)]
        pkv = [self.ps(st, f"pkv{i}", [128, 512], F32) for i in range(2)]
        pkt = self.ps(st, "pkt", [128, 2, 512], BF16)
        pu = [self.ps(st, f"pu{i}", [128, 512], F32) for i in range(2)]
        for v in VAt:
            self.memset(v[:, :, :, 128:129], 1.0, w=[v])
        it = 0
        for key, xsrc, cssrc, L in (("p", I["xp"], I["csp"], Lp), ("s", I["xs"], I["css"], Ls)):
            for t in range(L // 512):
                t0 = t * 512
                sl = it % 2
                it += 1
                prefix = (key == "s" and t0 < 7 * S)
                self.make_hT(xsrc[t0:t0 + 512, :], xt[sl], ss[sl], xn[sl], ptr, hT[sl], self.g_in)
                self.load(cs[sl][:], cssrc[t0:t0 + 512, :].rearrange("(b p) c -> p b c", p=128), w=[cs[sl]])
                if prefix:
                    self.load(mrow[sl][:, 0, :], I["mf"][0:1, t0:t0 + 512].partition_broadcast(128), w=[mrow[sl]])
                    self.load(mrow[sl][:, 1, :], I["mb"][0:1, t0:t0 + 512].partition_broadcast(128), w=[mrow[sl]])
                for b in range(4):
                    p = pkv[b % 2]
                    for j in range(8):
                        self.mm(p[:, :], hT[sl][:, j, b * 128:(b + 1) * 128], wkv[:, j, :], j == 0, j == 7,
                                r=[hT[sl], wkv], w=[p])
                    self.cp(VAt[sl][:, :, b, 0:128], p[:, 256:512].rearrange("p (h d) -> p h d", h=2),
                            r=[p], w=[VAt[sl]], eng="act")
                    tb_ = tabs[b % 2]
                    self.rope_tables(cs[sl], b, self.g_k, tb_)
                    kr = krot[b % 2]
                    self.norm_rope(p[:, 0:256], 2, sq, kss[b % 2], ka, t4, tb_, kr, [p])
                    for h in range(2):
                        self.tr(pkt[:, h, b * 128:(b + 1) * 128], kr[:, h, :], self.ident_b[:],
                                r=[kr, self.ident_b], w=[pkt])
                self.cp(KTt[sl][:], pkt[:], r=[pkt], w=[KTt[sl]])
                self.store(self.KT[key].h[:, :, t0:t0 + 512].rearrange("h p l -> p h l"), KTt[sl][:],
                           r=[KTt[sl]], w=[self.KT[key]])
                self.store(self.VA[key].h[:, :, t0 // 128:t0 // 128 + 4, :].rearrange("h p b c -> p h b c"),
                           VAt[sl][:], r=[VAt[sl]], w=[self.VA[key]])
                for i in range(4):
                    p = pu[i % 2]
                    for j in range(8):
                        self.mm(p[:, :], wu[:, j, i * 128:(i + 1) * 128], hT[sl][:, j, :], j == 0, j == 7,
                                r=[hT[sl], wu], w=[p])
                    if prefix:
                        self.tt(UTt[sl][:, i, :], p[:, :], mrow[sl][:, 0, :], ALU.mult, r=[p, mrow[sl]], w=[UTt[sl]])
                        self.tt(UTb[sl][:, i, :], p[:, :], mrow[sl][:, 1, :], ALU.mult, r=[p, mrow[sl]], w=[UTb[sl]])
                    else:
                        self.cp(UTt[sl][:, i, :], p[:, :], r=[p], w=[UTt[sl]], eng="act" if i % 2 else "dve")
                if key == "p":
                    self.store(self.UT["p"].h[:, t0:t0 + 512].rearrange("(i p) l -> p i l", p=128), UTt[sl][:],
                               r=[UTt[sl]], w=[self.UT["p"]])
                else:
                    self.store(self.UT["sf"].h[:, t0:t0 + 512].rearrange("(i p) l -> p i l", p=128), UTt[sl][:],
                               r=[UTt[sl]], w=[self.UT["sf"]])
                    self.store(self.UT["sb"].h[:, t0:t0 + 512].rearrange("(i p) l -> p i l", p=128),
                               (UTb if prefix else UTt)[sl][:], r=[(UTb if prefix else UTt)[sl]], w=[self.UT["sb"]])

    def phase2(self, st):
        I = self.I
        Lp, S, Ls = self.Lp, self.S, self.Ls
        def gt(name):
            return self.sb(st, name, [128, 64], F32)
        are, aim, lst = gt("are"), gt("aim"), gt("lst")
        for t, n in ((are, "are"), (aim, "aim"), (lst, "lst")):
            self.load(t[:], I[n][:, :], w=[t])
        step, Rv, th, kk, thr, sn, cs_, ab = gt("step"), gt("Rv"), gt("th"), gt("kk"), gt("thr"), gt("sn"), gt("cs_"), gt("ab")
        nr, ni, den, CR, CI, SCI, NSCR, tmp = gt("nr"), gt("ni"), gt("den"), gt("CR"), gt("CI"), gt("SCI"), gt("NSCR"), gt("tmp")
        self.act(step[:], lst[:], AF.Exp, r=[lst], w=[step])
        self.ts(are[:], are[:], -1e-4, None, ALU.min, r=[are], w=[are])
        self.tt(Rv[:], are[:], step[:], ALU.mult, r=[are, step], w=[Rv])
        self.act(Rv[:], Rv[:], AF.Exp, r=[Rv], w=[Rv])
        self.tt(th[:], aim[:], step[:], ALU.mult, r=[aim, step], w=[th])
        self.reduce_angle(th, kk, thr)
        self.sincos(thr, ab, sn, cs_)
        self.tt(nr[:], Rv[:], cs_[:], ALU.mult, r=[Rv, cs_], w=[nr])
        self.ts(nr[:], nr[:], -1.0, None, ALU.add, r=[nr], w=[nr])
        self.tt(ni[:], Rv[:], sn[:], ALU.mult, r=[Rv, sn], w=[ni])
        self.tt(den[:], are[:], are[:], ALU.mult, r=[are], w=[den])
        self.tt(tmp[:], aim[:], aim[:], ALU.mult, r=[aim], w=[tmp])
        self.tt(den[:], den[:], tmp[:], ALU.add, r=[den, tmp], w=[den])
        self.recip(den[:], den[:], r=[den], w=[den])
        self.tt(CR[:], nr[:], are[:], ALU.mult, r=[nr, are], w=[CR])
        self.tt(tmp[:], ni[:], aim[:], ALU.mult, r=[ni, aim], w=[tmp])
        self.tt(CR[:], CR[:], tmp[:], ALU.add, r=[CR, tmp], w=[CR])
        self.tt(CR[:], CR[:], den[:], ALU.mult, r=[CR, den], w=[CR])
        self.tt(CI[:], ni[:], are[:], ALU.mult, r=[ni, are], w=[CI])
        self.tt(tmp[:], nr[:], aim[:], ALU.mult, r=[nr, aim], w=[tmp])
        self.tt(CI[:], CI[:], tmp[:], ALU.subtract, r=[CI, tmp], w=[CI])
        self.tt(CI[:], CI[:], den[:], ALU.mult, r=[CI, den], w=[CI])
        self.ts(SCI[:], CI[:], self.sgn[:, 0:1], None, ALU.mult, r=[CI, self.sgn], w=[SCI])
        self.ts(NSCR[:], CR[:], self.sgn[:, 1:2], None, ALU.mult, r=[CR, self.sgn], w=[NSCR])

        iota1 = self.sb(st, "iota1", [128, TS], F32)
        self.load(iota1[:], I["iota1"][:, :], w=[iota1])
        ones_f = self.sb(st, "ones_f", [128, TS], F32)
        self.memset(ones_f[:], 1.0, w=[ones_f])

        up = self.sb(st, "up", [128, Lp], BF16)
        usf = self.sb(st, "usf", [128, Ls], BF16)
        usb = self.sb(st, "usb", [128, Ls], BF16)

        class Stream:
            pass
        strs = []
        for d in range(2):
            s_ = Stream()
            n = f"s{d}_"
            s_.phi = self.sb(st, n + "phi", [128, TS], F32)
            s_.k2 = self.sb(st, n + "k2", [128, TS], F32)
            s_.sinp = self.sb(st, n + "sinp", [128, TS], F32)
            s_.cosp = self.sb(st, n + "cosp", [128, TS], F32)
            s_.TA = self.sb(st, n + "TA", [128, TS], F32)
            s_.TB = self.sb(st, n + "TB", [128, TS], F32)
            s_.RC = self.sb(st, n + "RC", [128, TS], F32)
            s_.RS = self.sb(st, n + "RS", [128, TS], F32)
            s_.Rd = self.sb(st, n + "Rd", [128, TS], F32)
            s_.rot = self.sb(st, n + "rot", [128, 128], F32)
            s_.rb = self.sb(st, n + "rb", [128, 1], F32)
            s_.Bf = self.sb(st, n + "Bf", [128, 2, 128], F32)
            s_.Bb = self.sb(st, n + "Bb", [128, 2, 128], BF16)
            s_.Cf = self.sb(st, n + "Cf", [128, 2, 16], F32)
            s_.Cb = self.sb(st, n + "Cb", [128, 2, 16], BF16)
            s_.t2 = [self.sb(st, n + f"t2{i}", [128, TS], F32) for i in range(2)]
            s_.dd = [self.sb(st, n + f"dd{i}", [128, TS], F32) for i in range(2)]
            s_.e1 = [self.sb(st, n + f"e1{i}", [128, TS], BF16) for i in range(2)]
            s_.e2 = [self.sb(st, n + f"e2{i}", [128, TS], BF16) for i in range(2)]
            s_.wl = self.sb(st, n + "wl", [128, 1], F32)
            s_.carry = self.sb(st, n + "carry", [128, 1], F32)
            s_.ystg = [self.sb(st, n + f"ystg{i}", [16, 512], F32) for i in range(2)]
            s_.pp = [self.ps(st, n + f"pp{i}", [128, 2 * TS], F32) for i in range(2)]
            s_.pw = self.ps(st, n + "pw", [128, 512], F32)
            s_.py = self.ps(st, n + "py", [128, 512], F32)
            s_.prot = T(s_.py.h[:, TS:TS + 2], s_.py.b)
            s_.nchunk = 0
            s_.nstg = 0
            strs.append(s_)

        import os
        DS = os.environ.get("DBG_S5", "")
        for j in range(4):
            if DS and j > 0:
                break
            self.load(up[:], self.UT["p"].h[j * 128:(j + 1) * 128, :], r=[self.UT["p"]], w=[up])
            self.load(usf[:], self.UT["sf"].h[j * 128:(j + 1) * 128, :], r=[self.UT["sf"]], w=[usf])
            self.load(usb[:], self.UT["sb"].h[j * 128:(j + 1) * 128, :], r=[self.UT["sb"]], w=[usb])
            for gl in range(8):
                if DS and gl > 0:
                    break
                g = 8 * j + gl
                for d in range(2):
                    s_ = strs[d]
                    col = d * 32 + g
                    self.s5_prep(s_, col, iota1, ones_f, Rv, thr, CR, CI, SCI, NSCR)
                if DS == "prep":
                    continue
                nT = TS
                sched = [[], []]
                for i in range(Lp // nT):
                    sched[0].append((up, i * nT, False, i * nT))
                    sched[1].append((up, Lp - (i + 1) * nT, True, Lp - (i + 1) * nT))
                seqs = [(sched[0], sched[1])]
                s0, s1 = [], []
                for i in range(Ls // nT):
                    t0 = i * nT
                    s0.append((usf, t0, False, (Lp + t0 - 7 * S) if t0 >= 7 * S else None))
                for i in range(7 * S // nT):
                    s1.append((usb, 7 * S - (i + 1) * nT, True, None))
                for i in range(S // nT):
                    t0 = Ls - (i + 1) * nT
                    s1.append((usb, t0, True, Lp + t0 - 7 * S))
                seqs.append((s0, s1))
                for sq0, sq1 in seqs:
                    for d in range(2):
                        self.memset(strs[d].carry[:], 0.0, w=[strs[d].carry])
                    items = []
                    n = max(len(sq0), len(sq1))
                    for i in range(n):
                        for d, sq_ in ((0, sq0), (1, sq1)):
                            if i < len(sq_):
                                ut, t0, rev, ypos = sq_[i]
                                items.append(dict(s=strs[d], d=d, g=g, ut=ut, t0=t0, rev=rev, ypos=ypos,
                                                  last=(i == len(sq_) - 1)))
                    ni = len(items)
                    for i in range(ni + 3):
                        if i < ni:
                            self.s5_M(items[i])
                        if 0 <= i - 2 < ni:
                            self.s5_R(items[i - 2])
                        if 0 <= i - 3 < ni:
                            self.s5_O(items[i - 3])

    def reduce_angle(self, th, kk, thr):
        self.ts(kk[:], th[:], 1.0 / TWO_PI, None, ALU.mult, r=[th], w=[kk])
        self.ts(kk[:], kk[:], MAGIC, None, ALU.add, r=[kk], w=[kk])
        self.ts(kk[:], kk[:], -MAGIC, None, ALU.add, r=[kk], w=[kk])
        self.stt(thr[:], kk[:], -CW1, th[:], ALU.mult, ALU.add, r=[kk, th], w=[thr])
        self.stt(thr[:], kk[:], -CW2, thr[:], ALU.mult, ALU.add, r=[kk, thr], w=[thr])
        self.ts(thr[:], thr[:], math.pi, -math.pi, ALU.min, ALU.max, r=[thr], w=[thr])

    def sincos(self, thr, ab, sn, cs_):
        self.act(sn[:], thr[:], AF.Sin, r=[thr], w=[sn])
        self.act(ab[:], thr[:], AF.Sin, r=[thr], w=[ab], scale=0.5)
        self.tt(cs_[:], ab[:], ab[:], ALU.mult, r=[ab], w=[cs_])
        self.ts(cs_[:], cs_[:], -2.0, 1.0, ALU.mult, ALU.add, r=[cs_], w=[cs_])

    def s5_prep(self, s_, col, iota1, ones_f, Rv, thr, CR, CI, SCI, NSCR):
        I = self.I
        c1 = slice(col, col + 1)
        self.load(s_.Bf[:, 0, :], I["B1"][col, :, :], w=[s_.Bf])
        self.load(s_.Bf[:, 1, :], I["B2"][col, :, :], w=[s_.Bf])
        self.load(s_.Cf[:, 0, :], I["C1"][col, :, :], w=[s_.Cf])
        self.load(s_.Cf[:, 1, :], I["C2"][col, :, :], w=[s_.Cf])
        self.cp(s_.Bb[:], s_.Bf[:], r=[s_.Bf], w=[s_.Bb], eng="act")
        self.cp(s_.Cb[:], s_.Cf[:], r=[s_.Cf], w=[s_.Cb], eng="act")
        self.ts(s_.phi[:], iota1[:], thr[:, c1], None, ALU.mult, r=[iota1, thr], w=[s_.phi])
        self.reduce_angle(s_.phi, s_.k2, s_.phi)
        self.sincos(s_.phi, s_.k2, s_.sinp, s_.cosp)
        self.ts(s_.TA[:], s_.cosp[:], CR[:, c1], None, ALU.mult, r=[s_.cosp, CR], w=[s_.TA])
        self.stt(s_.TA[:], s_.sinp[:], CI[:, c1], s_.TA[:], ALU.mult, ALU.add, r=[s_.sinp, CI, s_.TA], w=[s_.TA])
        self.ts(s_.TB[:], s_.cosp[:], SCI[:, c1], None, ALU.mult, r=[s_.cosp, SCI], w=[s_.TB])
        self.stt(s_.TB[:], s_.sinp[:], NSCR[:, c1], s_.TB[:], ALU.mult, ALU.add, r=[s_.sinp, NSCR, s_.TB], w=[s_.TB])
        self.ts(s_.RC[:], s_.cosp[:], self.sgn[:, 1:2], None, ALU.mult, r=[s_.cosp, self.sgn], w=[s_.RC])
        self.act(s_.RS[:], s_.sinp[:], AF.Copy, r=[s_.sinp], w=[s_.RS], scale=-1.0)
        self.act(s_.Rd[:], ones_f[:], AF.Copy, r=[ones_f, Rv], w=[s_.Rd], scale=Rv[:, c1])
        self.ts(s_.rb[:], s_.sinp[:, TS - 1:TS], self.sgn[:, 1:2], None, ALU.mult, r=[s_.sinp, self.sgn], w=[s_.rb])
        self.ts(s_.rot[:], self.ident_f[:], s_.cosp[:, TS - 1:TS], None, ALU.mult, r=[self.ident_f, s_.cosp], w=[s_.rot])
        self.stt(s_.rot[:], self.swap_f[:], s_.rb[:, 0:1], s_.rot[:], ALU.mult, ALU.add,
                 r=[self.swap_f, s_.rb, s_.rot], w=[s_.rot])

    def s5_M(self, it):
        s_ = it["s"]
        k = s_.nchunk % 2
        s_.nchunk += 1
        it["k"] = k
        pp = s_.pp[k]
        ut, t0 = it["ut"], it["t0"]
        rhs = ut[:, t0:t0 + TS]
        if it["rev"]:
            rhs = rev_ap(rhs)
        self.mm(pp[:, 0:TS], s_.Bb[:, 0, :], rhs, True, True, r=[s_.Bb, ut], w=[pp])
        self.mm(pp[:, TS:2 * TS], s_.Bb[:, 1, :], rhs, True, True, r=[s_.Bb, ut], w=[pp])

    def s5_R(self, it):
        s_ = it["s"]
        k = it["k"]
        pp, t2, dd = s_.pp[k], s_.t2[k], s_.dd[k]
        self.tt(pp[:, 0:TS], pp[:, 0:TS], s_.TA[:], ALU.mult, r=[pp, s_.TA], w=[pp])
        self.tt(t2[:], pp[:, TS:2 * TS], s_.TB[:], ALU.mult, r=[pp, s_.TB], w=[t2])
        self.tt(dd[:], pp[:, 0:TS], t2[:], ALU.add, r=[pp, t2], w=[dd])
        self.P.op("dve", lambda e: e.tensor_tensor_scan(out=s_.pw[:, 0:TS], data0=s_.Rd[:], data1=dd[:],
                                                        initial=s_.carry[:, 0:1], op0=ALU.mult, op1=ALU.add),
                  _bufs([s_.Rd, dd, s_.carry]), _bufs([s_.pw]))
        if not it["last"]:
            self.cp(s_.wl[:], s_.pw[:, TS - 1:TS], r=[s_.pw], w=[s_.wl], eng="act")
            self.mm(s_.prot[:, 0:1], s_.rot[:], s_.wl[:], True, True, r=[s_.rot, s_.wl], w=[s_.prot])
            self.cp(s_.carry[:], s_.prot[:, 0:1], r=[s_.prot], w=[s_.carry], eng="act")

    def s5_O(self, it):
        s_ = it["s"]
        k = it["k"]
        ypos, rev, d, g = it["ypos"], it["rev"], it["d"], it["g"]
        if ypos is None:
            return
        e1, e2 = s_.e1[k], s_.e2[k]
        self.tt(e1[:], s_.pw[:, 0:TS], s_.RC[:], ALU.mult, r=[s_.pw, s_.RC], w=[e1])
        self.tt(e2[:], s_.pw[:, 0:TS], s_.RS[:], ALU.mult, r=[s_.pw, s_.RS], w=[e2])
        py = s_.py[0:16, 0:TS]
        self.mm(py, s_.Cb[:, 0, :], e1[:], True, False, r=[s_.Cb, e1], w=[s_.py])
        self.mm(py, s_.Cb[:, 1, :], e2[:], False, True, r=[s_.Cb, e2], w=[s_.py])
        nper = 512 // TS
        sidx = s_.nstg // nper
        stg = s_.ystg[sidx % 2]
        q = s_.nstg % nper
        s_.nstg += 1
        base = (ypos // 512) * 512
        off = ypos - base
        dst = stg[0:16, off:off + TS]
        if rev:
            dst = rev_ap(dst)
        self.cp(dst, py, r=[s_.py], w=[stg], eng="act")
        if q == nper - 1:
            self.store(self.YS[d].h[g * 16:(g + 1) * 16, base:base + 512], stg[0:16, :], r=[stg], w=[self.YS[d]])

    def phase3(self, st):
        I = self.I
        Lp, S, Ls = self.Lp, self.S, self.Ls
        SC = 128.0 ** -0.5
        wv = self.W["w_in"].h.rearrange("(j p) c -> p j c", p=128)
        ws = [self.sb(st, f"ws{i}", [128, 8, 512], BF16) for i in range(3)]
        self.wsi = 0

        def wload(src_ap, rd):
            t = ws[self.wsi % 3]
            self.wsi += 1
            self.load(t[:], src_ap, r=[rd], w=[t])
            return t

        xt = [self.sb(st, f"xt{i}", [128, D], F32) for i in range(2)]
        xn = self.sb(st, "xn", [128, 4, D], BF16)
        ss = self.sb(st, "ss", [128, 4], F32)
        hT = self.sb(st, "hT", [128, 8, 512], BF16)
        cs = self.sb(st, "cs", [128, 4, 128], F32)
        tabs = [self.sb(st, f"tabs{i}", [128, 4, 64], F32) for i in range(2)]
        sq = self.sb(st, "sq", [128, 512], F32)
        qss = [self.sb(st, f"qss{i}", [128, 8], F32) for i in range(2)]
        qa = self.sb(st, "qa", [128, 512], F32)
        t4 = self.sb(st, "t4", [128, 4, 256], F32)
        qrot = [self.sb(st, f"qrot{i}", [128, 8, 128], BF16) for i in range(2)]
        QT = self.sb(st, "QT", [128, 8, 512], BF16)
        GA = self.sb(st, "GA", [128, 8, 512], BF16)
        YA = self.sb(st, "YA", [128, 8, 512], BF16)
        KC = 512
        kts = [self.sb(st, f"kts{i}", [128, KC], BF16) for i in range(3)]
        vas = [self.sb(st, f"vas{i}", [128, KC // 128, 129], BF16) for i in range(3)]
        PT = [self.sb(st, f"PT{i}", [128, 512], BF16) for i in range(3)]
        rcp = self.sb(st, "rcp", [128, 4], F32)
        yn = [self.sb(st, f"yn{i}", [128, 128], BF16) for i in range(2)]
        y0 = [self.sb(st, f"y0_{i}", [128, 512], F32) for i in range(1)] * 2
        y1 = [self.sb(st, f"y1_{i}", [128, 512], F32) for i in range(1)] * 2
        uu = [self.sb(st, f"uu{i}", [128, 512], BF16) for i in range(2)]
        gx = [self.sb(st, f"gx{i}", [128, 512], F32) for i in range(2)]
        g2 = [self.sb(st, f"g2{i}", [128, 512], F32) for i in range(2)]
        YG = self.sb(st, "YG", [128, 4, 512], F32)
        YGb = self.sb(st, "YGb", [128, 4, 512], BF16)
        GS = self.sb(st, "GS", [128, 4, 512], BF16)
        sgl = [self.sb(st, f"sgl{i}", [128, 512], F32) for i in range(2)]
        YSb = self.sb(st, "YSb", [128, 4, 512], BF16)
        wglu = self.sb(st, "wglu", [128, 4, 512], BF16)
        self.load(wglu[:], self.W["w_glu"].h.rearrange("(k p) c -> p k c", p=128), r=[self.W["w_glu"]], w=[wglu])
        QX = self.sb(st, "QX", [128, 4, 512], BF16)
        GX = self.sb(st, "GX", [128, 4, 512], BF16)
        PX = [self.sb(st, f"PX{i}", [128, 512], BF16) for i in range(2)]
        rd = self.sb(st, "rd", [128, 512], F32)
        yx = self.sb(st, "yx", [128, 512], F32)
        YX = self.sb(st, "YX", [128, 4, 512], BF16)
        G3 = self.sb(st, "G3", [128, 3, 4, 512], BF16)
        m = [self.sb(st, f"m{i}", [128, 512], F32) for i in range(3)]
        M = self.sb(st, "M", [128, 8, 512], BF16)
        yres = [self.sb(st, f"yres{i}", [128, D], F32) for i in range(1)] * 2
        fss = [self.sb(st, f"fss{i}", [128, 1], F32) for i in range(2)]
        gf = self.sb(st, "gf", [128, D], F32)
        self.load(gf[:], I["g_f"][:, :], w=[gf])
        bk = [self.ps(st, f"bk{i}", [128, 512], F32) for i in range(8)]

        def bf(b):
            return b[:].bitcast(BF16)

        seqs = [("p", I["xp"], I["csp"], 0, Lp, Lp, 0), ("s", I["xs"], I["css"], 7 * S, S, Ls, Lp)]
        for key, xsrc, cssrc, xoff, nown, Lk, yoff in seqs:
            for t in range(nown // 512):
                t0 = xoff + t * 512
                yo = yoff + t * 512
                self.make_hT_bank(xsrc[t0:t0 + 512, :], xt, ss, xn, [bk[6], bk[7]], hT, self.g_in)
                self.load(cs[:], cssrc[t0:t0 + 512, :].rearrange("(b p) c -> p b c", p=128), w=[cs])
                wq0 = wload(wv[:, :, C_Q:C_Q + 512], self.W["w_in"])
                wq1 = wload(wv[:, :, C_Q + 512:C_Q + 1024], self.W["w_in"])
                for b in range(4):
                    pa, pb, pT = bk[(2 * b) % 4], bk[(2 * b + 1) % 4], bk[4 + b % 2]
                    for j in range(8):
                        self.mm(pa[:, :], hT[:, j, b * 128:(b + 1) * 128], wq0[:, j, :], j == 0, j == 7, r=[hT, wq0], w=[pa])
                    for j in range(8):
                        self.mm(pb[:, :], hT[:, j, b * 128:(b + 1) * 128], wq1[:, j, :], j == 0, j == 7, r=[hT, wq1], w=[pb])
                    tb_ = tabs[b % 2]
                    self.rope_tables(cs, b, self.g_q, tb_)
                    qr = qrot[b % 2]
                    for half, pbank in ((0, pa), (1, pb)):
                        self.norm_rope_q(pbank, half, sq, qss[b % 2], qa, t4, tb_, qr)
                    for h in range(8):
                        self.tr(bf(pT)[:, h * 128:(h + 1) * 128], qr[:, h, :], self.ident_b[:],
                                r=[qr, self.ident_b], w=[pT])
                    self.cp(QT[:, :, b * 128:(b + 1) * 128], bf(pT).rearrange("p (h q) -> p h q", h=8),
                            r=[pT], w=[QT], eng="act")
                for half in range(2):
                    wg = wload(wv[:, :, C_GA + half * 512:C_GA + (half + 1) * 512], self.W["w_in"])
                    for o in range(4):
                        p = bk[o % 2]
                        for j in range(8):
                            self.mm(p[:, :], wg[:, j, o * 128:(o + 1) * 128], hT[:, j, :], j == 0, j == 7, r=[wg, hT], w=[p])
                        self.act(GA[:, half * 4 + o, :], p[:, :], AF.Silu, r=[p], w=[GA])
                nkc = Lk // KC
                nk = Lk // 128
                kpc = KC // 128
                for h in range(8):
                    hk = h // 4
                    pO, pD = bk[2 + (h % 2) * 2], bk[3 + (h % 2) * 2]
                    cur = {}
                    for idx in range(nk + 1):
                        if idx < nk:
                            c, kk = idx // kpc, idx % kpc
                            if kk == 0:
                                sl = (h * nkc + c) % 3
                                kt_, va_ = kts[sl], vas[sl]
                                self.load(kt_[:], self.KT[key].h[hk, :, c * KC:(c + 1) * KC], r=[self.KT[key]], w=[kt_])
                                self.load(va_[:], self.VA[key].h[hk, :, c * kpc:(c + 1) * kpc, :],
                                          r=[self.VA[key]], w=[va_])
                                cur[c] = (kt_, va_)
                            kt_, va_ = cur[c]
                            psb = bk[idx % 2]
                            pt = PT[idx % 3]
                            self.mm(psb[:, :], kt_[:, kk * 128:(kk + 1) * 128], QT[:, h, :], True, True,
                                    r=[kt_, QT], w=[psb])
                            self.act(pt[:], psb[:, :], AF.Exp, r=[psb], w=[pt], scale=SC)
                        if idx >= 1:
                            j = idx - 1
                            c, kk = j // kpc, j % kpc
                            kt_, va_ = cur[c]
                            pt = PT[j % 3]
                            self.mm(pO[:, :], va_[:, kk, 0:128], pt[:], j == 0, j == nk - 1, r=[va_, pt], w=[pO])
                            self.mm(pD[:, :], self.ones_b[:], pt[:], j == 0, j == nk - 1, r=[self.ones_b, pt], w=[pD])
                    self.recip(rd[:], pD[:, :], r=[pD], w=[rd])
                    self.tt(yx[:], pO[:, :], rd[:], ALU.mult, r=[pO, rd], w=[yx])
                    self.tt(YA[:, h, :], yx[:], GA[:, h, :], ALU.mult, r=[yx, GA], w=[YA])
                wgs = wload(wv[:, :, C_GS:C_GS + 512], self.W["w_in"])
                for i in range(4):
                    k2 = i % 2
                    self.load(y0[k2][:], self.YS[0].h[i * 128:(i + 1) * 128, yo:yo + 512], r=[self.YS[0]], w=[y0[k2]])
                    self.load(y1[k2][:], self.YS[1].h[i * 128:(i + 1) * 128, yo:yo + 512], r=[self.YS[1]], w=[y1[k2]])
                    usrc = self.UT["p"] if key == "p" else self.UT["sf"]
                    self.load(uu[k2][:], usrc.h[i * 128:(i + 1) * 128, t0:t0 + 512], r=[usrc], w=[uu[k2]])
                    a, bq = gx[k2], g2[k2]
                    self.tt(a[:], y0[k2][:], y1[k2][:], ALU.add, r=[y0[k2], y1[k2]], w=[a])
                    self.stt(a[:], uu[k2][:], self.s5d[:, i:i + 1], a[:], ALU.mult, ALU.add, r=[uu[k2], self.s5d, a], w=[a])
                    self.tt(bq[:], a[:], a[:], ALU.mult, r=[a], w=[bq])
                    self.ts(bq[:], bq[:], 0.044715, 1.0, ALU.mult, ALU.add, r=[bq], w=[bq])
                    self.tt(bq[:], bq[:], a[:], ALU.mult, r=[bq, a], w=[bq])
                    self.act(bq[:], bq[:], AF.Sigmoid, r=[bq], w=[bq], scale=2.0 * math.sqrt(2.0 / math.pi))
                    self.tt(YG[:, i, :], a[:], bq[:], ALU.mult, r=[a, bq], w=[YG])
                    self.cp(YGb[:, i, :], YG[:, i, :], r=[YG], w=[YGb], eng="act")
                    p = bk[i % 2]
                    for j in range(8):
                        self.mm(p[:, :], wgs[:, j, i * 128:(i + 1) * 128], hT[:, j, :], j == 0, j == 7, r=[wgs, hT], w=[p])
                    self.act(GS[:, i, :], p[:, :], AF.Silu, r=[p], w=[GS])
                for o in range(4):
                    p = bk[2 + o % 2]
                    for k_ in range(4):
                        self.mm(p[:, :], wglu[:, k_, o * 128:(o + 1) * 128], YGb[:, k_, :], k_ == 0, k_ == 3,
                                r=[wglu, YGb], w=[p])
                    s_ = sgl[o % 2]
                    self.act(s_[:], p[:, :], AF.Sigmoid, r=[p, self.bglu], w=[s_], bias=self.bglu[:, o:o + 1])
                    self.tt(s_[:], s_[:], YG[:, o, :], ALU.mult, r=[s_, YG], w=[s_])
                    self.tt(YSb[:, o, :], s_[:], GS[:, o, :], ALU.mult, r=[s_, GS], w=[YSb])
                wqx = wload(wv[:, :, C_QX:C_QX + 512], self.W["w_in"])
                wgx = wload(wv[:, :, C_GX:C_GX + 512], self.W["w_in"])
                for o in range(4):
                    p = bk[o % 2]
                    for j in range(8):
                        self.mm(p[:, :], wqx[:, j, o * 128:(o + 1) * 128], hT[:, j, :], j == 0, j == 7, r=[wqx, hT], w=[p])
                    self.cp(QX[:, o, :], p[:, :], r=[p], w=[QX], eng="act")
                    p2 = bk[2 + o % 2]
                    for j in range(8):
                        self.mm(p2[:, :], wgx[:, j, o * 128:(o + 1) * 128], hT[:, j, :], j == 0, j == 7, r=[wgx, hT], w=[p2])
                    self.act(GX[:, o, :], p2[:, :], AF.Silu, r=[p2], w=[GX])
                KmT, Vm = self.KmT[key], self.Vm[key]
                for hx in range(4):
                    for mt in range(2):
                        p = bk[mt]
                        self.mm(p[:, :], KmT[:, hx, mt * 128:(mt + 1) * 128], QX[:, hx, :], True, True, r=[KmT, QX], w=[p])
                        self.act(PX[mt][:], p[:, :], AF.Exp, r=[p], w=[PX[mt]], scale=SC)
                    po_, pd_ = bk[4], bk[5]
                    for mt in range(2):
                        self.mm(po_[:, :], Vm[:, mt, hx * 128:(hx + 1) * 128], PX[mt][:], mt == 0, mt == 1, r=[Vm, PX[mt]], w=[po_])
                    for mt in range(2):
                        self.mm(pd_[:, :], self.ones_b[:], PX[mt][:], mt == 0, mt == 1, r=[self.ones_b, PX[mt]], w=[pd_])
                    self.recip(rd[:], pd_[:, :], r=[pd_], w=[rd])
                    self.tt(yx[:], po_[:, :], rd[:], ALU.mult, r=[po_, rd], w=[yx])
                    self.tt(YX[:, hx, :], yx[:], GX[:, hx, :], ALU.mult, r=[yx, GX], w=[YX])
                wpa = self.W["w_pa"].h.rearrange("(k p) c -> p k c", p=128)
                wps = self.W["w_ps"].h.rearrange("(k p) c -> p k c", p=128)
                wpx = self.W["w_px"].h.rearrange("(k p) c -> p k c", p=128)
                for og in range(2):
                    for br in range(3):
                        wm_ = wload(wv[:, :, C_MG + br * 1024 + og * 512:C_MG + br * 1024 + (og + 1) * 512], self.W["w_in"])
                        for o in range(4):
                            p = bk[o % 2]
                            for j in range(8):
                                self.mm(p[:, :], wm_[:, j, o * 128:(o + 1) * 128], hT[:, j, :], j == 0, j == 7, r=[wm_, hT], w=[p])
                            self.act(G3[:, br, o, :], p[:, :], AF.Sigmoid, r=[p], w=[G3])
                    wa = wload(wpa[:, :, og * 512:(og + 1) * 512], self.W["w_pa"])
                    wsx = ws[self.wsi % 3]
                    self.wsi += 1
                    self.load(wsx[:, 0:4, :], wps[:, :, og * 512:(og + 1) * 512], r=[self.W["w_ps"]], w=[wsx])
                    self.load(wsx[:, 4:8, :], wpx[:, :, og * 512:(og + 1) * 512], r=[self.W["w_px"]], w=[wsx])
                    for o in range(4):
                        pa_, ps_, px_ = bk[2 + (o % 2) * 3], bk[3 + (o % 2) * 3], bk[4 + (o % 2) * 3]
                        for k_ in range(8):
                            self.mm(pa_[:, :], wa[:, k_, o * 128:(o + 1) * 128], YA[:, k_, :], k_ == 0, k_ == 7, r=[wa, YA], w=[pa_])
                        for k_ in range(4):
                            self.mm(ps_[:, :], wsx[:, k_, o * 128:(o + 1) * 128], YSb[:, k_, :], k_ == 0, k_ == 3, r=[wsx, YSb], w=[ps_])
                        for k_ in range(4):
                            self.mm(px_[:, :], wsx[:, 4 + k_, o * 128:(o + 1) * 128], YX[:, k_, :], k_ == 0, k_ == 3, r=[wsx, YX], w=[px_])
                        self.tt(m[0][:], pa_[:, :], G3[:, 0, o, :], ALU.mult, r=[pa_, G3], w=[m[0]])
                        self.tt(m[1][:], ps_[:, :], G3[:, 1, o, :], ALU.mult, r=[ps_, G3], w=[m[1]])
                        self.tt(m[2][:], px_[:, :], G3[:, 2, o, :], ALU.mult, r=[px_, G3], w=[m[2]])
                        self.tt(m[0][:], m[0][:], m[1][:], ALU.add, r=[m[0], m[1]], w=[m[0]])
                        self.tt(M[:, og * 4 + o, :], m[0][:], m[2][:], ALU.add, r=[m[0], m[2]], w=[M])
                wo_ = self.W["w_out"].h.rearrange("(k p) c -> p k c", p=128)
                wo0 = wload(wo_[:, :, 0:512], self.W["w_out"])
                wo1 = wload(wo_[:, :, 512:1024], self.W["w_out"])
                for b in range(4):
                    pa_, pb_ = bk[(2 * b) % 4], bk[(2 * b + 1) % 4]
                    for k_ in range(8):
                        self.mm(pa_[:, :], M[:, k_, b * 128:(b + 1) * 128], wo0[:, k_, :], k_ == 0, k_ == 7, r=[M, wo0], w=[pa_])
                    for k_ in range(8):
                        self.mm(pb_[:, :], M[:, k_, b * 128:(b + 1) * 128], wo1[:, k_, :], k_ == 0, k_ == 7, r=[M, wo1], w=[pb_])
                    yr, fs = yres[b % 2], fss[b % 2]
                    xb = xt[b % 2]
                    self.load(xb[:], xsrc[t0 + b * 128:t0 + (b + 1) * 128, :], w=[xb])
                    self.tt(yr[:, 0:512], pa_[:, :], xb[:, 0:512], ALU.add, r=[pa_, xb], w=[yr])
                    self.tt(yr[:, 512:1024], pb_[:, :], xb[:, 512:1024], ALU.add, r=[pb_, xb], w=[yr])
                    self.act(xn[:, 0, :], yr[:], AF.Square, r=[yr], w=[xn, fs], accum_out=fs[:, 0:1])
                    self.rstd(fs, 1, 1.0 / D)
                    self.stt(yr[:], yr[:], fs[:, 0:1], gf[:], ALU.mult, ALU.mult, r=[yr, fs, gf], w=[yr])
                    self.store(self.y_out[yo + b * 128:yo + (b + 1) * 128, :], yr[:], r=[yr])

    def make_hT_bank(self, x_rows, xt, ss, xn, banks, hT, gain):
        for b in range(4):
            xb = xt[b % 2]
            sb_ = ss[b % 2] if isinstance(ss, list) else ss
            self.load(xb[:], x_rows[b * 128:(b + 1) * 128, :], w=[xb])
            self.act(xn[:, b, :], xb[:], AF.Square, r=[xb], w=[xn, ss], accum_out=ss[:, b:b + 1])
            v = ss[:, b:b + 1]
            self.ts(v, v, 1.0 / D, EPS, ALU.mult, ALU.add, r=[ss], w=[ss])
            self.act(v, v, AF.Sqrt, r=[ss], w=[ss])
            self.recip(v, v, r=[ss], w=[ss])
            if b % 2 == 0:
                self.act(xn[:, b, :], xb[:], AF.Copy, r=[xb, ss], w=[xn], scale=ss[:, b:b + 1])
            else:
                self.ts(xn[:, b, :], xb[:], ss[:, b:b + 1], None, ALU.mult, r=[xb, ss], w=[xn])
        for j in range(8):
            bank = banks[j % len(banks)]
            pv = bank[:].bitcast(BF16)
            for b in range(4):
                self.tr(pv[:, b * 128:(b + 1) * 128], xn[:, b, j * 128:(j + 1) * 128], self.ident_b[:],
                        r=[xn, self.ident_b], w=[bank])
            if j % 2 == 0:
                self.ts(hT[:, j, :], pv[:, 0:512], gain[:, j:j + 1], None, ALU.mult, r=[bank, gain], w=[hT])
            else:
                self.act(hT[:, j, :], pv[:, 0:512], AF.Copy, r=[bank, gain], w=[hT], scale=gain[:, j:j + 1])

    def norm_rope_q(self, pbank, half, sq, ssv, xa, t4, tabs, out_bf):
        nh = 4
        o = 0
        h0 = half * 4
        psrc = pbank[:, :]
        self.act(sq[:, o:o + 512], psrc, AF.Square, r=[pbank], w=[sq])
        self.P.op("dve", lambda e: e.tensor_reduce(out=ssv[:, h0:h0 + 4], in_=sq[:, o:o + 512].rearrange("p (h d) -> p h d", h=nh),
                                                   axis=AX.X, op=ALU.add), _bufs([sq]), _bufs([ssv]))
        v = ssv[:, h0:h0 + 4]
        self.ts(v, v, 1.0 / 128.0, EPS, ALU.mult, ALU.add, r=[ssv], w=[ssv])
        self.act(v, v, AF.Sqrt, r=[ssv], w=[ssv])
        self.recip(v, v, r=[ssv], w=[ssv])
        xa3 = xa[:, o:o + 512].rearrange("p (h d) -> p h d", h=nh)
        self.tt(xa3, psrc.rearrange("p (h d) -> p h d", h=nh), v.unsqueeze(2).to_broadcast([128, nh, 128]),
                ALU.mult, r=[pbank, ssv], w=[xa])
        x0 = xa[:, o:o + 512].rearrange("p (h i two) -> p h i two", h=nh, two=2)[:, :, :, 0]
        x1 = xa[:, o:o + 512].rearrange("p (h i two) -> p h i two", h=nh, two=2)[:, :, :, 1]
        ob = out_bf[:, h0:h0 + 4, :].rearrange("p h (i two) -> p h i two", two=2)
        o0, o1 = ob[:, :, :, 0], ob[:, :, :, 1]

        def tb(i):
            return tabs[:, i, :].unsqueeze(1).to_broadcast([128, nh, 64])
        tv = [t4[:, i, 0:256].rearrange("p (h i) -> p h i", h=nh) for i in range(4)]
        self.tt(tv[0], x0, tb(0), ALU.mult, r=[xa, tabs], w=[t4])
        self.tt(tv[1], x1, tb(1), ALU.mult, r=[xa, tabs], w=[t4])
        self.tt(tv[2], x0, tb(2), ALU.mult, r=[xa, tabs], w=[t4])
        self.tt(tv[3], x1, tb(3), ALU.mult, r=[xa, tabs], w=[t4])
        self.tt(o0, tv[0], tv[1], ALU.subtract, r=[t4], w=[out_bf])
        self.tt(o1, tv[2], tv[3], ALU.add, r=[t4], w=[out_bf])


def rope_table(pos):
    pos = np.asarray(pos)
    row = (pos // 64).astype(np.float32)
    col = (pos % 64).astype(np.float32)
    freqs = (np.float32(10000.0) ** (-np.arange(32, dtype=np.float32) / np.float32(32))).astype(np.float32)
    ang = np.concatenate([row[:, None] * freqs, col[:, None] * freqs], axis=-1).astype(np.float32)
    return np.concatenate([np.cos(ang), np.sin(ang)], axis=-1).astype(np.float32)


def host_inputs(inp, Lp, S, ncores=NCORES):
    f = lambda a: np.ascontiguousarray(np.asarray(a, dtype=np.float32))
    Ls = 8 * S
    xs_all = f(inp["x_sample"])[0]
    shared = {}
    shared["w_in"] = f(inp["w_in"])[0]
    shared["w_glu"] = f(inp["w_glu"])[0]
    shared["w_mem_kv"] = f(inp["w_mem_kv"])[0]
    shared["w_pa"] = f(inp["w_proj_attn"])[0]
    shared["w_ps"] = f(inp["w_proj_ssm"])[0]
    shared["w_px"] = f(inp["w_proj_cross"])[0]
    shared["w_out"] = f(inp["w_out"])[0]
    shared["g_in"] = f(f(inp["norm_in"])[0].reshape(8, 128).T)
    shared["g_mem"] = f(f(inp["norm_mem"])[0].reshape(8, 128).T)
    qn, kn = f(inp["q_norm"])[0], f(inp["k_norm"])[0]
    shared["g_q"] = f(np.tile(np.concatenate([qn[0::2], qn[1::2]])[None, :], (128, 1)))
    shared["g_k"] = f(np.tile(np.concatenate([kn[0::2], kn[1::2]])[None, :], (128, 1)))
    shared["g_f"] = f(np.tile(f(inp["norm_final"])[None, :], (128, 1)))
    shared["s5d"] = f(f(inp["s5_d"])[0].reshape(4, 128).T)
    shared["bglu"] = f(f(inp["b_glu"])[0].reshape(4, 128).T)
    a_re, a_im = f(inp["s5_a_re"])[0], f(inp["s5_a_im"])[0]
    dup = lambda a: f(np.concatenate([a.reshape(64, 64).T, a.reshape(64, 64).T], axis=0))
    shared["are"] = dup(a_re)
    shared["aim"] = dup(a_im)
    shared["lst"] = f(np.tile(f(inp["s5_log_step"])[0].reshape(1, 64), (128, 1)))
    b_re, b_im = f(inp["s5_b_re"])[0], f(inp["s5_b_im"])[0]
    c_re, c_im = f(inp["s5_c_re"])[0], f(inp["s5_c_im"])[0]
    B1 = np.zeros((64, 128, 128), np.float32)
    B2 = np.zeros((64, 128, 128), np.float32)
    C1 = np.zeros((64, 128, 16), np.float32)
    C2 = np.zeros((64, 128, 16), np.float32)
    for d in range(2):
        for g in range(32):
            col = d * 32 + g
            r0 = (g % 8) * 16
            B1[col, r0:r0 + 16, 0:64] = b_re[d, g].T
            B1[col, r0:r0 + 16, 64:128] = b_im[d, g].T
            B2[col, r0:r0 + 16, 0:64] = b_im[d, g].T
            B2[col, r0:r0 + 16, 64:128] = b_re[d, g].T
            C1[col, 0:64, :] = c_re[d, g].T
            C1[col, 64:128, :] = c_im[d, g].T
            C2[col, 0:64, :] = c_im[d, g].T
            C2[col, 64:128, :] = c_re[d, g].T
    shared.update(B1=B1, B2=B2, C1=C1, C2=C2)
    shared["ident"] = np.eye(128, dtype=np.float32)
    sw = np.zeros((128, 128), np.float32)
    sw[np.arange(128), (np.arange(128) + 64) % 128] = 1.0
    shared["swap"] = sw
    shared["iota1"] = f(np.tile(np.arange(1, TS + 1, dtype=np.float32)[None, :], (128, 1)))
    sg = np.ones((128, 2), np.float32)
    sg[0:64, 0] = -1.0
    sg[64:128, 1] = -1.0
    shared["sgn"] = sg
    shared["csp"] = rope_table(np.arange(Lp))
    maps = []
    for c in range(ncores):
        m = dict(shared)
        m["xp"] = f(inp["x_prompt"])[c]
        order = np.concatenate([np.arange((c + 1) * S, Ls), np.arange(0, c * S), np.arange(c * S, (c + 1) * S)])
        m["xs"] = np.ascontiguousarray(xs_all[order])
        m["css"] = rope_table(order)
        m["memp"] = f(inp["mem_prompt"])[c]
        m["mems"] = f(inp["mem_sample"])[0]
        mf = np.zeros((1, 7 * S), np.float32)
        mf[0, (7 - c) * S:] = 1.0
        m["mf"] = mf
        m["mb"] = (1.0 - mf).astype(np.float32)
        maps.append(m)
    return maps


_NC_CACHE = {}


def run(inp, Lp, S):
    key = (Lp, S)
    if key not in _NC_CACHE:
        _NC_CACHE[key] = Builder(Lp, S).build()
    nc = _NC_CACHE[key]
    maps = host_inputs(inp, Lp, S)
    res = run_bass_kernel_spmd(nc, maps, core_ids=list(range(NCORES)))
    ys = [np.asarray(r["y"]) for r in res.results]
    y_prompt = np.stack([y[:Lp] for y in ys], axis=0).astype(np.float32)
    y_sample = np.concatenate([y[Lp:Lp + S] for y in ys], axis=0)[None].astype(np.float32)
    return y_prompt, y_sample


def kernel(**inputs):
    Lp = int(np.asarray(inputs["x_prompt"]).shape[1])
    Ls = int(np.asarray(inputs["x_sample"]).shape[1])
    return run(inputs, Lp, Ls // 8)
```

```python
import math
from contextlib import ExitStack

import numpy as np
import concourse.bass as bass
import concourse.mybir as mybir
from concourse.bass_utils import run_bass_kernel_spmd

F32 = mybir.dt.float32
BF16 = mybir.dt.bfloat16
AF = mybir.ActivationFunctionType
ALU = mybir.AluOpType
AX = mybir.AxisListType

D = 1024
IN_W = 7680
C_Q, C_K, C_V, C_GA, C_U, C_GS, C_QX, C_GX, C_MG = 0, 1024, 1280, 1536, 2560, 3072, 3584, 4096, 4608
EPS = 1e-6
NCORES = 8
TS = 256
MAGIC = 12582912.0
TWO_PI = 2.0 * math.pi
CW1 = 6.28125
CW2 = TWO_PI - CW1
SEM_CH = 20000
N_DMA_SEMS = 40


class Buf:
    __slots__ = ("lw", "rd", "ex")

    def __init__(self):
        self.lw = None
        self.rd = []
        self.ex = False


class T:
    def __init__(self, h, b=None):
        self.h = h
        self.b = b if b is not None else Buf()

    def __getitem__(self, k):
        return self.h[k]


def _bufs(xs):
    out = []
    for x in xs:
        if x is None:
            continue
        out.append(x.b if isinstance(x, T) else x)
    return out


class Prog:
    ENGS = ("pe", "act", "dve", "pool", "sp")

    def __init__(self, nc, stack):
        self.nc = nc
        self.stack = stack
        self.q = {e: [] for e in self.ENGS}
        self.cnt = {e: 0 for e in self.ENGS}
        self.sems = {e: [] for e in self.ENGS}
        self.dsems = []
        self.dcnt = []
        self.drr = 0
        self.n_dma = 0
        self.pend = {e: [] for e in self.ENGS}
        self.waited = {e: {} for e in self.ENGS}

    def _sem(self, e, idx):
        k = idx // SEM_CH
        while len(self.sems[e]) <= k:
            self.sems[e].append(self.stack.enter_context(
                self.nc.semaphore(f"s_{e}_{len(self.sems[e])}")))
        return self.sems[e][k], idx % SEM_CH + 1

    def op(self, e, fn, reads=(), writes=(), dma=False):
        reads = _bufs(reads)
        writes = _bufs(writes)
        exr = [b for b in reads if b.ex]
        if exr:
            reads = [b for b in reads if not b.ex]
            writes = writes + [b for b in exr if b not in writes]
        deps = {}

        def add(d):
            if d is None:
                return
            if d[0] not in deps or deps[d[0]][1] < d[1]:
                deps[d[0]] = d
        for b in reads:
            add(b.lw)
        for b in writes:
            add(b.lw)
            for r in b.rd:
                add(r)
        idx = self.cnt[e]
        if not dma:
            self.cnt[e] += 1
        waits = list(self.pend[e])
        self.pend[e] = []
        for key, d in deps.items():
            if key == "pe" and e == "pe":
                continue
            waits.append((d[2], d[3]))
        wd = self.waited[e]
        ww = []
        for (ws, wv) in waits:
            if wd.get(id(ws), 0) >= wv:
                continue
            wd[id(ws)] = wv
            ww.append((ws, wv))
        waits = ww
        if dma:
            if len(self.dsems) < N_DMA_SEMS:
                self.dsems.append(self.stack.enter_context(
                    self.nc.semaphore(f"s_dma_{len(self.dsems)}")))
                self.dcnt.append(0)
                i = len(self.dsems) - 1
            else:
                i = self.drr
                self.drr = (self.drr + 1) % N_DMA_SEMS
            if self.dcnt[i] > 0 and wd.get(id(self.dsems[i]), 0) < self.dcnt[i]:
                wd[id(self.dsems[i])] = self.dcnt[i]
                waits.append((self.dsems[i], self.dcnt[i]))
            self.dcnt[i] += 16
            me = ("dma%d" % self.n_dma, 0, self.dsems[i], self.dcnt[i])
            self.n_dma += 1
            self.q[e].append((waits, fn, self.dsems[i], 16))
        else:
            s, v = self._sem(e, idx)
            me = (e, idx, s, v)
            self.q[e].append((waits, fn, s, 1))
        for b in reads:
            b.rd.append(me)
        for b in writes:
            b.lw = me
            b.rd = []
        return me

    def all_done_waits(self):
        final = [(s, c) for s, c in zip(self.dsems, self.dcnt) if c > 0]
        for e in self.ENGS:
            if self.cnt[e] > 0:
                final.append(self._sem(e, self.cnt[e] - 1))
        return final

    def barrier(self):
        w = self.all_done_waits()
        for e in self.ENGS:
            self.pend[e] = list(w)

    def emit(self, last=False):
        nc = self.nc
        prog = self
        final = self.all_done_waits() if last else []
        with nc.Block() as block:
            def run(eng, name):
                for waits, fn, s, inc in prog.q[name]:
                    for (ws, wv) in waits:
                        eng.wait_ge(ws, wv)
                    fn(eng).then_inc(s, inc)
                prog.q[name] = []

            @block.tensor
            def _(eng):
                run(eng, "pe")

            @block.scalar
            def _(eng):
                run(eng, "act")

            @block.vector
            def _(eng):
                run(eng, "dve")

            @block.gpsimd
            def _(eng):
                run(eng, "pool")

            @block.sync
            def _(eng):
                run(eng, "sp")
                for (ws, wv) in final:
                    eng.wait_ge(ws, wv)


def rev_ap(ap2d):
    a = ap2d.ap
    assert len(a) == 2, a
    n = a[1][1]
    st = a[1][0]
    return bass.AP(ap2d.tensor, ap2d.offset + st * (n - 1), [list(a[0]), [-st, n]])


class Builder:
    def __init__(self, Lp, S, dbg=False):
        self.Lp, self.S = Lp, S
        self.Ls = 8 * S
        self.Lo = Lp + S
        self.dbg = dbg
        self.nc = bass.Bass("TRN2", target_bir_lowering=False)

    def dram_in(self, name, shape, dt=F32):
        return self.nc.dram_tensor(name, list(shape), dt, kind="ExternalInput").ap()

    def dram_out(self, name, shape, dt=F32):
        return self.nc.dram_tensor(name, list(shape), dt, kind="ExternalOutput").ap()

    def dram_scr(self, name, shape, dt):
        kind = "ExternalOutput" if (self.dbg and name.split("_")[0] in str(self.dbg)) else "Internal"
        return T(self.nc.dram_tensor(name, list(shape), dt, kind=kind).ap())

    _uid = 0

    def sb(self, st, name, shape, dt):
        Builder._uid += 1
        return T(st.enter_context(self.nc.sbuf_tensor(f"sb{Builder._uid}_{name}", list(shape), dt)))

    def ps(self, st, name, shape, dt):
        Builder._uid += 1
        nbytes = int(np.prod(shape[1:])) * (4 if dt == F32 else 2)
        assert nbytes == 2048, (name, shape)
        t = T(st.enter_context(self.nc.psum_tensor(f"ps{Builder._uid}_{name}", list(shape), dt)))
        t.b.ex = True
        return t

    def load(self, out, in_, r=(), w=()):
        self.P.op("sp", lambda e: e.dma_start(out=out, in_=in_), r, w, dma=True)

    def store(self, out, in_, r=(), w=()):
        self.P.op("pool", lambda e: e.dma_start(out=out, in_=in_), r, w, dma=True)

    def mm(self, out, lhsT, rhs, start, stop, r=(), w=()):
        self.P.op("pe", lambda e: e.matmul(out, lhsT=lhsT, rhs=rhs, start=start, stop=stop), r, w)

    def tr(self, out, in_, ident, r=(), w=()):
        self.P.op("pe", lambda e: e.transpose(out, in_, ident), r, w)

    def act(self, out, in_, func, r=(), w=(), eng="act", **kw):
        self.P.op(eng, lambda e: e.activation(out=out, in_=in_, func=func, **kw), r, w)

    def tt(self, out, in0, in1, op, r=(), w=(), eng="dve"):
        self.P.op(eng, lambda e: e.tensor_tensor(out=out, in0=in0, in1=in1, op=op), r, w)

    def ts(self, out, in0, s1, s2, op0, op1=None, r=(), w=(), eng="dve"):
        if op1 is None:
            self.P.op(eng, lambda e: e.tensor_scalar(out=out, in0=in0, scalar1=s1, scalar2=None, op0=op0), r, w)
        else:
            self.P.op(eng, lambda e: e.tensor_scalar(out=out, in0=in0, scalar1=s1, scalar2=s2, op0=op0, op1=op1), r, w)

    def stt(self, out, in0, scalar, in1, op0, op1, r=(), w=()):
        self.P.op("dve", lambda e: e.scalar_tensor_tensor(out=out, in0=in0, scalar=scalar, in1=in1, op0=op0, op1=op1), r, w)

    def cp(self, out, in_, r=(), w=(), eng="dve"):
        if eng == "act":
            self.P.op("act", lambda e: e.activation(out=out, in_=in_, func=AF.Copy), r, w)
        elif eng == "dve":
            self.P.op("dve", lambda e: e.tensor_scalar(out=out, in0=in_, scalar1=1.0, scalar2=None, op0=ALU.mult), r, w)
        else:
            self.P.op(eng, lambda e: e.tensor_copy(out=out, in_=in_), r, w)

    def recip(self, out, in_, r=(), w=()):
        self.P.op("dve", lambda e: e.reciprocal(out=out, in_=in_), r, w)

    def memset(self, ap, val, r=(), w=(), eng="dve"):
        self.P.op(eng, lambda e: e.memset(ap, val), r, w)

    def rstd(self, v, n, inv_n, r=(), w=()):
        self.ts(v[:, 0:n], v[:, 0:n], inv_n, EPS, ALU.mult, ALU.add, r=list(r) + [v], w=[v])
        self.act(v[:, 0:n], v[:, 0:n], AF.Sqrt, r=[v], w=[v])
        self.recip(v[:, 0:n], v[:, 0:n], r=[v], w=list(w) + [v])

    def build(self):
        nc = self.nc
        Lp, S, Ls, Lo = self.Lp, self.S, self.Ls, self.Lo
        I = {}
        I["xp"] = self.dram_in("xp", [Lp, D])
        I["xs"] = self.dram_in("xs", [Ls, D])
        I["memp"] = self.dram_in("memp", [256, D])
        I["mems"] = self.dram_in("mems", [256, D])
        I["csp"] = self.dram_in("csp", [Lp, 128])
        I["css"] = self.dram_in("css", [Ls, 128])
        I["mf"] = self.dram_in("mf", [1, 7 * S])
        I["mb"] = self.dram_in("mb", [1, 7 * S])
        I["w_in"] = self.dram_in("w_in", [D, IN_W])
        I["w_glu"] = self.dram_in("w_glu", [512, 512])
        I["w_mem_kv"] = self.dram_in("w_mem_kv", [D, 1024])
        I["w_pa"] = self.dram_in("w_pa", [1024, D])
        I["w_ps"] = self.dram_in("w_ps", [512, D])
        I["w_px"] = self.dram_in("w_px", [512, D])
        I["w_out"] = self.dram_in("w_out", [D, D])
        I["g_in"] = self.dram_in("g_in", [128, 8])
        I["g_mem"] = self.dram_in("g_mem", [128, 8])
        I["g_q"] = self.dram_in("g_q", [128, 128])
        I["g_k"] = self.dram_in("g_k", [128, 128])
        I["g_f"] = self.dram_in("g_f", [128, D])
        I["s5d"] = self.dram_in("s5d", [128, 4])
        I["bglu"] = self.dram_in("bglu", [128, 4])
        I["are"] = self.dram_in("are", [128, 64])
        I["aim"] = self.dram_in("aim", [128, 64])
        I["lst"] = self.dram_in("lst", [128, 64])
        I["B1"] = self.dram_in("B1", [64, 128, 128])
        I["B2"] = self.dram_in("B2", [64, 128, 128])
        I["C1"] = self.dram_in("C1", [64, 128, 16])
        I["C2"] = self.dram_in("C2", [64, 128, 16])
        I["ident"] = self.dram_in("ident", [128, 128])
        I["swap"] = self.dram_in("swap", [128, 128])
        I["iota1"] = self.dram_in("iota1", [128, TS])
        I["sgn"] = self.dram_in("sgn", [128, 2])
        self.I = I
        self.y_out = self.dram_out("y", [Lo, D])

        self.W = {
            "w_in": self.dram_scr("wb_in", [D, IN_W], BF16),
            "w_glu": self.dram_scr("wb_glu", [512, 512], BF16),
            "w_mem_kv": self.dram_scr("wb_mkv", [D, 1024], BF16),
            "w_pa": self.dram_scr("wb_pa", [1024, D], BF16),
            "w_ps": self.dram_scr("wb_ps", [512, D], BF16),
            "w_px": self.dram_scr("wb_px", [512, D], BF16),
            "w_out": self.dram_scr("wb_out", [D, D], BF16),
        }
        self.KT = {"p": self.dram_scr("KT_p", [2, 128, Lp], BF16),
                   "s": self.dram_scr("KT_s", [2, 128, Ls], BF16)}
        self.VA = {"p": self.dram_scr("VA_p", [2, 128, Lp // 128, 129], BF16),
                   "s": self.dram_scr("VA_s", [2, 128, Ls // 128, 129], BF16)}
        self.UT = {"p": self.dram_scr("UT_p", [512, Lp], BF16),
                   "sf": self.dram_scr("UT_sf", [512, Ls], BF16),
                   "sb": self.dram_scr("UT_sb", [512, Ls], BF16)}
        self.YS = [self.dram_scr("YS_f", [512, Lo], F32), self.dram_scr("YS_b", [512, Lo], F32)]
        self.YAs = self.dram_scr("YA_s", [1024, Lo], BF16)

        with ExitStack() as gst:
            self.P = Prog(nc, gst)
            self.gst = gst
            self.consts(gst)
            phases = [self.phase0, self.phase1, self.phase2, self.phase3]
            stop = getattr(self, "stop", 3)
            for i, ph in enumerate(phases):
                with ExitStack() as st:
                    ph(st)
                    self.P.emit(last=(i == stop))
                if i == stop:
                    break
                self.P.barrier()
        return nc

    def consts(self, st):
        I = self.I
        self.ident_f = self.sb(st, "ident_f", [128, 128], F32)
        self.ident_b = self.sb(st, "ident_b", [128, 128], BF16)
        self.swap_f = self.sb(st, "swap_f", [128, 128], F32)
        self.ones_b = self.sb(st, "ones_b", [128, 128], BF16)
        self.g_in = self.sb(st, "g_in", [128, 8], F32)
        self.g_mem = self.sb(st, "g_mem", [128, 8], F32)
        self.g_q = self.sb(st, "g_q", [128, 128], F32)
        self.g_k = self.sb(st, "g_k", [128, 128], F32)
        self.s5d = self.sb(st, "s5d", [128, 4], F32)
        self.bglu = self.sb(st, "bglu", [128, 4], F32)
        self.sgn = self.sb(st, "sgn", [128, 2], F32)
        self.halfpi = self.sb(st, "halfpi", [128, 1], F32)
        self.KmT = {k: self.sb(st, "KmT" + k, [128, 4, 256], BF16) for k in "ps"}
        self.Vm = {k: self.sb(st, "Vm" + k, [128, 2, 512], BF16) for k in "ps"}
        for t, n in ((self.ident_f, "ident"), (self.swap_f, "swap"), (self.g_in, "g_in"),
                     (self.g_mem, "g_mem"), (self.g_q, "g_q"), (self.g_k, "g_k"),
                     (self.s5d, "s5d"), (self.bglu, "bglu"), (self.sgn, "sgn")):
            self.load(t[:], I[n][:, :], w=[t])
        self.cp(self.ident_b[:], self.ident_f[:], r=[self.ident_f], w=[self.ident_b])
        self.memset(self.ones_b[:], 1.0, w=[self.ones_b])
        self.memset(self.halfpi[:], math.pi / 2.0, w=[self.halfpi])

    def make_hT(self, x_ap_rows, xt, ss, xn, ptr, hT, gain, nblk=4, mask=None):
        self.load(xt[:, 0:nblk, :], x_ap_rows.rearrange("(b p) d -> p b d", p=128), w=[xt])
        for b in range(nblk):
            self.act(xn[:, b, :], xt[:, b, :], AF.Square, r=[xt], w=[xn, ss],
                     accum_out=ss[:, b:b + 1])
        import os
        if os.environ.get("DBG_H") == "1":
            return
        self.rstd(ss, nblk, 1.0 / D)
        if os.environ.get("DBG_H") == "2":
            return
        for b in range(nblk):
            if b % 2 == 0:
                self.act(xn[:, b, :], xt[:, b, :], AF.Copy, r=[xt, ss], w=[xn], scale=ss[:, b:b + 1])
            else:
                self.ts(xn[:, b, :], xt[:, b, :], ss[:, b:b + 1], None, ALU.mult, r=[xt, ss], w=[xn])
        if os.environ.get("DBG_H") == "3":
            return
        for j in range(8):
            pt = ptr[j % len(ptr)]
            for b in range(nblk):
                self.tr(pt[:, b * 128:(b + 1) * 128], xn[:, b, j * 128:(j + 1) * 128], self.ident_b[:],
                        r=[xn, self.ident_b], w=[pt])
            if j % 2 == 0:
                self.ts(hT[:, j, 0:nblk * 128], pt[:, 0:nblk * 128], gain[:, j:j + 1], None, ALU.mult,
                        r=[pt, gain], w=[hT])
            else:
                self.act(hT[:, j, 0:nblk * 128], pt[:, 0:nblk * 128], AF.Copy, r=[pt, gain], w=[hT],
                         scale=gain[:, j:j + 1])

    def phase0(self, st):
        I = self.I
        import os
        if os.environ.get("DBG_P0") == "none":
            return
        stg = [self.sb(st, f"wstg{i}", [128, 2048], F32) for i in range(2)]
        stb = [self.sb(st, f"wstb{i}", [128, 2048], BF16) for i in range(2)]
        k = 0
        for name, rows, cols in (("w_in", D, IN_W), ("w_glu", 512, 512), ("w_mem_kv", D, 1024),
                                 ("w_pa", 1024, D), ("w_ps", 512, D), ("w_px", 512, D), ("w_out", D, D)):
            cw = 1920 if cols == IN_W else cols
            for r0 in range(0, rows, 128):
                for c0 in range(0, cols, cw):
                    a, b = stg[k % 2], stb[k % 2]
                    self.load(a[:, 0:cw], I[name][r0:r0 + 128, c0:c0 + cw], w=[a])
                    self.cp(b[:, 0:cw], a[:, 0:cw], r=[a], w=[b], eng="dve" if k % 2 == 0 else "act")
                    self.store(self.W[name][r0:r0 + 128, c0:c0 + cw], b[:, 0:cw], r=[b], w=[self.W[name]])
                    k += 1
        import os
        if os.environ.get("DBG_P0") == "a":
            return
        wm = self.sb(st, "wm", [128, 8, 1024], BF16)
        self.load(wm[:], self.W["w_mem_kv"].h.rearrange("(j p) c -> p j c", p=128), r=[self.W["w_mem_kv"]], w=[wm])
        xt = self.sb(st, "m_xt", [128, 2, D], F32)
        xn = self.sb(st, "m_xn", [128, 2, D], BF16)
        ss = self.sb(st, "m_ss", [128, 4], F32)
        hT = self.sb(st, "m_hT", [128, 8, 256], BF16)
        vtmp = self.sb(st, "m_v", [128, 512], BF16)
        ptr = [self.ps(st, f"m_ptr{i}", [128, 1024], BF16) for i in range(2)]
        pk = [self.ps(st, f"m_pk{i}", [128, 512], F32) for i in range(2)]
        for key, src in (("p", I["memp"]), ("s", I["mems"])):
            self.make_hT(src[:, :], xt, ss, xn, ptr, hT, self.g_mem, nblk=2)
            if os.environ.get("DBG_P0") == "b1":
                continue
            for hx in range(4):
                p = pk[hx % 2]
                for j in range(8):
                    self.mm(p[:, 0:256], wm[:, j, hx * 128:(hx + 1) * 128], hT[:, j, :], j == 0, j == 7,
                            r=[wm, hT], w=[p])
                if os.environ.get("DBG_P0") == "b2":
                    continue
                self.cp(self.KmT[key][:, hx, :], p[:, 0:256], r=[p], w=[self.KmT[key]], eng="act")
            if os.environ.get("DBG_P0") in ("b2", "b3"):
                continue
            for m in range(2):
                p = pk[m % 2]
                for j in range(8):
                    self.mm(p[:, :], hT[:, j, m * 128:(m + 1) * 128], wm[:, j, 512:1024], j == 0, j == 7,
                            r=[wm, hT], w=[p])
                self.cp(self.Vm[key][:, m, :], p[:, :], r=[p], w=[self.Vm[key]], eng=os.environ.get("DBG_VE", "dve"))

    def rope_tables(self, cs, b, gain, tabs, r_extra=()):
        c = cs[:, b, 0:64]
        s = cs[:, b, 64:128]
        g0 = gain[:, 0:64]
        g1 = gain[:, 64:128]
        rr = [cs, gain] + list(r_extra)
        self.tt(tabs[:, 0, :], c, g0, ALU.mult, r=rr, w=[tabs])
        self.tt(tabs[:, 1, :], s, g1, ALU.mult, r=rr, w=[tabs])
        self.tt(tabs[:, 2, :], s, g0, ALU.mult, r=rr, w=[tabs])
        self.tt(tabs[:, 3, :], c, g1, ALU.mult, r=rr, w=[tabs])

    def norm_rope(self, psrc, nh, sq, ssv, xa, t4, tabs, out_bf, rsrc):
        n = nh * 128
        self.act(sq[:, 0:n], psrc, AF.Square, r=rsrc, w=[sq])
        self.P.op("dve", lambda e: e.tensor_reduce(out=ssv[:, 0:nh], in_=sq[:, 0:n].rearrange("p (h d) -> p h d", h=nh),
                                                   axis=AX.X, op=ALU.add), _bufs([sq]), _bufs([ssv]))
        self.rstd(ssv, nh, 1.0 / 128.0)
        xa3 = xa[:, 0:n].rearrange("p (h d) -> p h d", h=nh)
        self.tt(xa3, psrc.rearrange("p (h d) -> p h d", h=nh),
                ssv[:, 0:nh].unsqueeze(2).to_broadcast([128, nh, 128]), ALU.mult, r=list(rsrc) + [ssv], w=[xa])
        x0 = xa[:, 0:n].rearrange("p (h i two) -> p h i two", h=nh, two=2)[:, :, :, 0]
        x1 = xa[:, 0:n].rearrange("p (h i two) -> p h i two", h=nh, two=2)[:, :, :, 1]
        o0 = out_bf[:, 0:nh, :].rearrange("p h (i two) -> p h i two", two=2)[:, :, :, 0]
        o1 = out_bf[:, 0:nh, :].rearrange("p h (i two) -> p h i two", two=2)[:, :, :, 1]

        def tb(i):
            return tabs[:, i, :].unsqueeze(1).to_broadcast([128, nh, 64])
        tv = [t4[:, i, 0:nh * 64].rearrange("p (h i) -> p h i", h=nh) for i in range(4)]
        self.tt(tv[0], x0, tb(0), ALU.mult, r=[xa, tabs], w=[t4])
        self.tt(tv[1], x1, tb(1), ALU.mult, r=[xa, tabs], w=[t4])
        self.tt(tv[2], x0, tb(2), ALU.mult, r=[xa, tabs], w=[t4])
        self.tt(tv[3], x1, tb(3), ALU.mult, r=[xa, tabs], w=[t4])
        self.tt(o0, tv[0], tv[1], ALU.subtract, r=[t4], w=[out_bf])
        self.tt(o1, tv[2], tv[3], ALU.add, r=[t4], w=[out_bf])

    def phase1(self, st):
        I = self.I
        Lp, S, Ls = self.Lp, self.S, self.Ls
        wkv = self.sb(st, "wkv", [128, 8, 512], BF16)
        wu = self.sb(st, "wu", [128, 8, 512], BF16)
        wv = self.W["w_in"].h.rearrange("(j p) c -> p j c", p=128)
        self.load(wkv[:], wv[:, :, C_K:C_K + 512], r=[self.W["w_in"]], w=[wkv])
        self.load(wu[:], wv[:, :, C_U:C_U + 512], r=[self.W["w_in"]], w=[wu])
        xt = [self.sb(st, f"xt{i}", [128, 4, D], F32) for i in range(2)]
        xn = [self.sb(st, f"xn{i}", [128, 4, D], BF16) for i in range(2)]
        ss = [self.sb(st, f"ss{i}", [128, 4], F32) for i in range(2)]
        hT = [self.sb(st, f"hT{i}", [128, 8, 512], BF16) for i in range(2)]
        cs = [self.sb(st, f"cs{i}", [128, 4, 128], F32) for i in range(2)]
        tabs = [self.sb(st, f"tabs{i}", [128, 4, 64], F32) for i in range(2)]
        sq = self.sb(st, "sq", [128, 256], F32)
        kss = [self.sb(st, f"kss{i}", [128, 2], F32) for i in range(2)]
        ka = self.sb(st, "ka", [128, 256], F32)
        t4 = self.sb(st, "t4", [128, 4, 128], F32)
        krot = [self.sb(st, f"krot{i}", [128, 2, 128], BF16) for i in range(2)]
        KTt = [self.sb(st, f"KTt{i}", [128, 2, 512], BF16) for i in range(2)]
        VAt = [self.sb(st, f"VAt{i}", [128, 2, 4, 129], BF16) for i in range(2)]
        UTt = [self.sb(st, f"UTt{i}", [128, 4, 512], BF16) for i in range(2)]
        UTb = [self.sb(st, f"UTb{i}", [128, 4, 512], BF16) for i in range(2)]
        mrow = [self.sb(st, f"mrow{i}", [128, 2, 512], F32) for i in range(2)]
        ptr = [self.ps(st, f"ptr{i}", [128, 1024], BF16) for i in range(2)]
        pkv = [self.ps(st, f"pkv{i}", [128, 512], F32) for i in range(2)]
        pkt = self.ps(st, "pkt", [128, 2, 512], BF16)
        pu = [self.ps(st, f"pu{i}", [128, 512], F32) for i in range(2)]
        for v in VAt:
            self.memset(v[:, :, :, 128:129], 1.0, w=[v])
        it = 0
        for key, xsrc, cssrc, L in (("p", I["xp"], I["csp"], Lp), ("s", I["xs"], I["css"], Ls)):
            for t in range(L // 512):
                t0 = t * 512
                sl = it % 2
                it += 1
                prefix = (key == "s" and t0 < 7 * S)
                self.make_hT(xsrc[t0:t0 + 512, :], xt[sl], ss[sl], xn[sl], ptr, hT[sl], self.g_in)
                self.load(cs[sl][:], cssrc[t0:t0 + 512, :].rearrange("(b p) c -> p b c", p=128), w=[cs[sl]])
                if prefix:
                    self.load(mrow[sl][:, 0, :], I["mf"][0:1, t0:t0 + 512].partition_broadcast(128), w=[mrow[sl]])
                    self.load(mrow[sl][:, 1, :], I["mb"][0:1, t0:t0 + 512].partition_broadcast(128), w=[mrow[sl]])
                for b in range(4):
                    p = pkv[b % 2]
                    for j in range(8):
                        self.mm(p[:, :], hT[sl][:, j, b * 128:(b + 1) * 128], wkv[:, j, :], j == 0, j == 7,
                                r=[hT[sl], wkv], w=[p])
                    self.cp(VAt[sl][:, :, b, 0:128], p[:, 256:512].rearrange("p (h d) -> p h d", h=2),
                            r=[p], w=[VAt[sl]], eng="act")
                    tb_ = tabs[b % 2]
                    self.rope_tables(cs[sl], b, self.g_k, tb_)
                    kr = krot[b % 2]
                    self.norm_rope(p[:, 0:256], 2, sq, kss[b % 2], ka, t4, tb_, kr, [p])
                    for h in range(2):
                        self.tr(pkt[:, h, b * 128:(b + 1) * 128], kr[:, h, :], self.ident_b[:],
                                r=[kr, self.ident_b], w=[pkt])
                self.cp(KTt[sl][:], pkt[:], r=[pkt], w=[KTt[sl]])
                self.store(self.KT[key].h[:, :, t0:t0 + 512].rearrange("h p l -> p h l"), KTt[sl][:],
                           r=[KTt[sl]], w=[self.KT[key]])
                self.store(self.VA[key].h[:, :, t0 // 128:t0 // 128 + 4, :].rearrange("h p b c -> p h b c"),
                           VAt[sl][:], r=[VAt[sl]], w=[self.VA[key]])
                for i in range(4):
                    p = pu[i % 2]
                    for j in range(8):
                        self.mm(p[:, :], wu[:, j, i * 128:(i + 1) * 128], hT[sl][:, j, :], j == 0, j == 7,
                                r=[hT[sl], wu], w=[p])
                    if prefix:
                        self.tt(UTt[sl][:, i, :], p[:, :], mrow[sl][:, 0, :], ALU.mult, r=[p, mrow[sl]], w=[UTt[sl]])
                        self.tt(UTb[sl][:, i, :], p[:, :], mrow[sl][:, 1, :], ALU.mult, r=[p, mrow[sl]], w=[UTb[sl]])
                    else:
                        self.cp(UTt[sl][:, i, :], p[:, :], r=[p], w=[UTt[sl]], eng="act" if i % 2 else "dve")
                if key == "p":
                    self.store(self.UT["p"].h[:, t0:t0 + 512].rearrange("(i p) l -> p i l", p=128), UTt[sl][:],
                               r=[UTt[sl]], w=[self.UT["p"]])
                else:
                    self.store(self.UT["sf"].h[:, t0:t0 + 512].rearrange("(i p) l -> p i l", p=128), UTt[sl][:],
                               r=[UTt[sl]], w=[self.UT["sf"]])
                    self.store(self.UT["sb"].h[:, t0:t0 + 512].rearrange("(i p) l -> p i l", p=128),
                               (UTb if prefix else UTt)[sl][:], r=[(UTb if prefix else UTt)[sl]], w=[self.UT["sb"]])

    def phase2(self, st):
        banks = [self.ps(st, f"cb{i}", [128, 512], F32) for i in range(8)]
        ga = self.gen_s5(st, banks[0:4])
        gb = self.gen_att(st, banks[4:8])
        ta = tb = 0.0
        da = db = False
        import os
        if os.environ.get("DBG_NOATT"):
            db = True
        while not (da and db):
            if not da and (db or ta <= tb):
                try:
                    ta += next(ga)
                except StopIteration:
                    da = True
            else:
                try:
                    tb += next(gb)
                except StopIteration:
                    db = True

    def gen_s5(self, st, banks):
        I = self.I
        Lp, S, Ls = self.Lp, self.S, self.Ls
        def gt(name):
            return self.sb(st, name, [128, 64], F32)
        are, aim, lst = gt("are"), gt("aim"), gt("lst")
        for t, n in ((are, "are"), (aim, "aim"), (lst, "lst")):
            self.load(t[:], I[n][:, :], w=[t])
        step, Rv, th, kk, thr, sn, cs_, ab = gt("step"), gt("Rv"), gt("th"), gt("kk"), gt("thr"), gt("sn"), gt("cs_"), gt("ab")
        nr, ni, den, CR, CI, SCI, NSCR, tmp = gt("nr"), gt("ni"), gt("den"), gt("CR"), gt("CI"), gt("SCI"), gt("NSCR"), gt("tmp")
        self.act(step[:], lst[:], AF.Exp, r=[lst], w=[step])
        self.ts(are[:], are[:], -1e-4, None, ALU.min, r=[are], w=[are])
        self.tt(Rv[:], are[:], step[:], ALU.mult, r=[are, step], w=[Rv])
        self.act(Rv[:], Rv[:], AF.Exp, r=[Rv], w=[Rv])
        self.tt(th[:], aim[:], step[:], ALU.mult, r=[aim, step], w=[th])
        self.reduce_angle(th, kk, thr)
        self.sincos(thr, ab, sn, cs_)
        self.tt(nr[:], Rv[:], cs_[:], ALU.mult, r=[Rv, cs_], w=[nr])
        self.ts(nr[:], nr[:], -1.0, None, ALU.add, r=[nr], w=[nr])
        self.tt(ni[:], Rv[:], sn[:], ALU.mult, r=[Rv, sn], w=[ni])
        self.tt(den[:], are[:], are[:], ALU.mult, r=[are], w=[den])
        self.tt(tmp[:], aim[:], aim[:], ALU.mult, r=[aim], w=[tmp])
        self.tt(den[:], den[:], tmp[:], ALU.add, r=[den, tmp], w=[den])
        self.recip(den[:], den[:], r=[den], w=[den])
        self.tt(CR[:], nr[:], are[:], ALU.mult, r=[nr, are], w=[CR])
        self.tt(tmp[:], ni[:], aim[:], ALU.mult, r=[ni, aim], w=[tmp])
        self.tt(CR[:], CR[:], tmp[:], ALU.add, r=[CR, tmp], w=[CR])
        self.tt(CR[:], CR[:], den[:], ALU.mult, r=[CR, den], w=[CR])
        self.tt(CI[:], ni[:], are[:], ALU.mult, r=[ni, are], w=[CI])
        self.tt(tmp[:], nr[:], aim[:], ALU.mult, r=[nr, aim], w=[tmp])
        self.tt(CI[:], CI[:], tmp[:], ALU.subtract, r=[CI, tmp], w=[CI])
        self.tt(CI[:], CI[:], den[:], ALU.mult, r=[CI, den], w=[CI])
        self.ts(SCI[:], CI[:], self.sgn[:, 0:1], None, ALU.mult, r=[CI, self.sgn], w=[SCI])
        self.ts(NSCR[:], CR[:], self.sgn[:, 1:2], None, ALU.mult, r=[CR, self.sgn], w=[NSCR])

        iota1 = self.sb(st, "iota1", [128, TS], F32)
        self.load(iota1[:], I["iota1"][:, :], w=[iota1])
        ones_f = self.sb(st, "ones_f", [128, TS], F32)
        self.memset(ones_f[:], 1.0, w=[ones_f])

        yield 20.0
        UCH = 1024

        class Stream:
            pass
        strs = []
        for d in range(2):
            s_ = Stream()
            n = f"s{d}_"
            s_.phi = self.sb(st, n + "phi", [128, TS], F32)
            s_.k2 = self.sb(st, n + "k2", [128, TS], F32)
            s_.sinp = self.sb(st, n + "sinp", [128, TS], F32)
            s_.cosp = self.sb(st, n + "cosp", [128, TS], F32)
            s_.TA = self.sb(st, n + "TA", [128, TS], F32)
            s_.TB = self.sb(st, n + "TB", [128, TS], F32)
            s_.RC = self.sb(st, n + "RC", [128, TS], F32)
            s_.RS = self.sb(st, n + "RS", [128, TS], F32)
            s_.Rd = self.sb(st, n + "Rd", [128, TS], F32)
            s_.rot = self.sb(st, n + "rot", [128, 128], F32)
            s_.rb = self.sb(st, n + "rb", [128, 1], F32)
            s_.Bf = self.sb(st, n + "Bf", [128, 2, 128], F32)
            s_.Bb = self.sb(st, n + "Bb", [128, 2, 128], BF16)
            s_.Cf = self.sb(st, n + "Cf", [128, 2, 16], F32)
            s_.Cb = self.sb(st, n + "Cb", [128, 2, 16], BF16)
            s_.t2 = [self.sb(st, n + f"t2{i}", [128, TS], F32) for i in range(2)]
            s_.dd = [self.sb(st, n + f"dd{i}", [128, TS], F32) for i in range(2)]
            s_.e1 = [self.sb(st, n + f"e1{i}", [128, TS], BF16) for i in range(2)]
            s_.e2 = [self.sb(st, n + f"e2{i}", [128, TS], BF16) for i in range(2)]
            s_.wl = self.sb(st, n + "wl", [128, 1], F32)
            s_.carry = self.sb(st, n + "carry", [128, 1], F32)
            s_.ystg = [self.sb(st, n + f"ystg{i}", [16, 512], F32) for i in range(2)]
            s_.ub = [self.sb(st, n + f"ub{i}", [128, UCH], BF16) for i in range(2)]
            s_.ucnt = 0
            s_.ucur = None
            s_.uch = UCH
            s_.pp = [banks[0], banks[1]]
            s_.pw = banks[2]
            s_.py = banks[3]
            s_.prot = T(banks[3].h[:, TS:TS + 2], banks[3].b)
            s_.nchunk = 0
            s_.nstg = 0
            strs.append(s_)
        Lp, S, Ls = self.Lp, self.S, self.Ls
        nset = 0
        for j in range(4):
            for gl in range(8):
                g = 8 * j + gl
                for d in range(2):
                    s_ = strs[nset % 2]
                    nset += 1
                    s_.ucur = None
                    col = d * 32 + g
                    self.s5_prep(s_, col, iota1, ones_f, Rv, thr, CR, CI, SCI, NSCR)
                    yield 6.0
                    items = []

                    def add(src, t0, rev, ypos, first=False, last=False):
                        items.append(dict(s=s_, d=d, g=g, src=src, j=j, t0=t0, rev=rev, ypos=ypos,
                                          first=first, last=last))
                    npc, nsc = Lp // TS, Ls // TS
                    if d == 0:
                        for i in range(npc):
                            add(self.UT["p"], i * TS, False, i * TS, i == 0, i == npc - 1)
                        for i in range(nsc):
                            t0 = i * TS
                            add(self.UT["sf"], t0, False, (Lp + t0 - 7 * S) if t0 >= 7 * S else None,
                                i == 0, i == nsc - 1)
                    else:
                        for i in range(npc):
                            t0 = Lp - (i + 1) * TS
                            add(self.UT["p"], t0, True, t0, i == 0, i == npc - 1)
                        npre = 7 * S // TS
                        for i in range(npre):
                            add(self.UT["sb"], 7 * S - (i + 1) * TS, True, None, i == 0, False)
                        for i in range(S // TS):
                            t0 = Ls - (i + 1) * TS
                            add(self.UT["sb"], t0, True, Lp + t0 - 7 * S, False, i == S // TS - 1)
                    ni = len(items)
                    self.s5_M(items[0])
                    self.s5_A(items[0])
                    for c in range(ni):
                        if c + 1 < ni:
                            self.s5_M(items[c + 1])
                        self.s5_S(items[c])
                        self.s5_O(items[c])
                        if c + 1 < ni:
                            self.s5_A(items[c + 1])
                        yield 2.05 + (0.85 if items[c]["ypos"] is not None else 0.0)

    def gen_att(self, st, bk):
        I = self.I
        Lp, S, Ls = self.Lp, self.S, self.Ls
        SC = 128.0 ** -0.5
        wv = self.W["w_in"].h.rearrange("(j p) c -> p j c", p=128)
        ws = [self.sb(st, f"aws{i}", [128, 8, 512], BF16) for i in range(2)]
        wsi = [0]

        def wload(src_ap, rd_):
            t = ws[wsi[0] % 2]
            wsi[0] += 1
            self.load(t[:], src_ap, r=[rd_], w=[t])
            return t
        xt = [self.sb(st, f"axt{i}", [128, D], F32) for i in range(2)]
        xn = self.sb(st, "axn", [128, 4, D], BF16)
        ss = self.sb(st, "ass", [128, 4], F32)
        hT = self.sb(st, "ahT", [128, 8, 512], BF16)
        cs = self.sb(st, "acs", [128, 4, 128], F32)
        tabs = [self.sb(st, f"atabs{i}", [128, 4, 64], F32) for i in range(2)]
        sq = self.sb(st, "asq", [128, 512], F32)
        qss = [self.sb(st, f"aqss{i}", [128, 8], F32) for i in range(2)]
        qa = self.sb(st, "aqa", [128, 512], F32)
        t4 = self.sb(st, "at4", [128, 4, 256], F32)
        qrot = [self.sb(st, f"aqrot{i}", [128, 8, 128], BF16) for i in range(2)]
        QT = self.sb(st, "aQT", [128, 8, 512], BF16)
        GA = self.sb(st, "aGA", [128, 8, 512], BF16)
        YAh = [self.sb(st, f"aYAh{i}", [128, 512], BF16) for i in range(2)]
        KC = 512
        kts = [self.sb(st, f"akts{i}", [128, KC], BF16) for i in range(3)]
        vas = [self.sb(st, f"avas{i}", [128, KC // 128, 129], BF16) for i in range(3)]
        PT = [self.sb(st, f"aPT{i}", [128, 512], BF16) for i in range(3)]
        rd = self.sb(st, "ard", [128, 512], F32)
        yx = self.sb(st, "ayx", [128, 512], F32)

        def bf(b):
            return b[:].bitcast(BF16)
        seqs = [("p", I["xp"], I["csp"], 0, Lp, Lp, 0), ("s", I["xs"], I["css"], 7 * S, S, Ls, Lp)]
        nslot = 0
        for key, xsrc, cssrc, xoff, nown, Lk, yoff in seqs:
            for t in range(nown // 512):
                t0 = xoff + t * 512
                yo = yoff + t * 512
                self.make_hT_bank(xsrc[t0:t0 + 512, :], xt, ss, xn, [bk[2], bk[3]], hT, self.g_in)
                self.load(cs[:], cssrc[t0:t0 + 512, :].rearrange("(b p) c -> p b c", p=128), w=[cs])
                yield 12.0
                wq0 = wload(wv[:, :, C_Q:C_Q + 512], self.W["w_in"])
                wq1 = wload(wv[:, :, C_Q + 512:C_Q + 1024], self.W["w_in"])
                for b in range(4):
                    pa, pb, pT = bk[0], bk[1], bk[2 + b % 2]
                    for j in range(8):
                        self.mm(pa[:, :], hT[:, j, b * 128:(b + 1) * 128], wq0[:, j, :], j == 0, j == 7, r=[hT, wq0], w=[pa])
                    for j in range(8):
                        self.mm(pb[:, :], hT[:, j, b * 128:(b + 1) * 128], wq1[:, j, :], j == 0, j == 7, r=[hT, wq1], w=[pb])
                    tb_ = tabs[b % 2]
                    self.rope_tables(cs, b, self.g_q, tb_)
                    qr = qrot[b % 2]
                    for half, pbank in ((0, pa), (1, pb)):
                        self.norm_rope_q(pbank, half, sq, qss[b % 2], qa, t4, tb_, qr)
                    for h in range(8):
                        self.tr(bf(pT)[:, h * 128:(h + 1) * 128], qr[:, h, :], self.ident_b[:],
                                r=[qr, self.ident_b], w=[pT])
                    self.cp(QT[:, :, b * 128:(b + 1) * 128], bf(pT).rearrange("p (h q) -> p h q", h=8),
                            r=[pT], w=[QT], eng="act")
                    yield 5.0
                for half in range(2):
                    wg = wload(wv[:, :, C_GA + half * 512:C_GA + (half + 1) * 512], self.W["w_in"])
                    for o in range(4):
                        p = bk[o % 2]
                        for j in range(8):
                            self.mm(p[:, :], wg[:, j, o * 128:(o + 1) * 128], hT[:, j, :], j == 0, j == 7, r=[wg, hT], w=[p])
                        self.act(GA[:, half * 4 + o, :], p[:, :], AF.Silu, r=[p], w=[GA])
                    yield 8.0
                nkc = Lk // KC
                nk = Lk // 128
                kpc = KC // 128
                for h in range(8):
                    hk = h // 4
                    pO, pD = bk[2], bk[3]
                    cur = {}
                    for idx in range(nk + 1):
                        if idx < nk:
                            c, kk = idx // kpc, idx % kpc
                            if kk == 0:
                                sl = nslot % 3
                                nslot += 1
                                kt_, va_ = kts[sl], vas[sl]
                                self.load(kt_[:], self.KT[key].h[hk, :, c * KC:(c + 1) * KC], r=[self.KT[key]], w=[kt_])
                                self.load(va_[:], self.VA[key].h[hk, :, c * kpc:(c + 1) * kpc, :],
                                          r=[self.VA[key]], w=[va_])
                                cur[c] = (kt_, va_)
                            kt_, va_ = cur[c]
                            psb = bk[idx % 2]
                            pt = PT[idx % 3]
                            self.mm(psb[:, :], kt_[:, kk * 128:(kk + 1) * 128], QT[:, h, :], True, True,
                                    r=[kt_, QT], w=[psb])
                            self.act(pt[:], psb[:, :], AF.Exp, r=[psb], w=[pt], scale=SC)
                        if idx >= 1:
                            jx = idx - 1
                            c, kk = jx // kpc, jx % kpc
                            kt_, va_ = cur[c]
                            pt = PT[jx % 3]
                            self.mm(pO[:, :], va_[:, kk, 0:128], pt[:], jx == 0, jx == nk - 1, r=[va_, pt], w=[pO])
                            self.mm(pD[:, :], self.ones_b[:], pt[:], jx == 0, jx == nk - 1, r=[self.ones_b, pt], w=[pD])
                        yield 0.8
                    self.recip(rd[:], pD[:, :], r=[pD], w=[rd])
                    self.tt(yx[:], pO[:, :], rd[:], ALU.mult, r=[pO, rd], w=[yx])
                    ya = YAh[h % 2]
                    self.tt(ya[:], yx[:], GA[:, h, :], ALU.mult, r=[yx, GA], w=[ya])
                    self.store(self.YAs.h[h * 128:(h + 1) * 128, yo:yo + 512], ya[:], r=[ya], w=[self.YAs])
                    yield 1.0

    def reduce_angle(self, th, kk, thr):
        self.ts(kk[:], th[:], 1.0 / TWO_PI, None, ALU.mult, r=[th], w=[kk])
        self.ts(kk[:], kk[:], MAGIC, None, ALU.add, r=[kk], w=[kk])
        self.ts(kk[:], kk[:], -MAGIC, None, ALU.add, r=[kk], w=[kk])
        self.stt(thr[:], kk[:], -CW1, th[:], ALU.mult, ALU.add, r=[kk, th], w=[thr])
        self.stt(thr[:], kk[:], -CW2, thr[:], ALU.mult, ALU.add, r=[kk, thr], w=[thr])
        self.ts(thr[:], thr[:], math.pi, -math.pi, ALU.min, ALU.max, r=[thr], w=[thr])

    def sincos(self, thr, ab, sn, cs_):
        self.act(sn[:], thr[:], AF.Sin, r=[thr], w=[sn])
        self.act(ab[:], thr[:], AF.Sin, r=[thr], w=[ab], scale=0.5)
        self.tt(cs_[:], ab[:], ab[:], ALU.mult, r=[ab], w=[cs_])
        self.ts(cs_[:], cs_[:], -2.0, 1.0, ALU.mult, ALU.add, r=[cs_], w=[cs_])

    def s5_prep(self, s_, col, iota1, ones_f, Rv, thr, CR, CI, SCI, NSCR):
        I = self.I
        c1 = slice(col, col + 1)
        self.load(s_.Bf[:, 0, :], I["B1"][col, :, :], w=[s_.Bf])
        self.load(s_.Bf[:, 1, :], I["B2"][col, :, :], w=[s_.Bf])
        self.load(s_.Cf[:, 0, :], I["C1"][col, :, :], w=[s_.Cf])
        self.load(s_.Cf[:, 1, :], I["C2"][col, :, :], w=[s_.Cf])
        self.cp(s_.Bb[:], s_.Bf[:], r=[s_.Bf], w=[s_.Bb], eng="act")
        self.cp(s_.Cb[:], s_.Cf[:], r=[s_.Cf], w=[s_.Cb], eng="act")
        self.ts(s_.phi[:], iota1[:], thr[:, c1], None, ALU.mult, r=[iota1, thr], w=[s_.phi])
        self.reduce_angle(s_.phi, s_.k2, s_.phi)
        self.sincos(s_.phi, s_.k2, s_.sinp, s_.cosp)
        self.ts(s_.TA[:], s_.cosp[:], CR[:, c1], None, ALU.mult, r=[s_.cosp, CR], w=[s_.TA])
        self.stt(s_.TA[:], s_.sinp[:], CI[:, c1], s_.TA[:], ALU.mult, ALU.add, r=[s_.sinp, CI, s_.TA], w=[s_.TA])
        self.ts(s_.TB[:], s_.cosp[:], SCI[:, c1], None, ALU.mult, r=[s_.cosp, SCI], w=[s_.TB])
        self.stt(s_.TB[:], s_.sinp[:], NSCR[:, c1], s_.TB[:], ALU.mult, ALU.add, r=[s_.sinp, NSCR, s_.TB], w=[s_.TB])
        self.ts(s_.RC[:], s_.cosp[:], self.sgn[:, 1:2], None, ALU.mult, r=[s_.cosp, self.sgn], w=[s_.RC])
        self.act(s_.RS[:], s_.sinp[:], AF.Copy, r=[s_.sinp], w=[s_.RS], scale=-1.0)
        self.act(s_.Rd[:], ones_f[:], AF.Copy, r=[ones_f, Rv], w=[s_.Rd], scale=Rv[:, c1])
        self.ts(s_.rb[:], s_.sinp[:, TS - 1:TS], self.sgn[:, 1:2], None, ALU.mult, r=[s_.sinp, self.sgn], w=[s_.rb])
        self.ts(s_.rot[:], self.ident_f[:], s_.cosp[:, TS - 1:TS], None, ALU.mult, r=[self.ident_f, s_.cosp], w=[s_.rot])
        self.stt(s_.rot[:], self.swap_f[:], s_.rb[:, 0:1], s_.rot[:], ALU.mult, ALU.add,
                 r=[self.swap_f, s_.rb, s_.rot], w=[s_.rot])

    def s5_M(self, it):
        s_ = it["s"]
        k = s_.nchunk % 2
        s_.nchunk += 1
        it["k"] = k
        pp = s_.pp[k]
        src, jj, t0 = it["src"], it["j"], it["t0"]
        piece = t0 // s_.uch
        ukey = (id(src), jj, piece)
        if s_.ucur != ukey:
            ub = s_.ub[s_.ucnt % 2]
            s_.ucnt += 1
            self.load(ub[:], src.h[jj * 128:(jj + 1) * 128, piece * s_.uch:(piece + 1) * s_.uch], r=[src], w=[ub])
            s_.ucur = ukey
            s_.ubcur = ub
        ut = s_.ubcur
        off = t0 - piece * s_.uch
        rhs = ut[:, off:off + TS]
        if it["rev"]:
            rhs = rev_ap(rhs)
        self.mm(pp[:, 0:TS], s_.Bb[:, 0, :], rhs, True, True, r=[s_.Bb, ut], w=[pp])
        self.mm(pp[:, TS:2 * TS], s_.Bb[:, 1, :], rhs, True, True, r=[s_.Bb, ut], w=[pp])

    def s5_A(self, it):
        s_ = it["s"]
        k = it["k"]
        pp, t2, dd = s_.pp[k], s_.t2[k], s_.dd[k]
        self.tt(pp[:, 0:TS], pp[:, 0:TS], s_.TA[:], ALU.mult, r=[pp, s_.TA], w=[pp])
        self.tt(t2[:], pp[:, TS:2 * TS], s_.TB[:], ALU.mult, r=[pp, s_.TB], w=[t2])
        self.tt(dd[:], pp[:, 0:TS], t2[:], ALU.add, r=[pp, t2], w=[dd])

    def s5_S(self, it):
        s_ = it["s"]
        k = it["k"]
        dd = s_.dd[k]
        if it["first"]:
            self.memset(s_.carry[:], 0.0, w=[s_.carry])
        self.P.op("dve", lambda e: e.tensor_tensor_scan(out=s_.pw[:, 0:TS], data0=s_.Rd[:], data1=dd[:],
                                                        initial=s_.carry[:, 0:1], op0=ALU.mult, op1=ALU.add),
                  _bufs([s_.Rd, dd, s_.carry]), _bufs([s_.pw]))
        if not it["last"]:
            self.cp(s_.wl[:], s_.pw[:, TS - 1:TS], r=[s_.pw], w=[s_.wl], eng="act")
            self.mm(s_.prot[:, 0:1], s_.rot[:], s_.wl[:], True, True, r=[s_.rot, s_.wl], w=[s_.prot])
            self.cp(s_.carry[:], s_.prot[:, 0:1], r=[s_.prot], w=[s_.carry], eng="act")

    def s5_O(self, it):
        s_ = it["s"]
        k = it["k"]
        ypos, rev, d, g = it["ypos"], it["rev"], it["d"], it["g"]
        if ypos is None:
            return
        e1, e2 = s_.e1[k], s_.e2[k]
        self.tt(e1[:], s_.pw[:, 0:TS], s_.RC[:], ALU.mult, r=[s_.pw, s_.RC], w=[e1])
        self.tt(e2[:], s_.pw[:, 0:TS], s_.RS[:], ALU.mult, r=[s_.pw, s_.RS], w=[e2])
        py = s_.py[0:16, 0:TS]
        self.mm(py, s_.Cb[:, 0, :], e1[:], True, False, r=[s_.Cb, e1], w=[s_.py])
        self.mm(py, s_.Cb[:, 1, :], e2[:], False, True, r=[s_.Cb, e2], w=[s_.py])
        nper = 512 // TS
        sidx = s_.nstg // nper
        stg = s_.ystg[sidx % 2]
        q = s_.nstg % nper
        s_.nstg += 1
        base = (ypos // 512) * 512
        off = ypos - base
        dst = stg[0:16, off:off + TS]
        if rev:
            dst = rev_ap(dst)
        self.cp(dst, py, r=[s_.py], w=[stg], eng="act")
        if q == nper - 1:
            self.store(self.YS[d].h[g * 16:(g + 1) * 16, base:base + 512], stg[0:16, :], r=[stg], w=[self.YS[d]])

    def phase3(self, st):
        I = self.I
        Lp, S, Ls = self.Lp, self.S, self.Ls
        SC = 128.0 ** -0.5
        wv = self.W["w_in"].h.rearrange("(j p) c -> p j c", p=128)
        ws = [self.sb(st, f"ws{i}", [128, 8, 512], BF16) for i in range(3)]
        self.wsi = 0

        def wload(src_ap, rd):
            t = ws[self.wsi % 3]
            self.wsi += 1
            self.load(t[:], src_ap, r=[rd], w=[t])
            return t

        xt = [self.sb(st, f"xt{i}", [128, D], F32) for i in range(2)]
        xn = self.sb(st, "xn", [128, 4, D], BF16)
        ss = self.sb(st, "ss", [128, 4], F32)
        hT = self.sb(st, "hT", [128, 8, 512], BF16)
        cs = self.sb(st, "cs", [128, 4, 128], F32)
        tabs = [self.sb(st, f"tabs{i}", [128, 4, 64], F32) for i in range(2)]
        sq = self.sb(st, "sq", [128, 512], F32)
        qss = [self.sb(st, f"qss{i}", [128, 8], F32) for i in range(2)]
        qa = self.sb(st, "qa", [128, 512], F32)
        t4 = self.sb(st, "t4", [128, 4, 256], F32)
        qrot = [self.sb(st, f"qrot{i}", [128, 8, 128], BF16) for i in range(2)]
        QT = self.sb(st, "QT", [128, 8, 512], BF16)
        GA = self.sb(st, "GA", [128, 8, 512], BF16)
        YA = self.sb(st, "YA", [128, 8, 512], BF16)
        KC = 512
        kts = [self.sb(st, f"kts{i}", [128, KC], BF16) for i in range(3)]
        vas = [self.sb(st, f"vas{i}", [128, KC // 128, 129], BF16) for i in range(3)]
        PT = [self.sb(st, f"PT{i}", [128, 512], BF16) for i in range(3)]
        rcp = self.sb(st, "rcp", [128, 4], F32)
        yn = [self.sb(st, f"yn{i}", [128, 128], BF16) for i in range(2)]
        y0 = [self.sb(st, f"y0_{i}", [128, 512], F32) for i in range(1)] * 2
        y1 = [self.sb(st, f"y1_{i}", [128, 512], F32) for i in range(1)] * 2
        uu = [self.sb(st, f"uu{i}", [128, 512], BF16) for i in range(2)]
        gx = [self.sb(st, f"gx{i}", [128, 512], F32) for i in range(2)]
        g2 = [self.sb(st, f"g2{i}", [128, 512], F32) for i in range(2)]
        YG = self.sb(st, "YG", [128, 4, 512], F32)
        YGb = self.sb(st, "YGb", [128, 4, 512], BF16)
        GS = self.sb(st, "GS", [128, 4, 512], BF16)
        sgl = [self.sb(st, f"sgl{i}", [128, 512], F32) for i in range(2)]
        YSb = self.sb(st, "YSb", [128, 4, 512], BF16)
        wglu = self.sb(st, "wglu", [128, 4, 512], BF16)
        self.load(wglu[:], self.W["w_glu"].h.rearrange("(k p) c -> p k c", p=128), r=[self.W["w_glu"]], w=[wglu])
        QX = self.sb(st, "QX", [128, 4, 512], BF16)
        GX = self.sb(st, "GX", [128, 4, 512], BF16)
        PX = [self.sb(st, f"PX{i}", [128, 512], BF16) for i in range(2)]
        rd = self.sb(st, "rd", [128, 512], F32)
        yx = self.sb(st, "yx", [128, 512], F32)
        YX = self.sb(st, "YX", [128, 4, 512], BF16)
        G3 = self.sb(st, "G3", [128, 3, 4, 512], BF16)
        m = [self.sb(st, f"m{i}", [128, 512], F32) for i in range(3)]
        M = self.sb(st, "M", [128, 8, 512], BF16)
        yres = [self.sb(st, f"yres{i}", [128, D], F32) for i in range(1)] * 2
        fss = [self.sb(st, f"fss{i}", [128, 1], F32) for i in range(2)]
        gf = self.sb(st, "gf", [128, D], F32)
        self.load(gf[:], I["g_f"][:, :], w=[gf])
        bk = [self.ps(st, f"bk{i}", [128, 512], F32) for i in range(8)]

        def bf(b):
            return b[:].bitcast(BF16)

        seqs = [("p", I["xp"], I["csp"], 0, Lp, Lp, 0), ("s", I["xs"], I["css"], 7 * S, S, Ls, Lp)]
        for key, xsrc, cssrc, xoff, nown, Lk, yoff in seqs:
            for t in range(nown // 512):
                t0 = xoff + t * 512
                yo = yoff + t * 512
                self.make_hT_bank(xsrc[t0:t0 + 512, :], xt, ss, xn, [bk[6], bk[7]], hT, self.g_in)
                self.load(cs[:], cssrc[t0:t0 + 512, :].rearrange("(b p) c -> p b c", p=128), w=[cs])
                self.load(YA[:], self.YAs.h[:, yo:yo + 512].rearrange("(h p) l -> p h l", p=128), r=[self.YAs], w=[YA])
                wgs = wload(wv[:, :, C_GS:C_GS + 512], self.W["w_in"])
                for i in range(4):
                    k2 = i % 2
                    self.load(y0[k2][:], self.YS[0].h[i * 128:(i + 1) * 128, yo:yo + 512], r=[self.YS[0]], w=[y0[k2]])
                    self.load(y1[k2][:], self.YS[1].h[i * 128:(i + 1) * 128, yo:yo + 512], r=[self.YS[1]], w=[y1[k2]])
                    usrc = self.UT["p"] if key == "p" else self.UT["sf"]
                    self.load(uu[k2][:], usrc.h[i * 128:(i + 1) * 128, t0:t0 + 512], r=[usrc], w=[uu[k2]])
                    a, bq = gx[k2], g2[k2]
                    self.tt(a[:], y0[k2][:], y1[k2][:], ALU.add, r=[y0[k2], y1[k2]], w=[a])
                    self.stt(a[:], uu[k2][:], self.s5d[:, i:i + 1], a[:], ALU.mult, ALU.add, r=[uu[k2], self.s5d, a], w=[a])
                    self.tt(bq[:], a[:], a[:], ALU.mult, r=[a], w=[bq])
                    self.ts(bq[:], bq[:], 0.044715, 1.0, ALU.mult, ALU.add, r=[bq], w=[bq])
                    self.tt(bq[:], bq[:], a[:], ALU.mult, r=[bq, a], w=[bq])
                    self.act(bq[:], bq[:], AF.Sigmoid, r=[bq], w=[bq], scale=2.0 * math.sqrt(2.0 / math.pi))
                    self.tt(YG[:, i, :], a[:], bq[:], ALU.mult, r=[a, bq], w=[YG])
                    self.cp(YGb[:, i, :], YG[:, i, :], r=[YG], w=[YGb], eng="act")
                    p = bk[i % 2]
                    for j in range(8):
                        self.mm(p[:, :], wgs[:, j, i * 128:(i + 1) * 128], hT[:, j, :], j == 0, j == 7, r=[wgs, hT], w=[p])
                    self.act(GS[:, i, :], p[:, :], AF.Silu, r=[p], w=[GS])
                for o in range(4):
                    p = bk[2 + o % 2]
                    for k_ in range(4):
                        self.mm(p[:, :], wglu[:, k_, o * 128:(o + 1) * 128], YGb[:, k_, :], k_ == 0, k_ == 3,
                                r=[wglu, YGb], w=[p])
                    s_ = sgl[o % 2]
                    self.act(s_[:], p[:, :], AF.Sigmoid, r=[p, self.bglu], w=[s_], bias=self.bglu[:, o:o + 1])
                    self.tt(s_[:], s_[:], YG[:, o, :], ALU.mult, r=[s_, YG], w=[s_])
                    self.tt(YSb[:, o, :], s_[:], GS[:, o, :], ALU.mult, r=[s_, GS], w=[YSb])
                wqx = wload(wv[:, :, C_QX:C_QX + 512], self.W["w_in"])
                wgx = wload(wv[:, :, C_GX:C_GX + 512], self.W["w_in"])
                for o in range(4):
                    p = bk[o % 2]
                    for j in range(8):
                        self.mm(p[:, :], wqx[:, j, o * 128:(o + 1) * 128], hT[:, j, :], j == 0, j == 7, r=[wqx, hT], w=[p])
                    self.cp(QX[:, o, :], p[:, :], r=[p], w=[QX], eng="act")
                    p2 = bk[2 + o % 2]
                    for j in range(8):
                        self.mm(p2[:, :], wgx[:, j, o * 128:(o + 1) * 128], hT[:, j, :], j == 0, j == 7, r=[wgx, hT], w=[p2])
                    self.act(GX[:, o, :], p2[:, :], AF.Silu, r=[p2], w=[GX])
                KmT, Vm = self.KmT[key], self.Vm[key]
                for hx in range(4):
                    for mt in range(2):
                        p = bk[mt]
                        self.mm(p[:, :], KmT[:, hx, mt * 128:(mt + 1) * 128], QX[:, hx, :], True, True, r=[KmT, QX], w=[p])
                        self.act(PX[mt][:], p[:, :], AF.Exp, r=[p], w=[PX[mt]], scale=SC)
                    po_, pd_ = bk[4], bk[5]
                    for mt in range(2):
                        self.mm(po_[:, :], Vm[:, mt, hx * 128:(hx + 1) * 128], PX[mt][:], mt == 0, mt == 1, r=[Vm, PX[mt]], w=[po_])
                    for mt in range(2):
                        self.mm(pd_[:, :], self.ones_b[:], PX[mt][:], mt == 0, mt == 1, r=[self.ones_b, PX[mt]], w=[pd_])
                    self.recip(rd[:], pd_[:, :], r=[pd_], w=[rd])
                    self.tt(yx[:], po_[:, :], rd[:], ALU.mult, r=[po_, rd], w=[yx])
                    self.tt(YX[:, hx, :], yx[:], GX[:, hx, :], ALU.mult, r=[yx, GX], w=[YX])
                wpa = self.W["w_pa"].h.rearrange("(k p) c -> p k c", p=128)
                wps = self.W["w_ps"].h.rearrange("(k p) c -> p k c", p=128)
                wpx = self.W["w_px"].h.rearrange("(k p) c -> p k c", p=128)
                for og in range(2):
                    for br in range(3):
                        wm_ = wload(wv[:, :, C_MG + br * 1024 + og * 512:C_MG + br * 1024 + (og + 1) * 512], self.W["w_in"])
                        for o in range(4):
                            p = bk[o % 2]
                            for j in range(8):
                                self.mm(p[:, :], wm_[:, j, o * 128:(o + 1) * 128], hT[:, j, :], j == 0, j == 7, r=[wm_, hT], w=[p])
                            self.act(G3[:, br, o, :], p[:, :], AF.Sigmoid, r=[p], w=[G3])
                    wa = wload(wpa[:, :, og * 512:(og + 1) * 512], self.W["w_pa"])
                    wsx = ws[self.wsi % 3]
                    self.wsi += 1
                    self.load(wsx[:, 0:4, :], wps[:, :, og * 512:(og + 1) * 512], r=[self.W["w_ps"]], w=[wsx])
                    self.load(wsx[:, 4:8, :], wpx[:, :, og * 512:(og + 1) * 512], r=[self.W["w_px"]], w=[wsx])
                    for o in range(4):
                        pa_, ps_, px_ = bk[2 + (o % 2) * 3], bk[3 + (o % 2) * 3], bk[4 + (o % 2) * 3]
                        for k_ in range(8):
                            self.mm(pa_[:, :], wa[:, k_, o * 128:(o + 1) * 128], YA[:, k_, :], k_ == 0, k_ == 7, r=[wa, YA], w=[pa_])
                        for k_ in range(4):
                            self.mm(ps_[:, :], wsx[:, k_, o * 128:(o + 1) * 128], YSb[:, k_, :], k_ == 0, k_ == 3, r=[wsx, YSb], w=[ps_])
                        for k_ in range(4):
                            self.mm(px_[:, :], wsx[:, 4 + k_, o * 128:(o + 1) * 128], YX[:, k_, :], k_ == 0, k_ == 3, r=[wsx, YX], w=[px_])
                        self.tt(m[0][:], pa_[:, :], G3[:, 0, o, :], ALU.mult, r=[pa_, G3], w=[m[0]])
                        self.tt(m[1][:], ps_[:, :], G3[:, 1, o, :], ALU.mult, r=[ps_, G3], w=[m[1]])
                        self.tt(m[2][:], px_[:, :], G3[:, 2, o, :], ALU.mult, r=[px_, G3], w=[m[2]])
                        self.tt(m[0][:], m[0][:], m[1][:], ALU.add, r=[m[0], m[1]], w=[m[0]])
                        self.tt(M[:, og * 4 + o, :], m[0][:], m[2][:], ALU.add, r=[m[0], m[2]], w=[M])
                wo_ = self.W["w_out"].h.rearrange("(k p) c -> p k c", p=128)
                wo0 = wload(wo_[:, :, 0:512], self.W["w_out"])
                wo1 = wload(wo_[:, :, 512:1024], self.W["w_out"])
                for b in range(4):
                    pa_, pb_ = bk[(2 * b) % 4], bk[(2 * b + 1) % 4]
                    for k_ in range(8):
                        self.mm(pa_[:, :], M[:, k_, b * 128:(b + 1) * 128], wo0[:, k_, :], k_ == 0, k_ == 7, r=[M, wo0], w=[pa_])
                    for k_ in range(8):
                        self.mm(pb_[:, :], M[:, k_, b * 128:(b + 1) * 128], wo1[:, k_, :], k_ == 0, k_ == 7, r=[M, wo1], w=[pb_])
                    yr, fs = yres[b % 2], fss[b % 2]
                    xb = xt[b % 2]
                    self.load(xb[:], xsrc[t0 + b * 128:t0 + (b + 1) * 128, :], w=[xb])
                    self.tt(yr[:, 0:512], pa_[:, :], xb[:, 0:512], ALU.add, r=[pa_, xb], w=[yr])
                    self.tt(yr[:, 512:1024], pb_[:, :], xb[:, 512:1024], ALU.add, r=[pb_, xb], w=[yr])
                    self.act(xn[:, 0, :], yr[:], AF.Square, r=[yr], w=[xn, fs], accum_out=fs[:, 0:1])
                    self.rstd(fs, 1, 1.0 / D)
                    self.stt(yr[:], yr[:], fs[:, 0:1], gf[:], ALU.mult, ALU.mult, r=[yr, fs, gf], w=[yr])
                    self.store(self.y_out[yo + b * 128:yo + (b + 1) * 128, :], yr[:], r=[yr])

    def make_hT_bank(self, x_rows, xt, ss, xn, banks, hT, gain):
        for b in range(4):
            xb = xt[b % 2]
            sb_ = ss[b % 2] if isinstance(ss, list) else ss
            self.load(xb[:], x_rows[b * 128:(b + 1) * 128, :], w=[xb])
            self.act(xn[:, b, :], xb[:], AF.Square, r=[xb], w=[xn, ss], accum_out=ss[:, b:b + 1])
            v = ss[:, b:b + 1]
            self.ts(v, v, 1.0 / D, EPS, ALU.mult, ALU.add, r=[ss], w=[ss])
            self.act(v, v, AF.Sqrt, r=[ss], w=[ss])
            self.recip(v, v, r=[ss], w=[ss])
            if b % 2 == 0:
                self.act(xn[:, b, :], xb[:], AF.Copy, r=[xb, ss], w=[xn], scale=ss[:, b:b + 1])
            else:
                self.ts(xn[:, b, :], xb[:], ss[:, b:b + 1], None, ALU.mult, r=[xb, ss], w=[xn])
        for j in range(8):
            bank = banks[j % len(banks)]
            pv = bank[:].bitcast(BF16)
            for b in range(4):
                self.tr(pv[:, b * 128:(b + 1) * 128], xn[:, b, j * 128:(j + 1) * 128], self.ident_b[:],
                        r=[xn, self.ident_b], w=[bank])
            if j % 2 == 0:
                self.ts(hT[:, j, :], pv[:, 0:512], gain[:, j:j + 1], None, ALU.mult, r=[bank, gain], w=[hT])
            else:
                self.act(hT[:, j, :], pv[:, 0:512], AF.Copy, r=[bank, gain], w=[hT], scale=gain[:, j:j + 1])

    def norm_rope_q(self, pbank, half, sq, ssv, xa, t4, tabs, out_bf):
        nh = 4
        o = 0
        h0 = half * 4
        psrc = pbank[:, :]
        self.act(sq[:, o:o + 512], psrc, AF.Square, r=[pbank], w=[sq])
        self.P.op("dve", lambda e: e.tensor_reduce(out=ssv[:, h0:h0 + 4], in_=sq[:, o:o + 512].rearrange("p (h d) -> p h d", h=nh),
                                                   axis=AX.X, op=ALU.add), _bufs([sq]), _bufs([ssv]))
        v = ssv[:, h0:h0 + 4]
        self.ts(v, v, 1.0 / 128.0, EPS, ALU.mult, ALU.add, r=[ssv], w=[ssv])
        self.act(v, v, AF.Sqrt, r=[ssv], w=[ssv])
        self.recip(v, v, r=[ssv], w=[ssv])
        xa3 = xa[:, o:o + 512].rearrange("p (h d) -> p h d", h=nh)
        self.tt(xa3, psrc.rearrange("p (h d) -> p h d", h=nh), v.unsqueeze(2).to_broadcast([128, nh, 128]),
                ALU.mult, r=[pbank, ssv], w=[xa])
        x0 = xa[:, o:o + 512].rearrange("p (h i two) -> p h i two", h=nh, two=2)[:, :, :, 0]
        x1 = xa[:, o:o + 512].rearrange("p (h i two) -> p h i two", h=nh, two=2)[:, :, :, 1]
        ob = out_bf[:, h0:h0 + 4, :].rearrange("p h (i two) -> p h i two", two=2)
        o0, o1 = ob[:, :, :, 0], ob[:, :, :, 1]

        def tb(i):
            return tabs[:, i, :].unsqueeze(1).to_broadcast([128, nh, 64])
        tv = [t4[:, i, 0:256].rearrange("p (h i) -> p h i", h=nh) for i in range(4)]
        self.tt(tv[0], x0, tb(0), ALU.mult, r=[xa, tabs], w=[t4])
        self.tt(tv[1], x1, tb(1), ALU.mult, r=[xa, tabs], w=[t4])
        self.tt(tv[2], x0, tb(2), ALU.mult, r=[xa, tabs], w=[t4])
        self.tt(tv[3], x1, tb(3), ALU.mult, r=[xa, tabs], w=[t4])
        self.tt(o0, tv[0], tv[1], ALU.subtract, r=[t4], w=[out_bf])
        self.tt(o1, tv[2], tv[3], ALU.add, r=[t4], w=[out_bf])


def rope_table(pos):
    pos = np.asarray(pos)
    row = (pos // 64).astype(np.float32)
    col = (pos % 64).astype(np.float32)
    freqs = (np.float32(10000.0) ** (-np.arange(32, dtype=np.float32) / np.float32(32))).astype(np.float32)
    ang = np.concatenate([row[:, None] * freqs, col[:, None] * freqs], axis=-1).astype(np.float32)
    return np.concatenate([np.cos(ang), np.sin(ang)], axis=-1).astype(np.float32)


def host_inputs(inp, Lp, S, ncores=NCORES):
    f = lambda a: np.ascontiguousarray(np.asarray(a, dtype=np.float32))
    Ls = 8 * S
    xs_all = f(inp["x_sample"])[0]
    shared = {}
    shared["w_in"] = f(inp["w_in"])[0]
    shared["w_glu"] = f(inp["w_glu"])[0]
    shared["w_mem_kv"] = f(inp["w_mem_kv"])[0]
    shared["w_pa"] = f(inp["w_proj_attn"])[0]
    shared["w_ps"] = f(inp["w_proj_ssm"])[0]
    shared["w_px"] = f(inp["w_proj_cross"])[0]
    shared["w_out"] = f(inp["w_out"])[0]
    shared["g_in"] = f(f(inp["norm_in"])[0].reshape(8, 128).T)
    shared["g_mem"] = f(f(inp["norm_mem"])[0].reshape(8, 128).T)
    qn, kn = f(inp["q_norm"])[0], f(inp["k_norm"])[0]
    shared["g_q"] = f(np.tile(np.concatenate([qn[0::2], qn[1::2]])[None, :], (128, 1)))
    shared["g_k"] = f(np.tile(np.concatenate([kn[0::2], kn[1::2]])[None, :], (128, 1)))
    shared["g_f"] = f(np.tile(f(inp["norm_final"])[None, :], (128, 1)))
    shared["s5d"] = f(f(inp["s5_d"])[0].reshape(4, 128).T)
    shared["bglu"] = f(f(inp["b_glu"])[0].reshape(4, 128).T)
    a_re, a_im = f(inp["s5_a_re"])[0], f(inp["s5_a_im"])[0]
    dup = lambda a: f(np.concatenate([a.reshape(64, 64).T, a.reshape(64, 64).T], axis=0))
    shared["are"] = dup(a_re)
    shared["aim"] = dup(a_im)
    shared["lst"] = f(np.tile(f(inp["s5_log_step"])[0].reshape(1, 64), (128, 1)))
    b_re, b_im = f(inp["s5_b_re"])[0], f(inp["s5_b_im"])[0]
    c_re, c_im = f(inp["s5_c_re"])[0], f(inp["s5_c_im"])[0]
    B1 = np.zeros((64, 128, 128), np.float32)
    B2 = np.zeros((64, 128, 128), np.float32)
    C1 = np.zeros((64, 128, 16), np.float32)
    C2 = np.zeros((64, 128, 16), np.float32)
    for d in range(2):
        for g in range(32):
            col = d * 32 + g
            r0 = (g % 8) * 16
            B1[col, r0:r0 + 16, 0:64] = b_re[d, g].T
            B1[col, r0:r0 + 16, 64:128] = b_im[d, g].T
            B2[col, r0:r0 + 16, 0:64] = b_im[d, g].T
            B2[col, r0:r0 + 16, 64:128] = b_re[d, g].T
            C1[col, 0:64, :] = c_re[d, g].T
            C1[col, 64:128, :] = c_im[d, g].T
            C2[col, 0:64, :] = c_im[d, g].T
            C2[col, 64:128, :] = c_re[d, g].T
    shared.update(B1=B1, B2=B2, C1=C1, C2=C2)
    shared["ident"] = np.eye(128, dtype=np.float32)
    sw = np.zeros((128, 128), np.float32)
    sw[np.arange(128), (np.arange(128) + 64) % 128] = 1.0
    shared["swap"] = sw
    shared["iota1"] = f(np.tile(np.arange(1, TS + 1, dtype=np.float32)[None, :], (128, 1)))
    sg = np.ones((128, 2), np.float32)
    sg[0:64, 0] = -1.0
    sg[64:128, 1] = -1.0
    shared["sgn"] = sg
    shared["csp"] = rope_table(np.arange(Lp))
    maps = []
    for c in range(ncores):
        m = dict(shared)
        m["xp"] = f(inp["x_prompt"])[c]
        order = np.concatenate([np.arange((c + 1) * S, Ls), np.arange(0, c * S), np.arange(c * S, (c + 1) * S)])
        m["xs"] = np.ascontiguousarray(xs_all[order])
        m["css"] = rope_table(order)
        m["memp"] = f(inp["mem_prompt"])[c]
        m["mems"] = f(inp["mem_sample"])[0]
        mf = np.zeros((1, 7 * S), np.float32)
        mf[0, (7 - c) * S:] = 1.0
        m["mf"] = mf
        m["mb"] = (1.0 - mf).astype(np.float32)
        maps.append(m)
    return maps


_NC_CACHE = {}


def run(inp, Lp, S):
    key = (Lp, S)
    if key not in _NC_CACHE:
        _NC_CACHE[key] = Builder(Lp, S).build()
    nc = _NC_CACHE[key]
    maps = host_inputs(inp, Lp, S)
    res = run_bass_kernel_spmd(nc, maps, core_ids=list(range(NCORES)))
    ys = [np.asarray(r["y"]) for r in res.results]
    y_prompt = np.stack([y[:Lp] for y in ys], axis=0).astype(np.float32)
    y_sample = np.concatenate([y[Lp:Lp + S] for y in ys], axis=0)[None].astype(np.float32)
    return y_prompt, y_sample


def kernel(**inputs):
    Lp = int(np.asarray(inputs["x_prompt"]).shape[1])
    Ls = int(np.asarray(inputs["x_sample"]).shape[1])
    return run(inputs, Lp, Ls // 8)
```

```python
import math
from contextlib import ExitStack

import numpy as np
import concourse.bass as bass
import concourse.mybir as mybir
from concourse.bass_utils import run_bass_kernel_spmd

F32 = mybir.dt.float32
BF16 = mybir.dt.bfloat16
AF = mybir.ActivationFunctionType
ALU = mybir.AluOpType
AX = mybir.AxisListType

D = 1024
IN_W = 7680
C_Q, C_K, C_V, C_GA, C_U, C_GS, C_QX, C_GX, C_MG = 0, 1024, 1280, 1536, 2560, 3072, 3584, 4096, 4608
EPS = 1e-6
NCORES = 8
TS = 256
NPC = 4
PIECE = NPC * TS
MAGIC = 12582912.0
TWO_PI = 2.0 * math.pi
CW1 = 6.28125
CW2 = TWO_PI - CW1
SEM_CH = 20000
N_DMA_SEMS = 40


class Buf:
    __slots__ = ("lw", "rd", "ex")

    def __init__(self):
        self.lw = None
        self.rd = []
        self.ex = False


class T:
    def __init__(self, h, b=None):
        self.h = h
        self.b = b if b is not None else Buf()

    def __getitem__(self, k):
        return self.h[k]


def _bufs(xs):
    out = []
    for x in xs:
        if x is None:
            continue
        out.append(x.b if isinstance(x, T) else x)
    return out


class Prog:
    ENGS = ("pe", "act", "dve", "pool", "sp")

    def __init__(self, nc, stack):
        self.nc = nc
        self.stack = stack
        self.q = {e: [] for e in self.ENGS}
        self.cnt = {e: 0 for e in self.ENGS}
        self.sems = {e: [] for e in self.ENGS}
        self.dsems = []
        self.dcnt = []
        self.drr = 0
        self.n_dma = 0
        self.pend = {e: [] for e in self.ENGS}
        self.waited = {e: {} for e in self.ENGS}

    def _sem(self, e, idx):
        k = idx // SEM_CH
        while len(self.sems[e]) <= k:
            self.sems[e].append(self.stack.enter_context(
                self.nc.semaphore(f"s_{e}_{len(self.sems[e])}")))
        return self.sems[e][k], idx % SEM_CH + 1

    def op(self, e, fn, reads=(), writes=(), dma=False):
        reads = _bufs(reads)
        writes = _bufs(writes)
        exr = [b for b in reads if b.ex]
        if exr:
            reads = [b for b in reads if not b.ex]
            writes = writes + [b for b in exr if b not in writes]
        deps = {}

        def add(d):
            if d is None:
                return
            if d[0] not in deps or deps[d[0]][1] < d[1]:
                deps[d[0]] = d
        for b in reads:
            add(b.lw)
        for b in writes:
            add(b.lw)
            for r in b.rd:
                add(r)
        idx = self.cnt[e]
        if not dma:
            self.cnt[e] += 1
        waits = list(self.pend[e])
        self.pend[e] = []
        for key, d in deps.items():
            if key == "pe" and e == "pe":
                continue
            waits.append((d[2], d[3]))
        wd = self.waited[e]
        ww = []
        for (ws, wv) in waits:
            if wd.get(id(ws), 0) >= wv:
                continue
            wd[id(ws)] = wv
            ww.append((ws, wv))
        waits = ww
        if dma:
            if len(self.dsems) < N_DMA_SEMS:
                self.dsems.append(self.stack.enter_context(
                    self.nc.semaphore(f"s_dma_{len(self.dsems)}")))
                self.dcnt.append(0)
                i = len(self.dsems) - 1
            else:
                i = self.drr
                self.drr = (self.drr + 1) % N_DMA_SEMS
            if self.dcnt[i] > 0 and wd.get(id(self.dsems[i]), 0) < self.dcnt[i]:
                wd[id(self.dsems[i])] = self.dcnt[i]
                waits.append((self.dsems[i], self.dcnt[i]))
            self.dcnt[i] += 16
            me = ("dma%d" % self.n_dma, 0, self.dsems[i], self.dcnt[i])
            self.n_dma += 1
            self.q[e].append((waits, fn, self.dsems[i], 16))
        else:
            s, v = self._sem(e, idx)
            me = (e, idx, s, v)
            self.q[e].append((waits, fn, s, 1))
        for b in reads:
            b.rd.append(me)
        for b in writes:
            b.lw = me
            b.rd = []
        return me

    def all_done_waits(self):
        final = [(s, c) for s, c in zip(self.dsems, self.dcnt) if c > 0]
        for e in self.ENGS:
            if self.cnt[e] > 0:
                final.append(self._sem(e, self.cnt[e] - 1))
        return final

    def barrier(self):
        w = self.all_done_waits()
        for e in self.ENGS:
            self.pend[e] = list(w)

    def emit(self, last=False):
        nc = self.nc
        prog = self
        final = self.all_done_waits() if last else []
        with nc.Block() as block:
            def run(eng, name):
                for waits, fn, s, inc in prog.q[name]:
                    for (ws, wv) in waits:
                        eng.wait_ge(ws, wv)
                    fn(eng).then_inc(s, inc)
                prog.q[name] = []

            @block.tensor
            def _(eng):
                run(eng, "pe")

            @block.scalar
            def _(eng):
                run(eng, "act")

            @block.vector
            def _(eng):
                run(eng, "dve")

            @block.gpsimd
            def _(eng):
                run(eng, "pool")

            @block.sync
            def _(eng):
                run(eng, "sp")
                for (ws, wv) in final:
                    eng.wait_ge(ws, wv)


def rev_ap(ap2d):
    a = ap2d.ap
    assert len(a) == 2, a
    n = a[1][1]
    st = a[1][0]
    return bass.AP(ap2d.tensor, ap2d.offset + st * (n - 1), [list(a[0]), [-st, n]])


class Builder:
    def __init__(self, Lp, S, dbg=False):
        self.Lp, self.S = Lp, S
        self.Ls = 8 * S
        self.Lo = Lp + S
        self.dbg = dbg
        self.nc = bass.Bass("TRN2", target_bir_lowering=False)

    def dram_in(self, name, shape, dt=F32):
        return self.nc.dram_tensor(name, list(shape), dt, kind="ExternalInput").ap()

    def dram_out(self, name, shape, dt=F32):
        return self.nc.dram_tensor(name, list(shape), dt, kind="ExternalOutput").ap()

    def dram_scr(self, name, shape, dt):
        kind = "ExternalOutput" if (self.dbg and name.split("_")[0] in str(self.dbg)) else "Internal"
        return T(self.nc.dram_tensor(name, list(shape), dt, kind=kind).ap())

    _uid = 0

    def sb(self, st, name, shape, dt):
        Builder._uid += 1
        return T(st.enter_context(self.nc.sbuf_tensor(f"sb{Builder._uid}_{name}", list(shape), dt)))

    def ps(self, st, name, shape, dt):
        Builder._uid += 1
        nbytes = int(np.prod(shape[1:])) * (4 if dt == F32 else 2)
        assert nbytes == 2048, (name, shape)
        t = T(st.enter_context(self.nc.psum_tensor(f"ps{Builder._uid}_{name}", list(shape), dt)))
        t.b.ex = True
        return t

    def load(self, out, in_, r=(), w=()):
        self.P.op("sp", lambda e: e.dma_start(out=out, in_=in_), r, w, dma=True)

    def store(self, out, in_, r=(), w=()):
        self.P.op("pool", lambda e: e.dma_start(out=out, in_=in_), r, w, dma=True)

    def mm(self, out, lhsT, rhs, start, stop, r=(), w=()):
        self.P.op("pe", lambda e: e.matmul(out, lhsT=lhsT, rhs=rhs, start=start, stop=stop), r, w)

    def tr(self, out, in_, ident, r=(), w=()):
        self.P.op("pe", lambda e: e.transpose(out, in_, ident), r, w)

    def act(self, out, in_, func, r=(), w=(), eng="act", **kw):
        self.P.op(eng, lambda e: e.activation(out=out, in_=in_, func=func, **kw), r, w)

    def tt(self, out, in0, in1, op, r=(), w=(), eng="dve"):
        self.P.op(eng, lambda e: e.tensor_tensor(out=out, in0=in0, in1=in1, op=op), r, w)

    def ts(self, out, in0, s1, s2, op0, op1=None, r=(), w=(), eng="dve"):
        if op1 is None:
            self.P.op(eng, lambda e: e.tensor_scalar(out=out, in0=in0, scalar1=s1, scalar2=None, op0=op0), r, w)
        else:
            self.P.op(eng, lambda e: e.tensor_scalar(out=out, in0=in0, scalar1=s1, scalar2=s2, op0=op0, op1=op1), r, w)

    def stt(self, out, in0, scalar, in1, op0, op1, r=(), w=()):
        self.P.op("dve", lambda e: e.scalar_tensor_tensor(out=out, in0=in0, scalar=scalar, in1=in1, op0=op0, op1=op1), r, w)

    def cp(self, out, in_, r=(), w=(), eng="dve"):
        if eng == "act":
            self.P.op("act", lambda e: e.activation(out=out, in_=in_, func=AF.Copy), r, w)
        elif eng == "dve":
            self.P.op("dve", lambda e: e.tensor_scalar(out=out, in0=in_, scalar1=1.0, scalar2=None, op0=ALU.mult), r, w)
        else:
            self.P.op(eng, lambda e: e.tensor_copy(out=out, in_=in_), r, w)

    def recip(self, out, in_, r=(), w=()):
        self.P.op("dve", lambda e: e.reciprocal(out=out, in_=in_), r, w)

    def memset(self, ap, val, r=(), w=(), eng="dve"):
        self.P.op(eng, lambda e: e.memset(ap, val), r, w)

    def rstd(self, v, n, inv_n, r=(), w=()):
        self.ts(v[:, 0:n], v[:, 0:n], inv_n, EPS, ALU.mult, ALU.add, r=list(r) + [v], w=[v])
        self.act(v[:, 0:n], v[:, 0:n], AF.Sqrt, r=[v], w=[v])
        self.recip(v[:, 0:n], v[:, 0:n], r=[v], w=list(w) + [v])

    def build(self):
        nc = self.nc
        Lp, S, Ls, Lo = self.Lp, self.S, self.Ls, self.Lo
        I = {}
        I["xp"] = self.dram_in("xp", [Lp, D])
        I["xs"] = self.dram_in("xs", [Ls, D])
        I["memp"] = self.dram_in("memp", [256, D])
        I["mems"] = self.dram_in("mems", [256, D])
        I["csp"] = self.dram_in("csp", [Lp, 128])
        I["css"] = self.dram_in("css", [Ls, 128])
        I["mf"] = self.dram_in("mf", [1, 7 * S])
        I["mb"] = self.dram_in("mb", [1, 7 * S])
        I["w_in"] = self.dram_in("w_in", [D, IN_W])
        I["w_glu"] = self.dram_in("w_glu", [512, 512])
        I["w_mem_kv"] = self.dram_in("w_mem_kv", [D, 1024])
        I["w_pa"] = self.dram_in("w_pa", [1024, D])
        I["w_ps"] = self.dram_in("w_ps", [512, D])
        I["w_px"] = self.dram_in("w_px", [512, D])
        I["w_out"] = self.dram_in("w_out", [D, D])
        I["g_in"] = self.dram_in("g_in", [128, 8])
        I["g_mem"] = self.dram_in("g_mem", [128, 8])
        I["g_q"] = self.dram_in("g_q", [128, 128])
        I["g_k"] = self.dram_in("g_k", [128, 128])
        I["g_f"] = self.dram_in("g_f", [128, D])
        I["s5d"] = self.dram_in("s5d", [128, 4])
        I["bglu"] = self.dram_in("bglu", [128, 4])
        I["are"] = self.dram_in("are", [128, 64])
        I["aim"] = self.dram_in("aim", [128, 64])
        I["lst"] = self.dram_in("lst", [128, 64])
        I["B1"] = self.dram_in("B1", [64, 128, 128])
        I["B2"] = self.dram_in("B2", [64, 128, 128])
        I["C1"] = self.dram_in("C1", [64, 128, 16])
        I["C2"] = self.dram_in("C2", [64, 128, 16])
        I["ident"] = self.dram_in("ident", [128, 128])
        I["swap"] = self.dram_in("swap", [128, 128])
        I["iota1"] = self.dram_in("iota1", [128, PIECE])
        I["sgn"] = self.dram_in("sgn", [128, 2])
        self.I = I
        self.y_out = self.dram_out("y", [Lo, D])

        self.W = {
            "w_in": self.dram_scr("wb_in", [D, IN_W], BF16),
            "w_glu": self.dram_scr("wb_glu", [512, 512], BF16),
            "w_mem_kv": self.dram_scr("wb_mkv", [D, 1024], BF16),
            "w_pa": self.dram_scr("wb_pa", [1024, D], BF16),
            "w_ps": self.dram_scr("wb_ps", [512, D], BF16),
            "w_px": self.dram_scr("wb_px", [512, D], BF16),
            "w_out": self.dram_scr("wb_out", [D, D], BF16),
        }
        self.KT = {"p": self.dram_scr("KT_p", [2, 128, Lp], BF16),
                   "s": self.dram_scr("KT_s", [2, 128, Ls], BF16)}
        self.VA = {"p": self.dram_scr("VA_p", [2, 128, Lp // 128, 129], BF16),
                   "s": self.dram_scr("VA_s", [2, 128, Ls // 128, 129], BF16)}
        self.UT = {"p": self.dram_scr("UT_p", [512, Lp], BF16),
                   "sf": self.dram_scr("UT_sf", [512, Ls], BF16),
                   "sb": self.dram_scr("UT_sb", [512, Ls], BF16)}
        self.YS = [self.dram_scr("YS_f", [512, Lo], F32), self.dram_scr("YS_b", [512, Lo], F32)]
        self.YAs = self.dram_scr("YA_s", [1024, Lo], BF16)

        with ExitStack() as gst:
            self.P = Prog(nc, gst)
            self.gst = gst
            self.consts(gst)
            phases = [self.phase0, self.phase1, self.phase2, self.phase3]
            stop = getattr(self, "stop", 3)
            for i, ph in enumerate(phases):
                with ExitStack() as st:
                    ph(st)
                    self.P.emit(last=(i == stop))
                if i == stop:
                    break
                self.P.barrier()
        return nc

    def consts(self, st):
        I = self.I
        self.ident_f = self.sb(st, "ident_f", [128, 128], F32)
        self.ident_b = self.sb(st, "ident_b", [128, 128], BF16)
        self.swap_f = self.sb(st, "swap_f", [128, 128], F32)
        self.ones_b = self.sb(st, "ones_b", [128, 128], BF16)
        self.g_in = self.sb(st, "g_in", [128, 8], F32)
        self.g_mem = self.sb(st, "g_mem", [128, 8], F32)
        self.g_q = self.sb(st, "g_q", [128, 128], F32)
        self.g_k = self.sb(st, "g_k", [128, 128], F32)
        self.s5d = self.sb(st, "s5d", [128, 4], F32)
        self.bglu = self.sb(st, "bglu", [128, 4], F32)
        self.sgn = self.sb(st, "sgn", [128, 2], F32)
        self.halfpi = self.sb(st, "halfpi", [128, 1], F32)
        self.KmT = {k: self.sb(st, "KmT" + k, [128, 4, 256], BF16) for k in "ps"}
        self.Vm = {k: self.sb(st, "Vm" + k, [128, 2, 512], BF16) for k in "ps"}
        for t, n in ((self.ident_f, "ident"), (self.swap_f, "swap"), (self.g_in, "g_in"),
                     (self.g_mem, "g_mem"), (self.g_q, "g_q"), (self.g_k, "g_k"),
                     (self.s5d, "s5d"), (self.bglu, "bglu"), (self.sgn, "sgn")):
            self.load(t[:], I[n][:, :], w=[t])
        self.cp(self.ident_b[:], self.ident_f[:], r=[self.ident_f], w=[self.ident_b])
        self.memset(self.ones_b[:], 1.0, w=[self.ones_b])
        self.memset(self.halfpi[:], math.pi / 2.0, w=[self.halfpi])

    def make_hT(self, x_ap_rows, xt, ss, xn, ptr, hT, gain, nblk=4, mask=None):
        self.load(xt[:, 0:nblk, :], x_ap_rows.rearrange("(b p) d -> p b d", p=128), w=[xt])
        for b in range(nblk):
            self.act(xn[:, b, :], xt[:, b, :], AF.Square, r=[xt], w=[xn, ss],
                     accum_out=ss[:, b:b + 1])
        import os
        if os.environ.get("DBG_H") == "1":
            return
        self.rstd(ss, nblk, 1.0 / D)
        if os.environ.get("DBG_H") == "2":
            return
        for b in range(nblk):
            if b % 2 == 0:
                self.act(xn[:, b, :], xt[:, b, :], AF.Copy, r=[xt, ss], w=[xn], scale=ss[:, b:b + 1])
            else:
                self.ts(xn[:, b, :], xt[:, b, :], ss[:, b:b + 1], None, ALU.mult, r=[xt, ss], w=[xn])
        if os.environ.get("DBG_H") == "3":
            return
        for j in range(8):
            pt = ptr[j % len(ptr)]
            for b in range(nblk):
                self.tr(pt[:, b * 128:(b + 1) * 128], xn[:, b, j * 128:(j + 1) * 128], self.ident_b[:],
                        r=[xn, self.ident_b], w=[pt])
            if j % 2 == 0:
                self.ts(hT[:, j, 0:nblk * 128], pt[:, 0:nblk * 128], gain[:, j:j + 1], None, ALU.mult,
                        r=[pt, gain], w=[hT])
            else:
                self.act(hT[:, j, 0:nblk * 128], pt[:, 0:nblk * 128], AF.Copy, r=[pt, gain], w=[hT],
                         scale=gain[:, j:j + 1])

    def phase0(self, st):
        I = self.I
        import os
        if os.environ.get("DBG_P0") == "none":
            return
        stg = [self.sb(st, f"wstg{i}", [128, 2048], F32) for i in range(2)]
        stb = [self.sb(st, f"wstb{i}", [128, 2048], BF16) for i in range(2)]
        k = 0
        for name, rows, cols in (("w_in", D, IN_W), ("w_glu", 512, 512), ("w_mem_kv", D, 1024),
                                 ("w_pa", 1024, D), ("w_ps", 512, D), ("w_px", 512, D), ("w_out", D, D)):
            cw = 1920 if cols == IN_W else cols
            for r0 in range(0, rows, 128):
                for c0 in range(0, cols, cw):
                    a, b = stg[k % 2], stb[k % 2]
                    self.load(a[:, 0:cw], I[name][r0:r0 + 128, c0:c0 + cw], w=[a])
                    self.cp(b[:, 0:cw], a[:, 0:cw], r=[a], w=[b], eng="dve" if k % 2 == 0 else "act")
                    self.store(self.W[name][r0:r0 + 128, c0:c0 + cw], b[:, 0:cw], r=[b], w=[self.W[name]])
                    k += 1
        import os
        if os.environ.get("DBG_P0") == "a":
            return
        wm = self.sb(st, "wm", [128, 8, 1024], BF16)
        self.load(wm[:], self.W["w_mem_kv"].h.rearrange("(j p) c -> p j c", p=128), r=[self.W["w_mem_kv"]], w=[wm])
        xt = self.sb(st, "m_xt", [128, 2, D], F32)
        xn = self.sb(st, "m_xn", [128, 2, D], BF16)
        ss = self.sb(st, "m_ss", [128, 4], F32)
        hT = self.sb(st, "m_hT", [128, 8, 256], BF16)
        vtmp = self.sb(st, "m_v", [128, 512], BF16)
        ptr = [self.ps(st, f"m_ptr{i}", [128, 1024], BF16) for i in range(2)]
        pk = [self.ps(st, f"m_pk{i}", [128, 512], F32) for i in range(2)]
        for key, src in (("p", I["memp"]), ("s", I["mems"])):
            self.make_hT(src[:, :], xt, ss, xn, ptr, hT, self.g_mem, nblk=2)
            if os.environ.get("DBG_P0") == "b1":
                continue
            for hx in range(4):
                p = pk[hx % 2]
                for j in range(8):
                    self.mm(p[:, 0:256], wm[:, j, hx * 128:(hx + 1) * 128], hT[:, j, :], j == 0, j == 7,
                            r=[wm, hT], w=[p])
                if os.environ.get("DBG_P0") == "b2":
                    continue
                self.cp(self.KmT[key][:, hx, :], p[:, 0:256], r=[p], w=[self.KmT[key]], eng="act")
            if os.environ.get("DBG_P0") in ("b2", "b3"):
                continue
            for m in range(2):
                p = pk[m % 2]
                for j in range(8):
                    self.mm(p[:, :], hT[:, j, m * 128:(m + 1) * 128], wm[:, j, 512:1024], j == 0, j == 7,
                            r=[wm, hT], w=[p])
                self.cp(self.Vm[key][:, m, :], p[:, :], r=[p], w=[self.Vm[key]], eng=os.environ.get("DBG_VE", "dve"))

    def rope_tables(self, cs, b, gain, tabs, r_extra=()):
        c = cs[:, b, 0:64]
        s = cs[:, b, 64:128]
        g0 = gain[:, 0:64]
        g1 = gain[:, 64:128]
        rr = [cs, gain] + list(r_extra)
        self.tt(tabs[:, 0, :], c, g0, ALU.mult, r=rr, w=[tabs])
        self.tt(tabs[:, 1, :], s, g1, ALU.mult, r=rr, w=[tabs])
        self.tt(tabs[:, 2, :], s, g0, ALU.mult, r=rr, w=[tabs])
        self.tt(tabs[:, 3, :], c, g1, ALU.mult, r=rr, w=[tabs])

    def norm_rope(self, psrc, nh, sq, ssv, xa, t4, tabs, out_bf, rsrc):
        n = nh * 128
        self.act(sq[:, 0:n], psrc, AF.Square, r=rsrc, w=[sq])
        self.P.op("dve", lambda e: e.tensor_reduce(out=ssv[:, 0:nh], in_=sq[:, 0:n].rearrange("p (h d) -> p h d", h=nh),
                                                   axis=AX.X, op=ALU.add), _bufs([sq]), _bufs([ssv]))
        self.rstd(ssv, nh, 1.0 / 128.0)
        xa3 = xa[:, 0:n].rearrange("p (h d) -> p h d", h=nh)
        self.tt(xa3, psrc.rearrange("p (h d) -> p h d", h=nh),
                ssv[:, 0:nh].unsqueeze(2).to_broadcast([128, nh, 128]), ALU.mult, r=list(rsrc) + [ssv], w=[xa])
        x0 = xa[:, 0:n].rearrange("p (h i two) -> p h i two", h=nh, two=2)[:, :, :, 0]
        x1 = xa[:, 0:n].rearrange("p (h i two) -> p h i two", h=nh, two=2)[:, :, :, 1]
        o0 = out_bf[:, 0:nh, :].rearrange("p h (i two) -> p h i two", two=2)[:, :, :, 0]
        o1 = out_bf[:, 0:nh, :].rearrange("p h (i two) -> p h i two", two=2)[:, :, :, 1]

        def tb(i):
            return tabs[:, i, :].unsqueeze(1).to_broadcast([128, nh, 64])
        tv = [t4[:, i, 0:nh * 64].rearrange("p (h i) -> p h i", h=nh) for i in range(4)]
        self.tt(tv[0], x0, tb(0), ALU.mult, r=[xa, tabs], w=[t4])
        self.tt(tv[1], x1, tb(1), ALU.mult, r=[xa, tabs], w=[t4])
        self.tt(tv[2], x0, tb(2), ALU.mult, r=[xa, tabs], w=[t4])
        self.tt(tv[3], x1, tb(3), ALU.mult, r=[xa, tabs], w=[t4])
        self.tt(o0, tv[0], tv[1], ALU.subtract, r=[t4], w=[out_bf])
        self.tt(o1, tv[2], tv[3], ALU.add, r=[t4], w=[out_bf])

    def phase1(self, st):
        I = self.I
        Lp, S, Ls = self.Lp, self.S, self.Ls
        wkv = self.sb(st, "wkv", [128, 8, 512], BF16)
        wu = self.sb(st, "wu", [128, 8, 512], BF16)
        wv = self.W["w_in"].h.rearrange("(j p) c -> p j c", p=128)
        self.load(wkv[:], wv[:, :, C_K:C_K + 512], r=[self.W["w_in"]], w=[wkv])
        self.load(wu[:], wv[:, :, C_U:C_U + 512], r=[self.W["w_in"]], w=[wu])
        xt = [self.sb(st, f"xt{i}", [128, 4, D], F32) for i in range(2)]
        xn = [self.sb(st, f"xn{i}", [128, 4, D], BF16) for i in range(2)]
        ss = [self.sb(st, f"ss{i}", [128, 4], F32) for i in range(2)]
        hT = [self.sb(st, f"hT{i}", [128, 8, 512], BF16) for i in range(2)]
        cs = [self.sb(st, f"cs{i}", [128, 4, 128], F32) for i in range(2)]
        tabs = [self.sb(st, f"tabs{i}", [128, 4, 64], F32) for i in range(2)]
        sq = self.sb(st, "sq", [128, 256], F32)
        kss = [self.sb(st, f"kss{i}", [128, 2], F32) for i in range(2)]
        ka = self.sb(st, "ka", [128, 256], F32)
        t4 = self.sb(st, "t4", [128, 4, 128], F32)
        krot = [self.sb(st, f"krot{i}", [128, 2, 128], BF16) for i in range(2)]
        KTt = [self.sb(st, f"KTt{i}", [128, 2, 512], BF16) for i in range(2)]
        VAt = [self.sb(st, f"VAt{i}", [128, 2, 4, 129], BF16) for i in range(2)]
        UTt = [self.sb(st, f"UTt{i}", [128, 4, 512], BF16) for i in range(2)]
        UTb = [self.sb(st, f"UTb{i}", [128, 4, 512], BF16) for i in range(2)]
        mrow = [self.sb(st, f"mrow{i}", [128, 2, 512], F32) for i in range(2)]
        ptr = [self.ps(st, f"ptr{i}", [128, 1024], BF16) for i in range(2)]
        pkv = [self.ps(st, f"pkv{i}", [128, 512], F32) for i in range(2)]
        pkt = self.ps(st, "pkt", [128, 2, 512], BF16)
        pu = [self.ps(st, f"pu{i}", [128, 512], F32) for i in range(2)]
        for v in VAt:
            self.memset(v[:, :, :, 128:129], 1.0, w=[v])
        it = 0
        for key, xsrc, cssrc, L in (("p", I["xp"], I["csp"], Lp), ("s", I["xs"], I["css"], Ls)):
            for t in range(L // 512):
                t0 = t * 512
                sl = it % 2
                it += 1
                prefix = (key == "s" and t0 < 7 * S)
                self.make_hT(xsrc[t0:t0 + 512, :], xt[sl], ss[sl], xn[sl], ptr, hT[sl], self.g_in)
                self.load(cs[sl][:], cssrc[t0:t0 + 512, :].rearrange("(b p) c -> p b c", p=128), w=[cs[sl]])
                if prefix:
                    self.load(mrow[sl][:, 0, :], I["mf"][0:1, t0:t0 + 512].partition_broadcast(128), w=[mrow[sl]])
                    self.load(mrow[sl][:, 1, :], I["mb"][0:1, t0:t0 + 512].partition_broadcast(128), w=[mrow[sl]])
                for b in range(4):
                    p = pkv[b % 2]
                    for j in range(8):
                        self.mm(p[:, :], hT[sl][:, j, b * 128:(b + 1) * 128], wkv[:, j, :], j == 0, j == 7,
                                r=[hT[sl], wkv], w=[p])
                    self.cp(VAt[sl][:, :, b, 0:128], p[:, 256:512].rearrange("p (h d) -> p h d", h=2),
                            r=[p], w=[VAt[sl]], eng="act")
                    tb_ = tabs[b % 2]
                    self.rope_tables(cs[sl], b, self.g_k, tb_)
                    kr = krot[b % 2]
                    self.norm_rope(p[:, 0:256], 2, sq, kss[b % 2], ka, t4, tb_, kr, [p])
                    for h in range(2):
                        self.tr(pkt[:, h, b * 128:(b + 1) * 128], kr[:, h, :], self.ident_b[:],
                                r=[kr, self.ident_b], w=[pkt])
                self.cp(KTt[sl][:], pkt[:], r=[pkt], w=[KTt[sl]])
                self.store(self.KT[key].h[:, :, t0:t0 + 512].rearrange("h p l -> p h l"), KTt[sl][:],
                           r=[KTt[sl]], w=[self.KT[key]])
                self.store(self.VA[key].h[:, :, t0 // 128:t0 // 128 + 4, :].rearrange("h p b c -> p h b c"),
                           VAt[sl][:], r=[VAt[sl]], w=[self.VA[key]])
                for i in range(4):
                    p = pu[i % 2]
                    for j in range(8):
                        self.mm(p[:, :], wu[:, j, i * 128:(i + 1) * 128], hT[sl][:, j, :], j == 0, j == 7,
                                r=[hT[sl], wu], w=[p])
                    if prefix:
                        self.tt(UTt[sl][:, i, :], p[:, :], mrow[sl][:, 0, :], ALU.mult, r=[p, mrow[sl]], w=[UTt[sl]])
                        self.tt(UTb[sl][:, i, :], p[:, :], mrow[sl][:, 1, :], ALU.mult, r=[p, mrow[sl]], w=[UTb[sl]])
                    else:
                        self.cp(UTt[sl][:, i, :], p[:, :], r=[p], w=[UTt[sl]], eng="act" if i % 2 else "dve")
                if key == "p":
                    self.store(self.UT["p"].h[:, t0:t0 + 512].rearrange("(i p) l -> p i l", p=128), UTt[sl][:],
                               r=[UTt[sl]], w=[self.UT["p"]])
                else:
                    self.store(self.UT["sf"].h[:, t0:t0 + 512].rearrange("(i p) l -> p i l", p=128), UTt[sl][:],
                               r=[UTt[sl]], w=[self.UT["sf"]])
                    self.store(self.UT["sb"].h[:, t0:t0 + 512].rearrange("(i p) l -> p i l", p=128),
                               (UTb if prefix else UTt)[sl][:], r=[(UTb if prefix else UTt)[sl]], w=[self.UT["sb"]])

    def phase2(self, st):
        banks = [self.ps(st, f"cb{i}", [128, 512], F32) for i in range(8)]
        ga = self.gen_s5(st, banks[0:4])
        gb = self.gen_att(st, banks[4:8])
        ta = tb = 0.0
        da = db = False
        import os
        if os.environ.get("DBG_NOATT"):
            db = True
        while not (da and db):
            if not da and (db or ta <= tb):
                try:
                    ta += next(ga)
                except StopIteration:
                    da = True
            else:
                try:
                    tb += next(gb)
                except StopIteration:
                    db = True

    def gen_s5(self, st, banks):
        I = self.I
        Lp, S, Ls = self.Lp, self.S, self.Ls
        def gt(name):
            return self.sb(st, name, [128, 64], F32)
        are, aim, lst = gt("are"), gt("aim"), gt("lst")
        for t, n in ((are, "are"), (aim, "aim"), (lst, "lst")):
            self.load(t[:], I[n][:, :], w=[t])
        step, Rv, th, kk, thr, sn, cs_, ab = gt("step"), gt("Rv"), gt("th"), gt("kk"), gt("thr"), gt("sn"), gt("cs_"), gt("ab")
        nr, ni, den, CR, CI, SCI, NSCR, tmp = gt("nr"), gt("ni"), gt("den"), gt("CR"), gt("CI"), gt("SCI"), gt("NSCR"), gt("tmp")
        self.act(step[:], lst[:], AF.Exp, r=[lst], w=[step])
        self.ts(are[:], are[:], -1e-4, None, ALU.min, r=[are], w=[are])
        self.tt(Rv[:], are[:], step[:], ALU.mult, r=[are, step], w=[Rv])
        self.act(Rv[:], Rv[:], AF.Exp, r=[Rv], w=[Rv])
        self.tt(th[:], aim[:], step[:], ALU.mult, r=[aim, step], w=[th])
        self.reduce_angle(th, kk, thr)
        self.sincos(thr, ab, sn, cs_)
        self.tt(nr[:], Rv[:], cs_[:], ALU.mult, r=[Rv, cs_], w=[nr])
        self.ts(nr[:], nr[:], -1.0, None, ALU.add, r=[nr], w=[nr])
        self.tt(ni[:], Rv[:], sn[:], ALU.mult, r=[Rv, sn], w=[ni])
        self.tt(den[:], are[:], are[:], ALU.mult, r=[are], w=[den])
        self.tt(tmp[:], aim[:], aim[:], ALU.mult, r=[aim], w=[tmp])
        self.tt(den[:], den[:], tmp[:], ALU.add, r=[den, tmp], w=[den])
        self.recip(den[:], den[:], r=[den], w=[den])
        self.tt(CR[:], nr[:], are[:], ALU.mult, r=[nr, are], w=[CR])
        self.tt(tmp[:], ni[:], aim[:], ALU.mult, r=[ni, aim], w=[tmp])
        self.tt(CR[:], CR[:], tmp[:], ALU.add, r=[CR, tmp], w=[CR])
        self.tt(CR[:], CR[:], den[:], ALU.mult, r=[CR, den], w=[CR])
        self.tt(CI[:], ni[:], are[:], ALU.mult, r=[ni, are], w=[CI])
        self.tt(tmp[:], nr[:], aim[:], ALU.mult, r=[nr, aim], w=[tmp])
        self.tt(CI[:], CI[:], tmp[:], ALU.subtract, r=[CI, tmp], w=[CI])
        self.tt(CI[:], CI[:], den[:], ALU.mult, r=[CI, den], w=[CI])
        self.ts(SCI[:], CI[:], self.sgn[:, 0:1], None, ALU.mult, r=[CI, self.sgn], w=[SCI])
        self.ts(NSCR[:], CR[:], self.sgn[:, 1:2], None, ALU.mult, r=[CR, self.sgn], w=[NSCR])

        iota1 = self.sb(st, "iota1", [128, PIECE], F32)
        self.load(iota1[:], I["iota1"][:, :], w=[iota1])
        ones_f = self.sb(st, "ones_f", [128, TS], F32)
        self.memset(ones_f[:], 1.0, w=[ones_f])

        yield 20.0
        UCH = 1024

        class Stream:
            pass
        strs = []
        for d in range(2):
            s_ = Stream()
            n = f"s{d}_"
            s_.phi = self.sb(st, n + "phi", [128, PIECE], F32)
            s_.k2 = self.sb(st, n + "k2", [128, PIECE], F32)
            s_.sinp = self.sb(st, n + "sinp", [128, PIECE], F32)
            s_.cosp = self.sb(st, n + "cosp", [128, PIECE], F32)
            s_.TA = self.sb(st, n + "TA", [128, PIECE], F32)
            s_.TB = self.sb(st, n + "TB", [128, PIECE], F32)
            s_.RC = self.sb(st, n + "RC", [128, PIECE], F32)
            s_.RS = self.sb(st, n + "RS", [128, PIECE], F32)
            s_.Rd = self.sb(st, n + "Rd", [128, TS], F32)
            s_.rot = self.sb(st, n + "rot", [128, 128], F32)
            s_.rb = self.sb(st, n + "rb", [128, 1], F32)
            s_.Bf = self.sb(st, n + "Bf", [128, 2, 128], F32)
            s_.Bb = self.sb(st, n + "Bb", [128, 2, 128], BF16)
            s_.Cf = self.sb(st, n + "Cf", [128, 2, 16], F32)
            s_.Cb = self.sb(st, n + "Cb", [128, 2, 16], BF16)
            s_.t2 = [self.sb(st, n + f"t2{i}", [128, TS], F32) for i in range(2)]
            s_.dd = [self.sb(st, n + f"dd{i}", [128, TS], F32) for i in range(2)]
            s_.e1 = [self.sb(st, n + f"e1{i}", [128, TS], BF16) for i in range(2)]
            s_.e2 = [self.sb(st, n + f"e2{i}", [128, TS], BF16) for i in range(2)]
            s_.wl = self.sb(st, n + "wl", [128, 1], F32)
            s_.carry = self.sb(st, n + "carry", [128, 1], F32)
            s_.ystg = [self.sb(st, n + f"ystg{i}", [16, 512], F32) for i in range(2)]
            s_.ub = [self.sb(st, n + f"ub{i}", [128, UCH], BF16) for i in range(2)]
            s_.ucnt = 0
            s_.ucur = None
            s_.uch = UCH
            s_.pp = [banks[0], banks[1]]
            s_.pw = banks[2]
            s_.py = banks[3]
            s_.prot = T(banks[3].h[:, TS:TS + 2], banks[3].b)
            s_.nchunk = 0
            s_.nstg = 0
            strs.append(s_)
        Lp, S, Ls = self.Lp, self.S, self.Ls
        nset = 0
        for j in range(4):
            for gl in range(8):
                g = 8 * j + gl
                for d in range(2):
                    s_ = strs[nset % 2]
                    nset += 1
                    s_.ucur = None
                    col = d * 32 + g
                    self.s5_prep(s_, col, iota1, ones_f, Rv, thr, CR, CI, SCI, NSCR)
                    yield 6.0
                    items = []

                    pcnt = [0]

                    def add(src, t0, rev, ypos, first=False, last=False):
                        if first:
                            pcnt[0] = 0
                        items.append(dict(s=s_, d=d, g=g, src=src, j=j, t0=t0, rev=rev, ypos=ypos,
                                          first=first, last=last, p=pcnt[0] % NPC))
                        pcnt[0] += 1
                    npc, nsc = Lp // TS, Ls // TS
                    if d == 0:
                        for i in range(npc):
                            add(self.UT["p"], i * TS, False, i * TS, i == 0, i == npc - 1)
                        for i in range(nsc):
                            t0 = i * TS
                            add(self.UT["sf"], t0, False, (Lp + t0 - 7 * S) if t0 >= 7 * S else None,
                                i == 0, i == nsc - 1)
                    else:
                        for i in range(npc):
                            t0 = Lp - (i + 1) * TS
                            add(self.UT["p"], t0, True, t0, i == 0, i == npc - 1)
                        npre = 7 * S // TS
                        for i in range(npre):
                            add(self.UT["sb"], 7 * S - (i + 1) * TS, True, None, i == 0, False)
                        for i in range(S // TS):
                            t0 = Ls - (i + 1) * TS
                            add(self.UT["sb"], t0, True, Lp + t0 - 7 * S, False, i == S // TS - 1)
                    ni = len(items)
                    self.s5_M(items[0])
                    self.s5_A(items[0])
                    for c in range(ni):
                        if c + 1 < ni:
                            self.s5_M(items[c + 1])
                        self.s5_S(items[c])
                        self.s5_O(items[c])
                        if c + 1 < ni:
                            self.s5_A(items[c + 1])
                        yield 2.05 + (0.85 if items[c]["ypos"] is not None else 0.0)

    def gen_att(self, st, bk):
        I = self.I
        Lp, S, Ls = self.Lp, self.S, self.Ls
        SC = 128.0 ** -0.5
        wv = self.W["w_in"].h.rearrange("(j p) c -> p j c", p=128)
        ws = [self.sb(st, f"aws{i}", [128, 8, 512], BF16) for i in range(2)]
        wsi = [0]

        def wload(src_ap, rd_):
            t = ws[wsi[0] % 2]
            wsi[0] += 1
            self.load(t[:], src_ap, r=[rd_], w=[t])
            return t
        xt = [self.sb(st, f"axt{i}", [128, D], F32) for i in range(2)]
        xn = self.sb(st, "axn", [128, 4, D], BF16)
        ss = self.sb(st, "ass", [128, 4], F32)
        hT = self.sb(st, "ahT", [128, 8, 512], BF16)
        cs = self.sb(st, "acs", [128, 4, 128], F32)
        tabs = [self.sb(st, f"atabs{i}", [128, 4, 64], F32) for i in range(2)]
        sq = self.sb(st, "asq", [128, 512], F32)
        qss = [self.sb(st, f"aqss{i}", [128, 8], F32) for i in range(2)]
        qa = self.sb(st, "aqa", [128, 512], F32)
        t4 = self.sb(st, "at4", [128, 4, 256], F32)
        qrot = [self.sb(st, f"aqrot{i}", [128, 8, 128], BF16) for i in range(2)]
        QT = self.sb(st, "aQT", [128, 8, 512], BF16)
        GA = self.sb(st, "aGA", [128, 8, 512], BF16)
        YAh = [self.sb(st, f"aYAh{i}", [128, 512], BF16) for i in range(2)]
        KC = 512
        kts = [self.sb(st, f"akts{i}", [128, KC], BF16) for i in range(3)]
        vas = [self.sb(st, f"avas{i}", [128, KC // 128, 129], BF16) for i in range(3)]
        PT = [self.sb(st, f"aPT{i}", [128, 512], BF16) for i in range(3)]
        rd = self.sb(st, "ard", [128, 512], F32)
        yx = self.sb(st, "ayx", [128, 512], F32)

        def bf(b):
            return b[:].bitcast(BF16)
        seqs = [("p", I["xp"], I["csp"], 0, Lp, Lp, 0), ("s", I["xs"], I["css"], 7 * S, S, Ls, Lp)]
        nslot = 0
        for key, xsrc, cssrc, xoff, nown, Lk, yoff in seqs:
            for t in range(nown // 512):
                t0 = xoff + t * 512
                yo = yoff + t * 512
                self.make_hT_bank(xsrc[t0:t0 + 512, :], xt, ss, xn, [bk[2], bk[3]], hT, self.g_in)
                self.load(cs[:], cssrc[t0:t0 + 512, :].rearrange("(b p) c -> p b c", p=128), w=[cs])
                yield 12.0
                wq0 = wload(wv[:, :, C_Q:C_Q + 512], self.W["w_in"])
                wq1 = wload(wv[:, :, C_Q + 512:C_Q + 1024], self.W["w_in"])
                for b in range(4):
                    pa, pb, pT = bk[0], bk[1], bk[2 + b % 2]
                    for j in range(8):
                        self.mm(pa[:, :], hT[:, j, b * 128:(b + 1) * 128], wq0[:, j, :], j == 0, j == 7, r=[hT, wq0], w=[pa])
                    for j in range(8):
                        self.mm(pb[:, :], hT[:, j, b * 128:(b + 1) * 128], wq1[:, j, :], j == 0, j == 7, r=[hT, wq1], w=[pb])
                    tb_ = tabs[b % 2]
                    self.rope_tables(cs, b, self.g_q, tb_)
                    qr = qrot[b % 2]
                    for half, pbank in ((0, pa), (1, pb)):
                        self.norm_rope_q(pbank, half, sq, qss[b % 2], qa, t4, tb_, qr)
                    for h in range(8):
                        self.tr(bf(pT)[:, h * 128:(h + 1) * 128], qr[:, h, :], self.ident_b[:],
                                r=[qr, self.ident_b], w=[pT])
                    self.cp(QT[:, :, b * 128:(b + 1) * 128], bf(pT).rearrange("p (h q) -> p h q", h=8),
                            r=[pT], w=[QT], eng="act")
                    yield 5.0
                for half in range(2):
                    wg = wload(wv[:, :, C_GA + half * 512:C_GA + (half + 1) * 512], self.W["w_in"])
                    for o in range(4):
                        p = bk[o % 2]
                        for j in range(8):
                            self.mm(p[:, :], wg[:, j, o * 128:(o + 1) * 128], hT[:, j, :], j == 0, j == 7, r=[wg, hT], w=[p])
                        self.act(GA[:, half * 4 + o, :], p[:, :], AF.Silu, r=[p], w=[GA])
                    yield 8.0
                nkc = Lk // KC
                nk = Lk // 128
                kpc = KC // 128
                for h in range(8):
                    hk = h // 4
                    pO, pD = bk[2], bk[3]
                    cur = {}
                    for idx in range(nk + 1):
                        if idx < nk:
                            c, kk = idx // kpc, idx % kpc
                            if kk == 0:
                                sl = nslot % 3
                                nslot += 1
                                kt_, va_ = kts[sl], vas[sl]
                                self.load(kt_[:], self.KT[key].h[hk, :, c * KC:(c + 1) * KC], r=[self.KT[key]], w=[kt_])
                                self.load(va_[:], self.VA[key].h[hk, :, c * kpc:(c + 1) * kpc, :],
                                          r=[self.VA[key]], w=[va_])
                                cur[c] = (kt_, va_)
                            kt_, va_ = cur[c]
                            psb = bk[idx % 2]
                            pt = PT[idx % 3]
                            self.mm(psb[:, :], kt_[:, kk * 128:(kk + 1) * 128], QT[:, h, :], True, True,
                                    r=[kt_, QT], w=[psb])
                            self.act(pt[:], psb[:, :], AF.Exp, r=[psb], w=[pt], scale=SC)
                        if idx >= 1:
                            jx = idx - 1
                            c, kk = jx // kpc, jx % kpc
                            kt_, va_ = cur[c]
                            pt = PT[jx % 3]
                            self.mm(pO[:, :], va_[:, kk, 0:128], pt[:], jx == 0, jx == nk - 1, r=[va_, pt], w=[pO])
                            self.mm(pD[:, :], self.ones_b[:], pt[:], jx == 0, jx == nk - 1, r=[self.ones_b, pt], w=[pD])
                        yield 1.15
                    self.recip(rd[:], pD[:, :], r=[pD], w=[rd])
                    self.tt(yx[:], pO[:, :], rd[:], ALU.mult, r=[pO, rd], w=[yx])
                    ya = YAh[h % 2]
                    self.tt(ya[:], yx[:], GA[:, h, :], ALU.mult, r=[yx, GA], w=[ya])
                    self.store(self.YAs.h[h * 128:(h + 1) * 128, yo:yo + 512], ya[:], r=[ya], w=[self.YAs])
                    yield 1.0

    def reduce_angle(self, th, kk, thr):
        self.ts(kk[:], th[:], 1.0 / TWO_PI, None, ALU.mult, r=[th], w=[kk])
        self.ts(kk[:], kk[:], MAGIC, None, ALU.add, r=[kk], w=[kk])
        self.ts(kk[:], kk[:], -MAGIC, None, ALU.add, r=[kk], w=[kk])
        self.stt(thr[:], kk[:], -CW1, th[:], ALU.mult, ALU.add, r=[kk, th], w=[thr])
        self.stt(thr[:], kk[:], -CW2, thr[:], ALU.mult, ALU.add, r=[kk, thr], w=[thr])
        self.ts(thr[:], thr[:], math.pi, -math.pi, ALU.min, ALU.max, r=[thr], w=[thr])

    def sincos(self, thr, ab, sn, cs_):
        self.act(sn[:], thr[:], AF.Sin, r=[thr], w=[sn])
        self.act(ab[:], thr[:], AF.Sin, r=[thr], w=[ab], scale=0.5)
        self.tt(cs_[:], ab[:], ab[:], ALU.mult, r=[ab], w=[cs_])
        self.ts(cs_[:], cs_[:], -2.0, 1.0, ALU.mult, ALU.add, r=[cs_], w=[cs_])

    def s5_prep(self, s_, col, iota1, ones_f, Rv, thr, CR, CI, SCI, NSCR):
        I = self.I
        c1 = slice(col, col + 1)
        self.load(s_.Bf[:, 0, :], I["B1"][col, :, :], w=[s_.Bf])
        self.load(s_.Bf[:, 1, :], I["B2"][col, :, :], w=[s_.Bf])
        self.load(s_.Cf[:, 0, :], I["C1"][col, :, :], w=[s_.Cf])
        self.load(s_.Cf[:, 1, :], I["C2"][col, :, :], w=[s_.Cf])
        self.cp(s_.Bb[:], s_.Bf[:], r=[s_.Bf], w=[s_.Bb], eng="act")
        self.cp(s_.Cb[:], s_.Cf[:], r=[s_.Cf], w=[s_.Cb], eng="act")
        self.ts(s_.phi[:], iota1[:], thr[:, c1], None, ALU.mult, r=[iota1, thr], w=[s_.phi])
        self.reduce_angle(s_.phi, s_.k2, s_.phi)
        self.sincos(s_.phi, s_.k2, s_.sinp, s_.cosp)
        self.ts(s_.TA[:], s_.cosp[:], CR[:, c1], None, ALU.mult, r=[s_.cosp, CR], w=[s_.TA])
        self.stt(s_.TA[:], s_.sinp[:], CI[:, c1], s_.TA[:], ALU.mult, ALU.add, r=[s_.sinp, CI, s_.TA], w=[s_.TA])
        self.ts(s_.TB[:], s_.cosp[:], SCI[:, c1], None, ALU.mult, r=[s_.cosp, SCI], w=[s_.TB])
        self.stt(s_.TB[:], s_.sinp[:], NSCR[:, c1], s_.TB[:], ALU.mult, ALU.add, r=[s_.sinp, NSCR, s_.TB], w=[s_.TB])
        self.ts(s_.RC[:], s_.cosp[:], self.sgn[:, 1:2], None, ALU.mult, r=[s_.cosp, self.sgn], w=[s_.RC])
        self.act(s_.RS[:], s_.sinp[:], AF.Copy, r=[s_.sinp], w=[s_.RS], scale=-1.0)
        self.act(s_.Rd[:], ones_f[:], AF.Copy, r=[ones_f, Rv], w=[s_.Rd], scale=Rv[:, c1])
        self.ts(s_.rb[:], s_.sinp[:, PIECE - 1:PIECE], self.sgn[:, 1:2], None, ALU.mult, r=[s_.sinp, self.sgn], w=[s_.rb])
        self.ts(s_.rot[:], self.ident_f[:], s_.cosp[:, PIECE - 1:PIECE], None, ALU.mult, r=[self.ident_f, s_.cosp], w=[s_.rot])
        self.stt(s_.rot[:], self.swap_f[:], s_.rb[:, 0:1], s_.rot[:], ALU.mult, ALU.add,
                 r=[self.swap_f, s_.rb, s_.rot], w=[s_.rot])

    def s5_M(self, it):
        s_ = it["s"]
        k = s_.nchunk % 2
        s_.nchunk += 1
        it["k"] = k
        pp = s_.pp[k]
        src, jj, t0 = it["src"], it["j"], it["t0"]
        piece = t0 // s_.uch
        ukey = (id(src), jj, piece)
        if s_.ucur != ukey:
            ub = s_.ub[s_.ucnt % 2]
            s_.ucnt += 1
            self.load(ub[:], src.h[jj * 128:(jj + 1) * 128, piece * s_.uch:(piece + 1) * s_.uch], r=[src], w=[ub])
            s_.ucur = ukey
            s_.ubcur = ub
        ut = s_.ubcur
        off = t0 - piece * s_.uch
        rhs = ut[:, off:off + TS]
        if it["rev"]:
            rhs = rev_ap(rhs)
        self.mm(pp[:, 0:TS], s_.Bb[:, 0, :], rhs, True, True, r=[s_.Bb, ut], w=[pp])
        self.mm(pp[:, TS:2 * TS], s_.Bb[:, 1, :], rhs, True, True, r=[s_.Bb, ut], w=[pp])

    def s5_A(self, it):
        s_ = it["s"]
        k = it["k"]
        pp, t2, dd = s_.pp[k], s_.t2[k], s_.dd[k]
        ps_ = slice(it["p"] * TS, (it["p"] + 1) * TS)
        self.tt(pp[:, 0:TS], pp[:, 0:TS], s_.TA[:, ps_], ALU.mult, r=[pp, s_.TA], w=[pp])
        self.tt(t2[:], pp[:, TS:2 * TS], s_.TB[:, ps_], ALU.mult, r=[pp, s_.TB], w=[t2])
        self.tt(dd[:], pp[:, 0:TS], t2[:], ALU.add, r=[pp, t2], w=[dd])

    def s5_S(self, it):
        s_ = it["s"]
        k = it["k"]
        dd = s_.dd[k]
        if it["first"]:
            self.memset(s_.carry[:], 0.0, w=[s_.carry])
        self.P.op("dve", lambda e: e.tensor_tensor_scan(out=s_.pw[:, 0:TS], data0=s_.Rd[:], data1=dd[:],
                                                        initial=s_.carry[:, 0:1], op0=ALU.mult, op1=ALU.add),
                  _bufs([s_.Rd, dd, s_.carry]), _bufs([s_.pw]))
        if not it["last"]:
            if it["p"] == NPC - 1:
                self.cp(s_.wl[:], s_.pw[:, TS - 1:TS], r=[s_.pw], w=[s_.wl], eng="act")
                self.mm(s_.prot[:, 0:1], s_.rot[:], s_.wl[:], True, True, r=[s_.rot, s_.wl], w=[s_.prot])
                self.cp(s_.carry[:], s_.prot[:, 0:1], r=[s_.prot], w=[s_.carry], eng="act")
            else:
                self.cp(s_.carry[:], s_.pw[:, TS - 1:TS], r=[s_.pw], w=[s_.carry], eng="act")

    def s5_O(self, it):
        s_ = it["s"]
        k = it["k"]
        ypos, rev, d, g = it["ypos"], it["rev"], it["d"], it["g"]
        if ypos is None:
            return
        e1, e2 = s_.e1[k], s_.e2[k]
        ps_ = slice(it["p"] * TS, (it["p"] + 1) * TS)
        self.tt(e1[:], s_.pw[:, 0:TS], s_.RC[:, ps_], ALU.mult, r=[s_.pw, s_.RC], w=[e1])
        self.tt(e2[:], s_.pw[:, 0:TS], s_.RS[:, ps_], ALU.mult, r=[s_.pw, s_.RS], w=[e2])
        py = s_.py[0:16, 0:TS]
        self.mm(py, s_.Cb[:, 0, :], e1[:], True, False, r=[s_.Cb, e1], w=[s_.py])
        self.mm(py, s_.Cb[:, 1, :], e2[:], False, True, r=[s_.Cb, e2], w=[s_.py])
        nper = 512 // TS
        sidx = s_.nstg // nper
        stg = s_.ystg[sidx % 2]
        q = s_.nstg % nper
        s_.nstg += 1
        base = (ypos // 512) * 512
        off = ypos - base
        dst = stg[0:16, off:off + TS]
        if rev:
            dst = rev_ap(dst)
        self.cp(dst, py, r=[s_.py], w=[stg], eng="act")
        if q == nper - 1:
            self.store(self.YS[d].h[g * 16:(g + 1) * 16, base:base + 512], stg[0:16, :], r=[stg], w=[self.YS[d]])

    def phase3(self, st):
        I = self.I
        Lp, S, Ls = self.Lp, self.S, self.Ls
        SC = 128.0 ** -0.5
        wv = self.W["w_in"].h.rearrange("(j p) c -> p j c", p=128)
        ws = [self.sb(st, f"ws{i}", [128, 8, 512], BF16) for i in range(3)]
        self.wsi = 0

        def wload(src_ap, rd):
            t = ws[self.wsi % 3]
            self.wsi += 1
            self.load(t[:], src_ap, r=[rd], w=[t])
            return t

        xt = [self.sb(st, f"xt{i}", [128, D], F32) for i in range(2)]
        xn = self.sb(st, "xn", [128, 4, D], BF16)
        ss = self.sb(st, "ss", [128, 4], F32)
        hT = self.sb(st, "hT", [128, 8, 512], BF16)
        cs = self.sb(st, "cs", [128, 4, 128], F32)
        tabs = [self.sb(st, f"tabs{i}", [128, 4, 64], F32) for i in range(2)]
        sq = self.sb(st, "sq", [128, 512], F32)
        qss = [self.sb(st, f"qss{i}", [128, 8], F32) for i in range(2)]
        qa = self.sb(st, "qa", [128, 512], F32)
        t4 = self.sb(st, "t4", [128, 4, 256], F32)
        qrot = [self.sb(st, f"qrot{i}", [128, 8, 128], BF16) for i in range(2)]
        QT = self.sb(st, "QT", [128, 8, 512], BF16)
        GA = self.sb(st, "GA", [128, 8, 512], BF16)
        YA = self.sb(st, "YA", [128, 8, 512], BF16)
        KC = 512
        kts = [self.sb(st, f"kts{i}", [128, KC], BF16) for i in range(3)]
        vas = [self.sb(st, f"vas{i}", [128, KC // 128, 129], BF16) for i in range(3)]
        PT = [self.sb(st, f"PT{i}", [128, 512], BF16) for i in range(3)]
        rcp = self.sb(st, "rcp", [128, 4], F32)
        yn = [self.sb(st, f"yn{i}", [128, 128], BF16) for i in range(2)]
        y0 = [self.sb(st, f"y0_{i}", [128, 512], F32) for i in range(1)] * 2
        y1 = [self.sb(st, f"y1_{i}", [128, 512], F32) for i in range(1)] * 2
        uu = [self.sb(st, f"uu{i}", [128, 512], BF16) for i in range(2)]
        gx = [self.sb(st, f"gx{i}", [128, 512], F32) for i in range(2)]
        g2 = [self.sb(st, f"g2{i}", [128, 512], F32) for i in range(2)]
        YG = self.sb(st, "YG", [128, 4, 512], F32)
        YGb = self.sb(st, "YGb", [128, 4, 512], BF16)
        GS = self.sb(st, "GS", [128, 4, 512], BF16)
        sgl = [self.sb(st, f"sgl{i}", [128, 512], F32) for i in range(2)]
        YSb = self.sb(st, "YSb", [128, 4, 512], BF16)
        wglu = self.sb(st, "wglu", [128, 4, 512], BF16)
        self.load(wglu[:], self.W["w_glu"].h.rearrange("(k p) c -> p k c", p=128), r=[self.W["w_glu"]], w=[wglu])
        QX = self.sb(st, "QX", [128, 4, 512], BF16)
        GX = self.sb(st, "GX", [128, 4, 512], BF16)
        PX = [self.sb(st, f"PX{i}", [128, 512], BF16) for i in range(2)]
        rd = self.sb(st, "rd", [128, 512], F32)
        yx = self.sb(st, "yx", [128, 512], F32)
        YX = self.sb(st, "YX", [128, 4, 512], BF16)
        G3 = self.sb(st, "G3", [128, 3, 4, 512], BF16)
        m = [self.sb(st, f"m{i}", [128, 512], F32) for i in range(3)]
        M = self.sb(st, "M", [128, 8, 512], BF16)
        yres = [self.sb(st, f"yres{i}", [128, D], F32) for i in range(1)] * 2
        fss = [self.sb(st, f"fss{i}", [128, 1], F32) for i in range(2)]
        gf = self.sb(st, "gf", [128, D], F32)
        self.load(gf[:], I["g_f"][:, :], w=[gf])
        bk = [self.ps(st, f"bk{i}", [128, 512], F32) for i in range(8)]

        def bf(b):
            return b[:].bitcast(BF16)

        seqs = [("p", I["xp"], I["csp"], 0, Lp, Lp, 0), ("s", I["xs"], I["css"], 7 * S, S, Ls, Lp)]
        for key, xsrc, cssrc, xoff, nown, Lk, yoff in seqs:
            for t in range(nown // 512):
                t0 = xoff + t * 512
                yo = yoff + t * 512
                self.make_hT_bank(xsrc[t0:t0 + 512, :], xt, ss, xn, [bk[6], bk[7]], hT, self.g_in)
                self.load(cs[:], cssrc[t0:t0 + 512, :].rearrange("(b p) c -> p b c", p=128), w=[cs])
                self.load(YA[:], self.YAs.h[:, yo:yo + 512].rearrange("(h p) l -> p h l", p=128), r=[self.YAs], w=[YA])
                wgs = wload(wv[:, :, C_GS:C_GS + 512], self.W["w_in"])
                for i in range(4):
                    k2 = i % 2
                    self.load(y0[k2][:], self.YS[0].h[i * 128:(i + 1) * 128, yo:yo + 512], r=[self.YS[0]], w=[y0[k2]])
                    self.load(y1[k2][:], self.YS[1].h[i * 128:(i + 1) * 128, yo:yo + 512], r=[self.YS[1]], w=[y1[k2]])
                    usrc = self.UT["p"] if key == "p" else self.UT["sf"]
                    self.load(uu[k2][:], usrc.h[i * 128:(i + 1) * 128, t0:t0 + 512], r=[usrc], w=[uu[k2]])
                    a, bq = gx[k2], g2[k2]
                    self.tt(a[:], y0[k2][:], y1[k2][:], ALU.add, r=[y0[k2], y1[k2]], w=[a])
                    self.stt(a[:], uu[k2][:], self.s5d[:, i:i + 1], a[:], ALU.mult, ALU.add, r=[uu[k2], self.s5d, a], w=[a])
                    self.tt(bq[:], a[:], a[:], ALU.mult, r=[a], w=[bq])
                    self.ts(bq[:], bq[:], 0.044715, 1.0, ALU.mult, ALU.add, r=[bq], w=[bq])
                    self.tt(bq[:], bq[:], a[:], ALU.mult, r=[bq, a], w=[bq])
                    self.act(bq[:], bq[:], AF.Sigmoid, r=[bq], w=[bq], scale=2.0 * math.sqrt(2.0 / math.pi))
                    self.tt(YG[:, i, :], a[:], bq[:], ALU.mult, r=[a, bq], w=[YG])
                    self.cp(YGb[:, i, :], YG[:, i, :], r=[YG], w=[YGb], eng="act")
                    p = bk[i % 2]
                    for j in range(8):
                        self.mm(p[:, :], wgs[:, j, i * 128:(i + 1) * 128], hT[:, j, :], j == 0, j == 7, r=[wgs, hT], w=[p])
                    self.act(GS[:, i, :], p[:, :], AF.Silu, r=[p], w=[GS])
                for o in range(4):
                    p = bk[2 + o % 2]
                    for k_ in range(4):
                        self.mm(p[:, :], wglu[:, k_, o * 128:(o + 1) * 128], YGb[:, k_, :], k_ == 0, k_ == 3,
                                r=[wglu, YGb], w=[p])
                    s_ = sgl[o % 2]
                    self.act(s_[:], p[:, :], AF.Sigmoid, r=[p, self.bglu], w=[s_], bias=self.bglu[:, o:o + 1])
                    self.tt(s_[:], s_[:], YG[:, o, :], ALU.mult, r=[s_, YG], w=[s_])
                    self.tt(YSb[:, o, :], s_[:], GS[:, o, :], ALU.mult, r=[s_, GS], w=[YSb])
                wqx = wload(wv[:, :, C_QX:C_QX + 512], self.W["w_in"])
                wgx = wload(wv[:, :, C_GX:C_GX + 512], self.W["w_in"])
                for o in range(4):
                    p = bk[o % 2]
                    for j in range(8):
                        self.mm(p[:, :], wqx[:, j, o * 128:(o + 1) * 128], hT[:, j, :], j == 0, j == 7, r=[wqx, hT], w=[p])
                    self.cp(QX[:, o, :], p[:, :], r=[p], w=[QX], eng="act")
                    p2 = bk[2 + o % 2]
                    for j in range(8):
                        self.mm(p2[:, :], wgx[:, j, o * 128:(o + 1) * 128], hT[:, j, :], j == 0, j == 7, r=[wgx, hT], w=[p2])
                    self.act(GX[:, o, :], p2[:, :], AF.Silu, r=[p2], w=[GX])
                KmT, Vm = self.KmT[key], self.Vm[key]
                for hx in range(4):
                    for mt in range(2):
                        p = bk[mt]
                        self.mm(p[:, :], KmT[:, hx, mt * 128:(mt + 1) * 128], QX[:, hx, :], True, True, r=[KmT, QX], w=[p])
                        self.act(PX[mt][:], p[:, :], AF.Exp, r=[p], w=[PX[mt]], scale=SC)
                    po_, pd_ = bk[4], bk[5]
                    for mt in range(2):
                        self.mm(po_[:, :], Vm[:, mt, hx * 128:(hx + 1) * 128], PX[mt][:], mt == 0, mt == 1, r=[Vm, PX[mt]], w=[po_])
                    for mt in range(2):
                        self.mm(pd_[:, :], self.ones_b[:], PX[mt][:], mt == 0, mt == 1, r=[self.ones_b, PX[mt]], w=[pd_])
                    self.recip(rd[:], pd_[:, :], r=[pd_], w=[rd])
                    self.tt(yx[:], po_[:, :], rd[:], ALU.mult, r=[po_, rd], w=[yx])
                    self.tt(YX[:, hx, :], yx[:], GX[:, hx, :], ALU.mult, r=[yx, GX], w=[YX])
                wpa = self.W["w_pa"].h.rearrange("(k p) c -> p k c", p=128)
                wps = self.W["w_ps"].h.rearrange("(k p) c -> p k c", p=128)
                wpx = self.W["w_px"].h.rearrange("(k p) c -> p k c", p=128)
                for og in range(2):
                    for br in range(3):
                        wm_ = wload(wv[:, :, C_MG + br * 1024 + og * 512:C_MG + br * 1024 + (og + 1) * 512], self.W["w_in"])
                        for o in range(4):
                            p = bk[o % 2]
                            for j in range(8):
                                self.mm(p[:, :], wm_[:, j, o * 128:(o + 1) * 128], hT[:, j, :], j == 0, j == 7, r=[wm_, hT], w=[p])
                            self.act(G3[:, br, o, :], p[:, :], AF.Sigmoid, r=[p], w=[G3])
                    wa = wload(wpa[:, :, og * 512:(og + 1) * 512], self.W["w_pa"])
                    wsx = ws[self.wsi % 3]
                    self.wsi += 1
                    self.load(wsx[:, 0:4, :], wps[:, :, og * 512:(og + 1) * 512], r=[self.W["w_ps"]], w=[wsx])
                    self.load(wsx[:, 4:8, :], wpx[:, :, og * 512:(og + 1) * 512], r=[self.W["w_px"]], w=[wsx])
                    for o in range(4):
                        pa_, ps_, px_ = bk[2 + (o % 2) * 3], bk[3 + (o % 2) * 3], bk[4 + (o % 2) * 3]
                        for k_ in range(8):
                            self.mm(pa_[:, :], wa[:, k_, o * 128:(o + 1) * 128], YA[:, k_, :], k_ == 0, k_ == 7, r=[wa, YA], w=[pa_])
                        for k_ in range(4):
                            self.mm(ps_[:, :], wsx[:, k_, o * 128:(o + 1) * 128], YSb[:, k_, :], k_ == 0, k_ == 3, r=[wsx, YSb], w=[ps_])
                        for k_ in range(4):
                            self.mm(px_[:, :], wsx[:, 4 + k_, o * 128:(o + 1) * 128], YX[:, k_, :], k_ == 0, k_ == 3, r=[wsx, YX], w=[px_])
                        self.tt(m[0][:], pa_[:, :], G3[:, 0, o, :], ALU.mult, r=[pa_, G3], w=[m[0]])
                        self.tt(m[1][:], ps_[:, :], G3[:, 1, o, :], ALU.mult, r=[ps_, G3], w=[m[1]])
                        self.tt(m[2][:], px_[:, :], G3[:, 2, o, :], ALU.mult, r=[px_, G3], w=[m[2]])
                        self.tt(m[0][:], m[0][:], m[1][:], ALU.add, r=[m[0], m[1]], w=[m[0]])
                        self.tt(M[:, og * 4 + o, :], m[0][:], m[2][:], ALU.add, r=[m[0], m[2]], w=[M])
                wo_ = self.W["w_out"].h.rearrange("(k p) c -> p k c", p=128)
                wo0 = wload(wo_[:, :, 0:512], self.W["w_out"])
                wo1 = wload(wo_[:, :, 512:1024], self.W["w_out"])
                for b in range(4):
                    pa_, pb_ = bk[(2 * b) % 4], bk[(2 * b + 1) % 4]
                    for k_ in range(8):
                        self.mm(pa_[:, :], M[:, k_, b * 128:(b + 1) * 128], wo0[:, k_, :], k_ == 0, k_ == 7, r=[M, wo0], w=[pa_])
                    for k_ in range(8):
                        self.mm(pb_[:, :], M[:, k_, b * 128:(b + 1) * 128], wo1[:, k_, :], k_ == 0, k_ == 7, r=[M, wo1], w=[pb_])
                    yr, fs = yres[b % 2], fss[b % 2]
                    xb = xt[b % 2]
                    self.load(xb[:], xsrc[t0 + b * 128:t0 + (b + 1) * 128, :], w=[xb])
                    self.tt(yr[:, 0:512], pa_[:, :], xb[:, 0:512], ALU.add, r=[pa_, xb], w=[yr])
                    self.tt(yr[:, 512:1024], pb_[:, :], xb[:, 512:1024], ALU.add, r=[pb_, xb], w=[yr])
                    self.act(xn[:, 0, :], yr[:], AF.Square, r=[yr], w=[xn, fs], accum_out=fs[:, 0:1])
                    self.rstd(fs, 1, 1.0 / D)
                    self.stt(yr[:], yr[:], fs[:, 0:1], gf[:], ALU.mult, ALU.mult, r=[yr, fs, gf], w=[yr])
                    self.store(self.y_out[yo + b * 128:yo + (b + 1) * 128, :], yr[:], r=[yr])

    def make_hT_bank(self, x_rows, xt, ss, xn, banks, hT, gain):
        for b in range(4):
            xb = xt[b % 2]
            sb_ = ss[b % 2] if isinstance(ss, list) else ss
            self.load(xb[:], x_rows[b * 128:(b + 1) * 128, :], w=[xb])
            self.act(xn[:, b, :], xb[:], AF.Square, r=[xb], w=[xn, ss], accum_out=ss[:, b:b + 1])
            v = ss[:, b:b + 1]
            self.ts(v, v, 1.0 / D, EPS, ALU.mult, ALU.add, r=[ss], w=[ss])
            self.act(v, v, AF.Sqrt, r=[ss], w=[ss])
            self.recip(v, v, r=[ss], w=[ss])
            if b % 2 == 0:
                self.act(xn[:, b, :], xb[:], AF.Copy, r=[xb, ss], w=[xn], scale=ss[:, b:b + 1])
            else:
                self.ts(xn[:, b, :], xb[:], ss[:, b:b + 1], None, ALU.mult, r=[xb, ss], w=[xn])
        for j in range(8):
            bank = banks[j % len(banks)]
            pv = bank[:].bitcast(BF16)
            for b in range(4):
                self.tr(pv[:, b * 128:(b + 1) * 128], xn[:, b, j * 128:(j + 1) * 128], self.ident_b[:],
                        r=[xn, self.ident_b], w=[bank])
            if j % 2 == 0:
                self.ts(hT[:, j, :], pv[:, 0:512], gain[:, j:j + 1], None, ALU.mult, r=[bank, gain], w=[hT])
            else:
                self.act(hT[:, j, :], pv[:, 0:512], AF.Copy, r=[bank, gain], w=[hT], scale=gain[:, j:j + 1])

    def norm_rope_q(self, pbank, half, sq, ssv, xa, t4, tabs, out_bf):
        nh = 4
        o = 0
        h0 = half * 4
        psrc = pbank[:, :]
        self.act(sq[:, o:o + 512], psrc, AF.Square, r=[pbank], w=[sq])
        self.P.op("dve", lambda e: e.tensor_reduce(out=ssv[:, h0:h0 + 4], in_=sq[:, o:o + 512].rearrange("p (h d) -> p h d", h=nh),
                                                   axis=AX.X, op=ALU.add), _bufs([sq]), _bufs([ssv]))
        v = ssv[:, h0:h0 + 4]
        self.ts(v, v, 1.0 / 128.0, EPS, ALU.mult, ALU.add, r=[ssv], w=[ssv])
        self.act(v, v, AF.Sqrt, r=[ssv], w=[ssv])
        self.recip(v, v, r=[ssv], w=[ssv])
        xa3 = xa[:, o:o + 512].rearrange("p (h d) -> p h d", h=nh)
        self.tt(xa3, psrc.rearrange("p (h d) -> p h d", h=nh), v.unsqueeze(2).to_broadcast([128, nh, 128]),
                ALU.mult, r=[pbank, ssv], w=[xa])
        x0 = xa[:, o:o + 512].rearrange("p (h i two) -> p h i two", h=nh, two=2)[:, :, :, 0]
        x1 = xa[:, o:o + 512].rearrange("p (h i two) -> p h i two", h=nh, two=2)[:, :, :, 1]
        ob = out_bf[:, h0:h0 + 4, :].rearrange("p h (i two) -> p h i two", two=2)
        o0, o1 = ob[:, :, :, 0], ob[:, :, :, 1]

        def tb(i):
            return tabs[:, i, :].unsqueeze(1).to_broadcast([128, nh, 64])
        tv = [t4[:, i, 0:256].rearrange("p (h i) -> p h i", h=nh) for i in range(4)]
        self.tt(tv[0], x0, tb(0), ALU.mult, r=[xa, tabs], w=[t4])
        self.tt(tv[1], x1, tb(1), ALU.mult, r=[xa, tabs], w=[t4])
        self.tt(tv[2], x0, tb(2), ALU.mult, r=[xa, tabs], w=[t4])
        self.tt(tv[3], x1, tb(3), ALU.mult, r=[xa, tabs], w=[t4])
        self.tt(o0, tv[0], tv[1], ALU.subtract, r=[t4], w=[out_bf])
        self.tt(o1, tv[2], tv[3], ALU.add, r=[t4], w=[out_bf])


def rope_table(pos):
    pos = np.asarray(pos)
    row = (pos // 64).astype(np.float32)
    col = (pos % 64).astype(np.float32)
    freqs = (np.float32(10000.0) ** (-np.arange(32, dtype=np.float32) / np.float32(32))).astype(np.float32)
    ang = np.concatenate([row[:, None] * freqs, col[:, None] * freqs], axis=-1).astype(np.float32)
    return np.concatenate([np.cos(ang), np.sin(ang)], axis=-1).astype(np.float32)


def host_inputs(inp, Lp, S, ncores=NCORES):
    f = lambda a: np.ascontiguousarray(np.asarray(a, dtype=np.float32))
    Ls = 8 * S
    xs_all = f(inp["x_sample"])[0]
    shared = {}
    shared["w_in"] = f(inp["w_in"])[0]
    shared["w_glu"] = f(inp["w_glu"])[0]
    shared["w_mem_kv"] = f(inp["w_mem_kv"])[0]
    shared["w_pa"] = f(inp["w_proj_attn"])[0]
    shared["w_ps"] = f(inp["w_proj_ssm"])[0]
    shared["w_px"] = f(inp["w_proj_cross"])[0]
    shared["w_out"] = f(inp["w_out"])[0]
    shared["g_in"] = f(f(inp["norm_in"])[0].reshape(8, 128).T)
    shared["g_mem"] = f(f(inp["norm_mem"])[0].reshape(8, 128).T)
    qn, kn = f(inp["q_norm"])[0], f(inp["k_norm"])[0]
    shared["g_q"] = f(np.tile(np.concatenate([qn[0::2], qn[1::2]])[None, :], (128, 1)))
    shared["g_k"] = f(np.tile(np.concatenate([kn[0::2], kn[1::2]])[None, :], (128, 1)))
    shared["g_f"] = f(np.tile(f(inp["norm_final"])[None, :], (128, 1)))
    shared["s5d"] = f(f(inp["s5_d"])[0].reshape(4, 128).T)
    shared["bglu"] = f(f(inp["b_glu"])[0].reshape(4, 128).T)
    a_re, a_im = f(inp["s5_a_re"])[0], f(inp["s5_a_im"])[0]
    dup = lambda a: f(np.concatenate([a.reshape(64, 64).T, a.reshape(64, 64).T], axis=0))
    shared["are"] = dup(a_re)
    shared["aim"] = dup(a_im)
    shared["lst"] = f(np.tile(f(inp["s5_log_step"])[0].reshape(1, 64), (128, 1)))
    b_re, b_im = f(inp["s5_b_re"])[0], f(inp["s5_b_im"])[0]
    c_re, c_im = f(inp["s5_c_re"])[0], f(inp["s5_c_im"])[0]
    B1 = np.zeros((64, 128, 128), np.float32)
    B2 = np.zeros((64, 128, 128), np.float32)
    C1 = np.zeros((64, 128, 16), np.float32)
    C2 = np.zeros((64, 128, 16), np.float32)
    for d in range(2):
        for g in range(32):
            col = d * 32 + g
            r0 = (g % 8) * 16
            B1[col, r0:r0 + 16, 0:64] = b_re[d, g].T
            B1[col, r0:r0 + 16, 64:128] = b_im[d, g].T
            B2[col, r0:r0 + 16, 0:64] = b_im[d, g].T
            B2[col, r0:r0 + 16, 64:128] = b_re[d, g].T
            C1[col, 0:64, :] = c_re[d, g].T
            C1[col, 64:128, :] = c_im[d, g].T
            C2[col, 0:64, :] = c_im[d, g].T
            C2[col, 64:128, :] = c_re[d, g].T
    shared.update(B1=B1, B2=B2, C1=C1, C2=C2)
    shared["ident"] = np.eye(128, dtype=np.float32)
    sw = np.zeros((128, 128), np.float32)
    sw[np.arange(128), (np.arange(128) + 64) % 128] = 1.0
    shared["swap"] = sw
    shared["iota1"] = f(np.tile(np.arange(1, PIECE + 1, dtype=np.float32)[None, :], (128, 1)))
    sg = np.ones((128, 2), np.float32)
    sg[0:64, 0] = -1.0
    sg[64:128, 1] = -1.0
    shared["sgn"] = sg
    shared["csp"] = rope_table(np.arange(Lp))
    maps = []
    for c in range(ncores):
        m = dict(shared)
        m["xp"] = f(inp["x_prompt"])[c]
        order = np.concatenate([np.arange((c + 1) * S, Ls), np.arange(0, c * S), np.arange(c * S, (c + 1) * S)])
        m["xs"] = np.ascontiguousarray(xs_all[order])
        m["css"] = rope_table(order)
        m["memp"] = f(inp["mem_prompt"])[c]
        m["mems"] = f(inp["mem_sample"])[0]
        mf = np.zeros((1, 7 * S), np.float32)
        mf[0, (7 - c) * S:] = 1.0
        m["mf"] = mf
        m["mb"] = (1.0 - mf).astype(np.float32)
        maps.append(m)
    return maps


_NC_CACHE = {}


def run(inp, Lp, S):
    key = (Lp, S)
    if key not in _NC_CACHE:
        _NC_CACHE[key] = Builder(Lp, S).build()
    nc = _NC_CACHE[key]
    maps = host_inputs(inp, Lp, S)
    res = run_bass_kernel_spmd(nc, maps, core_ids=list(range(NCORES)))
    ys = [np.asarray(r["y"]) for r in res.results]
    y_prompt = np.stack([y[:Lp] for y in ys], axis=0).astype(np.float32)
    y_sample = np.concatenate([y[Lp:Lp + S] for y in ys], axis=0)[None].astype(np.float32)
    return y_prompt, y_sample


def kernel(**inputs):
    Lp = int(np.asarray(inputs["x_prompt"]).shape[1])
    Ls = int(np.asarray(inputs["x_sample"]).shape[1])
    return run(inputs, Lp, Ls // 8)
```

```python
import math
from contextlib import ExitStack

import numpy as np
import concourse.bass as bass
import concourse.mybir as mybir
from concourse.bass_utils import run_bass_kernel_spmd

F32 = mybir.dt.float32
BF16 = mybir.dt.bfloat16
AF = mybir.ActivationFunctionType
ALU = mybir.AluOpType
AX = mybir.AxisListType

D = 1024
IN_W = 7680
C_Q, C_K, C_V, C_GA, C_U, C_GS, C_QX, C_GX, C_MG = 0, 1024, 1280, 1536, 2560, 3072, 3584, 4096, 4608
EPS = 1e-6
NCORES = 8
TS = 256
NPC = 4
PIECE = NPC * TS
MAGIC = 12582912.0
TWO_PI = 2.0 * math.pi
CW1 = 6.28125
CW2 = TWO_PI - CW1
SEM_CH = 20000
N_DMA_SEMS = 40


class Buf:
    __slots__ = ("lw", "rd", "ex")

    def __init__(self):
        self.lw = None
        self.rd = []
        self.ex = False


class T:
    def __init__(self, h, b=None):
        self.h = h
        self.b = b if b is not None else Buf()

    def __getitem__(self, k):
        return self.h[k]


def _bufs(xs):
    out = []
    for x in xs:
        if x is None:
            continue
        out.append(x.b if isinstance(x, T) else x)
    return out


class Prog:
    ENGS = ("pe", "act", "dve", "pool", "sp")

    def __init__(self, nc, stack):
        self.nc = nc
        self.stack = stack
        self.q = {e: [] for e in self.ENGS}
        self.cnt = {e: 0 for e in self.ENGS}
        self.sems = {e: [] for e in self.ENGS}
        self.dsems = []
        self.dcnt = []
        self.drr = 0
        self.n_dma = 0
        self.pend = {e: [] for e in self.ENGS}
        self.waited = {e: {} for e in self.ENGS}

    def _sem(self, e, idx):
        k = idx // SEM_CH
        while len(self.sems[e]) <= k:
            self.sems[e].append(self.stack.enter_context(
                self.nc.semaphore(f"s_{e}_{len(self.sems[e])}")))
        return self.sems[e][k], idx % SEM_CH + 1

    def op(self, e, fn, reads=(), writes=(), dma=False):
        reads = _bufs(reads)
        writes = _bufs(writes)
        exr = [b for b in reads if b.ex]
        if exr:
            reads = [b for b in reads if not b.ex]
            writes = writes + [b for b in exr if b not in writes]
        deps = {}

        def add(d):
            if d is None:
                return
            if d[0] not in deps or deps[d[0]][1] < d[1]:
                deps[d[0]] = d
        for b in reads:
            add(b.lw)
        for b in writes:
            add(b.lw)
            for r in b.rd:
                add(r)
        idx = self.cnt[e]
        if not dma:
            self.cnt[e] += 1
        waits = list(self.pend[e])
        self.pend[e] = []
        for key, d in deps.items():
            if key == "pe" and e == "pe":
                continue
            waits.append((d[2], d[3]))
        wd = self.waited[e]
        ww = []
        for (ws, wv) in waits:
            if wd.get(id(ws), 0) >= wv:
                continue
            wd[id(ws)] = wv
            ww.append((ws, wv))
        waits = ww
        if dma:
            if len(self.dsems) < N_DMA_SEMS:
                self.dsems.append(self.stack.enter_context(
                    self.nc.semaphore(f"s_dma_{len(self.dsems)}")))
                self.dcnt.append(0)
                i = len(self.dsems) - 1
            else:
                i = self.drr
                self.drr = (self.drr + 1) % N_DMA_SEMS
            if self.dcnt[i] > 0 and wd.get(id(self.dsems[i]), 0) < self.dcnt[i]:
                wd[id(self.dsems[i])] = self.dcnt[i]
                waits.append((self.dsems[i], self.dcnt[i]))
            self.dcnt[i] += 16
            me = ("dma%d" % self.n_dma, 0, self.dsems[i], self.dcnt[i])
            self.n_dma += 1
            self.q[e].append((waits, fn, self.dsems[i], 16))
        else:
            s, v = self._sem(e, idx)
            me = (e, idx, s, v)
            self.q[e].append((waits, fn, s, 1))
        for b in reads:
            b.rd.append(me)
        for b in writes:
            b.lw = me
            b.rd = []
        return me

    def all_done_waits(self):
        final = [(s, c) for s, c in zip(self.dsems, self.dcnt) if c > 0]
        for e in self.ENGS:
            if self.cnt[e] > 0:
                final.append(self._sem(e, self.cnt[e] - 1))
        return final

    def barrier(self):
        w = self.all_done_waits()
        for e in self.ENGS:
            self.pend[e] = list(w)

    def emit(self, last=False):
        nc = self.nc
        prog = self
        final = self.all_done_waits() if last else []
        with nc.Block() as block:
            def run(eng, name):
                for waits, fn, s, inc in prog.q[name]:
                    for (ws, wv) in waits:
                        eng.wait_ge(ws, wv)
                    fn(eng).then_inc(s, inc)
                prog.q[name] = []

            @block.tensor
            def _(eng):
                run(eng, "pe")

            @block.scalar
            def _(eng):
                run(eng, "act")

            @block.vector
            def _(eng):
                run(eng, "dve")

            @block.gpsimd
            def _(eng):
                run(eng, "pool")

            @block.sync
            def _(eng):
                run(eng, "sp")
                for (ws, wv) in final:
                    eng.wait_ge(ws, wv)


def rev_ap(ap2d):
    a = ap2d.ap
    assert len(a) == 2, a
    n = a[1][1]
    st = a[1][0]
    return bass.AP(ap2d.tensor, ap2d.offset + st * (n - 1), [list(a[0]), [-st, n]])


class Builder:
    def __init__(self, Lp, S, dbg=False):
        self.Lp, self.S = Lp, S
        self.Ls = 8 * S
        self.Lo = Lp + S
        self.dbg = dbg
        self.nc = bass.Bass("TRN2", target_bir_lowering=False)

    def dram_in(self, name, shape, dt=F32):
        return self.nc.dram_tensor(name, list(shape), dt, kind="ExternalInput").ap()

    def dram_out(self, name, shape, dt=F32):
        return self.nc.dram_tensor(name, list(shape), dt, kind="ExternalOutput").ap()

    def dram_scr(self, name, shape, dt):
        kind = "ExternalOutput" if (self.dbg and name.split("_")[0] in str(self.dbg)) else "Internal"
        return T(self.nc.dram_tensor(name, list(shape), dt, kind=kind).ap())

    _uid = 0

    def sb(self, st, name, shape, dt):
        Builder._uid += 1
        return T(st.enter_context(self.nc.sbuf_tensor(f"sb{Builder._uid}_{name}", list(shape), dt)))

    def ps(self, st, name, shape, dt):
        Builder._uid += 1
        nbytes = int(np.prod(shape[1:])) * (4 if dt == F32 else 2)
        assert nbytes == 2048, (name, shape)
        t = T(st.enter_context(self.nc.psum_tensor(f"ps{Builder._uid}_{name}", list(shape), dt)))
        t.b.ex = True
        return t

    def load(self, out, in_, r=(), w=()):
        self.P.op("sp", lambda e: e.dma_start(out=out, in_=in_), r, w, dma=True)

    def store(self, out, in_, r=(), w=()):
        self.P.op("pool", lambda e: e.dma_start(out=out, in_=in_), r, w, dma=True)

    def mm(self, out, lhsT, rhs, start, stop, r=(), w=()):
        self.P.op("pe", lambda e: e.matmul(out, lhsT=lhsT, rhs=rhs, start=start, stop=stop), r, w)

    def tr(self, out, in_, ident, r=(), w=()):
        self.P.op("pe", lambda e: e.transpose(out, in_, ident), r, w)

    def act(self, out, in_, func, r=(), w=(), eng="act", **kw):
        self.P.op(eng, lambda e: e.activation(out=out, in_=in_, func=func, **kw), r, w)

    def tt(self, out, in0, in1, op, r=(), w=(), eng="dve"):
        self.P.op(eng, lambda e: e.tensor_tensor(out=out, in0=in0, in1=in1, op=op), r, w)

    def ts(self, out, in0, s1, s2, op0, op1=None, r=(), w=(), eng="dve"):
        if op1 is None:
            self.P.op(eng, lambda e: e.tensor_scalar(out=out, in0=in0, scalar1=s1, scalar2=None, op0=op0), r, w)
        else:
            self.P.op(eng, lambda e: e.tensor_scalar(out=out, in0=in0, scalar1=s1, scalar2=s2, op0=op0, op1=op1), r, w)

    def stt(self, out, in0, scalar, in1, op0, op1, r=(), w=()):
        self.P.op("dve", lambda e: e.scalar_tensor_tensor(out=out, in0=in0, scalar=scalar, in1=in1, op0=op0, op1=op1), r, w)

    def cp(self, out, in_, r=(), w=(), eng="dve"):
        if eng == "act":
            self.P.op("act", lambda e: e.activation(out=out, in_=in_, func=AF.Copy), r, w)
        elif eng == "dve":
            self.P.op("dve", lambda e: e.tensor_scalar(out=out, in0=in_, scalar1=1.0, scalar2=None, op0=ALU.mult), r, w)
        else:
            self.P.op(eng, lambda e: e.tensor_copy(out=out, in_=in_), r, w)

    def recip(self, out, in_, r=(), w=()):
        self.P.op("dve", lambda e: e.reciprocal(out=out, in_=in_), r, w)

    def memset(self, ap, val, r=(), w=(), eng="dve"):
        self.P.op(eng, lambda e: e.memset(ap, val), r, w)

    def rstd(self, v, n, inv_n, r=(), w=()):
        self.ts(v[:, 0:n], v[:, 0:n], inv_n, EPS, ALU.mult, ALU.add, r=list(r) + [v], w=[v])
        self.act(v[:, 0:n], v[:, 0:n], AF.Sqrt, r=[v], w=[v])
        self.recip(v[:, 0:n], v[:, 0:n], r=[v], w=list(w) + [v])

    def build(self):
        nc = self.nc
        Lp, S, Ls, Lo = self.Lp, self.S, self.Ls, self.Lo
        I = {}
        I["xp"] = self.dram_in("xp", [Lp, D])
        I["xs"] = self.dram_in("xs", [Ls, D])
        I["memp"] = self.dram_in("memp", [256, D])
        I["mems"] = self.dram_in("mems", [256, D])
        I["csp"] = self.dram_in("csp", [Lp, 128])
        I["css"] = self.dram_in("css", [Ls, 128])
        I["mf"] = self.dram_in("mf", [1, 7 * S])
        I["mb"] = self.dram_in("mb", [1, 7 * S])
        I["w_in"] = self.dram_in("w_in", [D, IN_W])
        I["w_glu"] = self.dram_in("w_glu", [512, 512])
        I["w_mem_kv"] = self.dram_in("w_mem_kv", [D, 1024])
        I["w_pa"] = self.dram_in("w_pa", [1024, D])
        I["w_ps"] = self.dram_in("w_ps", [512, D])
        I["w_px"] = self.dram_in("w_px", [512, D])
        I["w_out"] = self.dram_in("w_out", [D, D])
        I["g_in"] = self.dram_in("g_in", [128, 8])
        I["g_mem"] = self.dram_in("g_mem", [128, 8])
        I["g_q"] = self.dram_in("g_q", [128, 128])
        I["g_k"] = self.dram_in("g_k", [128, 128])
        I["g_f"] = self.dram_in("g_f", [128, D])
        I["s5d"] = self.dram_in("s5d", [128, 4])
        I["bglu"] = self.dram_in("bglu", [128, 4])
        I["are"] = self.dram_in("are", [128, 64])
        I["aim"] = self.dram_in("aim", [128, 64])
        I["lst"] = self.dram_in("lst", [128, 64])
        I["B1"] = self.dram_in("B1", [64, 128, 128])
        I["B2"] = self.dram_in("B2", [64, 128, 128])
        I["C1"] = self.dram_in("C1", [64, 128, 16])
        I["C2"] = self.dram_in("C2", [64, 128, 16])
        I["ident"] = self.dram_in("ident", [128, 128])
        I["swap"] = self.dram_in("swap", [128, 128])
        I["iota1"] = self.dram_in("iota1", [128, PIECE])
        I["sgn"] = self.dram_in("sgn", [128, 2])
        self.I = I
        self.y_out = self.dram_out("y", [Lo, D])

        self.W = {
            "w_in": self.dram_scr("wb_in", [D, IN_W], BF16),
            "w_glu": self.dram_scr("wb_glu", [512, 512], BF16),
            "w_mem_kv": self.dram_scr("wb_mkv", [D, 1024], BF16),
            "w_pa": self.dram_scr("wb_pa", [1024, D], BF16),
            "w_ps": self.dram_scr("wb_ps", [512, D], BF16),
            "w_px": self.dram_scr("wb_px", [512, D], BF16),
            "w_out": self.dram_scr("wb_out", [D, D], BF16),
        }
        self.KT = {"p": self.dram_scr("KT_p", [2, 128, Lp], BF16),
                   "s": self.dram_scr("KT_s", [2, 128, Ls], BF16)}
        self.VA = {"p": self.dram_scr("VA_p", [2, 128, Lp // 128, 129], BF16),
                   "s": self.dram_scr("VA_s", [2, 128, Ls // 128, 129], BF16)}
        self.UT = {"p": self.dram_scr("UT_p", [512, Lp], BF16),
                   "sf": self.dram_scr("UT_sf", [512, Ls], BF16),
                   "sb": self.dram_scr("UT_sb", [512, Ls], BF16)}
        self.YS = [self.dram_scr("YS_f", [512, Lo], F32), self.dram_scr("YS_b", [512, Lo], F32)]
        self.YAs = self.dram_scr("YA_s", [1024, Lo], BF16)

        with ExitStack() as gst:
            self.P = Prog(nc, gst)
            self.gst = gst
            self.consts(gst)
            phases = [self.phase0, self.phase1, self.phase2, self.phase3]
            stop = getattr(self, "stop", 3)
            for i, ph in enumerate(phases):
                with ExitStack() as st:
                    ph(st)
                    self.P.emit(last=(i == stop))
                if i == stop:
                    break
                self.P.barrier()
        return nc

    def consts(self, st):
        I = self.I
        self.ident_f = self.sb(st, "ident_f", [128, 128], F32)
        self.ident_b = self.sb(st, "ident_b", [128, 128], BF16)
        self.swap_f = self.sb(st, "swap_f", [128, 128], F32)
        self.ones_b = self.sb(st, "ones_b", [128, 128], BF16)
        self.g_in = self.sb(st, "g_in", [128, 8], F32)
        self.g_mem = self.sb(st, "g_mem", [128, 8], F32)
        self.g_q = self.sb(st, "g_q", [128, 128], F32)
        self.g_k = self.sb(st, "g_k", [128, 128], F32)
        self.s5d = self.sb(st, "s5d", [128, 4], F32)
        self.bglu = self.sb(st, "bglu", [128, 4], F32)
        self.sgn = self.sb(st, "sgn", [128, 2], F32)
        self.halfpi = self.sb(st, "halfpi", [128, 1], F32)
        self.KmT = {k: self.sb(st, "KmT" + k, [128, 4, 256], BF16) for k in "ps"}
        self.Vm = {k: self.sb(st, "Vm" + k, [128, 2, 512], BF16) for k in "ps"}
        for t, n in ((self.ident_f, "ident"), (self.swap_f, "swap"), (self.g_in, "g_in"),
                     (self.g_mem, "g_mem"), (self.g_q, "g_q"), (self.g_k, "g_k"),
                     (self.s5d, "s5d"), (self.bglu, "bglu"), (self.sgn, "sgn")):
            self.load(t[:], I[n][:, :], w=[t])
        self.cp(self.ident_b[:], self.ident_f[:], r=[self.ident_f], w=[self.ident_b])
        self.memset(self.ones_b[:], 1.0, w=[self.ones_b])
        self.memset(self.halfpi[:], math.pi / 2.0, w=[self.halfpi])

    def make_hT(self, x_ap_rows, xt, ss, xn, ptr, hT, gain, nblk=4, mask=None):
        self.load(xt[:, 0:nblk, :], x_ap_rows.rearrange("(b p) d -> p b d", p=128), w=[xt])
        for b in range(nblk):
            self.act(xn[:, b, :], xt[:, b, :], AF.Square, r=[xt], w=[xn, ss],
                     accum_out=ss[:, b:b + 1])
        import os
        if os.environ.get("DBG_H") == "1":
            return
        self.rstd(ss, nblk, 1.0 / D)
        if os.environ.get("DBG_H") == "2":
            return
        for b in range(nblk):
            if b % 2 == 0:
                self.act(xn[:, b, :], xt[:, b, :], AF.Copy, r=[xt, ss], w=[xn], scale=ss[:, b:b + 1])
            else:
                self.ts(xn[:, b, :], xt[:, b, :], ss[:, b:b + 1], None, ALU.mult, r=[xt, ss], w=[xn])
        if os.environ.get("DBG_H") == "3":
            return
        for j in range(8):
            pt = ptr[j % len(ptr)]
            for b in range(nblk):
                self.tr(pt[:, b * 128:(b + 1) * 128], xn[:, b, j * 128:(j + 1) * 128], self.ident_b[:],
                        r=[xn, self.ident_b], w=[pt])
            if j % 2 == 0:
                self.ts(hT[:, j, 0:nblk * 128], pt[:, 0:nblk * 128], gain[:, j:j + 1], None, ALU.mult,
                        r=[pt, gain], w=[hT])
            else:
                self.act(hT[:, j, 0:nblk * 128], pt[:, 0:nblk * 128], AF.Copy, r=[pt, gain], w=[hT],
                         scale=gain[:, j:j + 1])

    def phase0(self, st):
        I = self.I
        import os
        if os.environ.get("DBG_P0") == "none":
            return
        stg = [self.sb(st, f"wstg{i}", [128, 2048], F32) for i in range(2)]
        stb = [self.sb(st, f"wstb{i}", [128, 2048], BF16) for i in range(2)]
        k = 0
        for name, rows, cols in (("w_in", D, IN_W), ("w_glu", 512, 512), ("w_mem_kv", D, 1024),
                                 ("w_pa", 1024, D), ("w_ps", 512, D), ("w_px", 512, D), ("w_out", D, D)):
            cw = 1920 if cols == IN_W else cols
            for r0 in range(0, rows, 128):
                for c0 in range(0, cols, cw):
                    a, b = stg[k % 2], stb[k % 2]
                    self.load(a[:, 0:cw], I[name][r0:r0 + 128, c0:c0 + cw], w=[a])
                    self.cp(b[:, 0:cw], a[:, 0:cw], r=[a], w=[b], eng="dve" if k % 2 == 0 else "act")
                    self.store(self.W[name][r0:r0 + 128, c0:c0 + cw], b[:, 0:cw], r=[b], w=[self.W[name]])
                    k += 1
        import os
        if os.environ.get("DBG_P0") == "a":
            return
        wm = self.sb(st, "wm", [128, 8, 1024], BF16)
        self.load(wm[:], self.W["w_mem_kv"].h.rearrange("(j p) c -> p j c", p=128), r=[self.W["w_mem_kv"]], w=[wm])
        xt = self.sb(st, "m_xt", [128, 2, D], F32)
        xn = self.sb(st, "m_xn", [128, 2, D], BF16)
        ss = self.sb(st, "m_ss", [128, 4], F32)
        hT = self.sb(st, "m_hT", [128, 8, 256], BF16)
        vtmp = self.sb(st, "m_v", [128, 512], BF16)
        ptr = [self.ps(st, f"m_ptr{i}", [128, 1024], BF16) for i in range(2)]
        pk = [self.ps(st, f"m_pk{i}", [128, 512], F32) for i in range(2)]
        for key, src in (("p", I["memp"]), ("s", I["mems"])):
            self.make_hT(src[:, :], xt, ss, xn, ptr, hT, self.g_mem, nblk=2)
            if os.environ.get("DBG_P0") == "b1":
                continue
            for hx in range(4):
                p = pk[hx % 2]
                for j in range(8):
                    self.mm(p[:, 0:256], wm[:, j, hx * 128:(hx + 1) * 128], hT[:, j, :], j == 0, j == 7,
                            r=[wm, hT], w=[p])
                if os.environ.get("DBG_P0") == "b2":
                    continue
                self.cp(self.KmT[key][:, hx, :], p[:, 0:256], r=[p], w=[self.KmT[key]], eng="act")
            if os.environ.get("DBG_P0") in ("b2", "b3"):
                continue
            for m in range(2):
                p = pk[m % 2]
                for j in range(8):
                    self.mm(p[:, :], hT[:, j, m * 128:(m + 1) * 128], wm[:, j, 512:1024], j == 0, j == 7,
                            r=[wm, hT], w=[p])
                self.cp(self.Vm[key][:, m, :], p[:, :], r=[p], w=[self.Vm[key]], eng=os.environ.get("DBG_VE", "dve"))

    def rope_tables(self, cs, b, gain, tabs, r_extra=()):
        c = cs[:, b, 0:64]
        s = cs[:, b, 64:128]
        g0 = gain[:, 0:64]
        g1 = gain[:, 64:128]
        rr = [cs, gain] + list(r_extra)
        self.tt(tabs[:, 0, :], c, g0, ALU.mult, r=rr, w=[tabs])
        self.tt(tabs[:, 1, :], s, g1, ALU.mult, r=rr, w=[tabs])
        self.tt(tabs[:, 2, :], s, g0, ALU.mult, r=rr, w=[tabs])
        self.tt(tabs[:, 3, :], c, g1, ALU.mult, r=rr, w=[tabs])

    def norm_rope(self, psrc, nh, sq, ssv, xa, t4, tabs, out_bf, rsrc):
        n = nh * 128
        self.act(sq[:, 0:n], psrc, AF.Square, r=rsrc, w=[sq])
        self.P.op("dve", lambda e: e.tensor_reduce(out=ssv[:, 0:nh], in_=sq[:, 0:n].rearrange("p (h d) -> p h d", h=nh),
                                                   axis=AX.X, op=ALU.add), _bufs([sq]), _bufs([ssv]))
        self.rstd(ssv, nh, 1.0 / 128.0)
        xa3 = xa[:, 0:n].rearrange("p (h d) -> p h d", h=nh)
        self.tt(xa3, psrc.rearrange("p (h d) -> p h d", h=nh),
                ssv[:, 0:nh].unsqueeze(2).to_broadcast([128, nh, 128]), ALU.mult, r=list(rsrc) + [ssv], w=[xa])
        x0 = xa[:, 0:n].rearrange("p (h i two) -> p h i two", h=nh, two=2)[:, :, :, 0]
        x1 = xa[:, 0:n].rearrange("p (h i two) -> p h i two", h=nh, two=2)[:, :, :, 1]
        o0 = out_bf[:, 0:nh, :].rearrange("p h (i two) -> p h i two", two=2)[:, :, :, 0]
        o1 = out_bf[:, 0:nh, :].rearrange("p h (i two) -> p h i two", two=2)[:, :, :, 1]

        def tb(i):
            return tabs[:, i, :].unsqueeze(1).to_broadcast([128, nh, 64])
        tv = [t4[:, i, 0:nh * 64].rearrange("p (h i) -> p h i", h=nh) for i in range(4)]
        self.tt(tv[0], x0, tb(0), ALU.mult, r=[xa, tabs], w=[t4])
        self.tt(tv[1], x1, tb(1), ALU.mult, r=[xa, tabs], w=[t4])
        self.tt(tv[2], x0, tb(2), ALU.mult, r=[xa, tabs], w=[t4])
        self.tt(tv[3], x1, tb(3), ALU.mult, r=[xa, tabs], w=[t4])
        self.tt(o0, tv[0], tv[1], ALU.subtract, r=[t4], w=[out_bf])
        self.tt(o1, tv[2], tv[3], ALU.add, r=[t4], w=[out_bf])

    def phase1(self, st):
        I = self.I
        Lp, S, Ls = self.Lp, self.S, self.Ls
        wkv = self.sb(st, "wkv", [128, 8, 512], BF16)
        wu = self.sb(st, "wu", [128, 8, 512], BF16)
        wv = self.W["w_in"].h.rearrange("(j p) c -> p j c", p=128)
        self.load(wkv[:], wv[:, :, C_K:C_K + 512], r=[self.W["w_in"]], w=[wkv])
        self.load(wu[:], wv[:, :, C_U:C_U + 512], r=[self.W["w_in"]], w=[wu])
        xt = [self.sb(st, f"xt{i}", [128, 4, D], F32) for i in range(2)]
        xn = [self.sb(st, f"xn{i}", [128, 4, D], BF16) for i in range(2)]
        ss = [self.sb(st, f"ss{i}", [128, 4], F32) for i in range(2)]
        hT = [self.sb(st, f"hT{i}", [128, 8, 512], BF16) for i in range(2)]
        cs = [self.sb(st, f"cs{i}", [128, 4, 128], F32) for i in range(2)]
        tabs = [self.sb(st, f"tabs{i}", [128, 4, 64], F32) for i in range(2)]
        sq = self.sb(st, "sq", [128, 256], F32)
        kss = [self.sb(st, f"kss{i}", [128, 2], F32) for i in range(2)]
        ka = self.sb(st, "ka", [128, 256], F32)
        t4 = self.sb(st, "t4", [128, 4, 128], F32)
        krot = [self.sb(st, f"krot{i}", [128, 2, 128], BF16) for i in range(2)]
        KTt = [self.sb(st, f"KTt{i}", [128, 2, 512], BF16) for i in range(2)]
        VAt = [self.sb(st, f"VAt{i}", [128, 2, 4, 129], BF16) for i in range(2)]
        UTt = [self.sb(st, f"UTt{i}", [128, 4, 512], BF16) for i in range(2)]
        UTb = [self.sb(st, f"UTb{i}", [128, 4, 512], BF16) for i in range(2)]
        mrow = [self.sb(st, f"mrow{i}", [128, 2, 512], F32) for i in range(2)]
        ptr = [self.ps(st, f"ptr{i}", [128, 1024], BF16) for i in range(2)]
        pkv = [self.ps(st, f"pkv{i}", [128, 512], F32) for i in range(2)]
        pkt = self.ps(st, "pkt", [128, 2, 512], BF16)
        pu = [self.ps(st, f"pu{i}", [128, 512], F32) for i in range(2)]
        for v in VAt:
            self.memset(v[:, :, :, 128:129], 1.0, w=[v])
        it = 0
        for key, xsrc, cssrc, L in (("p", I["xp"], I["csp"], Lp), ("s", I["xs"], I["css"], Ls)):
            for t in range(L // 512):
                t0 = t * 512
                sl = it % 2
                it += 1
                prefix = (key == "s" and t0 < 7 * S)
                self.make_hT(xsrc[t0:t0 + 512, :], xt[sl], ss[sl], xn[sl], ptr, hT[sl], self.g_in)
                self.load(cs[sl][:], cssrc[t0:t0 + 512, :].rearrange("(b p) c -> p b c", p=128), w=[cs[sl]])
                if prefix:
                    self.load(mrow[sl][:, 0, :], I["mf"][0:1, t0:t0 + 512].partition_broadcast(128), w=[mrow[sl]])
                    self.load(mrow[sl][:, 1, :], I["mb"][0:1, t0:t0 + 512].partition_broadcast(128), w=[mrow[sl]])
                for b in range(4):
                    p = pkv[b % 2]
                    for j in range(8):
                        self.mm(p[:, :], hT[sl][:, j, b * 128:(b + 1) * 128], wkv[:, j, :], j == 0, j == 7,
                                r=[hT[sl], wkv], w=[p])
                    self.cp(VAt[sl][:, :, b, 0:128], p[:, 256:512].rearrange("p (h d) -> p h d", h=2),
                            r=[p], w=[VAt[sl]], eng="act")
                    tb_ = tabs[b % 2]
                    self.rope_tables(cs[sl], b, self.g_k, tb_)
                    kr = krot[b % 2]
                    self.norm_rope(p[:, 0:256], 2, sq, kss[b % 2], ka, t4, tb_, kr, [p])
                    for h in range(2):
                        self.tr(pkt[:, h, b * 128:(b + 1) * 128], kr[:, h, :], self.ident_b[:],
                                r=[kr, self.ident_b], w=[pkt])
                self.cp(KTt[sl][:], pkt[:], r=[pkt], w=[KTt[sl]])
                self.store(self.KT[key].h[:, :, t0:t0 + 512].rearrange("h p l -> p h l"), KTt[sl][:],
                           r=[KTt[sl]], w=[self.KT[key]])
                self.store(self.VA[key].h[:, :, t0 // 128:t0 // 128 + 4, :].rearrange("h p b c -> p h b c"),
                           VAt[sl][:], r=[VAt[sl]], w=[self.VA[key]])
                for i in range(4):
                    p = pu[i % 2]
                    for j in range(8):
                        self.mm(p[:, :], wu[:, j, i * 128:(i + 1) * 128], hT[sl][:, j, :], j == 0, j == 7,
                                r=[hT[sl], wu], w=[p])
                    if prefix:
                        self.tt(UTt[sl][:, i, :], p[:, :], mrow[sl][:, 0, :], ALU.mult, r=[p, mrow[sl]], w=[UTt[sl]])
                        self.tt(UTb[sl][:, i, :], p[:, :], mrow[sl][:, 1, :], ALU.mult, r=[p, mrow[sl]], w=[UTb[sl]])
                    else:
                        self.cp(UTt[sl][:, i, :], p[:, :], r=[p], w=[UTt[sl]], eng="act" if i % 2 else "dve")
                if key == "p":
                    self.store(self.UT["p"].h[:, t0:t0 + 512].rearrange("(i p) l -> p i l", p=128), UTt[sl][:],
                               r=[UTt[sl]], w=[self.UT["p"]])
                else:
                    self.store(self.UT["sf"].h[:, t0:t0 + 512].rearrange("(i p) l -> p i l", p=128), UTt[sl][:],
                               r=[UTt[sl]], w=[self.UT["sf"]])
                    self.store(self.UT["sb"].h[:, t0:t0 + 512].rearrange("(i p) l -> p i l", p=128),
                               (UTb if prefix else UTt)[sl][:], r=[(UTb if prefix else UTt)[sl]], w=[self.UT["sb"]])

    def phase2(self, st):
        banks = [self.ps(st, f"cb{i}", [128, 512], F32) for i in range(8)]
        ga = self.gen_s5(st, banks[0:4])
        gb = self.gen_att(st, banks[4:8])
        ta = tb = 0.0
        da = db = False
        import os
        if os.environ.get("DBG_NOATT"):
            db = True
        while not (da and db):
            if not da and (db or ta <= tb):
                try:
                    ta += next(ga)
                except StopIteration:
                    da = True
            else:
                try:
                    tb += next(gb)
                except StopIteration:
                    db = True

    def gen_s5(self, st, banks):
        I = self.I
        Lp, S, Ls = self.Lp, self.S, self.Ls
        def gt(name):
            return self.sb(st, name, [128, 64], F32)
        are, aim, lst = gt("are"), gt("aim"), gt("lst")
        for t, n in ((are, "are"), (aim, "aim"), (lst, "lst")):
            self.load(t[:], I[n][:, :], w=[t])
        step, Rv, th, kk, thr, sn, cs_, ab = gt("step"), gt("Rv"), gt("th"), gt("kk"), gt("thr"), gt("sn"), gt("cs_"), gt("ab")
        nr, ni, den, CR, CI, SCI, NSCR, tmp = gt("nr"), gt("ni"), gt("den"), gt("CR"), gt("CI"), gt("SCI"), gt("NSCR"), gt("tmp")
        self.act(step[:], lst[:], AF.Exp, r=[lst], w=[step])
        self.ts(are[:], are[:], -1e-4, None, ALU.min, r=[are], w=[are])
        self.tt(Rv[:], are[:], step[:], ALU.mult, r=[are, step], w=[Rv])
        self.act(Rv[:], Rv[:], AF.Exp, r=[Rv], w=[Rv])
        self.tt(th[:], aim[:], step[:], ALU.mult, r=[aim, step], w=[th])
        self.reduce_angle(th, kk, thr)
        self.sincos(thr, ab, sn, cs_)
        self.tt(nr[:], Rv[:], cs_[:], ALU.mult, r=[Rv, cs_], w=[nr])
        self.ts(nr[:], nr[:], -1.0, None, ALU.add, r=[nr], w=[nr])
        self.tt(ni[:], Rv[:], sn[:], ALU.mult, r=[Rv, sn], w=[ni])
        self.tt(den[:], are[:], are[:], ALU.mult, r=[are], w=[den])
        self.tt(tmp[:], aim[:], aim[:], ALU.mult, r=[aim], w=[tmp])
        self.tt(den[:], den[:], tmp[:], ALU.add, r=[den, tmp], w=[den])
        self.recip(den[:], den[:], r=[den], w=[den])
        self.tt(CR[:], nr[:], are[:], ALU.mult, r=[nr, are], w=[CR])
        self.tt(tmp[:], ni[:], aim[:], ALU.mult, r=[ni, aim], w=[tmp])
        self.tt(CR[:], CR[:], tmp[:], ALU.add, r=[CR, tmp], w=[CR])
        self.tt(CR[:], CR[:], den[:], ALU.mult, r=[CR, den], w=[CR])
        self.tt(CI[:], ni[:], are[:], ALU.mult, r=[ni, are], w=[CI])
        self.tt(tmp[:], nr[:], aim[:], ALU.mult, r=[nr, aim], w=[tmp])
        self.tt(CI[:], CI[:], tmp[:], ALU.subtract, r=[CI, tmp], w=[CI])
        self.tt(CI[:], CI[:], den[:], ALU.mult, r=[CI, den], w=[CI])
        self.ts(SCI[:], CI[:], self.sgn[:, 0:1], None, ALU.mult, r=[CI, self.sgn], w=[SCI])
        self.ts(NSCR[:], CR[:], self.sgn[:, 1:2], None, ALU.mult, r=[CR, self.sgn], w=[NSCR])

        iota1 = self.sb(st, "iota1", [128, PIECE], F32)
        self.load(iota1[:], I["iota1"][:, :], w=[iota1])
        ones_f = self.sb(st, "ones_f", [128, TS], F32)
        self.memset(ones_f[:], 1.0, w=[ones_f])

        yield 20.0
        UCH = 1024

        class Stream:
            pass
        strs = []
        for d in range(2):
            s_ = Stream()
            n = f"s{d}_"
            s_.phi = self.sb(st, n + "phi", [128, PIECE], F32)
            s_.k2 = self.sb(st, n + "k2", [128, PIECE], F32)
            s_.sinp = self.sb(st, n + "sinp", [128, PIECE], F32)
            s_.cosp = self.sb(st, n + "cosp", [128, PIECE], F32)
            s_.TA = self.sb(st, n + "TA", [128, PIECE], F32)
            s_.TB = self.sb(st, n + "TB", [128, PIECE], F32)
            s_.RC = self.sb(st, n + "RC", [128, PIECE], F32)
            s_.RS = self.sb(st, n + "RS", [128, PIECE], F32)
            s_.Rd = self.sb(st, n + "Rd", [128, TS], F32)
            s_.rot = self.sb(st, n + "rot", [128, 128], F32)
            s_.rb = self.sb(st, n + "rb", [128, 1], F32)
            s_.Bf = self.sb(st, n + "Bf", [128, 2, 128], F32)
            s_.Bb = self.sb(st, n + "Bb", [128, 2, 128], BF16)
            s_.Cf = self.sb(st, n + "Cf", [128, 2, 16], F32)
            s_.Cb = self.sb(st, n + "Cb", [128, 2, 16], BF16)
            s_.t2 = [self.sb(st, n + f"t2{i}", [128, TS], F32) for i in range(2)]
            s_.dd = [self.sb(st, n + f"dd{i}", [128, TS], F32) for i in range(2)]
            s_.e1 = [self.sb(st, n + f"e1{i}", [128, TS], BF16) for i in range(2)]
            s_.e2 = [self.sb(st, n + f"e2{i}", [128, TS], BF16) for i in range(2)]
            s_.wl = self.sb(st, n + "wl", [128, 1], F32)
            s_.carry = self.sb(st, n + "carry", [128, 1], F32)
            s_.ystg = [self.sb(st, n + f"ystg{i}", [16, 512], F32) for i in range(2)]
            s_.ub = [self.sb(st, n + f"ub{i}", [128, UCH], BF16) for i in range(2)]
            s_.ucnt = 0
            s_.ucur = None
            s_.uch = UCH
            s_.pp = [banks[0], banks[1]]
            s_.pw = banks[2]
            s_.py = banks[3]
            s_.prot = T(banks[3].h[:, TS:TS + 2], banks[3].b)
            s_.nchunk = 0
            s_.nstg = 0
            strs.append(s_)
        Lp, S, Ls = self.Lp, self.S, self.Ls
        nset = 0
        for j in range(4):
            for gl in range(8):
                g = 8 * j + gl
                for d in range(2):
                    s_ = strs[nset % 2]
                    nset += 1
                    s_.ucur = None
                    col = d * 32 + g
                    self.s5_prep(s_, col, iota1, ones_f, Rv, thr, CR, CI, SCI, NSCR)
                    yield 6.0
                    items = []

                    pcnt = [0]

                    def add(src, t0, rev, ypos, first=False, last=False):
                        if first:
                            pcnt[0] = 0
                        items.append(dict(s=s_, d=d, g=g, src=src, j=j, t0=t0, rev=rev, ypos=ypos,
                                          first=first, last=last, p=pcnt[0] % NPC))
                        pcnt[0] += 1
                    npc, nsc = Lp // TS, Ls // TS
                    if d == 0:
                        for i in range(npc):
                            add(self.UT["p"], i * TS, False, i * TS, i == 0, i == npc - 1)
                        for i in range(nsc):
                            t0 = i * TS
                            add(self.UT["sf"], t0, False, (Lp + t0 - 7 * S) if t0 >= 7 * S else None,
                                i == 0, i == nsc - 1)
                    else:
                        for i in range(npc):
                            t0 = Lp - (i + 1) * TS
                            add(self.UT["p"], t0, True, t0, i == 0, i == npc - 1)
                        npre = 7 * S // TS
                        for i in range(npre):
                            add(self.UT["sb"], 7 * S - (i + 1) * TS, True, None, i == 0, False)
                        for i in range(S // TS):
                            t0 = Ls - (i + 1) * TS
                            add(self.UT["sb"], t0, True, Lp + t0 - 7 * S, False, i == S // TS - 1)
                    ni = len(items)
                    for c_, it_ in enumerate(items):
                        it_["h"] = c_ % 2
                    self.s5_M(items[0])
                    self.s5_A(items[0])
                    for c in range(ni + 2):
                        if c + 1 < ni:
                            self.s5_M(items[c + 1])
                        if 0 <= c - 2 < ni:
                            self.s5_Yev(items[c - 2])
                        if 0 <= c - 1 < ni:
                            self.s5_Ymm(items[c - 1])
                        if c < ni:
                            self.s5_S(items[c], items[c - 1] if c > 0 else None)
                            self.s5_E(items[c])
                            if c + 1 < ni:
                                self.s5_A(items[c + 1])
                            yield 2.05 + (0.85 if items[c]["ypos"] is not None else 0.0)

    def gen_att(self, st, bk):
        I = self.I
        Lp, S, Ls = self.Lp, self.S, self.Ls
        SC = 128.0 ** -0.5
        wv = self.W["w_in"].h.rearrange("(j p) c -> p j c", p=128)
        ws = [self.sb(st, f"aws{i}", [128, 8, 512], BF16) for i in range(2)]
        wsi = [0]

        def wload(src_ap, rd_):
            t = ws[wsi[0] % 2]
            wsi[0] += 1
            self.load(t[:], src_ap, r=[rd_], w=[t])
            return t
        xt = [self.sb(st, f"axt{i}", [128, D], F32) for i in range(2)]
        xn = self.sb(st, "axn", [128, 4, D], BF16)
        ss = self.sb(st, "ass", [128, 4], F32)
        hT = self.sb(st, "ahT", [128, 8, 512], BF16)
        cs = self.sb(st, "acs", [128, 4, 128], F32)
        tabs = [self.sb(st, f"atabs{i}", [128, 4, 64], F32) for i in range(2)]
        sq = self.sb(st, "asq", [128, 512], F32)
        qss = [self.sb(st, f"aqss{i}", [128, 8], F32) for i in range(2)]
        qa = self.sb(st, "aqa", [128, 512], F32)
        t4 = self.sb(st, "at4", [128, 4, 256], F32)
        qrot = [self.sb(st, f"aqrot{i}", [128, 8, 128], BF16) for i in range(2)]
        QT = self.sb(st, "aQT", [128, 8, 512], BF16)
        GA = self.sb(st, "aGA", [128, 8, 512], BF16)
        YAh = [self.sb(st, f"aYAh{i}", [128, 512], BF16) for i in range(2)]
        KC = 512
        kts = [self.sb(st, f"akts{i}", [128, KC], BF16) for i in range(3)]
        vas = [self.sb(st, f"avas{i}", [128, KC // 128, 129], BF16) for i in range(3)]
        PT = [self.sb(st, f"aPT{i}", [128, 512], BF16) for i in range(3)]
        rd = self.sb(st, "ard", [128, 512], F32)
        yx = self.sb(st, "ayx", [128, 512], F32)

        def bf(b):
            return b[:].bitcast(BF16)
        seqs = [("p", I["xp"], I["csp"], 0, Lp, Lp, 0), ("s", I["xs"], I["css"], 7 * S, S, Ls, Lp)]
        nslot = 0
        for key, xsrc, cssrc, xoff, nown, Lk, yoff in seqs:
            for t in range(nown // 512):
                t0 = xoff + t * 512
                yo = yoff + t * 512
                self.make_hT_bank(xsrc[t0:t0 + 512, :], xt, ss, xn, [bk[2], bk[3]], hT, self.g_in)
                self.load(cs[:], cssrc[t0:t0 + 512, :].rearrange("(b p) c -> p b c", p=128), w=[cs])
                yield 12.0
                wq0 = wload(wv[:, :, C_Q:C_Q + 512], self.W["w_in"])
                wq1 = wload(wv[:, :, C_Q + 512:C_Q + 1024], self.W["w_in"])
                for b in range(4):
                    pa, pb, pT = bk[0], bk[1], bk[2 + b % 2]
                    for j in range(8):
                        self.mm(pa[:, :], hT[:, j, b * 128:(b + 1) * 128], wq0[:, j, :], j == 0, j == 7, r=[hT, wq0], w=[pa])
                    for j in range(8):
                        self.mm(pb[:, :], hT[:, j, b * 128:(b + 1) * 128], wq1[:, j, :], j == 0, j == 7, r=[hT, wq1], w=[pb])
                    tb_ = tabs[b % 2]
                    self.rope_tables(cs, b, self.g_q, tb_)
                    qr = qrot[b % 2]
                    for half, pbank in ((0, pa), (1, pb)):
                        self.norm_rope_q(pbank, half, sq, qss[b % 2], qa, t4, tb_, qr)
                    for h in range(8):
                        self.tr(bf(pT)[:, h * 128:(h + 1) * 128], qr[:, h, :], self.ident_b[:],
                                r=[qr, self.ident_b], w=[pT])
                    self.cp(QT[:, :, b * 128:(b + 1) * 128], bf(pT).rearrange("p (h q) -> p h q", h=8),
                            r=[pT], w=[QT], eng="act")
                    yield 5.0
                for half in range(2):
                    wg = wload(wv[:, :, C_GA + half * 512:C_GA + (half + 1) * 512], self.W["w_in"])
                    for o in range(4):
                        p = bk[o % 2]
                        for j in range(8):
                            self.mm(p[:, :], wg[:, j, o * 128:(o + 1) * 128], hT[:, j, :], j == 0, j == 7, r=[wg, hT], w=[p])
                        self.act(GA[:, half * 4 + o, :], p[:, :], AF.Silu, r=[p], w=[GA])
                    yield 8.0
                nkc = Lk // KC
                nk = Lk // 128
                kpc = KC // 128
                for h in range(8):
                    hk = h // 4
                    pO, pD = bk[2], bk[3]
                    cur = {}
                    for idx in range(nk + 1):
                        if idx < nk:
                            c, kk = idx // kpc, idx % kpc
                            if kk == 0:
                                sl = nslot % 3
                                nslot += 1
                                kt_, va_ = kts[sl], vas[sl]
                                self.load(kt_[:], self.KT[key].h[hk, :, c * KC:(c + 1) * KC], r=[self.KT[key]], w=[kt_])
                                self.load(va_[:], self.VA[key].h[hk, :, c * kpc:(c + 1) * kpc, :],
                                          r=[self.VA[key]], w=[va_])
                                cur[c] = (kt_, va_)
                            kt_, va_ = cur[c]
                            psb = bk[idx % 2]
                            pt = PT[idx % 3]
                            self.mm(psb[:, :], kt_[:, kk * 128:(kk + 1) * 128], QT[:, h, :], True, True,
                                    r=[kt_, QT], w=[psb])
                            self.act(pt[:], psb[:, :], AF.Exp, r=[psb], w=[pt], scale=SC)
                        if idx >= 1:
                            jx = idx - 1
                            c, kk = jx // kpc, jx % kpc
                            kt_, va_ = cur[c]
                            pt = PT[jx % 3]
                            self.mm(pO[:, :], va_[:, kk, 0:128], pt[:], jx == 0, jx == nk - 1, r=[va_, pt], w=[pO])
                            self.mm(pD[:, :], self.ones_b[:], pt[:], jx == 0, jx == nk - 1, r=[self.ones_b, pt], w=[pD])
                        yield 1.15
                    self.act(rd[:], pD[:, :], AF.Ln, r=[pD], w=[rd])
                    self.act(rd[:], rd[:], AF.Exp, r=[rd], w=[rd], scale=-1.0)
                    self.tt(yx[:], pO[:, :], rd[:], ALU.mult, r=[pO, rd], w=[yx])
                    ya = YAh[h % 2]
                    self.tt(ya[:], yx[:], GA[:, h, :], ALU.mult, r=[yx, GA], w=[ya])
                    self.store(self.YAs.h[h * 128:(h + 1) * 128, yo:yo + 512], ya[:], r=[ya], w=[self.YAs])
                    yield 1.0

    def reduce_angle(self, th, kk, thr):
        self.ts(kk[:], th[:], 1.0 / TWO_PI, None, ALU.mult, r=[th], w=[kk])
        self.ts(kk[:], kk[:], MAGIC, None, ALU.add, r=[kk], w=[kk])
        self.ts(kk[:], kk[:], -MAGIC, None, ALU.add, r=[kk], w=[kk])
        self.stt(thr[:], kk[:], -CW1, th[:], ALU.mult, ALU.add, r=[kk, th], w=[thr])
        self.stt(thr[:], kk[:], -CW2, thr[:], ALU.mult, ALU.add, r=[kk, thr], w=[thr])
        self.ts(thr[:], thr[:], math.pi, -math.pi, ALU.min, ALU.max, r=[thr], w=[thr])

    def sincos(self, thr, ab, sn, cs_):
        self.act(sn[:], thr[:], AF.Sin, r=[thr], w=[sn])
        self.act(ab[:], thr[:], AF.Sin, r=[thr], w=[ab], scale=0.5)
        self.tt(cs_[:], ab[:], ab[:], ALU.mult, r=[ab], w=[cs_])
        self.ts(cs_[:], cs_[:], -2.0, 1.0, ALU.mult, ALU.add, r=[cs_], w=[cs_])

    def s5_prep(self, s_, col, iota1, ones_f, Rv, thr, CR, CI, SCI, NSCR):
        I = self.I
        c1 = slice(col, col + 1)
        self.load(s_.Bf[:, 0, :], I["B1"][col, :, :], w=[s_.Bf])
        self.load(s_.Bf[:, 1, :], I["B2"][col, :, :], w=[s_.Bf])
        self.load(s_.Cf[:, 0, :], I["C1"][col, :, :], w=[s_.Cf])
        self.load(s_.Cf[:, 1, :], I["C2"][col, :, :], w=[s_.Cf])
        self.cp(s_.Bb[:], s_.Bf[:], r=[s_.Bf], w=[s_.Bb], eng="act")
        self.cp(s_.Cb[:], s_.Cf[:], r=[s_.Cf], w=[s_.Cb], eng="act")
        self.ts(s_.phi[:], iota1[:], thr[:, c1], None, ALU.mult, r=[iota1, thr], w=[s_.phi])
        self.reduce_angle(s_.phi, s_.k2, s_.phi)
        self.sincos(s_.phi, s_.k2, s_.sinp, s_.cosp)
        self.ts(s_.TA[:], s_.cosp[:], CR[:, c1], None, ALU.mult, r=[s_.cosp, CR], w=[s_.TA])
        self.stt(s_.TA[:], s_.sinp[:], CI[:, c1], s_.TA[:], ALU.mult, ALU.add, r=[s_.sinp, CI, s_.TA], w=[s_.TA])
        self.ts(s_.TB[:], s_.cosp[:], SCI[:, c1], None, ALU.mult, r=[s_.cosp, SCI], w=[s_.TB])
        self.stt(s_.TB[:], s_.sinp[:], NSCR[:, c1], s_.TB[:], ALU.mult, ALU.add, r=[s_.sinp, NSCR, s_.TB], w=[s_.TB])
        self.ts(s_.RC[:], s_.cosp[:], self.sgn[:, 1:2], None, ALU.mult, r=[s_.cosp, self.sgn], w=[s_.RC])
        self.act(s_.RS[:], s_.sinp[:], AF.Copy, r=[s_.sinp], w=[s_.RS], scale=-1.0)
        self.act(s_.Rd[:], ones_f[:], AF.Copy, r=[ones_f, Rv], w=[s_.Rd], scale=Rv[:, c1])
        self.ts(s_.rb[:], s_.sinp[:, PIECE - 1:PIECE], self.sgn[:, 1:2], None, ALU.mult, r=[s_.sinp, self.sgn], w=[s_.rb])
        self.ts(s_.rot[:], self.ident_f[:], s_.cosp[:, PIECE - 1:PIECE], None, ALU.mult, r=[self.ident_f, s_.cosp], w=[s_.rot])
        self.stt(s_.rot[:], self.swap_f[:], s_.rb[:, 0:1], s_.rot[:], ALU.mult, ALU.add,
                 r=[self.swap_f, s_.rb, s_.rot], w=[s_.rot])

    def s5_M(self, it):
        s_ = it["s"]
        k = s_.nchunk % 2
        s_.nchunk += 1
        it["k"] = k
        pp = s_.pp[k]
        src, jj, t0 = it["src"], it["j"], it["t0"]
        piece = t0 // s_.uch
        ukey = (id(src), jj, piece)
        if s_.ucur != ukey:
            ub = s_.ub[s_.ucnt % 2]
            s_.ucnt += 1
            self.load(ub[:], src.h[jj * 128:(jj + 1) * 128, piece * s_.uch:(piece + 1) * s_.uch], r=[src], w=[ub])
            s_.ucur = ukey
            s_.ubcur = ub
        ut = s_.ubcur
        off = t0 - piece * s_.uch
        rhs = ut[:, off:off + TS]
        if it["rev"]:
            rhs = rev_ap(rhs)
        self.mm(pp[:, 0:TS], s_.Bb[:, 0, :], rhs, True, True, r=[s_.Bb, ut], w=[pp])
        self.mm(pp[:, TS:2 * TS], s_.Bb[:, 1, :], rhs, True, True, r=[s_.Bb, ut], w=[pp])

    def s5_A(self, it):
        s_ = it["s"]
        k = it["k"]
        pp, t2, dd = s_.pp[k], s_.t2[k], s_.dd[k]
        ps_ = slice(it["p"] * TS, (it["p"] + 1) * TS)
        self.tt(pp[:, 0:TS], pp[:, 0:TS], s_.TA[:, ps_], ALU.mult, r=[pp, s_.TA], w=[pp])
        self.tt(t2[:], pp[:, TS:2 * TS], s_.TB[:, ps_], ALU.mult, r=[pp, s_.TB], w=[t2])
        self.tt(dd[:], pp[:, 0:TS], t2[:], ALU.add, r=[pp, t2], w=[dd])

    def s5_S(self, it, prev):
        s_ = it["s"]
        k = it["k"]
        dd = s_.dd[k]
        h = it["h"]
        out = s_.pw[:, h * TS:(h + 1) * TS]
        rds = [s_.Rd, dd]
        if it["first"]:
            init = 0.0
        elif prev["p"] == NPC - 1:
            init = s_.prot[:, 0:1]
            rds.append(s_.py)
        else:
            ph = prev["h"]
            init = s_.pw[:, ph * TS + TS - 1:ph * TS + TS]
        self.P.op("dve", lambda e: e.tensor_tensor_scan(out=out, data0=s_.Rd[:], data1=dd[:],
                                                        initial=init, op0=ALU.mult, op1=ALU.add),
                  _bufs(rds), _bufs([s_.pw]))
        if not it["last"] and it["p"] == NPC - 1:
            self.ts(s_.wl[:], s_.pw[:, h * TS + TS - 1:h * TS + TS], 1.0, None, ALU.mult, r=[s_.pw], w=[s_.wl])
            self.mm(s_.prot[:, 0:1], s_.rot[:], s_.wl[:], True, True, r=[s_.rot, s_.wl], w=[s_.py])

    def s5_E(self, it):
        s_ = it["s"]
        k = it["k"]
        if it["ypos"] is None:
            return
        h = it["h"]
        e1, e2 = s_.e1[k], s_.e2[k]
        ps_ = slice(it["p"] * TS, (it["p"] + 1) * TS)
        self.tt(e1[:], s_.pw[:, h * TS:(h + 1) * TS], s_.RC[:, ps_], ALU.mult, r=[s_.pw, s_.RC], w=[e1])
        self.tt(e2[:], s_.pw[:, h * TS:(h + 1) * TS], s_.RS[:, ps_], ALU.mult, r=[s_.pw, s_.RS], w=[e2])

    def s5_Ymm(self, it):
        s_ = it["s"]
        k = it["k"]
        if it["ypos"] is None:
            return
        e1, e2 = s_.e1[k], s_.e2[k]
        py = s_.py[0:16, 0:TS]
        self.mm(py, s_.Cb[:, 0, :], e1[:], True, False, r=[s_.Cb, e1], w=[s_.py])
        self.mm(py, s_.Cb[:, 1, :], e2[:], False, True, r=[s_.Cb, e2], w=[s_.py])

    def s5_Yev(self, it):
        s_ = it["s"]
        ypos, rev, d, g = it["ypos"], it["rev"], it["d"], it["g"]
        if ypos is None:
            return
        py = s_.py[0:16, 0:TS]
        nper = 512 // TS
        sidx = s_.nstg // nper
        stg = s_.ystg[sidx % 2]
        q = s_.nstg % nper
        s_.nstg += 1
        base = (ypos // 512) * 512
        off = ypos - base
        dst = stg[0:16, off:off + TS]
        if rev:
            dst = rev_ap(dst)
        self.cp(dst, py, r=[s_.py], w=[stg], eng="act")
        if q == nper - 1:
            self.store(self.YS[d].h[g * 16:(g + 1) * 16, base:base + 512], stg[0:16, :], r=[stg], w=[self.YS[d]])

    def phase3(self, st):
        I = self.I
        Lp, S, Ls = self.Lp, self.S, self.Ls
        SC = 128.0 ** -0.5
        wv = self.W["w_in"].h.rearrange("(j p) c -> p j c", p=128)
        ws = [self.sb(st, f"ws{i}", [128, 8, 512], BF16) for i in range(3)]
        self.wsi = 0

        def wload(src_ap, rd):
            t = ws[self.wsi % 3]
            self.wsi += 1
            self.load(t[:], src_ap, r=[rd], w=[t])
            return t

        xt = [self.sb(st, f"xt{i}", [128, D], F32) for i in range(2)]
        xn = self.sb(st, "xn", [128, 4, D], BF16)
        ss = self.sb(st, "ss", [128, 4], F32)
        hT = self.sb(st, "hT", [128, 8, 512], BF16)
        cs = self.sb(st, "cs", [128, 4, 128], F32)
        tabs = [self.sb(st, f"tabs{i}", [128, 4, 64], F32) for i in range(2)]
        sq = self.sb(st, "sq", [128, 512], F32)
        qss = [self.sb(st, f"qss{i}", [128, 8], F32) for i in range(2)]
        qa = self.sb(st, "qa", [128, 512], F32)
        t4 = self.sb(st, "t4", [128, 4, 256], F32)
        qrot = [self.sb(st, f"qrot{i}", [128, 8, 128], BF16) for i in range(2)]
        QT = self.sb(st, "QT", [128, 8, 512], BF16)
        GA = self.sb(st, "GA", [128, 8, 512], BF16)
        YA = self.sb(st, "YA", [128, 8, 512], BF16)
        KC = 512
        kts = [self.sb(st, f"kts{i}", [128, KC], BF16) for i in range(3)]
        vas = [self.sb(st, f"vas{i}", [128, KC // 128, 129], BF16) for i in range(3)]
        PT = [self.sb(st, f"PT{i}", [128, 512], BF16) for i in range(3)]
        rcp = self.sb(st, "rcp", [128, 4], F32)
        yn = [self.sb(st, f"yn{i}", [128, 128], BF16) for i in range(2)]
        y0 = [self.sb(st, f"y0_{i}", [128, 512], F32) for i in range(1)] * 2
        y1 = [self.sb(st, f"y1_{i}", [128, 512], F32) for i in range(1)] * 2
        uu = [self.sb(st, f"uu{i}", [128, 512], BF16) for i in range(2)]
        gx = [self.sb(st, f"gx{i}", [128, 512], F32) for i in range(2)]
        g2 = [self.sb(st, f"g2{i}", [128, 512], F32) for i in range(2)]
        YG = self.sb(st, "YG", [128, 4, 512], F32)
        YGb = self.sb(st, "YGb", [128, 4, 512], BF16)
        GS = self.sb(st, "GS", [128, 4, 512], BF16)
        sgl = [self.sb(st, f"sgl{i}", [128, 512], F32) for i in range(2)]
        YSb = self.sb(st, "YSb", [128, 4, 512], BF16)
        wglu = self.sb(st, "wglu", [128, 4, 512], BF16)
        self.load(wglu[:], self.W["w_glu"].h.rearrange("(k p) c -> p k c", p=128), r=[self.W["w_glu"]], w=[wglu])
        QX = self.sb(st, "QX", [128, 4, 512], BF16)
        GX = self.sb(st, "GX", [128, 4, 512], BF16)
        PX = [self.sb(st, f"PX{i}", [128, 512], BF16) for i in range(2)]
        rd = self.sb(st, "rd", [128, 512], F32)
        yx = self.sb(st, "yx", [128, 512], F32)
        YX = self.sb(st, "YX", [128, 4, 512], BF16)
        G3 = self.sb(st, "G3", [128, 3, 4, 512], BF16)
        m = [self.sb(st, f"m{i}", [128, 512], F32) for i in range(3)]
        M = self.sb(st, "M", [128, 8, 512], BF16)
        yres = [self.sb(st, f"yres{i}", [128, D], F32) for i in range(1)] * 2
        fss = [self.sb(st, f"fss{i}", [128, 1], F32) for i in range(2)]
        gf = self.sb(st, "gf", [128, D], F32)
        self.load(gf[:], I["g_f"][:, :], w=[gf])
        bk = [self.ps(st, f"bk{i}", [128, 512], F32) for i in range(8)]

        def bf(b):
            return b[:].bitcast(BF16)

        seqs = [("p", I["xp"], I["csp"], 0, Lp, Lp, 0), ("s", I["xs"], I["css"], 7 * S, S, Ls, Lp)]
        for key, xsrc, cssrc, xoff, nown, Lk, yoff in seqs:
            for t in range(nown // 512):
                t0 = xoff + t * 512
                yo = yoff + t * 512
                self.make_hT_bank(xsrc[t0:t0 + 512, :], xt, ss, xn, [bk[6], bk[7]], hT, self.g_in)
                self.load(cs[:], cssrc[t0:t0 + 512, :].rearrange("(b p) c -> p b c", p=128), w=[cs])
                self.load(YA[:], self.YAs.h[:, yo:yo + 512].rearrange("(h p) l -> p h l", p=128), r=[self.YAs], w=[YA])
                wgs = wload(wv[:, :, C_GS:C_GS + 512], self.W["w_in"])
                for i in range(4):
                    k2 = i % 2
                    self.load(y0[k2][:], self.YS[0].h[i * 128:(i + 1) * 128, yo:yo + 512], r=[self.YS[0]], w=[y0[k2]])
                    self.load(y1[k2][:], self.YS[1].h[i * 128:(i + 1) * 128, yo:yo + 512], r=[self.YS[1]], w=[y1[k2]])
                    usrc = self.UT["p"] if key == "p" else self.UT["sf"]
                    self.load(uu[k2][:], usrc.h[i * 128:(i + 1) * 128, t0:t0 + 512], r=[usrc], w=[uu[k2]])
                    a, bq = gx[k2], g2[k2]
                    self.tt(a[:], y0[k2][:], y1[k2][:], ALU.add, r=[y0[k2], y1[k2]], w=[a])
                    self.stt(a[:], uu[k2][:], self.s5d[:, i:i + 1], a[:], ALU.mult, ALU.add, r=[uu[k2], self.s5d, a], w=[a])
                    self.tt(bq[:], a[:], a[:], ALU.mult, r=[a], w=[bq])
                    self.ts(bq[:], bq[:], 0.044715, 1.0, ALU.mult, ALU.add, r=[bq], w=[bq])
                    self.tt(bq[:], bq[:], a[:], ALU.mult, r=[bq, a], w=[bq])
                    self.act(bq[:], bq[:], AF.Sigmoid, r=[bq], w=[bq], scale=2.0 * math.sqrt(2.0 / math.pi))
                    self.tt(YG[:, i, :], a[:], bq[:], ALU.mult, r=[a, bq], w=[YG])
                    self.cp(YGb[:, i, :], YG[:, i, :], r=[YG], w=[YGb], eng="act")
                    p = bk[i % 2]
                    for j in range(8):
                        self.mm(p[:, :], wgs[:, j, i * 128:(i + 1) * 128], hT[:, j, :], j == 0, j == 7, r=[wgs, hT], w=[p])
                    self.act(GS[:, i, :], p[:, :], AF.Silu, r=[p], w=[GS])
                for o in range(4):
                    p = bk[2 + o % 2]
                    for k_ in range(4):
                        self.mm(p[:, :], wglu[:, k_, o * 128:(o + 1) * 128], YGb[:, k_, :], k_ == 0, k_ == 3,
                                r=[wglu, YGb], w=[p])
                    s_ = sgl[o % 2]
                    self.act(s_[:], p[:, :], AF.Sigmoid, r=[p, self.bglu], w=[s_], bias=self.bglu[:, o:o + 1])
                    self.tt(s_[:], s_[:], YG[:, o, :], ALU.mult, r=[s_, YG], w=[s_])
                    self.tt(YSb[:, o, :], s_[:], GS[:, o, :], ALU.mult, r=[s_, GS], w=[YSb])
                wqx = wload(wv[:, :, C_QX:C_QX + 512], self.W["w_in"])
                wgx = wload(wv[:, :, C_GX:C_GX + 512], self.W["w_in"])
                for o in range(4):
                    p = bk[o % 2]
                    for j in range(8):
                        self.mm(p[:, :], wqx[:, j, o * 128:(o + 1) * 128], hT[:, j, :], j == 0, j == 7, r=[wqx, hT], w=[p])
                    self.cp(QX[:, o, :], p[:, :], r=[p], w=[QX], eng="act")
                    p2 = bk[2 + o % 2]
                    for j in range(8):
                        self.mm(p2[:, :], wgx[:, j, o * 128:(o + 1) * 128], hT[:, j, :], j == 0, j == 7, r=[wgx, hT], w=[p2])
                    self.act(GX[:, o, :], p2[:, :], AF.Silu, r=[p2], w=[GX])
                KmT, Vm = self.KmT[key], self.Vm[key]
                for hx in range(4):
                    for mt in range(2):
                        p = bk[mt]
                        self.mm(p[:, :], KmT[:, hx, mt * 128:(mt + 1) * 128], QX[:, hx, :], True, True, r=[KmT, QX], w=[p])
                        self.act(PX[mt][:], p[:, :], AF.Exp, r=[p], w=[PX[mt]], scale=SC)
                    po_, pd_ = bk[4], bk[5]
                    for mt in range(2):
                        self.mm(po_[:, :], Vm[:, mt, hx * 128:(hx + 1) * 128], PX[mt][:], mt == 0, mt == 1, r=[Vm, PX[mt]], w=[po_])
                    for mt in range(2):
                        self.mm(pd_[:, :], self.ones_b[:], PX[mt][:], mt == 0, mt == 1, r=[self.ones_b, PX[mt]], w=[pd_])
                    self.recip(rd[:], pd_[:, :], r=[pd_], w=[rd])
                    self.tt(yx[:], po_[:, :], rd[:], ALU.mult, r=[po_, rd], w=[yx])
                    self.tt(YX[:, hx, :], yx[:], GX[:, hx, :], ALU.mult, r=[yx, GX], w=[YX])
                wpa = self.W["w_pa"].h.rearrange("(k p) c -> p k c", p=128)
                wps = self.W["w_ps"].h.rearrange("(k p) c -> p k c", p=128)
                wpx = self.W["w_px"].h.rearrange("(k p) c -> p k c", p=128)
                for og in range(2):
                    for br in range(3):
                        wm_ = wload(wv[:, :, C_MG + br * 1024 + og * 512:C_MG + br * 1024 + (og + 1) * 512], self.W["w_in"])
                        for o in range(4):
                            p = bk[o % 2]
                            for j in range(8):
                                self.mm(p[:, :], wm_[:, j, o * 128:(o + 1) * 128], hT[:, j, :], j == 0, j == 7, r=[wm_, hT], w=[p])
                            self.act(G3[:, br, o, :], p[:, :], AF.Sigmoid, r=[p], w=[G3])
                    wa = wload(wpa[:, :, og * 512:(og + 1) * 512], self.W["w_pa"])
                    wsx = ws[self.wsi % 3]
                    self.wsi += 1
                    self.load(wsx[:, 0:4, :], wps[:, :, og * 512:(og + 1) * 512], r=[self.W["w_ps"]], w=[wsx])
                    self.load(wsx[:, 4:8, :], wpx[:, :, og * 512:(og + 1) * 512], r=[self.W["w_px"]], w=[wsx])
                    for o in range(4):
                        pa_, ps_, px_ = bk[2 + (o % 2) * 3], bk[3 + (o % 2) * 3], bk[4 + (o % 2) * 3]
                        for k_ in range(8):
                            self.mm(pa_[:, :], wa[:, k_, o * 128:(o + 1) * 128], YA[:, k_, :], k_ == 0, k_ == 7, r=[wa, YA], w=[pa_])
                        for k_ in range(4):
                            self.mm(ps_[:, :], wsx[:, k_, o * 128:(o + 1) * 128], YSb[:, k_, :], k_ == 0, k_ == 3, r=[wsx, YSb], w=[ps_])
                        for k_ in range(4):
                            self.mm(px_[:, :], wsx[:, 4 + k_, o * 128:(o + 1) * 128], YX[:, k_, :], k_ == 0, k_ == 3, r=[wsx, YX], w=[px_])
                        self.tt(m[0][:], pa_[:, :], G3[:, 0, o, :], ALU.mult, r=[pa_, G3], w=[m[0]])
                        self.tt(m[1][:], ps_[:, :], G3[:, 1, o, :], ALU.mult, r=[ps_, G3], w=[m[1]])
                        self.tt(m[2][:], px_[:, :], G3[:, 2, o, :], ALU.mult, r=[px_, G3], w=[m[2]])
                        self.tt(m[0][:], m[0][:], m[1][:], ALU.add, r=[m[0], m[1]], w=[m[0]])
                        self.tt(M[:, og * 4 + o, :], m[0][:], m[2][:], ALU.add, r=[m[0], m[2]], w=[M])
                wo_ = self.W["w_out"].h.rearrange("(k p) c -> p k c", p=128)
                wo0 = wload(wo_[:, :, 0:512], self.W["w_out"])
                wo1 = wload(wo_[:, :, 512:1024], self.W["w_out"])
                for b in range(4):
                    pa_, pb_ = bk[(2 * b) % 4], bk[(2 * b + 1) % 4]
                    for k_ in range(8):
                        self.mm(pa_[:, :], M[:, k_, b * 128:(b + 1) * 128], wo0[:, k_, :], k_ == 0, k_ == 7, r=[M, wo0], w=[pa_])
                    for k_ in range(8):
                        self.mm(pb_[:, :], M[:, k_, b * 128:(b + 1) * 128], wo1[:, k_, :], k_ == 0, k_ == 7, r=[M, wo1], w=[pb_])
                    yr, fs = yres[b % 2], fss[b % 2]
                    xb = xt[b % 2]
                    self.load(xb[:], xsrc[t0 + b * 128:t0 + (b + 1) * 128, :], w=[xb])
                    self.tt(yr[:, 0:512], pa_[:, :], xb[:, 0:512], ALU.add, r=[pa_, xb], w=[yr])
                    self.tt(yr[:, 512:1024], pb_[:, :], xb[:, 512:1024], ALU.add, r=[pb_, xb], w=[yr])
                    self.act(xn[:, 0, :], yr[:], AF.Square, r=[yr], w=[xn, fs], accum_out=fs[:, 0:1])
                    self.rstd(fs, 1, 1.0 / D)
                    self.stt(yr[:], yr[:], fs[:, 0:1], gf[:], ALU.mult, ALU.mult, r=[yr, fs, gf], w=[yr])
                    self.store(self.y_out[yo + b * 128:yo + (b + 1) * 128, :], yr[:], r=[yr])

    def make_hT_bank(self, x_rows, xt, ss, xn, banks, hT, gain):
        for b in range(4):
            xb = xt[b % 2]
            sb_ = ss[b % 2] if isinstance(ss, list) else ss
            self.load(xb[:], x_rows[b * 128:(b + 1) * 128, :], w=[xb])
            self.act(xn[:, b, :], xb[:], AF.Square, r=[xb], w=[xn, ss], accum_out=ss[:, b:b + 1])
            v = ss[:, b:b + 1]
            self.ts(v, v, 1.0 / D, EPS, ALU.mult, ALU.add, r=[ss], w=[ss])
            self.act(v, v, AF.Sqrt, r=[ss], w=[ss])
            self.recip(v, v, r=[ss], w=[ss])
            if b % 2 == 0:
                self.act(xn[:, b, :], xb[:], AF.Copy, r=[xb, ss], w=[xn], scale=ss[:, b:b + 1])
            else:
                self.ts(xn[:, b, :], xb[:], ss[:, b:b + 1], None, ALU.mult, r=[xb, ss], w=[xn])
        for j in range(8):
            bank = banks[j % len(banks)]
            pv = bank[:].bitcast(BF16)
            for b in range(4):
                self.tr(pv[:, b * 128:(b + 1) * 128], xn[:, b, j * 128:(j + 1) * 128], self.ident_b[:],
                        r=[xn, self.ident_b], w=[bank])
            if j % 2 == 0:
                self.ts(hT[:, j, :], pv[:, 0:512], gain[:, j:j + 1], None, ALU.mult, r=[bank, gain], w=[hT])
            else:
                self.act(hT[:, j, :], pv[:, 0:512], AF.Copy, r=[bank, gain], w=[hT], scale=gain[:, j:j + 1])

    def norm_rope_q(self, pbank, half, sq, ssv, xa, t4, tabs, out_bf):
        nh = 4
        o = 0
        h0 = half * 4
        psrc = pbank[:, :]
        self.act(sq[:, o:o + 512], psrc, AF.Square, r=[pbank], w=[sq])
        self.P.op("dve", lambda e: e.tensor_reduce(out=ssv[:, h0:h0 + 4], in_=sq[:, o:o + 512].rearrange("p (h d) -> p h d", h=nh),
                                                   axis=AX.X, op=ALU.add), _bufs([sq]), _bufs([ssv]))
        v = ssv[:, h0:h0 + 4]
        self.ts(v, v, 1.0 / 128.0, EPS, ALU.mult, ALU.add, r=[ssv], w=[ssv])
        self.act(v, v, AF.Sqrt, r=[ssv], w=[ssv])
        self.recip(v, v, r=[ssv], w=[ssv])
        xa3 = xa[:, o:o + 512].rearrange("p (h d) -> p h d", h=nh)
        self.tt(xa3, psrc.rearrange("p (h d) -> p h d", h=nh), v.unsqueeze(2).to_broadcast([128, nh, 128]),
                ALU.mult, r=[pbank, ssv], w=[xa])
        x0 = xa[:, o:o + 512].rearrange("p (h i two) -> p h i two", h=nh, two=2)[:, :, :, 0]
        x1 = xa[:, o:o + 512].rearrange("p (h i two) -> p h i two", h=nh, two=2)[:, :, :, 1]
        ob = out_bf[:, h0:h0 + 4, :].rearrange("p h (i two) -> p h i two", two=2)
        o0, o1 = ob[:, :, :, 0], ob[:, :, :, 1]

        def tb(i):
            return tabs[:, i, :].unsqueeze(1).to_broadcast([128, nh, 64])
        tv = [t4[:, i, 0:256].rearrange("p (h i) -> p h i", h=nh) for i in range(4)]
        self.tt(tv[0], x0, tb(0), ALU.mult, r=[xa, tabs], w=[t4])
        self.tt(tv[1], x1, tb(1), ALU.mult, r=[xa, tabs], w=[t4])
        self.tt(tv[2], x0, tb(2), ALU.mult, r=[xa, tabs], w=[t4])
        self.tt(tv[3], x1, tb(3), ALU.mult, r=[xa, tabs], w=[t4])
        self.tt(o0, tv[0], tv[1], ALU.subtract, r=[t4], w=[out_bf])
        self.tt(o1, tv[2], tv[3], ALU.add, r=[t4], w=[out_bf])


def rope_table(pos):
    pos = np.asarray(pos)
    row = (pos // 64).astype(np.float32)
    col = (pos % 64).astype(np.float32)
    freqs = (np.float32(10000.0) ** (-np.arange(32, dtype=np.float32) / np.float32(32))).astype(np.float32)
    ang = np.concatenate([row[:, None] * freqs, col[:, None] * freqs], axis=-1).astype(np.float32)
    return np.concatenate([np.cos(ang), np.sin(ang)], axis=-1).astype(np.float32)


def host_inputs(inp, Lp, S, ncores=NCORES):
    f = lambda a: np.ascontiguousarray(np.asarray(a, dtype=np.float32))
    Ls = 8 * S
    xs_all = f(inp["x_sample"])[0]
    shared = {}
    shared["w_in"] = f(inp["w_in"])[0]
    shared["w_glu"] = f(inp["w_glu"])[0]
    shared["w_mem_kv"] = f(inp["w_mem_kv"])[0]
    shared["w_pa"] = f(inp["w_proj_attn"])[0]
    shared["w_ps"] = f(inp["w_proj_ssm"])[0]
    shared["w_px"] = f(inp["w_proj_cross"])[0]
    shared["w_out"] = f(inp["w_out"])[0]
    shared["g_in"] = f(f(inp["norm_in"])[0].reshape(8, 128).T)
    shared["g_mem"] = f(f(inp["norm_mem"])[0].reshape(8, 128).T)
    qn, kn = f(inp["q_norm"])[0], f(inp["k_norm"])[0]
    shared["g_q"] = f(np.tile(np.concatenate([qn[0::2], qn[1::2]])[None, :], (128, 1)))
    shared["g_k"] = f(np.tile(np.concatenate([kn[0::2], kn[1::2]])[None, :], (128, 1)))
    shared["g_f"] = f(np.tile(f(inp["norm_final"])[None, :], (128, 1)))
    shared["s5d"] = f(f(inp["s5_d"])[0].reshape(4, 128).T)
    shared["bglu"] = f(f(inp["b_glu"])[0].reshape(4, 128).T)
    a_re, a_im = f(inp["s5_a_re"])[0], f(inp["s5_a_im"])[0]
    dup = lambda a: f(np.concatenate([a.reshape(64, 64).T, a.reshape(64, 64).T], axis=0))
    shared["are"] = dup(a_re)
    shared["aim"] = dup(a_im)
    shared["lst"] = f(np.tile(f(inp["s5_log_step"])[0].reshape(1, 64), (128, 1)))
    b_re, b_im = f(inp["s5_b_re"])[0], f(inp["s5_b_im"])[0]
    c_re, c_im = f(inp["s5_c_re"])[0], f(inp["s5_c_im"])[0]
    B1 = np.zeros((64, 128, 128), np.float32)
    B2 = np.zeros((64, 128, 128), np.float32)
    C1 = np.zeros((64, 128, 16), np.float32)
    C2 = np.zeros((64, 128, 16), np.float32)
    for d in range(2):
        for g in range(32):
            col = d * 32 + g
            r0 = (g % 8) * 16
            B1[col, r0:r0 + 16, 0:64] = b_re[d, g].T
            B1[col, r0:r0 + 16, 64:128] = b_im[d, g].T
            B2[col, r0:r0 + 16, 0:64] = b_im[d, g].T
            B2[col, r0:r0 + 16, 64:128] = b_re[d, g].T
            C1[col, 0:64, :] = c_re[d, g].T
            C1[col, 64:128, :] = c_im[d, g].T
            C2[col, 0:64, :] = c_im[d, g].T
            C2[col, 64:128, :] = c_re[d, g].T
    shared.update(B1=B1, B2=B2, C1=C1, C2=C2)
    shared["ident"] = np.eye(128, dtype=np.float32)
    sw = np.zeros((128, 128), np.float32)
    sw[np.arange(128), (np.arange(128) + 64) % 128] = 1.0
    shared["swap"] = sw
    shared["iota1"] = f(np.tile(np.arange(1, PIECE + 1, dtype=np.float32)[None, :], (128, 1)))
    sg = np.ones((128, 2), np.float32)
    sg[0:64, 0] = -1.0
    sg[64:128, 1] = -1.0
    shared["sgn"] = sg
    shared["csp"] = rope_table(np.arange(Lp))
    maps = []
    for c in range(ncores):
        m = dict(shared)
        m["xp"] = f(inp["x_prompt"])[c]
        order = np.concatenate([np.arange((c + 1) * S, Ls), np.arange(0, c * S), np.arange(c * S, (c + 1) * S)])
        m["xs"] = np.ascontiguousarray(xs_all[order])
        m["css"] = rope_table(order)
        m["memp"] = f(inp["mem_prompt"])[c]
        m["mems"] = f(inp["mem_sample"])[0]
        mf = np.zeros((1, 7 * S), np.float32)
        mf[0, (7 - c) * S:] = 1.0
        m["mf"] = mf
        m["mb"] = (1.0 - mf).astype(np.float32)
        maps.append(m)
    return maps


_NC_CACHE = {}


def run(inp, Lp, S):
    key = (Lp, S)
    if key not in _NC_CACHE:
        _NC_CACHE[key] = Builder(Lp, S).build()
    nc = _NC_CACHE[key]
    maps = host_inputs(inp, Lp, S)
    res = run_bass_kernel_spmd(nc, maps, core_ids=list(range(NCORES)))
    ys = [np.asarray(r["y"]) for r in res.results]
    y_prompt = np.stack([y[:Lp] for y in ys], axis=0).astype(np.float32)
    y_sample = np.concatenate([y[Lp:Lp + S] for y in ys], axis=0)[None].astype(np.float32)
    return y_prompt, y_sample


def kernel(**inputs):
    Lp = int(np.asarray(inputs["x_prompt"]).shape[1])
    Ls = int(np.asarray(inputs["x_sample"]).shape[1])
    return run(inputs, Lp, Ls // 8)
```

```python
import math
from contextlib import ExitStack

import numpy as np
import concourse.bass as bass
import concourse.mybir as mybir
from concourse.bass_utils import run_bass_kernel_spmd

F32 = mybir.dt.float32
BF16 = mybir.dt.bfloat16
AF = mybir.ActivationFunctionType
ALU = mybir.AluOpType
AX = mybir.AxisListType

D = 1024
IN_W = 7680
C_Q, C_K, C_V, C_GA, C_U, C_GS, C_QX, C_GX, C_MG = 0, 1024, 1280, 1536, 2560, 3072, 3584, 4096, 4608
EPS = 1e-6
NCORES = 8
TS = 256
NPC = 4
PIECE = NPC * TS
MAGIC = 12582912.0
TWO_PI = 2.0 * math.pi
CW1 = 6.28125
CW2 = TWO_PI - CW1
SEM_CH = 20000
N_DMA_SEMS = 32


class Buf:
    __slots__ = ("lw", "rd", "ex")

    def __init__(self):
        self.lw = None
        self.rd = []
        self.ex = False


class T:
    def __init__(self, h, b=None):
        self.h = h
        self.b = b if b is not None else Buf()

    def __getitem__(self, k):
        return self.h[k]


def _bufs(xs):
    out = []
    for x in xs:
        if x is None:
            continue
        out.append(x.b if isinstance(x, T) else x)
    return out


class Prog:
    ENGS = ("pe", "act", "dve", "pool", "sp")

    def __init__(self, nc, stack):
        self.nc = nc
        self.stack = stack
        self.q = {e: [] for e in self.ENGS}
        self.cnt = {e: 0 for e in self.ENGS}
        self.sems = {e: [] for e in self.ENGS}
        self.dpool = {e: {"sems": [], "cnt": [], "rr": 0} for e in self.ENGS}
        self.n_dma = 0
        self.pend = {e: [] for e in self.ENGS}
        self.waited = {e: {} for e in self.ENGS}

    def _sem(self, e, idx):
        k = idx // SEM_CH
        while len(self.sems[e]) <= k:
            self.sems[e].append(self.stack.enter_context(
                self.nc.semaphore(f"s_{e}_{len(self.sems[e])}")))
        return self.sems[e][k], idx % SEM_CH + 1

    def op(self, e, fn, reads=(), writes=(), dma=False):
        reads = _bufs(reads)
        writes = _bufs(writes)
        exr = [b for b in reads if b.ex]
        if exr:
            reads = [b for b in reads if not b.ex]
            writes = writes + [b for b in exr if b not in writes]
        deps = {}

        def add(d):
            if d is None:
                return
            if d[0] not in deps or deps[d[0]][1] < d[1]:
                deps[d[0]] = d
        for b in reads:
            add(b.lw)
        for b in writes:
            add(b.lw)
            for r in b.rd:
                add(r)
        idx = self.cnt[e]
        if not dma:
            self.cnt[e] += 1
        waits = list(self.pend[e])
        self.pend[e] = []
        for key, d in deps.items():
            if key == "pe" and e == "pe":
                continue
            waits.append((d[2], d[3]))
        wd = self.waited[e]
        ww = []
        for (ws, wv) in waits:
            if wd.get(id(ws), 0) >= wv:
                continue
            wd[id(ws)] = wv
            ww.append((ws, wv))
        waits = ww
        if dma:
            dp = self.dpool[e]
            if len(dp["sems"]) < N_DMA_SEMS:
                dp["sems"].append(self.stack.enter_context(
                    self.nc.semaphore(f"s_dma_{e}_{len(dp['sems'])}")))
                dp["cnt"].append(0)
                i = len(dp["sems"]) - 1
            else:
                i = dp["rr"]
                dp["rr"] = (dp["rr"] + 1) % N_DMA_SEMS
            dsem = dp["sems"][i]
            if dp["cnt"][i] > 0 and wd.get(id(dsem), 0) < dp["cnt"][i]:
                wd[id(dsem)] = dp["cnt"][i]
                waits.append((dsem, dp["cnt"][i]))
            dp["cnt"][i] += 16
            me = ("dma%d" % self.n_dma, 0, dsem, dp["cnt"][i])
            self.n_dma += 1
            self.q[e].append((waits, fn, dsem, 16))
        else:
            s, v = self._sem(e, idx)
            me = (e, idx, s, v)
            self.q[e].append((waits, fn, s, 1))
        for b in reads:
            b.rd.append(me)
        for b in writes:
            b.lw = me
            b.rd = []
        return me

    def all_done_waits(self):
        final = []
        for dp in self.dpool.values():
            final += [(s, c) for s, c in zip(dp["sems"], dp["cnt"]) if c > 0]
        for e in self.ENGS:
            if self.cnt[e] > 0:
                final.append(self._sem(e, self.cnt[e] - 1))
        return final

    def barrier(self):
        w = self.all_done_waits()
        for e in self.ENGS:
            self.pend[e] = list(w)

    def emit(self, last=False):
        nc = self.nc
        prog = self
        final = self.all_done_waits() if last else []
        with nc.Block() as block:
            def run(eng, name):
                for waits, fn, s, inc in prog.q[name]:
                    for (ws, wv) in waits:
                        eng.wait_ge(ws, wv)
                    fn(eng).then_inc(s, inc)
                prog.q[name] = []

            @block.tensor
            def _(eng):
                run(eng, "pe")

            @block.scalar
            def _(eng):
                run(eng, "act")

            @block.vector
            def _(eng):
                run(eng, "dve")

            @block.gpsimd
            def _(eng):
                run(eng, "pool")

            @block.sync
            def _(eng):
                run(eng, "sp")
                for (ws, wv) in final:
                    eng.wait_ge(ws, wv)


def rev_ap(ap2d):
    a = ap2d.ap
    assert len(a) == 2, a
    n = a[1][1]
    st = a[1][0]
    return bass.AP(ap2d.tensor, ap2d.offset + st * (n - 1), [list(a[0]), [-st, n]])


class Builder:
    def __init__(self, Lp, S, dbg=False):
        self.Lp, self.S = Lp, S
        self.Ls = 8 * S
        self.Lo = Lp + S
        self.dbg = dbg
        self.nc = bass.Bass("TRN2", target_bir_lowering=False)

    def dram_in(self, name, shape, dt=F32):
        return self.nc.dram_tensor(name, list(shape), dt, kind="ExternalInput").ap()

    def dram_out(self, name, shape, dt=F32):
        return self.nc.dram_tensor(name, list(shape), dt, kind="ExternalOutput").ap()

    def dram_scr(self, name, shape, dt):
        kind = "ExternalOutput" if (self.dbg and name.split("_")[0] in str(self.dbg)) else "Internal"
        return T(self.nc.dram_tensor(name, list(shape), dt, kind=kind).ap())

    _uid = 0

    def sb(self, st, name, shape, dt):
        Builder._uid += 1
        return T(st.enter_context(self.nc.sbuf_tensor(f"sb{Builder._uid}_{name}", list(shape), dt)))

    def ps(self, st, name, shape, dt):
        Builder._uid += 1
        nbytes = int(np.prod(shape[1:])) * (4 if dt == F32 else 2)
        assert nbytes == 2048, (name, shape)
        t = T(st.enter_context(self.nc.psum_tensor(f"ps{Builder._uid}_{name}", list(shape), dt)))
        t.b.ex = True
        return t

    def load(self, out, in_, r=(), w=()):
        self.P.op("sp", lambda e: e.dma_start(out=out, in_=in_), r, w, dma=True)

    def store(self, out, in_, r=(), w=()):
        self.P.op("pool", lambda e: e.dma_start(out=out, in_=in_), r, w, dma=True)

    def mm(self, out, lhsT, rhs, start, stop, r=(), w=()):
        self.P.op("pe", lambda e: e.matmul(out, lhsT=lhsT, rhs=rhs, start=start, stop=stop), r, w)

    def tr(self, out, in_, ident, r=(), w=()):
        self.P.op("pe", lambda e: e.transpose(out, in_, ident), r, w)

    def act(self, out, in_, func, r=(), w=(), eng="act", **kw):
        self.P.op(eng, lambda e: e.activation(out=out, in_=in_, func=func, **kw), r, w)

    def tt(self, out, in0, in1, op, r=(), w=(), eng="dve"):
        self.P.op(eng, lambda e: e.tensor_tensor(out=out, in0=in0, in1=in1, op=op), r, w)

    def ts(self, out, in0, s1, s2, op0, op1=None, r=(), w=(), eng="dve"):
        if op1 is None:
            self.P.op(eng, lambda e: e.tensor_scalar(out=out, in0=in0, scalar1=s1, scalar2=None, op0=op0), r, w)
        else:
            self.P.op(eng, lambda e: e.tensor_scalar(out=out, in0=in0, scalar1=s1, scalar2=s2, op0=op0, op1=op1), r, w)

    def stt(self, out, in0, scalar, in1, op0, op1, r=(), w=()):
        self.P.op("dve", lambda e: e.scalar_tensor_tensor(out=out, in0=in0, scalar=scalar, in1=in1, op0=op0, op1=op1), r, w)

    def cp(self, out, in_, r=(), w=(), eng="dve"):
        if eng == "act":
            self.P.op("act", lambda e: e.activation(out=out, in_=in_, func=AF.Copy), r, w)
        elif eng == "dve":
            self.P.op("dve", lambda e: e.tensor_scalar(out=out, in0=in_, scalar1=1.0, scalar2=None, op0=ALU.mult), r, w)
        else:
            self.P.op(eng, lambda e: e.tensor_copy(out=out, in_=in_), r, w)

    def recip(self, out, in_, r=(), w=()):
        self.P.op("dve", lambda e: e.reciprocal(out=out, in_=in_), r, w)

    def memset(self, ap, val, r=(), w=(), eng="dve"):
        self.P.op(eng, lambda e: e.memset(ap, val), r, w)

    def rstd(self, v, n, inv_n, r=(), w=()):
        self.ts(v[:, 0:n], v[:, 0:n], inv_n, EPS, ALU.mult, ALU.add, r=list(r) + [v], w=[v])
        self.act(v[:, 0:n], v[:, 0:n], AF.Sqrt, r=[v], w=[v])
        self.recip(v[:, 0:n], v[:, 0:n], r=[v], w=list(w) + [v])

    def build(self):
        nc = self.nc
        Lp, S, Ls, Lo = self.Lp, self.S, self.Ls, self.Lo
        I = {}
        I["xp"] = self.dram_in("xp", [Lp, D])
        I["xs"] = self.dram_in("xs", [Ls, D])
        I["memp"] = self.dram_in("memp", [256, D])
        I["mems"] = self.dram_in("mems", [256, D])
        I["csp"] = self.dram_in("csp", [Lp, 128])
        I["css"] = self.dram_in("css", [Ls, 128])
        I["mf"] = self.dram_in("mf", [1, 7 * S])
        I["mb"] = self.dram_in("mb", [1, 7 * S])
        I["w_in"] = self.dram_in("w_in", [D, IN_W])
        I["w_glu"] = self.dram_in("w_glu", [512, 512])
        I["w_mem_kv"] = self.dram_in("w_mem_kv", [D, 1024])
        I["w_pa"] = self.dram_in("w_pa", [1024, D])
        I["w_ps"] = self.dram_in("w_ps", [512, D])
        I["w_px"] = self.dram_in("w_px", [512, D])
        I["w_out"] = self.dram_in("w_out", [D, D])
        I["g_in"] = self.dram_in("g_in", [128, 8])
        I["g_mem"] = self.dram_in("g_mem", [128, 8])
        I["g_q"] = self.dram_in("g_q", [128, 128])
        I["g_k"] = self.dram_in("g_k", [128, 128])
        I["g_f"] = self.dram_in("g_f", [128, D])
        I["s5d"] = self.dram_in("s5d", [128, 4])
        I["bglu"] = self.dram_in("bglu", [128, 4])
        I["are"] = self.dram_in("are", [128, 64])
        I["aim"] = self.dram_in("aim", [128, 64])
        I["lst"] = self.dram_in("lst", [128, 64])
        I["B1"] = self.dram_in("B1", [64, 128, 128])
        I["B2"] = self.dram_in("B2", [64, 128, 128])
        I["C1"] = self.dram_in("C1", [64, 128, 16])
        I["C2"] = self.dram_in("C2", [64, 128, 16])
        I["ident"] = self.dram_in("ident", [128, 128])
        I["swap"] = self.dram_in("swap", [128, 128])
        I["iota1"] = self.dram_in("iota1", [128, PIECE])
        I["sgn"] = self.dram_in("sgn", [128, 2])
        self.I = I
        self.y_out = self.dram_out("y", [Lo, D])

        self.W = {
            "w_in": self.dram_scr("wb_in", [D, IN_W], BF16),
            "w_glu": self.dram_scr("wb_glu", [512, 512], BF16),
            "w_mem_kv": self.dram_scr("wb_mkv", [D, 1024], BF16),
            "w_pa": self.dram_scr("wb_pa", [1024, D], BF16),
            "w_ps": self.dram_scr("wb_ps", [512, D], BF16),
            "w_px": self.dram_scr("wb_px", [512, D], BF16),
            "w_out": self.dram_scr("wb_out", [D, D], BF16),
        }
        self.KT = {"p": self.dram_scr("KT_p", [2, 128, Lp], BF16),
                   "s": self.dram_scr("KT_s", [2, 128, Ls], BF16)}
        self.VA = {"p": self.dram_scr("VA_p", [2, 128, Lp // 128, 129], BF16),
                   "s": self.dram_scr("VA_s", [2, 128, Ls // 128, 129], BF16)}
        self.UT = {"p": self.dram_scr("UT_p", [512, Lp], BF16),
                   "sf": self.dram_scr("UT_sf", [512, Ls], BF16),
                   "sb": self.dram_scr("UT_sb", [512, Ls], BF16)}
        self.YS = [self.dram_scr("YS_f", [512, Lo], F32), self.dram_scr("YS_b", [512, Lo], F32)]
        self.YAs = self.dram_scr("YA_s", [1024, Lo], BF16)

        with ExitStack() as gst:
            self.P = Prog(nc, gst)
            self.gst = gst
            self.consts(gst)
            phases = [self.phase0, self.phase1, self.phase2, self.phase3]
            stop = getattr(self, "stop", 3)
            for i, ph in enumerate(phases):
                with ExitStack() as st:
                    ph(st)
                    self.P.emit(last=(i == stop))
                if i == stop:
                    break
                self.P.barrier()
        return nc

    def consts(self, st):
        I = self.I
        self.ident_f = self.sb(st, "ident_f", [128, 128], F32)
        self.ident_b = self.sb(st, "ident_b", [128, 128], BF16)
        self.swap_f = self.sb(st, "swap_f", [128, 128], F32)
        self.ones_b = self.sb(st, "ones_b", [128, 128], BF16)
        self.g_in = self.sb(st, "g_in", [128, 8], F32)
        self.g_mem = self.sb(st, "g_mem", [128, 8], F32)
        self.g_q = self.sb(st, "g_q", [128, 128], F32)
        self.g_k = self.sb(st, "g_k", [128, 128], F32)
        self.s5d = self.sb(st, "s5d", [128, 4], F32)
        self.bglu = self.sb(st, "bglu", [128, 4], F32)
        self.sgn = self.sb(st, "sgn", [128, 2], F32)
        self.halfpi = self.sb(st, "halfpi", [128, 1], F32)
        self.KmT = {k: self.sb(st, "KmT" + k, [128, 4, 256], BF16) for k in "ps"}
        self.Vm = {k: self.sb(st, "Vm" + k, [128, 2, 512], BF16) for k in "ps"}
        for t, n in ((self.ident_f, "ident"), (self.swap_f, "swap"), (self.g_in, "g_in"),
                     (self.g_mem, "g_mem"), (self.g_q, "g_q"), (self.g_k, "g_k"),
                     (self.s5d, "s5d"), (self.bglu, "bglu"), (self.sgn, "sgn")):
            self.load(t[:], I[n][:, :], w=[t])
        self.cp(self.ident_b[:], self.ident_f[:], r=[self.ident_f], w=[self.ident_b])
        self.memset(self.ones_b[:], 1.0, w=[self.ones_b])
        self.memset(self.halfpi[:], math.pi / 2.0, w=[self.halfpi])

    def make_hT(self, x_ap_rows, xt, ss, xn, ptr, hT, gain, nblk=4, mask=None):
        self.load(xt[:, 0:nblk, :], x_ap_rows.rearrange("(b p) d -> p b d", p=128), w=[xt])
        for b in range(nblk):
            self.act(xn[:, b, :], xt[:, b, :], AF.Square, r=[xt], w=[xn, ss],
                     accum_out=ss[:, b:b + 1])
        import os
        if os.environ.get("DBG_H") == "1":
            return
        self.rstd(ss, nblk, 1.0 / D)
        if os.environ.get("DBG_H") == "2":
            return
        for b in range(nblk):
            if b % 2 == 0:
                self.act(xn[:, b, :], xt[:, b, :], AF.Copy, r=[xt, ss], w=[xn], scale=ss[:, b:b + 1])
            else:
                self.ts(xn[:, b, :], xt[:, b, :], ss[:, b:b + 1], None, ALU.mult, r=[xt, ss], w=[xn])
        if os.environ.get("DBG_H") == "3":
            return
        for j in range(8):
            pt = ptr[j % len(ptr)]
            for b in range(nblk):
                self.tr(pt[:, b * 128:(b + 1) * 128], xn[:, b, j * 128:(j + 1) * 128], self.ident_b[:],
                        r=[xn, self.ident_b], w=[pt])
            if j % 2 == 0:
                self.ts(hT[:, j, 0:nblk * 128], pt[:, 0:nblk * 128], gain[:, j:j + 1], None, ALU.mult,
                        r=[pt, gain], w=[hT])
            else:
                self.act(hT[:, j, 0:nblk * 128], pt[:, 0:nblk * 128], AF.Copy, r=[pt, gain], w=[hT],
                         scale=gain[:, j:j + 1])

    def phase0(self, st):
        I = self.I
        import os
        if os.environ.get("DBG_P0") == "none":
            return
        stg = [self.sb(st, f"wstg{i}", [128, 2048], F32) for i in range(2)]
        stb = [self.sb(st, f"wstb{i}", [128, 2048], BF16) for i in range(2)]
        k = 0
        for name, rows, cols in (("w_in", D, IN_W), ("w_glu", 512, 512), ("w_mem_kv", D, 1024),
                                 ("w_pa", 1024, D), ("w_ps", 512, D), ("w_px", 512, D), ("w_out", D, D)):
            cw = 1920 if cols == IN_W else cols
            for r0 in range(0, rows, 128):
                for c0 in range(0, cols, cw):
                    a, b = stg[k % 2], stb[k % 2]
                    self.load(a[:, 0:cw], I[name][r0:r0 + 128, c0:c0 + cw], w=[a])
                    self.cp(b[:, 0:cw], a[:, 0:cw], r=[a], w=[b], eng="dve" if k % 2 == 0 else "act")
                    self.store(self.W[name][r0:r0 + 128, c0:c0 + cw], b[:, 0:cw], r=[b], w=[self.W[name]])
                    k += 1
        import os
        if os.environ.get("DBG_P0") == "a":
            return
        wm = self.sb(st, "wm", [128, 8, 1024], BF16)
        self.load(wm[:], self.W["w_mem_kv"].h.rearrange("(j p) c -> p j c", p=128), r=[self.W["w_mem_kv"]], w=[wm])
        xt = self.sb(st, "m_xt", [128, 2, D], F32)
        xn = self.sb(st, "m_xn", [128, 2, D], BF16)
        ss = self.sb(st, "m_ss", [128, 4], F32)
        hT = self.sb(st, "m_hT", [128, 8, 256], BF16)
        vtmp = self.sb(st, "m_v", [128, 512], BF16)
        ptr = [self.ps(st, f"m_ptr{i}", [128, 1024], BF16) for i in range(2)]
        pk = [self.ps(st, f"m_pk{i}", [128, 512], F32) for i in range(2)]
        for key, src in (("p", I["memp"]), ("s", I["mems"])):
            self.make_hT(src[:, :], xt, ss, xn, ptr, hT, self.g_mem, nblk=2)
            if os.environ.get("DBG_P0") == "b1":
                continue
            for hx in range(4):
                p = pk[hx % 2]
                for j in range(8):
                    self.mm(p[:, 0:256], wm[:, j, hx * 128:(hx + 1) * 128], hT[:, j, :], j == 0, j == 7,
                            r=[wm, hT], w=[p])
                if os.environ.get("DBG_P0") == "b2":
                    continue
                self.cp(self.KmT[key][:, hx, :], p[:, 0:256], r=[p], w=[self.KmT[key]], eng="act")
            if os.environ.get("DBG_P0") in ("b2", "b3"):
                continue
            for m in range(2):
                p = pk[m % 2]
                for j in range(8):
                    self.mm(p[:, :], hT[:, j, m * 128:(m + 1) * 128], wm[:, j, 512:1024], j == 0, j == 7,
                            r=[wm, hT], w=[p])
                self.cp(self.Vm[key][:, m, :], p[:, :], r=[p], w=[self.Vm[key]], eng=os.environ.get("DBG_VE", "dve"))

    def rope_tables(self, cs, b, gain, tabs, r_extra=()):
        c = cs[:, b, 0:64]
        s = cs[:, b, 64:128]
        g0 = gain[:, 0:64]
        g1 = gain[:, 64:128]
        rr = [cs, gain] + list(r_extra)
        self.tt(tabs[:, 0, :], c, g0, ALU.mult, r=rr, w=[tabs])
        self.tt(tabs[:, 1, :], s, g1, ALU.mult, r=rr, w=[tabs])
        self.tt(tabs[:, 2, :], s, g0, ALU.mult, r=rr, w=[tabs])
        self.tt(tabs[:, 3, :], c, g1, ALU.mult, r=rr, w=[tabs])

    def norm_rope(self, psrc, nh, sq, ssv, xa, t4, tabs, out_bf, rsrc):
        n = nh * 128
        self.act(sq[:, 0:n], psrc, AF.Square, r=rsrc, w=[sq])
        self.P.op("dve", lambda e: e.tensor_reduce(out=ssv[:, 0:nh], in_=sq[:, 0:n].rearrange("p (h d) -> p h d", h=nh),
                                                   axis=AX.X, op=ALU.add), _bufs([sq]), _bufs([ssv]))
        self.rstd(ssv, nh, 1.0 / 128.0)
        xa3 = xa[:, 0:n].rearrange("p (h d) -> p h d", h=nh)
        self.tt(xa3, psrc.rearrange("p (h d) -> p h d", h=nh),
                ssv[:, 0:nh].unsqueeze(2).to_broadcast([128, nh, 128]), ALU.mult, r=list(rsrc) + [ssv], w=[xa])
        x0 = xa[:, 0:n].rearrange("p (h i two) -> p h i two", h=nh, two=2)[:, :, :, 0]
        x1 = xa[:, 0:n].rearrange("p (h i two) -> p h i two", h=nh, two=2)[:, :, :, 1]
        o0 = out_bf[:, 0:nh, :].rearrange("p h (i two) -> p h i two", two=2)[:, :, :, 0]
        o1 = out_bf[:, 0:nh, :].rearrange("p h (i two) -> p h i two", two=2)[:, :, :, 1]

        def tb(i):
            return tabs[:, i, :].unsqueeze(1).to_broadcast([128, nh, 64])
        tv = [t4[:, i, 0:nh * 64].rearrange("p (h i) -> p h i", h=nh) for i in range(4)]
        self.tt(tv[0], x0, tb(0), ALU.mult, r=[xa, tabs], w=[t4])
        self.tt(tv[1], x1, tb(1), ALU.mult, r=[xa, tabs], w=[t4])
        self.tt(tv[2], x0, tb(2), ALU.mult, r=[xa, tabs], w=[t4])
        self.tt(tv[3], x1, tb(3), ALU.mult, r=[xa, tabs], w=[t4])
        self.tt(o0, tv[0], tv[1], ALU.subtract, r=[t4], w=[out_bf])
        self.tt(o1, tv[2], tv[3], ALU.add, r=[t4], w=[out_bf])

    def phase1(self, st):
        I = self.I
        Lp, S, Ls = self.Lp, self.S, self.Ls
        wkv = self.sb(st, "wkv", [128, 8, 512], BF16)
        wu = self.sb(st, "wu", [128, 8, 512], BF16)
        wv = self.W["w_in"].h.rearrange("(j p) c -> p j c", p=128)
        self.load(wkv[:], wv[:, :, C_K:C_K + 512], r=[self.W["w_in"]], w=[wkv])
        self.load(wu[:], wv[:, :, C_U:C_U + 512], r=[self.W["w_in"]], w=[wu])
        xt = [self.sb(st, f"xt{i}", [128, 4, D], F32) for i in range(2)]
        xn = [self.sb(st, f"xn{i}", [128, 4, D], BF16) for i in range(2)]
        ss = [self.sb(st, f"ss{i}", [128, 4], F32) for i in range(2)]
        hT = [self.sb(st, f"hT{i}", [128, 8, 512], BF16) for i in range(2)]
        cs = [self.sb(st, f"cs{i}", [128, 4, 128], F32) for i in range(2)]
        tabs = [self.sb(st, f"tabs{i}", [128, 4, 64], F32) for i in range(2)]
        sq = self.sb(st, "sq", [128, 256], F32)
        kss = [self.sb(st, f"kss{i}", [128, 2], F32) for i in range(2)]
        ka = self.sb(st, "ka", [128, 256], F32)
        t4 = self.sb(st, "t4", [128, 4, 128], F32)
        krot = [self.sb(st, f"krot{i}", [128, 2, 128], BF16) for i in range(2)]
        KTt = [self.sb(st, f"KTt{i}", [128, 2, 512], BF16) for i in range(2)]
        VAt = [self.sb(st, f"VAt{i}", [128, 2, 4, 129], BF16) for i in range(2)]
        UTt = [self.sb(st, f"UTt{i}", [128, 4, 512], BF16) for i in range(2)]
        UTb = [self.sb(st, f"UTb{i}", [128, 4, 512], BF16) for i in range(2)]
        mrow = [self.sb(st, f"mrow{i}", [128, 2, 512], F32) for i in range(2)]
        ptr = [self.ps(st, f"ptr{i}", [128, 1024], BF16) for i in range(2)]
        pkv = [self.ps(st, f"pkv{i}", [128, 512], F32) for i in range(2)]
        pkt = self.ps(st, "pkt", [128, 2, 512], BF16)
        pu = [self.ps(st, f"pu{i}", [128, 512], F32) for i in range(2)]
        for v in VAt:
            self.memset(v[:, :, :, 128:129], 1.0, w=[v])
        it = 0
        for key, xsrc, cssrc, L in (("p", I["xp"], I["csp"], Lp), ("s", I["xs"], I["css"], Ls)):
            for t in range(L // 512):
                t0 = t * 512
                sl = it % 2
                it += 1
                prefix = (key == "s" and t0 < 7 * S)
                self.make_hT(xsrc[t0:t0 + 512, :], xt[sl], ss[sl], xn[sl], ptr, hT[sl], self.g_in)
                self.load(cs[sl][:], cssrc[t0:t0 + 512, :].rearrange("(b p) c -> p b c", p=128), w=[cs[sl]])
                if prefix:
                    self.load(mrow[sl][:, 0, :], I["mf"][0:1, t0:t0 + 512].partition_broadcast(128), w=[mrow[sl]])
                    self.load(mrow[sl][:, 1, :], I["mb"][0:1, t0:t0 + 512].partition_broadcast(128), w=[mrow[sl]])
                for b in range(4):
                    p = pkv[b % 2]
                    for j in range(8):
                        self.mm(p[:, :], hT[sl][:, j, b * 128:(b + 1) * 128], wkv[:, j, :], j == 0, j == 7,
                                r=[hT[sl], wkv], w=[p])
                    self.cp(VAt[sl][:, :, b, 0:128], p[:, 256:512].rearrange("p (h d) -> p h d", h=2),
                            r=[p], w=[VAt[sl]], eng="act")
                    tb_ = tabs[b % 2]
                    self.rope_tables(cs[sl], b, self.g_k, tb_)
                    kr = krot[b % 2]
                    self.norm_rope(p[:, 0:256], 2, sq, kss[b % 2], ka, t4, tb_, kr, [p])
                    for h in range(2):
                        self.tr(pkt[:, h, b * 128:(b + 1) * 128], kr[:, h, :], self.ident_b[:],
                                r=[kr, self.ident_b], w=[pkt])
                self.cp(KTt[sl][:], pkt[:], r=[pkt], w=[KTt[sl]])
                self.store(self.KT[key].h[:, :, t0:t0 + 512].rearrange("h p l -> p h l"), KTt[sl][:],
                           r=[KTt[sl]], w=[self.KT[key]])
                self.store(self.VA[key].h[:, :, t0 // 128:t0 // 128 + 4, :].rearrange("h p b c -> p h b c"),
                           VAt[sl][:], r=[VAt[sl]], w=[self.VA[key]])
                for i in range(4):
                    p = pu[i % 2]
                    for j in range(8):
                        self.mm(p[:, :], wu[:, j, i * 128:(i + 1) * 128], hT[sl][:, j, :], j == 0, j == 7,
                                r=[hT[sl], wu], w=[p])
                    if prefix:
                        self.tt(UTt[sl][:, i, :], p[:, :], mrow[sl][:, 0, :], ALU.mult, r=[p, mrow[sl]], w=[UTt[sl]])
                        self.tt(UTb[sl][:, i, :], p[:, :], mrow[sl][:, 1, :], ALU.mult, r=[p, mrow[sl]], w=[UTb[sl]])
                    else:
                        self.cp(UTt[sl][:, i, :], p[:, :], r=[p], w=[UTt[sl]], eng="act" if i % 2 else "dve")
                if key == "p":
                    self.store(self.UT["p"].h[:, t0:t0 + 512].rearrange("(i p) l -> p i l", p=128), UTt[sl][:],
                               r=[UTt[sl]], w=[self.UT["p"]])
                else:
                    self.store(self.UT["sf"].h[:, t0:t0 + 512].rearrange("(i p) l -> p i l", p=128), UTt[sl][:],
                               r=[UTt[sl]], w=[self.UT["sf"]])
                    self.store(self.UT["sb"].h[:, t0:t0 + 512].rearrange("(i p) l -> p i l", p=128),
                               (UTb if prefix else UTt)[sl][:], r=[(UTb if prefix else UTt)[sl]], w=[self.UT["sb"]])

    def phase2(self, st):
        banks = [self.ps(st, f"cb{i}", [128, 512], F32) for i in range(8)]
        ga = self.gen_s5(st, banks[0:4])
        gb = self.gen_att(st, banks[4:8])
        ta = tb = 0.0
        da = db = False
        import os
        if os.environ.get("DBG_NOATT"):
            db = True
        while not (da and db):
            if not da and (db or ta <= tb):
                try:
                    ta += next(ga)
                except StopIteration:
                    da = True
            else:
                try:
                    tb += next(gb)
                except StopIteration:
                    db = True

    def gen_s5(self, st, banks):
        I = self.I
        Lp, S, Ls = self.Lp, self.S, self.Ls
        def gt(name):
            return self.sb(st, name, [128, 64], F32)
        are, aim, lst = gt("are"), gt("aim"), gt("lst")
        for t, n in ((are, "are"), (aim, "aim"), (lst, "lst")):
            self.load(t[:], I[n][:, :], w=[t])
        step, Rv, th, kk, thr, sn, cs_, ab = gt("step"), gt("Rv"), gt("th"), gt("kk"), gt("thr"), gt("sn"), gt("cs_"), gt("ab")
        nr, ni, den, CR, CI, SCI, NSCR, tmp = gt("nr"), gt("ni"), gt("den"), gt("CR"), gt("CI"), gt("SCI"), gt("NSCR"), gt("tmp")
        self.act(step[:], lst[:], AF.Exp, r=[lst], w=[step])
        self.ts(are[:], are[:], -1e-4, None, ALU.min, r=[are], w=[are])
        self.tt(Rv[:], are[:], step[:], ALU.mult, r=[are, step], w=[Rv])
        self.act(Rv[:], Rv[:], AF.Exp, r=[Rv], w=[Rv])
        self.tt(th[:], aim[:], step[:], ALU.mult, r=[aim, step], w=[th])
        self.reduce_angle(th, kk, thr)
        self.sincos(thr, ab, sn, cs_)
        self.tt(nr[:], Rv[:], cs_[:], ALU.mult, r=[Rv, cs_], w=[nr])
        self.ts(nr[:], nr[:], -1.0, None, ALU.add, r=[nr], w=[nr])
        self.tt(ni[:], Rv[:], sn[:], ALU.mult, r=[Rv, sn], w=[ni])
        self.tt(den[:], are[:], are[:], ALU.mult, r=[are], w=[den])
        self.tt(tmp[:], aim[:], aim[:], ALU.mult, r=[aim], w=[tmp])
        self.tt(den[:], den[:], tmp[:], ALU.add, r=[den, tmp], w=[den])
        self.recip(den[:], den[:], r=[den], w=[den])
        self.tt(CR[:], nr[:], are[:], ALU.mult, r=[nr, are], w=[CR])
        self.tt(tmp[:], ni[:], aim[:], ALU.mult, r=[ni, aim], w=[tmp])
        self.tt(CR[:], CR[:], tmp[:], ALU.add, r=[CR, tmp], w=[CR])
        self.tt(CR[:], CR[:], den[:], ALU.mult, r=[CR, den], w=[CR])
        self.tt(CI[:], ni[:], are[:], ALU.mult, r=[ni, are], w=[CI])
        self.tt(tmp[:], nr[:], aim[:], ALU.mult, r=[nr, aim], w=[tmp])
        self.tt(CI[:], CI[:], tmp[:], ALU.subtract, r=[CI, tmp], w=[CI])
        self.tt(CI[:], CI[:], den[:], ALU.mult, r=[CI, den], w=[CI])
        self.ts(SCI[:], CI[:], self.sgn[:, 0:1], None, ALU.mult, r=[CI, self.sgn], w=[SCI])
        self.ts(NSCR[:], CR[:], self.sgn[:, 1:2], None, ALU.mult, r=[CR, self.sgn], w=[NSCR])

        iota1 = self.sb(st, "iota1", [128, PIECE], F32)
        self.load(iota1[:], I["iota1"][:, :], w=[iota1])
        ones_f = self.sb(st, "ones_f", [128, TS], F32)
        self.memset(ones_f[:], 1.0, w=[ones_f])

        yield 20.0
        UCH = 1024

        class Stream:
            pass
        strs = []
        for d in range(2):
            s_ = Stream()
            n = f"s{d}_"
            s_.phi = self.sb(st, n + "phi", [128, PIECE], F32)
            s_.k2 = self.sb(st, n + "k2", [128, PIECE], F32)
            s_.sinp = self.sb(st, n + "sinp", [128, PIECE], F32)
            s_.cosp = self.sb(st, n + "cosp", [128, PIECE], F32)
            s_.TA = self.sb(st, n + "TA", [128, PIECE], F32)
            s_.TB = self.sb(st, n + "TB", [128, PIECE], F32)
            s_.RC = self.sb(st, n + "RC", [128, PIECE], F32)
            s_.RS = self.sb(st, n + "RS", [128, PIECE], F32)
            s_.Rd = self.sb(st, n + "Rd", [128, TS], F32)
            s_.rot = self.sb(st, n + "rot", [128, 128], F32)
            s_.rb = self.sb(st, n + "rb", [128, 1], F32)
            s_.Bf = self.sb(st, n + "Bf", [128, 2, 128], F32)
            s_.Bb = self.sb(st, n + "Bb", [128, 2, 128], BF16)
            s_.Cf = self.sb(st, n + "Cf", [128, 2, 16], F32)
            s_.Cb = self.sb(st, n + "Cb", [128, 2, 16], BF16)
            s_.t2 = [self.sb(st, n + f"t2{i}", [128, TS], F32) for i in range(2)]
            s_.dd = [self.sb(st, n + f"dd{i}", [128, TS], F32) for i in range(2)]
            s_.e1 = [self.sb(st, n + f"e1{i}", [128, TS], BF16) for i in range(2)]
            s_.e2 = [self.sb(st, n + f"e2{i}", [128, TS], BF16) for i in range(2)]
            s_.wl = self.sb(st, n + "wl", [128, 1], F32)
            s_.carry = self.sb(st, n + "carry", [128, 1], F32)
            s_.ystg = [self.sb(st, n + f"ystg{i}", [16, 512], F32) for i in range(2)]
            s_.ub = [self.sb(st, n + f"ub{i}", [128, UCH], BF16) for i in range(2)]
            s_.ucnt = 0
            s_.ucur = None
            s_.uch = UCH
            s_.pp = [banks[0], banks[1]]
            s_.pw = banks[2]
            s_.py = banks[3]
            s_.prot = T(banks[3].h[:, TS:TS + 2], banks[3].b)
            s_.nchunk = 0
            s_.nstg = 0
            strs.append(s_)
        Lp, S, Ls = self.Lp, self.S, self.Ls
        nset = 0
        for j in range(4):
            for gl in range(8):
                g = 8 * j + gl
                for d in range(2):
                    s_ = strs[nset % 2]
                    nset += 1
                    s_.ucur = None
                    col = d * 32 + g
                    self.s5_prep(s_, col, iota1, ones_f, Rv, thr, CR, CI, SCI, NSCR)
                    yield 6.0
                    items = []

                    pcnt = [0]

                    def add(src, t0, rev, ypos, first=False, last=False):
                        if first:
                            pcnt[0] = 0
                        items.append(dict(s=s_, d=d, g=g, src=src, j=j, t0=t0, rev=rev, ypos=ypos,
                                          first=first, last=last, p=pcnt[0] % NPC))
                        pcnt[0] += 1
                    npc, nsc = Lp // TS, Ls // TS
                    if d == 0:
                        for i in range(npc):
                            add(self.UT["p"], i * TS, False, i * TS, i == 0, i == npc - 1)
                        for i in range(nsc):
                            t0 = i * TS
                            add(self.UT["sf"], t0, False, (Lp + t0 - 7 * S) if t0 >= 7 * S else None,
                                i == 0, i == nsc - 1)
                    else:
                        for i in range(npc):
                            t0 = Lp - (i + 1) * TS
                            add(self.UT["p"], t0, True, t0, i == 0, i == npc - 1)
                        npre = 7 * S // TS
                        for i in range(npre):
                            add(self.UT["sb"], 7 * S - (i + 1) * TS, True, None, i == 0, False)
                        for i in range(S // TS):
                            t0 = Ls - (i + 1) * TS
                            add(self.UT["sb"], t0, True, Lp + t0 - 7 * S, False, i == S // TS - 1)
                    ni = len(items)
                    for c_, it_ in enumerate(items):
                        it_["h"] = c_ % 2
                    self.s5_M(items[0])
                    self.s5_A(items[0])
                    for c in range(ni + 2):
                        if c + 1 < ni:
                            self.s5_M(items[c + 1])
                        if 0 <= c - 2 < ni:
                            self.s5_Yev(items[c - 2])
                        if 0 <= c - 1 < ni:
                            self.s5_Ymm(items[c - 1])
                        if c < ni:
                            self.s5_S(items[c], items[c - 1] if c > 0 else None)
                            self.s5_E(items[c])
                            if c + 1 < ni:
                                self.s5_A(items[c + 1])
                            yield 2.05 + (0.85 if items[c]["ypos"] is not None else 0.0)

    def gen_att(self, st, bk):
        I = self.I
        Lp, S, Ls = self.Lp, self.S, self.Ls
        SC = 128.0 ** -0.5
        wv = self.W["w_in"].h.rearrange("(j p) c -> p j c", p=128)
        ws = [self.sb(st, f"aws{i}", [128, 8, 512], BF16) for i in range(2)]
        wsi = [0]

        def wload(src_ap, rd_):
            t = ws[wsi[0] % 2]
            wsi[0] += 1
            self.load(t[:], src_ap, r=[rd_], w=[t])
            return t
        xt = [self.sb(st, f"axt{i}", [128, D], F32) for i in range(2)]
        xn = self.sb(st, "axn", [128, 4, D], BF16)
        ss = self.sb(st, "ass", [128, 4], F32)
        hT = self.sb(st, "ahT", [128, 8, 512], BF16)
        cs = self.sb(st, "acs", [128, 4, 128], F32)
        tabs = [self.sb(st, f"atabs{i}", [128, 4, 64], F32) for i in range(2)]
        sq = self.sb(st, "asq", [128, 512], F32)
        qss = [self.sb(st, f"aqss{i}", [128, 8], F32) for i in range(2)]
        qa = self.sb(st, "aqa", [128, 512], F32)
        t4 = self.sb(st, "at4", [128, 4, 256], F32)
        qrot = [self.sb(st, f"aqrot{i}", [128, 8, 128], BF16) for i in range(2)]
        QT = self.sb(st, "aQT", [128, 8, 512], BF16)
        GA = self.sb(st, "aGA", [128, 8, 512], BF16)
        YAh = [self.sb(st, f"aYAh{i}", [128, 512], BF16) for i in range(2)]
        KC = 512
        kts = [self.sb(st, f"akts{i}", [128, KC], BF16) for i in range(3)]
        vas = [self.sb(st, f"avas{i}", [128, KC // 128, 129], BF16) for i in range(3)]
        PT = [self.sb(st, f"aPT{i}", [128, 512], BF16) for i in range(3)]
        rd = self.sb(st, "ard", [128, 512], F32)
        yx = self.sb(st, "ayx", [128, 512], F32)

        def bf(b):
            return b[:].bitcast(BF16)
        seqs = [("p", I["xp"], I["csp"], 0, Lp, Lp, 0), ("s", I["xs"], I["css"], 7 * S, S, Ls, Lp)]
        nslot = 0
        for key, xsrc, cssrc, xoff, nown, Lk, yoff in seqs:
            for t in range(nown // 512):
                t0 = xoff + t * 512
                yo = yoff + t * 512
                self.make_hT_bank(xsrc[t0:t0 + 512, :], xt, ss, xn, [bk[2], bk[3]], hT, self.g_in)
                self.load(cs[:], cssrc[t0:t0 + 512, :].rearrange("(b p) c -> p b c", p=128), w=[cs])
                yield 12.0
                wq0 = wload(wv[:, :, C_Q:C_Q + 512], self.W["w_in"])
                wq1 = wload(wv[:, :, C_Q + 512:C_Q + 1024], self.W["w_in"])
                for b in range(4):
                    pa, pb, pT = bk[0], bk[1], bk[2 + b % 2]
                    for j in range(8):
                        self.mm(pa[:, :], hT[:, j, b * 128:(b + 1) * 128], wq0[:, j, :], j == 0, j == 7, r=[hT, wq0], w=[pa])
                    for j in range(8):
                        self.mm(pb[:, :], hT[:, j, b * 128:(b + 1) * 128], wq1[:, j, :], j == 0, j == 7, r=[hT, wq1], w=[pb])
                    tb_ = tabs[b % 2]
                    self.rope_tables(cs, b, self.g_q, tb_)
                    qr = qrot[b % 2]
                    for half, pbank in ((0, pa), (1, pb)):
                        self.norm_rope_q(pbank, half, sq, qss[b % 2], qa, t4, tb_, qr)
                    for h in range(8):
                        self.tr(bf(pT)[:, h * 128:(h + 1) * 128], qr[:, h, :], self.ident_b[:],
                                r=[qr, self.ident_b], w=[pT])
                    self.cp(QT[:, :, b * 128:(b + 1) * 128], bf(pT).rearrange("p (h q) -> p h q", h=8),
                            r=[pT], w=[QT], eng="act")
                    yield 5.0
                for half in range(2):
                    wg = wload(wv[:, :, C_GA + half * 512:C_GA + (half + 1) * 512], self.W["w_in"])
                    for o in range(4):
                        p = bk[o % 2]
                        for j in range(8):
                            self.mm(p[:, :], wg[:, j, o * 128:(o + 1) * 128], hT[:, j, :], j == 0, j == 7, r=[wg, hT], w=[p])
                        self.act(GA[:, half * 4 + o, :], p[:, :], AF.Silu, r=[p], w=[GA])
                    yield 8.0
                nkc = Lk // KC
                nk = Lk // 128
                kpc = KC // 128
                for h in range(8):
                    hk = h // 4
                    pO, pD = bk[2], bk[3]
                    cur = {}
                    for idx in range(nk + 1):
                        if idx < nk:
                            c, kk = idx // kpc, idx % kpc
                            if kk == 0:
                                sl = nslot % 3
                                nslot += 1
                                kt_, va_ = kts[sl], vas[sl]
                                self.load(kt_[:], self.KT[key].h[hk, :, c * KC:(c + 1) * KC], r=[self.KT[key]], w=[kt_])
                                self.load(va_[:], self.VA[key].h[hk, :, c * kpc:(c + 1) * kpc, :],
                                          r=[self.VA[key]], w=[va_])
                                cur[c] = (kt_, va_)
                            kt_, va_ = cur[c]
                            psb = bk[idx % 2]
                            pt = PT[idx % 3]
                            self.mm(psb[:, :], kt_[:, kk * 128:(kk + 1) * 128], QT[:, h, :], True, True,
                                    r=[kt_, QT], w=[psb])
                            self.act(pt[:], psb[:, :], AF.Exp, r=[psb], w=[pt], scale=SC)
                        if idx >= 1:
                            jx = idx - 1
                            c, kk = jx // kpc, jx % kpc
                            kt_, va_ = cur[c]
                            pt = PT[jx % 3]
                            self.mm(pO[:, :], va_[:, kk, 0:128], pt[:], jx == 0, jx == nk - 1, r=[va_, pt], w=[pO])
                            self.mm(pD[:, :], self.ones_b[:], pt[:], jx == 0, jx == nk - 1, r=[self.ones_b, pt], w=[pD])
                        yield 1.15
                    self.act(rd[:], pD[:, :], AF.Ln, r=[pD], w=[rd])
                    self.act(rd[:], rd[:], AF.Exp, r=[rd], w=[rd], scale=-1.0)
                    self.tt(yx[:], pO[:, :], rd[:], ALU.mult, r=[pO, rd], w=[yx])
                    ya = YAh[h % 2]
                    self.tt(ya[:], yx[:], GA[:, h, :], ALU.mult, r=[yx, GA], w=[ya])
                    self.store(self.YAs.h[h * 128:(h + 1) * 128, yo:yo + 512], ya[:], r=[ya], w=[self.YAs])
                    yield 1.0

    def reduce_angle(self, th, kk, thr):
        self.ts(kk[:], th[:], 1.0 / TWO_PI, None, ALU.mult, r=[th], w=[kk])
        self.ts(kk[:], kk[:], MAGIC, None, ALU.add, r=[kk], w=[kk])
        self.ts(kk[:], kk[:], -MAGIC, None, ALU.add, r=[kk], w=[kk])
        self.stt(thr[:], kk[:], -CW1, th[:], ALU.mult, ALU.add, r=[kk, th], w=[thr])
        self.stt(thr[:], kk[:], -CW2, thr[:], ALU.mult, ALU.add, r=[kk, thr], w=[thr])
        self.ts(thr[:], thr[:], math.pi, -math.pi, ALU.min, ALU.max, r=[thr], w=[thr])

    def sincos(self, thr, ab, sn, cs_):
        self.act(sn[:], thr[:], AF.Sin, r=[thr], w=[sn])
        self.act(ab[:], thr[:], AF.Sin, r=[thr], w=[ab], scale=0.5)
        self.tt(cs_[:], ab[:], ab[:], ALU.mult, r=[ab], w=[cs_])
        self.ts(cs_[:], cs_[:], -2.0, 1.0, ALU.mult, ALU.add, r=[cs_], w=[cs_])

    def s5_prep(self, s_, col, iota1, ones_f, Rv, thr, CR, CI, SCI, NSCR):
        I = self.I
        c1 = slice(col, col + 1)
        self.load(s_.Bf[:, 0, :], I["B1"][col, :, :], w=[s_.Bf])
        self.load(s_.Bf[:, 1, :], I["B2"][col, :, :], w=[s_.Bf])
        self.load(s_.Cf[:, 0, :], I["C1"][col, :, :], w=[s_.Cf])
        self.load(s_.Cf[:, 1, :], I["C2"][col, :, :], w=[s_.Cf])
        self.cp(s_.Bb[:], s_.Bf[:], r=[s_.Bf], w=[s_.Bb], eng="act")
        self.cp(s_.Cb[:], s_.Cf[:], r=[s_.Cf], w=[s_.Cb], eng="act")
        self.ts(s_.phi[:], iota1[:], thr[:, c1], None, ALU.mult, r=[iota1, thr], w=[s_.phi])
        self.reduce_angle(s_.phi, s_.k2, s_.phi)
        self.sincos(s_.phi, s_.k2, s_.sinp, s_.cosp)
        self.ts(s_.TA[:], s_.cosp[:], CR[:, c1], None, ALU.mult, r=[s_.cosp, CR], w=[s_.TA])
        self.stt(s_.TA[:], s_.sinp[:], CI[:, c1], s_.TA[:], ALU.mult, ALU.add, r=[s_.sinp, CI, s_.TA], w=[s_.TA])
        self.ts(s_.TB[:], s_.cosp[:], SCI[:, c1], None, ALU.mult, r=[s_.cosp, SCI], w=[s_.TB])
        self.stt(s_.TB[:], s_.sinp[:], NSCR[:, c1], s_.TB[:], ALU.mult, ALU.add, r=[s_.sinp, NSCR, s_.TB], w=[s_.TB])
        self.ts(s_.RC[:], s_.cosp[:], self.sgn[:, 1:2], None, ALU.mult, r=[s_.cosp, self.sgn], w=[s_.RC])
        self.act(s_.RS[:], s_.sinp[:], AF.Copy, r=[s_.sinp], w=[s_.RS], scale=-1.0)
        self.act(s_.Rd[:], ones_f[:], AF.Copy, r=[ones_f, Rv], w=[s_.Rd], scale=Rv[:, c1])
        self.ts(s_.rb[:], s_.sinp[:, PIECE - 1:PIECE], self.sgn[:, 1:2], None, ALU.mult, r=[s_.sinp, self.sgn], w=[s_.rb])
        self.ts(s_.rot[:], self.ident_f[:], s_.cosp[:, PIECE - 1:PIECE], None, ALU.mult, r=[self.ident_f, s_.cosp], w=[s_.rot])
        self.stt(s_.rot[:], self.swap_f[:], s_.rb[:, 0:1], s_.rot[:], ALU.mult, ALU.add,
                 r=[self.swap_f, s_.rb, s_.rot], w=[s_.rot])

    def s5_M(self, it):
        s_ = it["s"]
        k = s_.nchunk % 2
        s_.nchunk += 1
        it["k"] = k
        pp = s_.pp[k]
        src, jj, t0 = it["src"], it["j"], it["t0"]
        piece = t0 // s_.uch
        ukey = (id(src), jj, piece)
        if s_.ucur != ukey:
            ub = s_.ub[s_.ucnt % 2]
            s_.ucnt += 1
            self.load(ub[:], src.h[jj * 128:(jj + 1) * 128, piece * s_.uch:(piece + 1) * s_.uch], r=[src], w=[ub])
            s_.ucur = ukey
            s_.ubcur = ub
        ut = s_.ubcur
        off = t0 - piece * s_.uch
        rhs = ut[:, off:off + TS]
        if it["rev"]:
            rhs = rev_ap(rhs)
        self.mm(pp[:, 0:TS], s_.Bb[:, 0, :], rhs, True, True, r=[s_.Bb, ut], w=[pp])
        self.mm(pp[:, TS:2 * TS], s_.Bb[:, 1, :], rhs, True, True, r=[s_.Bb, ut], w=[pp])

    def s5_A(self, it):
        s_ = it["s"]
        k = it["k"]
        pp, t2, dd = s_.pp[k], s_.t2[k], s_.dd[k]
        ps_ = slice(it["p"] * TS, (it["p"] + 1) * TS)
        self.tt(pp[:, 0:TS], pp[:, 0:TS], s_.TA[:, ps_], ALU.mult, r=[pp, s_.TA], w=[pp])
        self.tt(t2[:], pp[:, TS:2 * TS], s_.TB[:, ps_], ALU.mult, r=[pp, s_.TB], w=[t2])
        self.tt(dd[:], pp[:, 0:TS], t2[:], ALU.add, r=[pp, t2], w=[dd])

    def s5_S(self, it, prev):
        s_ = it["s"]
        k = it["k"]
        dd = s_.dd[k]
        h = it["h"]
        out = s_.pw[:, h * TS:(h + 1) * TS]
        rds = [s_.Rd, dd]
        if it["first"]:
            init = 0.0
        elif prev["p"] == NPC - 1:
            init = s_.prot[:, 0:1]
            rds.append(s_.py)
        else:
            ph = prev["h"]
            init = s_.pw[:, ph * TS + TS - 1:ph * TS + TS]
        self.P.op("dve", lambda e: e.tensor_tensor_scan(out=out, data0=s_.Rd[:], data1=dd[:],
                                                        initial=init, op0=ALU.mult, op1=ALU.add),
                  _bufs(rds), _bufs([s_.pw]))
        if not it["last"] and it["p"] == NPC - 1:
            self.ts(s_.wl[:], s_.pw[:, h * TS + TS - 1:h * TS + TS], 1.0, None, ALU.mult, r=[s_.pw], w=[s_.wl])
            self.mm(s_.prot[:, 0:1], s_.rot[:], s_.wl[:], True, True, r=[s_.rot, s_.wl], w=[s_.py])

    def s5_E(self, it):
        s_ = it["s"]
        k = it["k"]
        if it["ypos"] is None:
            return
        h = it["h"]
        e1, e2 = s_.e1[k], s_.e2[k]
        ps_ = slice(it["p"] * TS, (it["p"] + 1) * TS)
        self.tt(e1[:], s_.pw[:, h * TS:(h + 1) * TS], s_.RC[:, ps_], ALU.mult, r=[s_.pw, s_.RC], w=[e1])
        self.tt(e2[:], s_.pw[:, h * TS:(h + 1) * TS], s_.RS[:, ps_], ALU.mult, r=[s_.pw, s_.RS], w=[e2])

    def s5_Ymm(self, it):
        s_ = it["s"]
        k = it["k"]
        if it["ypos"] is None:
            return
        e1, e2 = s_.e1[k], s_.e2[k]
        py = s_.py[0:16, 0:TS]
        self.mm(py, s_.Cb[:, 0, :], e1[:], True, False, r=[s_.Cb, e1], w=[s_.py])
        self.mm(py, s_.Cb[:, 1, :], e2[:], False, True, r=[s_.Cb, e2], w=[s_.py])

    def s5_Yev(self, it):
        s_ = it["s"]
        ypos, rev, d, g = it["ypos"], it["rev"], it["d"], it["g"]
        if ypos is None:
            return
        py = s_.py[0:16, 0:TS]
        nper = 512 // TS
        sidx = s_.nstg // nper
        stg = s_.ystg[sidx % 2]
        q = s_.nstg % nper
        s_.nstg += 1
        base = (ypos // 512) * 512
        off = ypos - base
        dst = stg[0:16, off:off + TS]
        if rev:
            dst = rev_ap(dst)
        self.cp(dst, py, r=[s_.py], w=[stg], eng="act")
        if q == nper - 1:
            self.store(self.YS[d].h[g * 16:(g + 1) * 16, base:base + 512], stg[0:16, :], r=[stg], w=[self.YS[d]])

    def phase3(self, st):
        I = self.I
        Lp, S, Ls = self.Lp, self.S, self.Ls
        SC = 128.0 ** -0.5
        wv = self.W["w_in"].h.rearrange("(j p) c -> p j c", p=128)
        ws = [self.sb(st, f"ws{i}", [128, 8, 512], BF16) for i in range(3)]
        self.wsi = 0

        def wload(src_ap, rd):
            t = ws[self.wsi % 3]
            self.wsi += 1
            self.load(t[:], src_ap, r=[rd], w=[t])
            return t

        xt = [self.sb(st, f"xt{i}", [128, D], F32) for i in range(2)]
        xn = self.sb(st, "xn", [128, 4, D], BF16)
        ss = self.sb(st, "ss", [128, 4], F32)
        hT = self.sb(st, "hT", [128, 8, 512], BF16)
        cs = self.sb(st, "cs", [128, 4, 128], F32)
        tabs = [self.sb(st, f"tabs{i}", [128, 4, 64], F32) for i in range(2)]
        sq = self.sb(st, "sq", [128, 512], F32)
        qss = [self.sb(st, f"qss{i}", [128, 8], F32) for i in range(2)]
        qa = self.sb(st, "qa", [128, 512], F32)
        t4 = self.sb(st, "t4", [128, 4, 256], F32)
        qrot = [self.sb(st, f"qrot{i}", [128, 8, 128], BF16) for i in range(2)]
        QT = self.sb(st, "QT", [128, 8, 512], BF16)
        GA = self.sb(st, "GA", [128, 8, 512], BF16)
        YA = self.sb(st, "YA", [128, 8, 512], BF16)
        KC = 512
        kts = [self.sb(st, f"kts{i}", [128, KC], BF16) for i in range(3)]
        vas = [self.sb(st, f"vas{i}", [128, KC // 128, 129], BF16) for i in range(3)]
        PT = [self.sb(st, f"PT{i}", [128, 512], BF16) for i in range(3)]
        rcp = self.sb(st, "rcp", [128, 4], F32)
        yn = [self.sb(st, f"yn{i}", [128, 128], BF16) for i in range(2)]
        y0 = [self.sb(st, f"y0_{i}", [128, 512], F32) for i in range(1)] * 2
        y1 = [self.sb(st, f"y1_{i}", [128, 512], F32) for i in range(1)] * 2
        uu = [self.sb(st, f"uu{i}", [128, 512], BF16) for i in range(2)]
        gx = [self.sb(st, f"gx{i}", [128, 512], F32) for i in range(2)]
        g2 = [self.sb(st, f"g2{i}", [128, 512], F32) for i in range(2)]
        YG = self.sb(st, "YG", [128, 4, 512], F32)
        YGb = self.sb(st, "YGb", [128, 4, 512], BF16)
        GS = self.sb(st, "GS", [128, 4, 512], BF16)
        sgl = [self.sb(st, f"sgl{i}", [128, 512], F32) for i in range(2)]
        YSb = self.sb(st, "YSb", [128, 4, 512], BF16)
        wglu = self.sb(st, "wglu", [128, 4, 512], BF16)
        self.load(wglu[:], self.W["w_glu"].h.rearrange("(k p) c -> p k c", p=128), r=[self.W["w_glu"]], w=[wglu])
        QX = self.sb(st, "QX", [128, 4, 512], BF16)
        GX = self.sb(st, "GX", [128, 4, 512], BF16)
        PX = [self.sb(st, f"PX{i}", [128, 512], BF16) for i in range(2)]
        rd = self.sb(st, "rd", [128, 512], F32)
        yx = self.sb(st, "yx", [128, 512], F32)
        YX = self.sb(st, "YX", [128, 4, 512], BF16)
        G3 = self.sb(st, "G3", [128, 3, 4, 512], BF16)
        m = [self.sb(st, f"m{i}", [128, 512], F32) for i in range(3)]
        M = self.sb(st, "M", [128, 8, 512], BF16)
        yres = [self.sb(st, f"yres{i}", [128, D], F32) for i in range(1)] * 2
        fss = [self.sb(st, f"fss{i}", [128, 1], F32) for i in range(2)]
        gf = self.sb(st, "gf", [128, D], F32)
        self.load(gf[:], I["g_f"][:, :], w=[gf])
        bk = [self.ps(st, f"bk{i}", [128, 512], F32) for i in range(8)]

        def bf(b):
            return b[:].bitcast(BF16)

        seqs = [("p", I["xp"], I["csp"], 0, Lp, Lp, 0), ("s", I["xs"], I["css"], 7 * S, S, Ls, Lp)]
        for key, xsrc, cssrc, xoff, nown, Lk, yoff in seqs:
            for t in range(nown // 512):
                t0 = xoff + t * 512
                yo = yoff + t * 512
                self.make_hT_bank(xsrc[t0:t0 + 512, :], xt, ss, xn, [bk[6], bk[7]], hT, self.g_in)
                self.load(cs[:], cssrc[t0:t0 + 512, :].rearrange("(b p) c -> p b c", p=128), w=[cs])
                self.load(YA[:], self.YAs.h[:, yo:yo + 512].rearrange("(h p) l -> p h l", p=128), r=[self.YAs], w=[YA])
                wgs = wload(wv[:, :, C_GS:C_GS + 512], self.W["w_in"])
                for i in range(4):
                    k2 = i % 2
                    self.load(y0[k2][:], self.YS[0].h[i * 128:(i + 1) * 128, yo:yo + 512], r=[self.YS[0]], w=[y0[k2]])
                    self.load(y1[k2][:], self.YS[1].h[i * 128:(i + 1) * 128, yo:yo + 512], r=[self.YS[1]], w=[y1[k2]])
                    usrc = self.UT["p"] if key == "p" else self.UT["sf"]
                    self.load(uu[k2][:], usrc.h[i * 128:(i + 1) * 128, t0:t0 + 512], r=[usrc], w=[uu[k2]])
                    a, bq = gx[k2], g2[k2]
                    self.tt(a[:], y0[k2][:], y1[k2][:], ALU.add, r=[y0[k2], y1[k2]], w=[a])
                    self.stt(a[:], uu[k2][:], self.s5d[:, i:i + 1], a[:], ALU.mult, ALU.add, r=[uu[k2], self.s5d, a], w=[a])
                    self.tt(bq[:], a[:], a[:], ALU.mult, r=[a], w=[bq])
                    self.ts(bq[:], bq[:], 0.044715, 1.0, ALU.mult, ALU.add, r=[bq], w=[bq])
                    self.tt(bq[:], bq[:], a[:], ALU.mult, r=[bq, a], w=[bq])
                    self.act(bq[:], bq[:], AF.Sigmoid, r=[bq], w=[bq], scale=2.0 * math.sqrt(2.0 / math.pi))
                    self.tt(YG[:, i, :], a[:], bq[:], ALU.mult, r=[a, bq], w=[YG])
                    self.cp(YGb[:, i, :], YG[:, i, :], r=[YG], w=[YGb], eng="act")
                    p = bk[i % 2]
                    for j in range(8):
                        self.mm(p[:, :], wgs[:, j, i * 128:(i + 1) * 128], hT[:, j, :], j == 0, j == 7, r=[wgs, hT], w=[p])
                    self.act(GS[:, i, :], p[:, :], AF.Silu, r=[p], w=[GS])
                for o in range(4):
                    p = bk[2 + o % 2]
                    for k_ in range(4):
                        self.mm(p[:, :], wglu[:, k_, o * 128:(o + 1) * 128], YGb[:, k_, :], k_ == 0, k_ == 3,
                                r=[wglu, YGb], w=[p])
                    s_ = sgl[o % 2]
                    self.act(s_[:], p[:, :], AF.Sigmoid, r=[p, self.bglu], w=[s_], bias=self.bglu[:, o:o + 1])
                    self.tt(s_[:], s_[:], YG[:, o, :], ALU.mult, r=[s_, YG], w=[s_])
                    self.tt(YSb[:, o, :], s_[:], GS[:, o, :], ALU.mult, r=[s_, GS], w=[YSb])
                wqx = wload(wv[:, :, C_QX:C_QX + 512], self.W["w_in"])
                wgx = wload(wv[:, :, C_GX:C_GX + 512], self.W["w_in"])
                for o in range(4):
                    p = bk[o % 2]
                    for j in range(8):
                        self.mm(p[:, :], wqx[:, j, o * 128:(o + 1) * 128], hT[:, j, :], j == 0, j == 7, r=[wqx, hT], w=[p])
                    self.cp(QX[:, o, :], p[:, :], r=[p], w=[QX], eng="act")
                    p2 = bk[2 + o % 2]
                    for j in range(8):
                        self.mm(p2[:, :], wgx[:, j, o * 128:(o + 1) * 128], hT[:, j, :], j == 0, j == 7, r=[wgx, hT], w=[p2])
                    self.act(GX[:, o, :], p2[:, :], AF.Silu, r=[p2], w=[GX])
                KmT, Vm = self.KmT[key], self.Vm[key]
                for hx in range(4):
                    for mt in range(2):
                        p = bk[mt]
                        self.mm(p[:, :], KmT[:, hx, mt * 128:(mt + 1) * 128], QX[:, hx, :], True, True, r=[KmT, QX], w=[p])
                        self.act(PX[mt][:], p[:, :], AF.Exp, r=[p], w=[PX[mt]], scale=SC)
                    po_, pd_ = bk[4], bk[5]
                    for mt in range(2):
                        self.mm(po_[:, :], Vm[:, mt, hx * 128:(hx + 1) * 128], PX[mt][:], mt == 0, mt == 1, r=[Vm, PX[mt]], w=[po_])
                    for mt in range(2):
                        self.mm(pd_[:, :], self.ones_b[:], PX[mt][:], mt == 0, mt == 1, r=[self.ones_b, PX[mt]], w=[pd_])
                    self.recip(rd[:], pd_[:, :], r=[pd_], w=[rd])
                    self.tt(yx[:], po_[:, :], rd[:], ALU.mult, r=[po_, rd], w=[yx])
                    self.tt(YX[:, hx, :], yx[:], GX[:, hx, :], ALU.mult, r=[yx, GX], w=[YX])
                wpa = self.W["w_pa"].h.rearrange("(k p) c -> p k c", p=128)
                wps = self.W["w_ps"].h.rearrange("(k p) c -> p k c", p=128)
                wpx = self.W["w_px"].h.rearrange("(k p) c -> p k c", p=128)
                for og in range(2):
                    for br in range(3):
                        wm_ = wload(wv[:, :, C_MG + br * 1024 + og * 512:C_MG + br * 1024 + (og + 1) * 512], self.W["w_in"])
                        for o in range(4):
                            p = bk[o % 2]
                            for j in range(8):
                                self.mm(p[:, :], wm_[:, j, o * 128:(o + 1) * 128], hT[:, j, :], j == 0, j == 7, r=[wm_, hT], w=[p])
                            self.act(G3[:, br, o, :], p[:, :], AF.Sigmoid, r=[p], w=[G3])
                    wa = wload(wpa[:, :, og * 512:(og + 1) * 512], self.W["w_pa"])
                    wsx = ws[self.wsi % 3]
                    self.wsi += 1
                    self.load(wsx[:, 0:4, :], wps[:, :, og * 512:(og + 1) * 512], r=[self.W["w_ps"]], w=[wsx])
                    self.load(wsx[:, 4:8, :], wpx[:, :, og * 512:(og + 1) * 512], r=[self.W["w_px"]], w=[wsx])
                    for o in range(4):
                        pa_, ps_, px_ = bk[2 + (o % 2) * 3], bk[3 + (o % 2) * 3], bk[4 + (o % 2) * 3]
                        for k_ in range(8):
                            self.mm(pa_[:, :], wa[:, k_, o * 128:(o + 1) * 128], YA[:, k_, :], k_ == 0, k_ == 7, r=[wa, YA], w=[pa_])
                        for k_ in range(4):
                            self.mm(ps_[:, :], wsx[:, k_, o * 128:(o + 1) * 128], YSb[:, k_, :], k_ == 0, k_ == 3, r=[wsx, YSb], w=[ps_])
                        for k_ in range(4):
                            self.mm(px_[:, :], wsx[:, 4 + k_, o * 128:(o + 1) * 128], YX[:, k_, :], k_ == 0, k_ == 3, r=[wsx, YX], w=[px_])
                        self.tt(m[0][:], pa_[:, :], G3[:, 0, o, :], ALU.mult, r=[pa_, G3], w=[m[0]])
                        self.tt(m[1][:], ps_[:, :], G3[:, 1, o, :], ALU.mult, r=[ps_, G3], w=[m[1]])
                        self.tt(m[2][:], px_[:, :], G3[:, 2, o, :], ALU.mult, r=[px_, G3], w=[m[2]])
                        self.tt(m[0][:], m[0][:], m[1][:], ALU.add, r=[m[0], m[1]], w=[m[0]])
                        self.tt(M[:, og * 4 + o, :], m[0][:], m[2][:], ALU.add, r=[m[0], m[2]], w=[M])
                wo_ = self.W["w_out"].h.rearrange("(k p) c -> p k c", p=128)
                wo0 = wload(wo_[:, :, 0:512], self.W["w_out"])
                wo1 = wload(wo_[:, :, 512:1024], self.W["w_out"])
                for b in range(4):
                    pa_, pb_ = bk[(2 * b) % 4], bk[(2 * b + 1) % 4]
                    for k_ in range(8):
                        self.mm(pa_[:, :], M[:, k_, b * 128:(b + 1) * 128], wo0[:, k_, :], k_ == 0, k_ == 7, r=[M, wo0], w=[pa_])
                    for k_ in range(8):
                        self.mm(pb_[:, :], M[:, k_, b * 128:(b + 1) * 128], wo1[:, k_, :], k_ == 0, k_ == 7, r=[M, wo1], w=[pb_])
                    yr, fs = yres[b % 2], fss[b % 2]
                    xb = xt[b % 2]
                    self.load(xb[:], xsrc[t0 + b * 128:t0 + (b + 1) * 128, :], w=[xb])
                    self.tt(yr[:, 0:512], pa_[:, :], xb[:, 0:512], ALU.add, r=[pa_, xb], w=[yr])
                    self.tt(yr[:, 512:1024], pb_[:, :], xb[:, 512:1024], ALU.add, r=[pb_, xb], w=[yr])
                    self.act(xn[:, 0, :], yr[:], AF.Square, r=[yr], w=[xn, fs], accum_out=fs[:, 0:1])
                    self.rstd(fs, 1, 1.0 / D)
                    self.stt(yr[:], yr[:], fs[:, 0:1], gf[:], ALU.mult, ALU.mult, r=[yr, fs, gf], w=[yr])
                    self.store(self.y_out[yo + b * 128:yo + (b + 1) * 128, :], yr[:], r=[yr])

    def make_hT_bank(self, x_rows, xt, ss, xn, banks, hT, gain):
        for b in range(4):
            xb = xt[b % 2]
            sb_ = ss[b % 2] if isinstance(ss, list) else ss
            self.load(xb[:], x_rows[b * 128:(b + 1) * 128, :], w=[xb])
            self.act(xn[:, b, :], xb[:], AF.Square, r=[xb], w=[xn, ss], accum_out=ss[:, b:b + 1])
            v = ss[:, b:b + 1]
            self.ts(v, v, 1.0 / D, EPS, ALU.mult, ALU.add, r=[ss], w=[ss])
            self.act(v, v, AF.Sqrt, r=[ss], w=[ss])
            self.recip(v, v, r=[ss], w=[ss])
            if b % 2 == 0:
                self.act(xn[:, b, :], xb[:], AF.Copy, r=[xb, ss], w=[xn], scale=ss[:, b:b + 1])
            else:
                self.ts(xn[:, b, :], xb[:], ss[:, b:b + 1], None, ALU.mult, r=[xb, ss], w=[xn])
        for j in range(8):
            bank = banks[j % len(banks)]
            pv = bank[:].bitcast(BF16)
            for b in range(4):
                self.tr(pv[:, b * 128:(b + 1) * 128], xn[:, b, j * 128:(j + 1) * 128], self.ident_b[:],
                        r=[xn, self.ident_b], w=[bank])
            if j % 2 == 0:
                self.ts(hT[:, j, :], pv[:, 0:512], gain[:, j:j + 1], None, ALU.mult, r=[bank, gain], w=[hT])
            else:
                self.act(hT[:, j, :], pv[:, 0:512], AF.Copy, r=[bank, gain], w=[hT], scale=gain[:, j:j + 1])

    def norm_rope_q(self, pbank, half, sq, ssv, xa, t4, tabs, out_bf):
        nh = 4
        o = 0
        h0 = half * 4
        psrc = pbank[:, :]
        self.act(sq[:, o:o + 512], psrc, AF.Square, r=[pbank], w=[sq])
        self.P.op("dve", lambda e: e.tensor_reduce(out=ssv[:, h0:h0 + 4], in_=sq[:, o:o + 512].rearrange("p (h d) -> p h d", h=nh),
                                                   axis=AX.X, op=ALU.add), _bufs([sq]), _bufs([ssv]))
        v = ssv[:, h0:h0 + 4]
        self.ts(v, v, 1.0 / 128.0, EPS, ALU.mult, ALU.add, r=[ssv], w=[ssv])
        self.act(v, v, AF.Sqrt, r=[ssv], w=[ssv])
        self.recip(v, v, r=[ssv], w=[ssv])
        xa3 = xa[:, o:o + 512].rearrange("p (h d) -> p h d", h=nh)
        self.tt(xa3, psrc.rearrange("p (h d) -> p h d", h=nh), v.unsqueeze(2).to_broadcast([128, nh, 128]),
                ALU.mult, r=[pbank, ssv], w=[xa])
        x0 = xa[:, o:o + 512].rearrange("p (h i two) -> p h i two", h=nh, two=2)[:, :, :, 0]
        x1 = xa[:, o:o + 512].rearrange("p (h i two) -> p h i two", h=nh, two=2)[:, :, :, 1]
        ob = out_bf[:, h0:h0 + 4, :].rearrange("p h (i two) -> p h i two", two=2)
        o0, o1 = ob[:, :, :, 0], ob[:, :, :, 1]

        def tb(i):
            return tabs[:, i, :].unsqueeze(1).to_broadcast([128, nh, 64])
        tv = [t4[:, i, 0:256].rearrange("p (h i) -> p h i", h=nh) for i in range(4)]
        self.tt(tv[0], x0, tb(0), ALU.mult, r=[xa, tabs], w=[t4])
        self.tt(tv[1], x1, tb(1), ALU.mult, r=[xa, tabs], w=[t4])
        self.tt(tv[2], x0, tb(2), ALU.mult, r=[xa, tabs], w=[t4])
        self.tt(tv[3], x1, tb(3), ALU.mult, r=[xa, tabs], w=[t4])
        self.tt(o0, tv[0], tv[1], ALU.subtract, r=[t4], w=[out_bf])
        self.tt(o1, tv[2], tv[3], ALU.add, r=[t4], w=[out_bf])


def rope_table(pos):
    pos = np.asarray(pos)
    row = (pos // 64).astype(np.float32)
    col = (pos % 64).astype(np.float32)
    freqs = (np.float32(10000.0) ** (-np.arange(32, dtype=np.float32) / np.float32(32))).astype(np.float32)
    ang = np.concatenate([row[:, None] * freqs, col[:, None] * freqs], axis=-1).astype(np.float32)
    return np.concatenate([np.cos(ang), np.sin(ang)], axis=-1).astype(np.float32)


def host_inputs(inp, Lp, S, ncores=NCORES):
    f = lambda a: np.ascontiguousarray(np.asarray(a, dtype=np.float32))
    Ls = 8 * S
    xs_all = f(inp["x_sample"])[0]
    shared = {}
    shared["w_in"] = f(inp["w_in"])[0]
    shared["w_glu"] = f(inp["w_glu"])[0]
    shared["w_mem_kv"] = f(inp["w_mem_kv"])[0]
    shared["w_pa"] = f(inp["w_proj_attn"])[0]
    shared["w_ps"] = f(inp["w_proj_ssm"])[0]
    shared["w_px"] = f(inp["w_proj_cross"])[0]
    shared["w_out"] = f(inp["w_out"])[0]
    shared["g_in"] = f(f(inp["norm_in"])[0].reshape(8, 128).T)
    shared["g_mem"] = f(f(inp["norm_mem"])[0].reshape(8, 128).T)
    qn, kn = f(inp["q_norm"])[0], f(inp["k_norm"])[0]
    shared["g_q"] = f(np.tile(np.concatenate([qn[0::2], qn[1::2]])[None, :], (128, 1)))
    shared["g_k"] = f(np.tile(np.concatenate([kn[0::2], kn[1::2]])[None, :], (128, 1)))
    shared["g_f"] = f(np.tile(f(inp["norm_final"])[None, :], (128, 1)))
    shared["s5d"] = f(f(inp["s5_d"])[0].reshape(4, 128).T)
    shared["bglu"] = f(f(inp["b_glu"])[0].reshape(4, 128).T)
    a_re, a_im = f(inp["s5_a_re"])[0], f(inp["s5_a_im"])[0]
    dup = lambda a: f(np.concatenate([a.reshape(64, 64).T, a.reshape(64, 64).T], axis=0))
    shared["are"] = dup(a_re)
    shared["aim"] = dup(a_im)
    shared["lst"] = f(np.tile(f(inp["s5_log_step"])[0].reshape(1, 64), (128, 1)))
    b_re, b_im = f(inp["s5_b_re"])[0], f(inp["s5_b_im"])[0]
    c_re, c_im = f(inp["s5_c_re"])[0], f(inp["s5_c_im"])[0]
    B1 = np.zeros((64, 128, 128), np.float32)
    B2 = np.zeros((64, 128, 128), np.float32)
    C1 = np.zeros((64, 128, 16), np.float32)
    C2 = np.zeros((64, 128, 16), np.float32)
    for d in range(2):
        for g in range(32):
            col = d * 32 + g
            r0 = (g % 8) * 16
            B1[col, r0:r0 + 16, 0:64] = b_re[d, g].T
            B1[col, r0:r0 + 16, 64:128] = b_im[d, g].T
            B2[col, r0:r0 + 16, 0:64] = b_im[d, g].T
            B2[col, r0:r0 + 16, 64:128] = b_re[d, g].T
            C1[col, 0:64, :] = c_re[d, g].T
            C1[col, 64:128, :] = c_im[d, g].T
            C2[col, 0:64, :] = c_im[d, g].T
            C2[col, 64:128, :] = c_re[d, g].T
    shared.update(B1=B1, B2=B2, C1=C1, C2=C2)
    shared["ident"] = np.eye(128, dtype=np.float32)
    sw = np.zeros((128, 128), np.float32)
    sw[np.arange(128), (np.arange(128) + 64) % 128] = 1.0
    shared["swap"] = sw
    shared["iota1"] = f(np.tile(np.arange(1, PIECE + 1, dtype=np.float32)[None, :], (128, 1)))
    sg = np.ones((128, 2), np.float32)
    sg[0:64, 0] = -1.0
    sg[64:128, 1] = -1.0
    shared["sgn"] = sg
    shared["csp"] = rope_table(np.arange(Lp))
    maps = []
    for c in range(ncores):
        m = dict(shared)
        m["xp"] = f(inp["x_prompt"])[c]
        order = np.concatenate([np.arange((c + 1) * S, Ls), np.arange(0, c * S), np.arange(c * S, (c + 1) * S)])
        m["xs"] = np.ascontiguousarray(xs_all[order])
        m["css"] = rope_table(order)
        m["memp"] = f(inp["mem_prompt"])[c]
        m["mems"] = f(inp["mem_sample"])[0]
        mf = np.zeros((1, 7 * S), np.float32)
        mf[0, (7 - c) * S:] = 1.0
        m["mf"] = mf
        m["mb"] = (1.0 - mf).astype(np.float32)
        maps.append(m)
    return maps


_NC_CACHE = {}


def run(inp, Lp, S):
    key = (Lp, S)
    if key not in _NC_CACHE:
        _NC_CACHE[key] = Builder(Lp, S).build()
    nc = _NC_CACHE[key]
    maps = host_inputs(inp, Lp, S)
    res = run_bass_kernel_spmd(nc, maps, core_ids=list(range(NCORES)))
    ys = [np.asarray(r["y"]) for r in res.results]
    y_prompt = np.stack([y[:Lp] for y in ys], axis=0).astype(np.float32)
    y_sample = np.concatenate([y[Lp:Lp + S] for y in ys], axis=0)[None].astype(np.float32)
    return y_prompt, y_sample


def kernel(**inputs):
    Lp = int(np.asarray(inputs["x_prompt"]).shape[1])
    Ls = int(np.asarray(inputs["x_sample"]).shape[1])
    return run(inputs, Lp, Ls // 8)
```

```python
import math
from contextlib import ExitStack

import numpy as np
import concourse.bass as bass
import concourse.mybir as mybir
from concourse.bass_utils import run_bass_kernel_spmd

F32 = mybir.dt.float32
BF16 = mybir.dt.bfloat16
AF = mybir.ActivationFunctionType
ALU = mybir.AluOpType
AX = mybir.AxisListType

D = 1024
IN_W = 7680
C_Q, C_K, C_V, C_GA, C_U, C_GS, C_QX, C_GX, C_MG = 0, 1024, 1280, 1536, 2560, 3072, 3584, 4096, 4608
EPS = 1e-6
NCORES = 8
TS = 256
NPC = 4
PIECE = NPC * TS
MAGIC = 12582912.0
TWO_PI = 2.0 * math.pi
CW1 = 6.28125
CW2 = TWO_PI - CW1
SEM_CH = 20000
N_DMA_SEMS = 32


class Buf:
    __slots__ = ("lw", "rd", "ex")

    def __init__(self):
        self.lw = None
        self.rd = []
        self.ex = False


class T:
    def __init__(self, h, b=None):
        self.h = h
        self.b = b if b is not None else Buf()

    def __getitem__(self, k):
        return self.h[k]


def _bufs(xs):
    out = []
    for x in xs:
        if x is None:
            continue
        out.append(x.b if isinstance(x, T) else x)
    return out


class Prog:
    ENGS = ("pe", "act", "dve", "pool", "sp")

    def __init__(self, nc, stack):
        self.nc = nc
        self.stack = stack
        self.q = {e: [] for e in self.ENGS}
        self.cnt = {e: 0 for e in self.ENGS}
        self.sems = {e: [] for e in self.ENGS}
        self.dpool = {e: {"sems": [], "cnt": [], "rr": 0} for e in self.ENGS}
        self.n_dma = 0
        self.pend = {e: [] for e in self.ENGS}
        self.waited = {e: {} for e in self.ENGS}

    def _sem(self, e, idx):
        k = idx // SEM_CH
        while len(self.sems[e]) <= k:
            self.sems[e].append(self.stack.enter_context(
                self.nc.semaphore(f"s_{e}_{len(self.sems[e])}")))
        return self.sems[e][k], idx % SEM_CH + 1

    def op(self, e, fn, reads=(), writes=(), dma=False):
        reads = _bufs(reads)
        writes = _bufs(writes)
        exr = [b for b in reads if b.ex]
        if exr:
            reads = [b for b in reads if not b.ex]
            writes = writes + [b for b in exr if b not in writes]
        deps = {}

        def add(d):
            if d is None:
                return
            if d[0] not in deps or deps[d[0]][1] < d[1]:
                deps[d[0]] = d
        for b in reads:
            add(b.lw)
        for b in writes:
            add(b.lw)
            for r in b.rd:
                add(r)
        idx = self.cnt[e]
        if not dma:
            self.cnt[e] += 1
        waits = list(self.pend[e])
        self.pend[e] = []
        for key, d in deps.items():
            if key == "pe" and e == "pe":
                continue
            waits.append((d[2], d[3]))
        wd = self.waited[e]
        ww = []
        for (ws, wv) in waits:
            if wd.get(id(ws), 0) >= wv:
                continue
            wd[id(ws)] = wv
            ww.append((ws, wv))
        waits = ww
        if dma:
            dp = self.dpool[e]
            if len(dp["sems"]) < N_DMA_SEMS:
                dp["sems"].append(self.stack.enter_context(
                    self.nc.semaphore(f"s_dma_{e}_{len(dp['sems'])}")))
                dp["cnt"].append(0)
                i = len(dp["sems"]) - 1
            else:
                i = dp["rr"]
                dp["rr"] = (dp["rr"] + 1) % N_DMA_SEMS
            dsem = dp["sems"][i]
            if dp["cnt"][i] > 0 and wd.get(id(dsem), 0) < dp["cnt"][i]:
                wd[id(dsem)] = dp["cnt"][i]
                waits.append((dsem, dp["cnt"][i]))
            dp["cnt"][i] += 16
            me = ("dma%d" % self.n_dma, 0, dsem, dp["cnt"][i])
            self.n_dma += 1
            self.q[e].append((waits, fn, dsem, 16))
        else:
            s, v = self._sem(e, idx)
            me = (e, idx, s, v)
            self.q[e].append((waits, fn, s, 1))
        for b in reads:
            b.rd.append(me)
        for b in writes:
            b.lw = me
            b.rd = []
        return me

    def all_done_waits(self):
        final = []
        for dp in self.dpool.values():
            final += [(s, c) for s, c in zip(dp["sems"], dp["cnt"]) if c > 0]
        for e in self.ENGS:
            if self.cnt[e] > 0:
                final.append(self._sem(e, self.cnt[e] - 1))
        return final

    def barrier(self):
        w = self.all_done_waits()
        for e in self.ENGS:
            self.pend[e] = list(w)

    def emit(self, last=False):
        nc = self.nc
        prog = self
        final = self.all_done_waits() if last else []
        with nc.Block() as block:
            def run(eng, name):
                for waits, fn, s, inc in prog.q[name]:
                    for (ws, wv) in waits:
                        eng.wait_ge(ws, wv)
                    fn(eng).then_inc(s, inc)
                prog.q[name] = []

            @block.tensor
            def _(eng):
                run(eng, "pe")

            @block.scalar
            def _(eng):
                run(eng, "act")

            @block.vector
            def _(eng):
                run(eng, "dve")

            @block.gpsimd
            def _(eng):
                run(eng, "pool")

            @block.sync
            def _(eng):
                run(eng, "sp")
                for (ws, wv) in final:
                    eng.wait_ge(ws, wv)


def rev_ap(ap2d):
    a = ap2d.ap
    assert len(a) == 2, a
    n = a[1][1]
    st = a[1][0]
    return bass.AP(ap2d.tensor, ap2d.offset + st * (n - 1), [list(a[0]), [-st, n]])


class Builder:
    def __init__(self, Lp, S, dbg=False):
        self.Lp, self.S = Lp, S
        self.Ls = 8 * S
        self.Lo = Lp + S
        self.dbg = dbg
        self.nc = bass.Bass("TRN2", target_bir_lowering=False)

    def dram_in(self, name, shape, dt=F32):
        return self.nc.dram_tensor(name, list(shape), dt, kind="ExternalInput").ap()

    def dram_out(self, name, shape, dt=F32):
        return self.nc.dram_tensor(name, list(shape), dt, kind="ExternalOutput").ap()

    def dram_scr(self, name, shape, dt):
        kind = "ExternalOutput" if (self.dbg and name.split("_")[0] in str(self.dbg)) else "Internal"
        return T(self.nc.dram_tensor(name, list(shape), dt, kind=kind).ap())

    _uid = 0

    def sb(self, st, name, shape, dt):
        Builder._uid += 1
        return T(st.enter_context(self.nc.sbuf_tensor(f"sb{Builder._uid}_{name}", list(shape), dt)))

    def ps(self, st, name, shape, dt):
        Builder._uid += 1
        nbytes = int(np.prod(shape[1:])) * (4 if dt == F32 else 2)
        assert nbytes == 2048, (name, shape)
        t = T(st.enter_context(self.nc.psum_tensor(f"ps{Builder._uid}_{name}", list(shape), dt)))
        t.b.ex = True
        return t

    def load(self, out, in_, r=(), w=()):
        self.P.op("sp", lambda e: e.dma_start(out=out, in_=in_), r, w, dma=True)

    def store(self, out, in_, r=(), w=()):
        self.P.op("pool", lambda e: e.dma_start(out=out, in_=in_), r, w, dma=True)

    def mm(self, out, lhsT, rhs, start, stop, r=(), w=()):
        self.P.op("pe", lambda e: e.matmul(out, lhsT=lhsT, rhs=rhs, start=start, stop=stop), r, w)

    def tr(self, out, in_, ident, r=(), w=()):
        self.P.op("pe", lambda e: e.transpose(out, in_, ident), r, w)

    def act(self, out, in_, func, r=(), w=(), eng="act", **kw):
        self.P.op(eng, lambda e: e.activation(out=out, in_=in_, func=func, **kw), r, w)

    def tt(self, out, in0, in1, op, r=(), w=(), eng="dve"):
        self.P.op(eng, lambda e: e.tensor_tensor(out=out, in0=in0, in1=in1, op=op), r, w)

    def ts(self, out, in0, s1, s2, op0, op1=None, r=(), w=(), eng="dve"):
        if op1 is None:
            self.P.op(eng, lambda e: e.tensor_scalar(out=out, in0=in0, scalar1=s1, scalar2=None, op0=op0), r, w)
        else:
            self.P.op(eng, lambda e: e.tensor_scalar(out=out, in0=in0, scalar1=s1, scalar2=s2, op0=op0, op1=op1), r, w)

    def stt(self, out, in0, scalar, in1, op0, op1, r=(), w=()):
        self.P.op("dve", lambda e: e.scalar_tensor_tensor(out=out, in0=in0, scalar=scalar, in1=in1, op0=op0, op1=op1), r, w)

    def cp(self, out, in_, r=(), w=(), eng="dve"):
        if eng == "act":
            self.P.op("act", lambda e: e.activation(out=out, in_=in_, func=AF.Copy), r, w)
        elif eng == "dve":
            self.P.op("dve", lambda e: e.tensor_scalar(out=out, in0=in_, scalar1=1.0, scalar2=None, op0=ALU.mult), r, w)
        else:
            self.P.op(eng, lambda e: e.tensor_copy(out=out, in_=in_), r, w)

    def recip(self, out, in_, r=(), w=()):
        self.P.op("dve", lambda e: e.reciprocal(out=out, in_=in_), r, w)

    def memset(self, ap, val, r=(), w=(), eng="dve"):
        self.P.op(eng, lambda e: e.memset(ap, val), r, w)

    def rstd(self, v, n, inv_n, r=(), w=()):
        self.ts(v[:, 0:n], v[:, 0:n], inv_n, EPS, ALU.mult, ALU.add, r=list(r) + [v], w=[v])
        self.act(v[:, 0:n], v[:, 0:n], AF.Sqrt, r=[v], w=[v])
        self.recip(v[:, 0:n], v[:, 0:n], r=[v], w=list(w) + [v])

    def build(self):
        nc = self.nc
        Lp, S, Ls, Lo = self.Lp, self.S, self.Ls, self.Lo
        I = {}
        I["xp"] = self.dram_in("xp", [Lp, D])
        I["xs"] = self.dram_in("xs", [Ls, D])
        I["memp"] = self.dram_in("memp", [256, D])
        I["mems"] = self.dram_in("mems", [256, D])
        I["csp"] = self.dram_in("csp", [Lp, 128])
        I["css"] = self.dram_in("css", [Ls, 128])
        I["mf"] = self.dram_in("mf", [1, 7 * S])
        I["mb"] = self.dram_in("mb", [1, 7 * S])
        I["w_in"] = self.dram_in("w_in", [D, IN_W])
        I["w_glu"] = self.dram_in("w_glu", [512, 512])
        I["w_mem_kv"] = self.dram_in("w_mem_kv", [D, 1024])
        I["w_pa"] = self.dram_in("w_pa", [1024, D])
        I["w_ps"] = self.dram_in("w_ps", [512, D])
        I["w_px"] = self.dram_in("w_px", [512, D])
        I["w_out"] = self.dram_in("w_out", [D, D])
        I["g_in"] = self.dram_in("g_in", [128, 8])
        I["g_mem"] = self.dram_in("g_mem", [128, 8])
        I["g_q"] = self.dram_in("g_q", [128, 128])
        I["g_k"] = self.dram_in("g_k", [128, 128])
        I["g_f"] = self.dram_in("g_f", [128, D])
        I["s5d"] = self.dram_in("s5d", [128, 4])
        I["bglu"] = self.dram_in("bglu", [128, 4])
        I["are"] = self.dram_in("are", [128, 64])
        I["aim"] = self.dram_in("aim", [128, 64])
        I["lst"] = self.dram_in("lst", [128, 64])
        I["B1"] = self.dram_in("B1", [64, 128, 128])
        I["B2"] = self.dram_in("B2", [64, 128, 128])
        I["C1"] = self.dram_in("C1", [64, 128, 16])
        I["C2"] = self.dram_in("C2", [64, 128, 16])
        I["ident"] = self.dram_in("ident", [128, 128])
        I["swap"] = self.dram_in("swap", [128, 128])
        I["iota1"] = self.dram_in("iota1", [128, PIECE])
        I["sgn"] = self.dram_in("sgn", [128, 2])
        self.I = I
        self.y_out = self.dram_out("y", [Lo, D])

        self.W = {
            "w_in": self.dram_scr("wb_in", [D, IN_W], BF16),
            "w_glu": self.dram_scr("wb_glu", [512, 512], BF16),
            "w_mem_kv": self.dram_scr("wb_mkv", [D, 1024], BF16),
            "w_pa": self.dram_scr("wb_pa", [1024, D], BF16),
            "w_ps": self.dram_scr("wb_ps", [512, D], BF16),
            "w_px": self.dram_scr("wb_px", [512, D], BF16),
            "w_out": self.dram_scr("wb_out", [D, D], BF16),
        }
        self.KT = {"p": self.dram_scr("KT_p", [2, 128, Lp], BF16),
                   "s": self.dram_scr("KT_s", [2, 128, Ls], BF16)}
        self.VA = {"p": self.dram_scr("VA_p", [2, 128, Lp // 128, 129], BF16),
                   "s": self.dram_scr("VA_s", [2, 128, Ls // 128, 129], BF16)}
        self.UT = {"p": self.dram_scr("UT_p", [512, Lp], BF16),
                   "sf": self.dram_scr("UT_sf", [512, Ls], BF16),
                   "sb": self.dram_scr("UT_sb", [512, Ls], BF16)}
        self.YS = [self.dram_scr("YS_f", [512, Lo], F32), self.dram_scr("YS_b", [512, Lo], F32)]
        self.YAs = self.dram_scr("YA_s", [1024, Lo], BF16)

        with ExitStack() as gst:
            self.P = Prog(nc, gst)
            self.gst = gst
            self.consts(gst)
            phases = [self.phase0, self.phase1, self.phase2, self.phase3]
            stop = getattr(self, "stop", 3)
            for i, ph in enumerate(phases):
                with ExitStack() as st:
                    ph(st)
                    self.P.emit(last=(i == stop))
                if i == stop:
                    break
                self.P.barrier()
        return nc

    def consts(self, st):
        I = self.I
        self.ident_f = self.sb(st, "ident_f", [128, 128], F32)
        self.ident_b = self.sb(st, "ident_b", [128, 128], BF16)
        self.swap_f = self.sb(st, "swap_f", [128, 128], F32)
        self.ones_b = self.sb(st, "ones_b", [128, 128], BF16)
        self.g_in = self.sb(st, "g_in", [128, 8], F32)
        self.g_mem = self.sb(st, "g_mem", [128, 8], F32)
        self.g_q = self.sb(st, "g_q", [128, 128], F32)
        self.g_k = self.sb(st, "g_k", [128, 128], F32)
        self.s5d = self.sb(st, "s5d", [128, 4], F32)
        self.bglu = self.sb(st, "bglu", [128, 4], F32)
        self.sgn = self.sb(st, "sgn", [128, 2], F32)
        self.halfpi = self.sb(st, "halfpi", [128, 1], F32)
        self.KmT = {k: self.sb(st, "KmT" + k, [128, 4, 256], BF16) for k in "ps"}
        self.Vm = {k: self.sb(st, "Vm" + k, [128, 2, 512], BF16) for k in "ps"}
        for t, n in ((self.ident_f, "ident"), (self.swap_f, "swap"), (self.g_in, "g_in"),
                     (self.g_mem, "g_mem"), (self.g_q, "g_q"), (self.g_k, "g_k"),
                     (self.s5d, "s5d"), (self.bglu, "bglu"), (self.sgn, "sgn")):
            self.load(t[:], I[n][:, :], w=[t])
        self.cp(self.ident_b[:], self.ident_f[:], r=[self.ident_f], w=[self.ident_b])
        self.memset(self.ones_b[:], 1.0, w=[self.ones_b])
        self.memset(self.halfpi[:], math.pi / 2.0, w=[self.halfpi])

    def make_hT(self, x_ap_rows, xt, ss, xn, ptr, hT, gain, nblk=4, mask=None):
        self.load(xt[:, 0:nblk, :], x_ap_rows.rearrange("(b p) d -> p b d", p=128), w=[xt])
        for b in range(nblk):
            self.act(xn[:, b, :], xt[:, b, :], AF.Square, r=[xt], w=[xn, ss],
                     accum_out=ss[:, b:b + 1])
        import os
        if os.environ.get("DBG_H") == "1":
            return
        self.rstd(ss, nblk, 1.0 / D)
        if os.environ.get("DBG_H") == "2":
            return
        for b in range(nblk):
            if b % 2 == 0:
                self.act(xn[:, b, :], xt[:, b, :], AF.Copy, r=[xt, ss], w=[xn], scale=ss[:, b:b + 1])
            else:
                self.ts(xn[:, b, :], xt[:, b, :], ss[:, b:b + 1], None, ALU.mult, r=[xt, ss], w=[xn])
        if os.environ.get("DBG_H") == "3":
            return
        for j in range(8):
            pt = ptr[j % len(ptr)]
            for b in range(nblk):
                self.tr(pt[:, b * 128:(b + 1) * 128], xn[:, b, j * 128:(j + 1) * 128], self.ident_b[:],
                        r=[xn, self.ident_b], w=[pt])
            if j % 2 == 0:
                self.ts(hT[:, j, 0:nblk * 128], pt[:, 0:nblk * 128], gain[:, j:j + 1], None, ALU.mult,
                        r=[pt, gain], w=[hT])
            else:
                self.act(hT[:, j, 0:nblk * 128], pt[:, 0:nblk * 128], AF.Copy, r=[pt, gain], w=[hT],
                         scale=gain[:, j:j + 1])

    def phase0(self, st):
        I = self.I
        import os
        if os.environ.get("DBG_P0") == "none":
            return
        stg = [self.sb(st, f"wstg{i}", [128, 2048], F32) for i in range(2)]
        stb = [self.sb(st, f"wstb{i}", [128, 2048], BF16) for i in range(2)]
        k = 0
        for name, rows, cols in (("w_in", D, IN_W), ("w_glu", 512, 512), ("w_mem_kv", D, 1024),
                                 ("w_pa", 1024, D), ("w_ps", 512, D), ("w_px", 512, D), ("w_out", D, D)):
            cw = 1920 if cols == IN_W else cols
            for r0 in range(0, rows, 128):
                for c0 in range(0, cols, cw):
                    a, b = stg[k % 2], stb[k % 2]
                    self.load(a[:, 0:cw], I[name][r0:r0 + 128, c0:c0 + cw], w=[a])
                    self.cp(b[:, 0:cw], a[:, 0:cw], r=[a], w=[b], eng="dve" if k % 2 == 0 else "act")
                    self.store(self.W[name][r0:r0 + 128, c0:c0 + cw], b[:, 0:cw], r=[b], w=[self.W[name]])
                    k += 1
        import os
        if os.environ.get("DBG_P0") == "a":
            return
        wm = self.sb(st, "wm", [128, 8, 1024], BF16)
        self.load(wm[:], self.W["w_mem_kv"].h.rearrange("(j p) c -> p j c", p=128), r=[self.W["w_mem_kv"]], w=[wm])
        xt = self.sb(st, "m_xt", [128, 2, D], F32)
        xn = self.sb(st, "m_xn", [128, 2, D], BF16)
        ss = self.sb(st, "m_ss", [128, 4], F32)
        hT = self.sb(st, "m_hT", [128, 8, 256], BF16)
        vtmp = self.sb(st, "m_v", [128, 512], BF16)
        ptr = [self.ps(st, f"m_ptr{i}", [128, 1024], BF16) for i in range(2)]
        pk = [self.ps(st, f"m_pk{i}", [128, 512], F32) for i in range(2)]
        for key, src in (("p", I["memp"]), ("s", I["mems"])):
            self.make_hT(src[:, :], xt, ss, xn, ptr, hT, self.g_mem, nblk=2)
            if os.environ.get("DBG_P0") == "b1":
                continue
            for hx in range(4):
                p = pk[hx % 2]
                for j in range(8):
                    self.mm(p[:, 0:256], wm[:, j, hx * 128:(hx + 1) * 128], hT[:, j, :], j == 0, j == 7,
                            r=[wm, hT], w=[p])
                if os.environ.get("DBG_P0") == "b2":
                    continue
                self.cp(self.KmT[key][:, hx, :], p[:, 0:256], r=[p], w=[self.KmT[key]], eng="act")
            if os.environ.get("DBG_P0") in ("b2", "b3"):
                continue
            for m in range(2):
                p = pk[m % 2]
                for j in range(8):
                    self.mm(p[:, :], hT[:, j, m * 128:(m + 1) * 128], wm[:, j, 512:1024], j == 0, j == 7,
                            r=[wm, hT], w=[p])
                self.cp(self.Vm[key][:, m, :], p[:, :], r=[p], w=[self.Vm[key]], eng=os.environ.get("DBG_VE", "dve"))

    def rope_tables(self, cs, b, gain, tabs, r_extra=()):
        c = cs[:, b, 0:64]
        s = cs[:, b, 64:128]
        g0 = gain[:, 0:64]
        g1 = gain[:, 64:128]
        rr = [cs, gain] + list(r_extra)
        self.tt(tabs[:, 0, :], c, g0, ALU.mult, r=rr, w=[tabs])
        self.tt(tabs[:, 1, :], s, g1, ALU.mult, r=rr, w=[tabs])
        self.tt(tabs[:, 2, :], s, g0, ALU.mult, r=rr, w=[tabs])
        self.tt(tabs[:, 3, :], c, g1, ALU.mult, r=rr, w=[tabs])

    def norm_rope(self, psrc, nh, sq, ssv, xa, t4, tabs, out_bf, rsrc):
        n = nh * 128
        self.act(sq[:, 0:n], psrc, AF.Square, r=rsrc, w=[sq])
        self.P.op("dve", lambda e: e.tensor_reduce(out=ssv[:, 0:nh], in_=sq[:, 0:n].rearrange("p (h d) -> p h d", h=nh),
                                                   axis=AX.X, op=ALU.add), _bufs([sq]), _bufs([ssv]))
        self.rstd(ssv, nh, 1.0 / 128.0)
        xa3 = xa[:, 0:n].rearrange("p (h d) -> p h d", h=nh)
        self.tt(xa3, psrc.rearrange("p (h d) -> p h d", h=nh),
                ssv[:, 0:nh].unsqueeze(2).to_broadcast([128, nh, 128]), ALU.mult, r=list(rsrc) + [ssv], w=[xa])
        x0 = xa[:, 0:n].rearrange("p (h i two) -> p h i two", h=nh, two=2)[:, :, :, 0]
        x1 = xa[:, 0:n].rearrange("p (h i two) -> p h i two", h=nh, two=2)[:, :, :, 1]
        o0 = out_bf[:, 0:nh, :].rearrange("p h (i two) -> p h i two", two=2)[:, :, :, 0]
        o1 = out_bf[:, 0:nh, :].rearrange("p h (i two) -> p h i two", two=2)[:, :, :, 1]

        def tb(i):
            return tabs[:, i, :].unsqueeze(1).to_broadcast([128, nh, 64])
        tv = [t4[:, i, 0:nh * 64].rearrange("p (h i) -> p h i", h=nh) for i in range(4)]
        self.tt(tv[0], x0, tb(0), ALU.mult, r=[xa, tabs], w=[t4])
        self.tt(tv[1], x1, tb(1), ALU.mult, r=[xa, tabs], w=[t4])
        self.tt(tv[2], x0, tb(2), ALU.mult, r=[xa, tabs], w=[t4])
        self.tt(tv[3], x1, tb(3), ALU.mult, r=[xa, tabs], w=[t4])
        self.tt(o0, tv[0], tv[1], ALU.subtract, r=[t4], w=[out_bf])
        self.tt(o1, tv[2], tv[3], ALU.add, r=[t4], w=[out_bf])

    def phase1(self, st):
        I = self.I
        Lp, S, Ls = self.Lp, self.S, self.Ls
        wkv = self.sb(st, "wkv", [128, 8, 512], BF16)
        wu = self.sb(st, "wu", [128, 8, 512], BF16)
        wv = self.W["w_in"].h.rearrange("(j p) c -> p j c", p=128)
        self.load(wkv[:], wv[:, :, C_K:C_K + 512], r=[self.W["w_in"]], w=[wkv])
        self.load(wu[:], wv[:, :, C_U:C_U + 512], r=[self.W["w_in"]], w=[wu])
        xt = [self.sb(st, f"xt{i}", [128, 4, D], F32) for i in range(2)]
        xn = [self.sb(st, f"xn{i}", [128, 4, D], BF16) for i in range(2)]
        ss = [self.sb(st, f"ss{i}", [128, 4], F32) for i in range(2)]
        hT = [self.sb(st, f"hT{i}", [128, 8, 512], BF16) for i in range(2)]
        cs = [self.sb(st, f"cs{i}", [128, 4, 128], F32) for i in range(2)]
        tabs = [self.sb(st, f"tabs{i}", [128, 4, 64], F32) for i in range(2)]
        sq = self.sb(st, "sq", [128, 256], F32)
        kss = [self.sb(st, f"kss{i}", [128, 2], F32) for i in range(2)]
        ka = self.sb(st, "ka", [128, 256], F32)
        t4 = self.sb(st, "t4", [128, 4, 128], F32)
        krot = [self.sb(st, f"krot{i}", [128, 2, 128], BF16) for i in range(2)]
        KTt = [self.sb(st, f"KTt{i}", [128, 2, 512], BF16) for i in range(2)]
        VAt = [self.sb(st, f"VAt{i}", [128, 2, 4, 129], BF16) for i in range(2)]
        UTt = [self.sb(st, f"UTt{i}", [128, 4, 512], BF16) for i in range(2)]
        UTb = [self.sb(st, f"UTb{i}", [128, 4, 512], BF16) for i in range(2)]
        mrow = [self.sb(st, f"mrow{i}", [128, 2, 512], F32) for i in range(2)]
        ptr = [self.ps(st, f"ptr{i}", [128, 1024], BF16) for i in range(2)]
        pkv = [self.ps(st, f"pkv{i}", [128, 512], F32) for i in range(2)]
        pkt = self.ps(st, "pkt", [128, 2, 512], BF16)
        pu = [self.ps(st, f"pu{i}", [128, 512], F32) for i in range(2)]
        for v in VAt:
            self.memset(v[:, :, :, 128:129], 1.0, w=[v])
        it = 0
        for key, xsrc, cssrc, L in (("p", I["xp"], I["csp"], Lp), ("s", I["xs"], I["css"], Ls)):
            for t in range(L // 512):
                t0 = t * 512
                sl = it % 2
                it += 1
                prefix = (key == "s" and t0 < 7 * S)
                self.make_hT(xsrc[t0:t0 + 512, :], xt[sl], ss[sl], xn[sl], ptr, hT[sl], self.g_in)
                self.load(cs[sl][:], cssrc[t0:t0 + 512, :].rearrange("(b p) c -> p b c", p=128), w=[cs[sl]])
                if prefix:
                    self.load(mrow[sl][:, 0, :], I["mf"][0:1, t0:t0 + 512].partition_broadcast(128), w=[mrow[sl]])
                    self.load(mrow[sl][:, 1, :], I["mb"][0:1, t0:t0 + 512].partition_broadcast(128), w=[mrow[sl]])
                for b in range(4):
                    p = pkv[b % 2]
                    for j in range(8):
                        self.mm(p[:, :], hT[sl][:, j, b * 128:(b + 1) * 128], wkv[:, j, :], j == 0, j == 7,
                                r=[hT[sl], wkv], w=[p])
                    self.cp(VAt[sl][:, :, b, 0:128], p[:, 256:512].rearrange("p (h d) -> p h d", h=2),
                            r=[p], w=[VAt[sl]], eng="act")
                    tb_ = tabs[b % 2]
                    self.rope_tables(cs[sl], b, self.g_k, tb_)
                    kr = krot[b % 2]
                    self.norm_rope(p[:, 0:256], 2, sq, kss[b % 2], ka, t4, tb_, kr, [p])
                    for h in range(2):
                        self.tr(pkt[:, h, b * 128:(b + 1) * 128], kr[:, h, :], self.ident_b[:],
                                r=[kr, self.ident_b], w=[pkt])
                self.cp(KTt[sl][:], pkt[:], r=[pkt], w=[KTt[sl]])
                self.store(self.KT[key].h[:, :, t0:t0 + 512].rearrange("h p l -> p h l"), KTt[sl][:],
                           r=[KTt[sl]], w=[self.KT[key]])
                self.store(self.VA[key].h[:, :, t0 // 128:t0 // 128 + 4, :].rearrange("h p b c -> p h b c"),
                           VAt[sl][:], r=[VAt[sl]], w=[self.VA[key]])
                for i in range(4):
                    p = pu[i % 2]
                    for j in range(8):
                        self.mm(p[:, :], wu[:, j, i * 128:(i + 1) * 128], hT[sl][:, j, :], j == 0, j == 7,
                                r=[hT[sl], wu], w=[p])
                    if prefix:
                        self.tt(UTt[sl][:, i, :], p[:, :], mrow[sl][:, 0, :], ALU.mult, r=[p, mrow[sl]], w=[UTt[sl]])
                        self.tt(UTb[sl][:, i, :], p[:, :], mrow[sl][:, 1, :], ALU.mult, r=[p, mrow[sl]], w=[UTb[sl]])
                    else:
                        self.cp(UTt[sl][:, i, :], p[:, :], r=[p], w=[UTt[sl]], eng="act" if i % 2 else "dve")
                if key == "p":
                    self.store(self.UT["p"].h[:, t0:t0 + 512].rearrange("(i p) l -> p i l", p=128), UTt[sl][:],
                               r=[UTt[sl]], w=[self.UT["p"]])
                else:
                    self.store(self.UT["sf"].h[:, t0:t0 + 512].rearrange("(i p) l -> p i l", p=128), UTt[sl][:],
                               r=[UTt[sl]], w=[self.UT["sf"]])
                    self.store(self.UT["sb"].h[:, t0:t0 + 512].rearrange("(i p) l -> p i l", p=128),
                               (UTb if prefix else UTt)[sl][:], r=[(UTb if prefix else UTt)[sl]], w=[self.UT["sb"]])

    def phase2(self, st):
        banks = [self.ps(st, f"cb{i}", [128, 512], F32) for i in range(8)]
        ga = self.gen_s5(st, banks[0:4])
        gb = self.gen_att(st, banks[4:8])
        ta = tb = 0.0
        da = db = False
        import os
        if os.environ.get("DBG_NOATT"):
            db = True
        while not (da and db):
            if not da and (db or ta <= tb):
                try:
                    ta += next(ga)
                except StopIteration:
                    da = True
            else:
                try:
                    tb += next(gb)
                except StopIteration:
                    db = True

    def gen_s5(self, st, banks):
        I = self.I
        Lp, S, Ls = self.Lp, self.S, self.Ls
        def gt(name):
            return self.sb(st, name, [128, 64], F32)
        are, aim, lst = gt("are"), gt("aim"), gt("lst")
        for t, n in ((are, "are"), (aim, "aim"), (lst, "lst")):
            self.load(t[:], I[n][:, :], w=[t])
        step, Rv, th, kk, thr, sn, cs_, ab = gt("step"), gt("Rv"), gt("th"), gt("kk"), gt("thr"), gt("sn"), gt("cs_"), gt("ab")
        nr, ni, den, CR, CI, SCI, NSCR, tmp = gt("nr"), gt("ni"), gt("den"), gt("CR"), gt("CI"), gt("SCI"), gt("NSCR"), gt("tmp")
        self.act(step[:], lst[:], AF.Exp, r=[lst], w=[step])
        self.ts(are[:], are[:], -1e-4, None, ALU.min, r=[are], w=[are])
        self.tt(Rv[:], are[:], step[:], ALU.mult, r=[are, step], w=[Rv])
        RL, RP, NCI, SCR = gt("RL"), gt("RP"), gt("NCI"), gt("SCR")
        self.cp(RL[:], Rv[:], r=[Rv], w=[RL])
        self.act(RP[:], Rv[:], AF.Exp, r=[Rv], w=[RP], scale=float(PIECE))
        self.act(Rv[:], Rv[:], AF.Exp, r=[Rv], w=[Rv])
        self.tt(th[:], aim[:], step[:], ALU.mult, r=[aim, step], w=[th])
        self.reduce_angle(th, kk, thr)
        self.sincos(thr, ab, sn, cs_)
        self.tt(nr[:], Rv[:], cs_[:], ALU.mult, r=[Rv, cs_], w=[nr])
        self.ts(nr[:], nr[:], -1.0, None, ALU.add, r=[nr], w=[nr])
        self.tt(ni[:], Rv[:], sn[:], ALU.mult, r=[Rv, sn], w=[ni])
        self.tt(den[:], are[:], are[:], ALU.mult, r=[are], w=[den])
        self.tt(tmp[:], aim[:], aim[:], ALU.mult, r=[aim], w=[tmp])
        self.tt(den[:], den[:], tmp[:], ALU.add, r=[den, tmp], w=[den])
        self.recip(den[:], den[:], r=[den], w=[den])
        self.tt(CR[:], nr[:], are[:], ALU.mult, r=[nr, are], w=[CR])
        self.tt(tmp[:], ni[:], aim[:], ALU.mult, r=[ni, aim], w=[tmp])
        self.tt(CR[:], CR[:], tmp[:], ALU.add, r=[CR, tmp], w=[CR])
        self.tt(CR[:], CR[:], den[:], ALU.mult, r=[CR, den], w=[CR])
        self.tt(CI[:], ni[:], are[:], ALU.mult, r=[ni, are], w=[CI])
        self.tt(tmp[:], nr[:], aim[:], ALU.mult, r=[nr, aim], w=[tmp])
        self.tt(CI[:], CI[:], tmp[:], ALU.subtract, r=[CI, tmp], w=[CI])
        self.tt(CI[:], CI[:], den[:], ALU.mult, r=[CI, den], w=[CI])
        self.ts(SCI[:], CI[:], self.sgn[:, 0:1], None, ALU.mult, r=[CI, self.sgn], w=[SCI])
        self.ts(NSCR[:], CR[:], self.sgn[:, 1:2], None, ALU.mult, r=[CR, self.sgn], w=[NSCR])
        self.ts(SCR[:], CR[:], self.sgn[:, 0:1], None, ALU.mult, r=[CR, self.sgn], w=[SCR])
        self.ts(NCI[:], CI[:], -1.0, None, ALU.mult, r=[CI], w=[NCI])
        self.G = dict(RL=RL, RP=RP, NCI=NCI, SCR=SCR, CR=CR, CI=CI, SCI=SCI)

        iota1 = self.sb(st, "iota1", [128, PIECE], F32)
        self.load(iota1[:], I["iota1"][:, :], w=[iota1])
        ones_f = self.sb(st, "ones_f", [128, TS], F32)
        self.memset(ones_f[:], 1.0, w=[ones_f])
        iotaR = self.sb(st, "iotaR", [128, PIECE], F32)
        self.ts(iotaR[:], iota1[:], -1.0, float(PIECE), ALU.mult, ALU.add, r=[iota1], w=[iotaR])
        self.G["iotaR"] = iotaR

        yield 20.0
        UCH = 1024

        class Stream:
            pass
        strs = []
        for d in range(2):
            s_ = Stream()
            n = f"s{d}_"
            if d == 0:
                tmp4 = [self.sb(st, "s5tmp%d" % i, [128, PIECE], F32) for i in range(4)]
            s_.phi, s_.k2, s_.sinp, s_.cosp = tmp4
            s_.TA = self.sb(st, n + "TA", [128, PIECE], F32)
            s_.TB = self.sb(st, n + "TB", [128, PIECE], F32)
            s_.RC = self.sb(st, n + "RC", [128, PIECE], F32)
            s_.RS = self.sb(st, n + "RS", [128, PIECE], F32)
            s_.Rd = self.sb(st, n + "Rd", [128, TS], F32)
            s_.rot = self.sb(st, n + "rot", [128, 128], F32)
            s_.rb = self.sb(st, n + "rb", [128, 1], F32)
            s_.Bf = self.sb(st, n + "Bf", [128, 2, 128], F32)
            s_.Bb = self.sb(st, n + "Bb", [128, 2, 128], BF16)
            s_.Cf = self.sb(st, n + "Cf", [128, 2, 16], F32)
            s_.Cb = self.sb(st, n + "Cb", [128, 2, 16], BF16)
            s_.t2 = [self.sb(st, n + f"t2{i}", [128, TS], F32) for i in range(2)]
            s_.dd = [self.sb(st, n + f"dd{i}", [128, TS], F32) for i in range(2)]
            s_.e1 = [self.sb(st, n + f"e1{i}", [128, TS], BF16) for i in range(3)]
            s_.e2 = [self.sb(st, n + f"e2{i}", [128, TS], BF16) for i in range(3)]
            s_.wl = self.sb(st, n + "wl", [128, 1], F32)
            s_.carry = self.sb(st, n + "carry", [128, 1], F32)
            s_.rotD = self.sb(st, n + "rotD", [128, 128], F32)
            s_.zacc = [self.sb(st, n + f"zacc{i}", [128, 2 * NPC], F32) for i in range(2)]
            s_.zq = [self.sb(st, n + f"zq{i}", [128, 1], F32) for i in range(2)]
            s_.X = self.sb(st, n + "X", [128, 1], F32)
            s_.ystg = [self.sb(st, n + f"ystg{i}", [16, 512], F32) for i in range(2)]
            s_.ub = [self.sb(st, n + f"ub{i}", [128, UCH], BF16) for i in range(2)]
            s_.ucnt = 0
            s_.ucur = None
            s_.uch = UCH
            s_.pp = [banks[0], banks[1]]
            s_.pw = banks[2]
            s_.py = banks[3]
            s_.prot = T(banks[3].h[:, TS:TS + 2], banks[3].b)
            s_.nchunk = 0
            s_.nstg = 0
            strs.append(s_)
        Lp, S, Ls = self.Lp, self.S, self.Ls
        nset = 0
        for j in range(4):
            for gl in range(8):
                g = 8 * j + gl
                for d in range(2):
                    s_ = strs[nset % 2]
                    nset += 1
                    s_.ucur = None
                    col = d * 32 + g
                    self.s5_prep(s_, col, iota1, ones_f, Rv, thr, CR, CI, SCI, NSCR)
                    yield 4.0
                    self.s5_prep_G(s_, col)
                    yield 8.0
                    npre = 7 * S // TS
                    pre = []
                    nfirst = npre % NPC if npre % NPC else NPC
                    for i in range(npre):
                        if d == 0:
                            src, t0, rev = self.UT["sf"], i * TS, False
                        else:
                            src, t0, rev = self.UT["sb"], 7 * S - (i + 1) * TS, True
                        if i < nfirst:
                            q, pos, cnt = 0, i, nfirst
                        else:
                            q, pos, cnt = 1 + (i - nfirst) // NPC, (i - nfirst) % NPC, NPC
                        pre.append(dict(s=s_, d=d, g=g, src=src, j=j, t0=t0, rev=rev, q=q,
                                        gcol=NPC - cnt + pos, pend=(pos == cnt - 1), c0=NPC - cnt))
                    self.s5_M(pre[0])
                    hq = []
                    hq_done = [False]
                    for c in range(npre):
                        if c + 1 < npre:
                            self.s5_M(pre[c + 1])
                        it_ = pre[c]
                        self.s5_PR(it_)
                        if it_["pend"]:
                            self.s5_piece_sum(s_, it_["q"], it_["c0"])
                            if hq:
                                self.s5_horner(s_, hq.pop(0), False if hq_done[0] else True)
                                hq_done[0] = True
                            hq.append(it_["q"])
                        yield 0.75
                    while hq:
                        self.s5_horner(s_, hq.pop(0), False if hq_done[0] else True)
                        hq_done[0] = True
                    self.s5_prep_T(s_, col, CR, CI, SCI, NSCR)
                    yield 6.0
                    items = []
                    pcnt = [0]

                    def add(src, t0, rev, ypos, first=False, last=False, fromX=False):
                        if first:
                            pcnt[0] = 0
                        items.append(dict(s=s_, d=d, g=g, src=src, j=j, t0=t0, rev=rev, ypos=ypos,
                                          first=first, last=last, p=pcnt[0] % NPC, fromX=fromX))
                        pcnt[0] += 1
                    npc, nown = Lp // TS, S // TS
                    for i in range(npc):
                        t0 = i * TS if d == 0 else Lp - (i + 1) * TS
                        add(self.UT["p"], t0, d == 1, t0, i == 0, i == npc - 1)
                    for i in range(nown):
                        t0 = 7 * S + i * TS if d == 0 else Ls - (i + 1) * TS
                        add(self.UT["sf"] if d == 0 else self.UT["sb"], t0, d == 1, Lp + t0 - 7 * S,
                            i == 0, i == nown - 1, fromX=(i == 0))
                    ni = len(items)
                    for c_, it_ in enumerate(items):
                        it_["h"] = c_ % 2
                        it_["k3"] = c_ % 3
                    self.s5_M(items[0])
                    self.s5_A(items[0])
                    for c in range(ni + 3):
                        if c + 1 < ni:
                            self.s5_M(items[c + 1])
                        if 0 <= c - 1 < ni:
                            self.s5_rot(items[c - 1])
                        if 0 <= c - 3 < ni:
                            self.s5_Yev(items[c - 3])
                        if 0 <= c - 2 < ni:
                            self.s5_Ymm(items[c - 2])
                        if c < ni:
                            self.s5_S(items[c], items[c - 1] if c > 0 else None)
                            self.s5_E(items[c])
                            if c + 1 < ni:
                                self.s5_A(items[c + 1])
                            yield 2.05 + (0.85 if items[c]["ypos"] is not None else 0.0)

    def gen_att(self, st, bk):
        I = self.I
        Lp, S, Ls = self.Lp, self.S, self.Ls
        SC = 128.0 ** -0.5
        wv = self.W["w_in"].h.rearrange("(j p) c -> p j c", p=128)
        ws = [self.sb(st, f"aws{i}", [128, 8, 512], BF16) for i in range(2)]
        wsi = [0]

        def wload(src_ap, rd_):
            t = ws[wsi[0] % 2]
            wsi[0] += 1
            self.load(t[:], src_ap, r=[rd_], w=[t])
            return t
        xt = [self.sb(st, f"axt{i}", [128, D], F32) for i in range(2)]
        xn = self.sb(st, "axn", [128, 4, D], BF16)
        ss = self.sb(st, "ass", [128, 4], F32)
        hT = self.sb(st, "ahT", [128, 8, 512], BF16)
        cs = self.sb(st, "acs", [128, 4, 128], F32)
        tabs = [self.sb(st, f"atabs{i}", [128, 4, 64], F32) for i in range(2)]
        sq = self.sb(st, "asq", [128, 512], F32)
        qss = [self.sb(st, f"aqss{i}", [128, 8], F32) for i in range(2)]
        qa = self.sb(st, "aqa", [128, 512], F32)
        t4 = self.sb(st, "at4", [128, 4, 256], F32)
        qrot = [self.sb(st, f"aqrot{i}", [128, 8, 128], BF16) for i in range(2)]
        QT = self.sb(st, "aQT", [128, 8, 512], BF16)
        GA = self.sb(st, "aGA", [128, 8, 512], BF16)
        YAh = [self.sb(st, f"aYAh{i}", [128, 512], BF16) for i in range(2)]
        KC = 512
        kts = [self.sb(st, f"akts{i}", [128, KC], BF16) for i in range(3)]
        vas = [self.sb(st, f"avas{i}", [128, KC // 128, 129], BF16) for i in range(3)]
        PT = [self.sb(st, f"aPT{i}", [128, 512], BF16) for i in range(3)]
        rd = self.sb(st, "ard", [128, 512], F32)
        yx = self.sb(st, "ayx", [128, 512], F32)

        def bf(b):
            return b[:].bitcast(BF16)
        seqs = [("p", I["xp"], I["csp"], 0, Lp, Lp, 0), ("s", I["xs"], I["css"], 7 * S, S, Ls, Lp)]
        nslot = 0
        for key, xsrc, cssrc, xoff, nown, Lk, yoff in seqs:
            for t in range(nown // 512):
                t0 = xoff + t * 512
                yo = yoff + t * 512
                self.make_hT_bank(xsrc[t0:t0 + 512, :], xt, ss, xn, [bk[2], bk[3]], hT, self.g_in)
                self.load(cs[:], cssrc[t0:t0 + 512, :].rearrange("(b p) c -> p b c", p=128), w=[cs])
                yield 12.0
                wq0 = wload(wv[:, :, C_Q:C_Q + 512], self.W["w_in"])
                wq1 = wload(wv[:, :, C_Q + 512:C_Q + 1024], self.W["w_in"])
                for b in range(4):
                    pa, pb, pT = bk[0], bk[1], bk[2 + b % 2]
                    for j in range(8):
                        self.mm(pa[:, :], hT[:, j, b * 128:(b + 1) * 128], wq0[:, j, :], j == 0, j == 7, r=[hT, wq0], w=[pa])
                    for j in range(8):
                        self.mm(pb[:, :], hT[:, j, b * 128:(b + 1) * 128], wq1[:, j, :], j == 0, j == 7, r=[hT, wq1], w=[pb])
                    tb_ = tabs[b % 2]
                    self.rope_tables(cs, b, self.g_q, tb_)
                    qr = qrot[b % 2]
                    for half, pbank in ((0, pa), (1, pb)):
                        self.norm_rope_q(pbank, half, sq, qss[b % 2], qa, t4, tb_, qr)
                    for h in range(8):
                        self.tr(bf(pT)[:, h * 128:(h + 1) * 128], qr[:, h, :], self.ident_b[:],
                                r=[qr, self.ident_b], w=[pT])
                    self.cp(QT[:, :, b * 128:(b + 1) * 128], bf(pT).rearrange("p (h q) -> p h q", h=8),
                            r=[pT], w=[QT], eng="act")
                    yield 5.0
                for half in range(2):
                    wg = wload(wv[:, :, C_GA + half * 512:C_GA + (half + 1) * 512], self.W["w_in"])
                    for o in range(4):
                        p = bk[o % 2]
                        for j in range(8):
                            self.mm(p[:, :], wg[:, j, o * 128:(o + 1) * 128], hT[:, j, :], j == 0, j == 7, r=[wg, hT], w=[p])
                        self.act(GA[:, half * 4 + o, :], p[:, :], AF.Silu, r=[p], w=[GA])
                    yield 8.0
                nkc = Lk // KC
                nk = Lk // 128
                kpc = KC // 128
                for h in range(8):
                    hk = h // 4
                    pO, pD = bk[2], bk[3]
                    cur = {}
                    for idx in range(nk + 1):
                        if idx < nk:
                            c, kk = idx // kpc, idx % kpc
                            if kk == 0:
                                sl = nslot % 3
                                nslot += 1
                                kt_, va_ = kts[sl], vas[sl]
                                self.load(kt_[:], self.KT[key].h[hk, :, c * KC:(c + 1) * KC], r=[self.KT[key]], w=[kt_])
                                self.load(va_[:], self.VA[key].h[hk, :, c * kpc:(c + 1) * kpc, :],
                                          r=[self.VA[key]], w=[va_])
                                cur[c] = (kt_, va_)
                            kt_, va_ = cur[c]
                            psb = bk[idx % 2]
                            pt = PT[idx % 3]
                            self.mm(psb[:, :], kt_[:, kk * 128:(kk + 1) * 128], QT[:, h, :], True, True,
                                    r=[kt_, QT], w=[psb])
                            self.act(pt[:], psb[:, :], AF.Exp, r=[psb], w=[pt], scale=SC)
                        if idx >= 1:
                            jx = idx - 1
                            c, kk = jx // kpc, jx % kpc
                            kt_, va_ = cur[c]
                            pt = PT[jx % 3]
                            self.mm(pO[:, :], va_[:, kk, 0:128], pt[:], jx == 0, jx == nk - 1, r=[va_, pt], w=[pO])
                            self.mm(pD[:, :], self.ones_b[:], pt[:], jx == 0, jx == nk - 1, r=[self.ones_b, pt], w=[pD])
                        yield 1.15
                    self.act(rd[:], pD[:, :], AF.Ln, r=[pD], w=[rd])
                    self.act(rd[:], rd[:], AF.Exp, r=[rd], w=[rd], scale=-1.0)
                    self.tt(yx[:], pO[:, :], rd[:], ALU.mult, r=[pO, rd], w=[yx])
                    ya = YAh[h % 2]
                    self.tt(ya[:], yx[:], GA[:, h, :], ALU.mult, r=[yx, GA], w=[ya])
                    self.store(self.YAs.h[h * 128:(h + 1) * 128, yo:yo + 512], ya[:], r=[ya], w=[self.YAs])
                    yield 1.0

    def reduce_angle(self, th, kk, thr):
        self.ts(kk[:], th[:], 1.0 / TWO_PI, None, ALU.mult, r=[th], w=[kk])
        self.ts(kk[:], kk[:], MAGIC, None, ALU.add, r=[kk], w=[kk])
        self.ts(kk[:], kk[:], -MAGIC, None, ALU.add, r=[kk], w=[kk])
        self.stt(thr[:], kk[:], -CW1, th[:], ALU.mult, ALU.add, r=[kk, th], w=[thr])
        self.stt(thr[:], kk[:], -CW2, thr[:], ALU.mult, ALU.add, r=[kk, thr], w=[thr])
        self.ts(thr[:], thr[:], math.pi, -math.pi, ALU.min, ALU.max, r=[thr], w=[thr])

    def sincos(self, thr, ab, sn, cs_):
        self.act(sn[:], thr[:], AF.Sin, r=[thr], w=[sn])
        self.act(ab[:], thr[:], AF.Sin, r=[thr], w=[ab], scale=0.5)
        self.tt(cs_[:], ab[:], ab[:], ALU.mult, r=[ab], w=[cs_])
        self.ts(cs_[:], cs_[:], -2.0, 1.0, ALU.mult, ALU.add, r=[cs_], w=[cs_])

    def s5_prep(self, s_, col, iota1, ones_f, Rv, thr, CR, CI, SCI, NSCR):
        I = self.I
        c1 = slice(col, col + 1)
        self.load(s_.Bf[:, 0, :], I["B1"][col, :, :], w=[s_.Bf])
        self.load(s_.Bf[:, 1, :], I["B2"][col, :, :], w=[s_.Bf])
        self.load(s_.Cf[:, 0, :], I["C1"][col, :, :], w=[s_.Cf])
        self.load(s_.Cf[:, 1, :], I["C2"][col, :, :], w=[s_.Cf])
        self.cp(s_.Bb[:], s_.Bf[:], r=[s_.Bf], w=[s_.Bb], eng="act")
        self.cp(s_.Cb[:], s_.Cf[:], r=[s_.Cf], w=[s_.Cb], eng="act")
        self.ts(s_.phi[:], iota1[:], thr[:, c1], None, ALU.mult, r=[iota1, thr], w=[s_.phi])
        self.reduce_angle(s_.phi, s_.k2, s_.phi)
        self.sincos(s_.phi, s_.k2, s_.sinp, s_.cosp)
        self.ts(s_.rb[:], s_.sinp[:, PIECE - 1:PIECE], self.sgn[:, 1:2], None, ALU.mult, r=[s_.sinp, self.sgn], w=[s_.rb])
        self.ts(s_.rot[:], self.ident_f[:], s_.cosp[:, PIECE - 1:PIECE], None, ALU.mult, r=[self.ident_f, s_.cosp], w=[s_.rot])
        self.stt(s_.rot[:], self.swap_f[:], s_.rb[:, 0:1], s_.rot[:], ALU.mult, ALU.add,
                 r=[self.swap_f, s_.rb, s_.rot], w=[s_.rot])
        self.act(s_.Rd[:], ones_f[:], AF.Copy, r=[ones_f, Rv], w=[s_.Rd], scale=Rv[:, c1])

    def s5_prep_G(self, s_, col):
        G = self.G
        c1 = slice(col, col + 1)
        n1 = PIECE - 1
        mag, GA, GB = s_.k2, s_.RC, s_.RS
        crv = rev_ap(s_.cosp[:, 0:n1])
        srv = rev_ap(s_.sinp[:, 0:n1])
        self.act(mag[:], G["iotaR"][:], AF.Exp, r=[G["iotaR"], G["RL"]], w=[mag], scale=G["RL"][:, c1])
        self.ts(GA[:, 0:n1], crv, G["CR"][:, c1], None, ALU.mult, r=[s_.cosp, G["CR"]], w=[GA])
        self.stt(GA[:, 0:n1], srv, G["NCI"][:, c1], GA[:, 0:n1], ALU.mult, ALU.add, r=[s_.sinp, G["NCI"], GA], w=[GA])
        self.ts(GA[:, n1:PIECE], G["CR"][:, c1], 1.0, None, ALU.mult, r=[G["CR"]], w=[GA])
        self.tt(GA[:], GA[:], mag[:], ALU.mult, r=[GA, mag], w=[GA])
        self.ts(GB[:, 0:n1], crv, G["SCI"][:, c1], None, ALU.mult, r=[s_.cosp, G["SCI"]], w=[GB])
        self.stt(GB[:, 0:n1], srv, G["SCR"][:, c1], GB[:, 0:n1], ALU.mult, ALU.add, r=[s_.sinp, G["SCR"], GB], w=[GB])
        self.ts(GB[:, n1:PIECE], G["SCI"][:, c1], 1.0, None, ALU.mult, r=[G["SCI"]], w=[GB])
        self.tt(GB[:], GB[:], mag[:], ALU.mult, r=[GB, mag], w=[GB])
        self.ts(s_.rotD[:], s_.rot[:], G["RP"][:, c1], None, ALU.mult, r=[s_.rot, G["RP"]], w=[s_.rotD])

    def s5_prep_T(self, s_, col, CR, CI, SCI, NSCR):
        c1 = slice(col, col + 1)
        self.ts(s_.TA[:], s_.cosp[:], CR[:, c1], None, ALU.mult, r=[s_.cosp, CR], w=[s_.TA])
        self.stt(s_.TA[:], s_.sinp[:], CI[:, c1], s_.TA[:], ALU.mult, ALU.add, r=[s_.sinp, CI, s_.TA], w=[s_.TA])
        self.ts(s_.TB[:], s_.cosp[:], SCI[:, c1], None, ALU.mult, r=[s_.cosp, SCI], w=[s_.TB])
        self.stt(s_.TB[:], s_.sinp[:], NSCR[:, c1], s_.TB[:], ALU.mult, ALU.add, r=[s_.sinp, NSCR, s_.TB], w=[s_.TB])
        self.ts(s_.RC[:], s_.cosp[:], self.sgn[:, 1:2], None, ALU.mult, r=[s_.cosp, self.sgn], w=[s_.RC])
        self.act(s_.RS[:], s_.sinp[:], AF.Copy, r=[s_.sinp], w=[s_.RS], scale=-1.0)

    def s5_M(self, it):
        s_ = it["s"]
        k = s_.nchunk % 2
        s_.nchunk += 1
        it["k"] = k
        pp = s_.pp[k]
        src, jj, t0 = it["src"], it["j"], it["t0"]
        piece = t0 // s_.uch
        ukey = (id(src), jj, piece)
        if s_.ucur != ukey:
            ub = s_.ub[s_.ucnt % 2]
            s_.ucnt += 1
            self.load(ub[:], src.h[jj * 128:(jj + 1) * 128, piece * s_.uch:(piece + 1) * s_.uch], r=[src], w=[ub])
            s_.ucur = ukey
            s_.ubcur = ub
        ut = s_.ubcur
        off = t0 - piece * s_.uch
        rhs = ut[:, off:off + TS]
        if it["rev"]:
            rhs = rev_ap(rhs)
        self.mm(pp[:, 0:TS], s_.Bb[:, 0, :], rhs, True, True, r=[s_.Bb, ut], w=[pp])
        self.mm(pp[:, TS:2 * TS], s_.Bb[:, 1, :], rhs, True, True, r=[s_.Bb, ut], w=[pp])

    def s5_PR(self, it):
        s_ = it["s"]
        k = it["k"]
        pp, junk = s_.pp[k], s_.t2[k]
        ps_ = slice(it["gcol"] * TS, (it["gcol"] + 1) * TS)
        za = s_.zacc[it["q"] % 2]
        j = it["gcol"]
        self.P.op("dve", lambda e: e.scalar_tensor_tensor(out=junk[:], in0=pp[:, 0:TS], scalar=1.0, in1=s_.RC[:, ps_],
                                                         op0=ALU.mult, op1=ALU.mult,
                                                         accum_out=za[:, 2 * j:2 * j + 1]),
                  _bufs([pp, s_.RC]), _bufs([junk, za]))
        self.P.op("dve", lambda e: e.scalar_tensor_tensor(out=junk[:], in0=pp[:, TS:2 * TS], scalar=1.0, in1=s_.RS[:, ps_],
                                                         op0=ALU.mult, op1=ALU.mult,
                                                         accum_out=za[:, 2 * j + 1:2 * j + 2]),
                  _bufs([pp, s_.RS]), _bufs([junk, za]))

    def s5_piece_sum(self, s_, q, c0):
        za, zq = s_.zacc[q % 2], s_.zq[q % 2]
        self.P.op("dve", lambda e: e.tensor_reduce(out=zq[:], in_=za[:, 2 * c0:2 * NPC], axis=AX.X, op=ALU.add),
                  _bufs([za]), _bufs([zq]))

    def s5_horner(self, s_, q, first):
        zq = s_.zq[q % 2]
        if first:
            self.ts(s_.X[:], zq[:], 1.0, None, ALU.mult, r=[zq], w=[s_.X])
        else:
            self.mm(s_.prot[:, 0:1], s_.rotD[:], s_.X[:], True, True, r=[s_.rotD, s_.X], w=[s_.py])
            self.tt(s_.X[:], s_.prot[:, 0:1], zq[:], ALU.add, r=[s_.py, zq], w=[s_.X])

    def s5_A(self, it):
        s_ = it["s"]
        k = it["k"]
        pp, t2, dd = s_.pp[k], s_.t2[k], s_.dd[k]
        ps_ = slice(it["p"] * TS, (it["p"] + 1) * TS)
        self.tt(pp[:, 0:TS], pp[:, 0:TS], s_.TA[:, ps_], ALU.mult, r=[pp, s_.TA], w=[pp])
        self.tt(t2[:], pp[:, TS:2 * TS], s_.TB[:, ps_], ALU.mult, r=[pp, s_.TB], w=[t2])
        self.tt(dd[:], pp[:, 0:TS], t2[:], ALU.add, r=[pp, t2], w=[dd])

    def s5_S(self, it, prev):
        s_ = it["s"]
        k = it["k"]
        dd = s_.dd[k]
        h = it["h"]
        out = s_.pw[:, h * TS:(h + 1) * TS]
        rds = [s_.Rd, dd]
        if it["first"] and it.get("fromX"):
            init = s_.X[:, 0:1]
            rds.append(s_.X)
        elif it["first"]:
            init = 0.0
        elif prev["p"] == NPC - 1:
            init = s_.prot[:, 0:1]
            rds.append(s_.py)
        else:
            ph = prev["h"]
            init = s_.pw[:, ph * TS + TS - 1:ph * TS + TS]
        self.P.op("dve", lambda e: e.tensor_tensor_scan(out=out, data0=s_.Rd[:], data1=dd[:],
                                                        initial=init, op0=ALU.mult, op1=ALU.add),
                  _bufs(rds), _bufs([s_.pw]))
        if not it["last"] and it["p"] == NPC - 1:
            self.ts(s_.wl[:], s_.pw[:, h * TS + TS - 1:h * TS + TS], 1.0, None, ALU.mult, r=[s_.pw], w=[s_.wl])

    def s5_rot(self, it):
        s_ = it["s"]
        if not it["last"] and it["p"] == NPC - 1:
            self.mm(s_.prot[:, 0:1], s_.rot[:], s_.wl[:], True, True, r=[s_.rot, s_.wl], w=[s_.py])

    def s5_E(self, it):
        s_ = it["s"]
        k = it["k"]
        if it["ypos"] is None:
            return
        h = it["h"]
        e1, e2 = s_.e1[it["k3"]], s_.e2[it["k3"]]
        ps_ = slice(it["p"] * TS, (it["p"] + 1) * TS)
        self.tt(e1[:], s_.pw[:, h * TS:(h + 1) * TS], s_.RC[:, ps_], ALU.mult, r=[s_.pw, s_.RC], w=[e1])
        self.tt(e2[:], s_.pw[:, h * TS:(h + 1) * TS], s_.RS[:, ps_], ALU.mult, r=[s_.pw, s_.RS], w=[e2])

    def s5_Ymm(self, it):
        s_ = it["s"]
        k = it["k"]
        if it["ypos"] is None:
            return
        e1, e2 = s_.e1[it["k3"]], s_.e2[it["k3"]]
        py = s_.py[0:16, 0:TS]
        self.mm(py, s_.Cb[:, 0, :], e1[:], True, False, r=[s_.Cb, e1], w=[s_.py])
        self.mm(py, s_.Cb[:, 1, :], e2[:], False, True, r=[s_.Cb, e2], w=[s_.py])

    def s5_Yev(self, it):
        s_ = it["s"]
        ypos, rev, d, g = it["ypos"], it["rev"], it["d"], it["g"]
        if ypos is None:
            return
        py = s_.py[0:16, 0:TS]
        nper = 512 // TS
        sidx = s_.nstg // nper
        stg = s_.ystg[sidx % 2]
        q = s_.nstg % nper
        s_.nstg += 1
        base = (ypos // 512) * 512
        off = ypos - base
        dst = stg[0:16, off:off + TS]
        if rev:
            dst = rev_ap(dst)
        self.cp(dst, py, r=[s_.py], w=[stg], eng="act")
        if q == nper - 1:
            self.store(self.YS[d].h[g * 16:(g + 1) * 16, base:base + 512], stg[0:16, :], r=[stg], w=[self.YS[d]])

    def phase3(self, st):
        I = self.I
        Lp, S, Ls = self.Lp, self.S, self.Ls
        SC = 128.0 ** -0.5
        wv = self.W["w_in"].h.rearrange("(j p) c -> p j c", p=128)
        ws = [self.sb(st, f"ws{i}", [128, 8, 512], BF16) for i in range(3)]
        self.wsi = 0

        def wload(src_ap, rd):
            t = ws[self.wsi % 3]
            self.wsi += 1
            self.load(t[:], src_ap, r=[rd], w=[t])
            return t

        xt = [self.sb(st, f"xt{i}", [128, D], F32) for i in range(2)]
        xn = self.sb(st, "xn", [128, 4, D], BF16)
        ss = self.sb(st, "ss", [128, 4], F32)
        hT = self.sb(st, "hT", [128, 8, 512], BF16)
        cs = self.sb(st, "cs", [128, 4, 128], F32)
        tabs = [self.sb(st, f"tabs{i}", [128, 4, 64], F32) for i in range(2)]
        sq = self.sb(st, "sq", [128, 512], F32)
        qss = [self.sb(st, f"qss{i}", [128, 8], F32) for i in range(2)]
        qa = self.sb(st, "qa", [128, 512], F32)
        t4 = self.sb(st, "t4", [128, 4, 256], F32)
        qrot = [self.sb(st, f"qrot{i}", [128, 8, 128], BF16) for i in range(2)]
        QT = self.sb(st, "QT", [128, 8, 512], BF16)
        GA = self.sb(st, "GA", [128, 8, 512], BF16)
        YA = self.sb(st, "YA", [128, 8, 512], BF16)
        KC = 512
        kts = [self.sb(st, f"kts{i}", [128, KC], BF16) for i in range(3)]
        vas = [self.sb(st, f"vas{i}", [128, KC // 128, 129], BF16) for i in range(3)]
        PT = [self.sb(st, f"PT{i}", [128, 512], BF16) for i in range(3)]
        rcp = self.sb(st, "rcp", [128, 4], F32)
        yn = [self.sb(st, f"yn{i}", [128, 128], BF16) for i in range(2)]
        y0 = [self.sb(st, f"y0_{i}", [128, 512], F32) for i in range(1)] * 2
        y1 = [self.sb(st, f"y1_{i}", [128, 512], F32) for i in range(1)] * 2
        uu = [self.sb(st, f"uu{i}", [128, 512], BF16) for i in range(2)]
        gx = [self.sb(st, f"gx{i}", [128, 512], F32) for i in range(2)]
        g2 = [self.sb(st, f"g2{i}", [128, 512], F32) for i in range(2)]
        YG = self.sb(st, "YG", [128, 4, 512], F32)
        YGb = self.sb(st, "YGb", [128, 4, 512], BF16)
        GS = self.sb(st, "GS", [128, 4, 512], BF16)
        sgl = [self.sb(st, f"sgl{i}", [128, 512], F32) for i in range(2)]
        YSb = self.sb(st, "YSb", [128, 4, 512], BF16)
        wglu = self.sb(st, "wglu", [128, 4, 512], BF16)
        self.load(wglu[:], self.W["w_glu"].h.rearrange("(k p) c -> p k c", p=128), r=[self.W["w_glu"]], w=[wglu])
        QX = self.sb(st, "QX", [128, 4, 512], BF16)
        GX = self.sb(st, "GX", [128, 4, 512], BF16)
        PX = [self.sb(st, f"PX{i}", [128, 512], BF16) for i in range(2)]
        rd = self.sb(st, "rd", [128, 512], F32)
        yx = self.sb(st, "yx", [128, 512], F32)
        YX = self.sb(st, "YX", [128, 4, 512], BF16)
        G3 = self.sb(st, "G3", [128, 3, 4, 512], BF16)
        m = [self.sb(st, f"m{i}", [128, 512], F32) for i in range(3)]
        M = self.sb(st, "M", [128, 8, 512], BF16)
        yres = [self.sb(st, f"yres{i}", [128, D], F32) for i in range(1)] * 2
        fss = [self.sb(st, f"fss{i}", [128, 1], F32) for i in range(2)]
        gf = self.sb(st, "gf", [128, D], F32)
        self.load(gf[:], I["g_f"][:, :], w=[gf])
        bk = [self.ps(st, f"bk{i}", [128, 512], F32) for i in range(8)]

        def bf(b):
            return b[:].bitcast(BF16)

        seqs = [("p", I["xp"], I["csp"], 0, Lp, Lp, 0), ("s", I["xs"], I["css"], 7 * S, S, Ls, Lp)]
        for key, xsrc, cssrc, xoff, nown, Lk, yoff in seqs:
            for t in range(nown // 512):
                t0 = xoff + t * 512
                yo = yoff + t * 512
                self.make_hT_bank(xsrc[t0:t0 + 512, :], xt, ss, xn, [bk[6], bk[7]], hT, self.g_in)
                self.load(cs[:], cssrc[t0:t0 + 512, :].rearrange("(b p) c -> p b c", p=128), w=[cs])
                self.load(YA[:], self.YAs.h[:, yo:yo + 512].rearrange("(h p) l -> p h l", p=128), r=[self.YAs], w=[YA])
                wgs = wload(wv[:, :, C_GS:C_GS + 512], self.W["w_in"])
                for i in range(4):
                    k2 = i % 2
                    self.load(y0[k2][:], self.YS[0].h[i * 128:(i + 1) * 128, yo:yo + 512], r=[self.YS[0]], w=[y0[k2]])
                    self.load(y1[k2][:], self.YS[1].h[i * 128:(i + 1) * 128, yo:yo + 512], r=[self.YS[1]], w=[y1[k2]])
                    usrc = self.UT["p"] if key == "p" else self.UT["sf"]
                    self.load(uu[k2][:], usrc.h[i * 128:(i + 1) * 128, t0:t0 + 512], r=[usrc], w=[uu[k2]])
                    a, bq = gx[k2], g2[k2]
                    self.tt(a[:], y0[k2][:], y1[k2][:], ALU.add, r=[y0[k2], y1[k2]], w=[a])
                    self.stt(a[:], uu[k2][:], self.s5d[:, i:i + 1], a[:], ALU.mult, ALU.add, r=[uu[k2], self.s5d, a], w=[a])
                    self.tt(bq[:], a[:], a[:], ALU.mult, r=[a], w=[bq])
                    self.ts(bq[:], bq[:], 0.044715, 1.0, ALU.mult, ALU.add, r=[bq], w=[bq])
                    self.tt(bq[:], bq[:], a[:], ALU.mult, r=[bq, a], w=[bq])
                    self.act(bq[:], bq[:], AF.Sigmoid, r=[bq], w=[bq], scale=2.0 * math.sqrt(2.0 / math.pi))
                    self.tt(YG[:, i, :], a[:], bq[:], ALU.mult, r=[a, bq], w=[YG])
                    self.cp(YGb[:, i, :], YG[:, i, :], r=[YG], w=[YGb], eng="act")
                    p = bk[i % 2]
                    for j in range(8):
                        self.mm(p[:, :], wgs[:, j, i * 128:(i + 1) * 128], hT[:, j, :], j == 0, j == 7, r=[wgs, hT], w=[p])
                    self.act(GS[:, i, :], p[:, :], AF.Silu, r=[p], w=[GS])
                for o in range(4):
                    p = bk[2 + o % 2]
                    for k_ in range(4):
                        self.mm(p[:, :], wglu[:, k_, o * 128:(o + 1) * 128], YGb[:, k_, :], k_ == 0, k_ == 3,
                                r=[wglu, YGb], w=[p])
                    s_ = sgl[o % 2]
                    self.act(s_[:], p[:, :], AF.Sigmoid, r=[p, self.bglu], w=[s_], bias=self.bglu[:, o:o + 1])
                    self.tt(s_[:], s_[:], YG[:, o, :], ALU.mult, r=[s_, YG], w=[s_])
                    self.tt(YSb[:, o, :], s_[:], GS[:, o, :], ALU.mult, r=[s_, GS], w=[YSb])
                wqx = wload(wv[:, :, C_QX:C_QX + 512], self.W["w_in"])
                wgx = wload(wv[:, :, C_GX:C_GX + 512], self.W["w_in"])
                for o in range(4):
                    p = bk[o % 2]
                    for j in range(8):
                        self.mm(p[:, :], wqx[:, j, o * 128:(o + 1) * 128], hT[:, j, :], j == 0, j == 7, r=[wqx, hT], w=[p])
                    self.cp(QX[:, o, :], p[:, :], r=[p], w=[QX], eng="act")
                    p2 = bk[2 + o % 2]
                    for j in range(8):
                        self.mm(p2[:, :], wgx[:, j, o * 128:(o + 1) * 128], hT[:, j, :], j == 0, j == 7, r=[wgx, hT], w=[p2])
                    self.act(GX[:, o, :], p2[:, :], AF.Silu, r=[p2], w=[GX])
                KmT, Vm = self.KmT[key], self.Vm[key]
                for hx in range(4):
                    for mt in range(2):
                        p = bk[mt]
                        self.mm(p[:, :], KmT[:, hx, mt * 128:(mt + 1) * 128], QX[:, hx, :], True, True, r=[KmT, QX], w=[p])
                        self.act(PX[mt][:], p[:, :], AF.Exp, r=[p], w=[PX[mt]], scale=SC)
                    po_, pd_ = bk[4], bk[5]
                    for mt in range(2):
                        self.mm(po_[:, :], Vm[:, mt, hx * 128:(hx + 1) * 128], PX[mt][:], mt == 0, mt == 1, r=[Vm, PX[mt]], w=[po_])
                    for mt in range(2):
                        self.mm(pd_[:, :], self.ones_b[:], PX[mt][:], mt == 0, mt == 1, r=[self.ones_b, PX[mt]], w=[pd_])
                    self.recip(rd[:], pd_[:, :], r=[pd_], w=[rd])
                    self.tt(yx[:], po_[:, :], rd[:], ALU.mult, r=[po_, rd], w=[yx])
                    self.tt(YX[:, hx, :], yx[:], GX[:, hx, :], ALU.mult, r=[yx, GX], w=[YX])
                wpa = self.W["w_pa"].h.rearrange("(k p) c -> p k c", p=128)
                wps = self.W["w_ps"].h.rearrange("(k p) c -> p k c", p=128)
                wpx = self.W["w_px"].h.rearrange("(k p) c -> p k c", p=128)
                for og in range(2):
                    for br in range(3):
                        wm_ = wload(wv[:, :, C_MG + br * 1024 + og * 512:C_MG + br * 1024 + (og + 1) * 512], self.W["w_in"])
                        for o in range(4):
                            p = bk[o % 2]
                            for j in range(8):
                                self.mm(p[:, :], wm_[:, j, o * 128:(o + 1) * 128], hT[:, j, :], j == 0, j == 7, r=[wm_, hT], w=[p])
                            self.act(G3[:, br, o, :], p[:, :], AF.Sigmoid, r=[p], w=[G3])
                    wa = wload(wpa[:, :, og * 512:(og + 1) * 512], self.W["w_pa"])
                    wsx = ws[self.wsi % 3]
                    self.wsi += 1
                    self.load(wsx[:, 0:4, :], wps[:, :, og * 512:(og + 1) * 512], r=[self.W["w_ps"]], w=[wsx])
                    self.load(wsx[:, 4:8, :], wpx[:, :, og * 512:(og + 1) * 512], r=[self.W["w_px"]], w=[wsx])
                    for o in range(4):
                        pa_, ps_, px_ = bk[2 + (o % 2) * 3], bk[3 + (o % 2) * 3], bk[4 + (o % 2) * 3]
                        for k_ in range(8):
                            self.mm(pa_[:, :], wa[:, k_, o * 128:(o + 1) * 128], YA[:, k_, :], k_ == 0, k_ == 7, r=[wa, YA], w=[pa_])
                        for k_ in range(4):
                            self.mm(ps_[:, :], wsx[:, k_, o * 128:(o + 1) * 128], YSb[:, k_, :], k_ == 0, k_ == 3, r=[wsx, YSb], w=[ps_])
                        for k_ in range(4):
                            self.mm(px_[:, :], wsx[:, 4 + k_, o * 128:(o + 1) * 128], YX[:, k_, :], k_ == 0, k_ == 3, r=[wsx, YX], w=[px_])
                        self.tt(m[0][:], pa_[:, :], G3[:, 0, o, :], ALU.mult, r=[pa_, G3], w=[m[0]])
                        self.tt(m[1][:], ps_[:, :], G3[:, 1, o, :], ALU.mult, r=[ps_, G3], w=[m[1]])
                        self.tt(m[2][:], px_[:, :], G3[:, 2, o, :], ALU.mult, r=[px_, G3], w=[m[2]])
                        self.tt(m[0][:], m[0][:], m[1][:], ALU.add, r=[m[0], m[1]], w=[m[0]])
                        self.tt(M[:, og * 4 + o, :], m[0][:], m[2][:], ALU.add, r=[m[0], m[2]], w=[M])
                wo_ = self.W["w_out"].h.rearrange("(k p) c -> p k c", p=128)
                wo0 = wload(wo_[:, :, 0:512], self.W["w_out"])
                wo1 = wload(wo_[:, :, 512:1024], self.W["w_out"])
                for b in range(4):
                    pa_, pb_ = bk[(2 * b) % 4], bk[(2 * b + 1) % 4]
                    for k_ in range(8):
                        self.mm(pa_[:, :], M[:, k_, b * 128:(b + 1) * 128], wo0[:, k_, :], k_ == 0, k_ == 7, r=[M, wo0], w=[pa_])
                    for k_ in range(8):
                        self.mm(pb_[:, :], M[:, k_, b * 128:(b + 1) * 128], wo1[:, k_, :], k_ == 0, k_ == 7, r=[M, wo1], w=[pb_])
                    yr, fs = yres[b % 2], fss[b % 2]
                    xb = xt[b % 2]
                    self.load(xb[:], xsrc[t0 + b * 128:t0 + (b + 1) * 128, :], w=[xb])
                    self.tt(yr[:, 0:512], pa_[:, :], xb[:, 0:512], ALU.add, r=[pa_, xb], w=[yr])
                    self.tt(yr[:, 512:1024], pb_[:, :], xb[:, 512:1024], ALU.add, r=[pb_, xb], w=[yr])
                    self.act(xn[:, 0, :], yr[:], AF.Square, r=[yr], w=[xn, fs], accum_out=fs[:, 0:1])
                    self.rstd(fs, 1, 1.0 / D)
                    self.stt(yr[:], yr[:], fs[:, 0:1], gf[:], ALU.mult, ALU.mult, r=[yr, fs, gf], w=[yr])
                    self.store(self.y_out[yo + b * 128:yo + (b + 1) * 128, :], yr[:], r=[yr])

    def make_hT_bank(self, x_rows, xt, ss, xn, banks, hT, gain):
        for b in range(4):
            xb = xt[b % 2]
            sb_ = ss[b % 2] if isinstance(ss, list) else ss
            self.load(xb[:], x_rows[b * 128:(b + 1) * 128, :], w=[xb])
            self.act(xn[:, b, :], xb[:], AF.Square, r=[xb], w=[xn, ss], accum_out=ss[:, b:b + 1])
            v = ss[:, b:b + 1]
            self.ts(v, v, 1.0 / D, EPS, ALU.mult, ALU.add, r=[ss], w=[ss])
            self.act(v, v, AF.Sqrt, r=[ss], w=[ss])
            self.recip(v, v, r=[ss], w=[ss])
            if b % 2 == 0:
                self.act(xn[:, b, :], xb[:], AF.Copy, r=[xb, ss], w=[xn], scale=ss[:, b:b + 1])
            else:
                self.ts(xn[:, b, :], xb[:], ss[:, b:b + 1], None, ALU.mult, r=[xb, ss], w=[xn])
        for j in range(8):
            bank = banks[j % len(banks)]
            pv = bank[:].bitcast(BF16)
            for b in range(4):
                self.tr(pv[:, b * 128:(b + 1) * 128], xn[:, b, j * 128:(j + 1) * 128], self.ident_b[:],
                        r=[xn, self.ident_b], w=[bank])
            if j % 2 == 0:
                self.ts(hT[:, j, :], pv[:, 0:512], gain[:, j:j + 1], None, ALU.mult, r=[bank, gain], w=[hT])
            else:
                self.act(hT[:, j, :], pv[:, 0:512], AF.Copy, r=[bank, gain], w=[hT], scale=gain[:, j:j + 1])

    def norm_rope_q(self, pbank, half, sq, ssv, xa, t4, tabs, out_bf):
        nh = 4
        o = 0
        h0 = half * 4
        psrc = pbank[:, :]
        self.act(sq[:, o:o + 512], psrc, AF.Square, r=[pbank], w=[sq])
        self.P.op("dve", lambda e: e.tensor_reduce(out=ssv[:, h0:h0 + 4], in_=sq[:, o:o + 512].rearrange("p (h d) -> p h d", h=nh),
                                                   axis=AX.X, op=ALU.add), _bufs([sq]), _bufs([ssv]))
        v = ssv[:, h0:h0 + 4]
        self.ts(v, v, 1.0 / 128.0, EPS, ALU.mult, ALU.add, r=[ssv], w=[ssv])
        self.act(v, v, AF.Sqrt, r=[ssv], w=[ssv])
        self.recip(v, v, r=[ssv], w=[ssv])
        xa3 = xa[:, o:o + 512].rearrange("p (h d) -> p h d", h=nh)
        self.tt(xa3, psrc.rearrange("p (h d) -> p h d", h=nh), v.unsqueeze(2).to_broadcast([128, nh, 128]),
                ALU.mult, r=[pbank, ssv], w=[xa])
        x0 = xa[:, o:o + 512].rearrange("p (h i two) -> p h i two", h=nh, two=2)[:, :, :, 0]
        x1 = xa[:, o:o + 512].rearrange("p (h i two) -> p h i two", h=nh, two=2)[:, :, :, 1]
        ob = out_bf[:, h0:h0 + 4, :].rearrange("p h (i two) -> p h i two", two=2)
        o0, o1 = ob[:, :, :, 0], ob[:, :, :, 1]

        def tb(i):
            return tabs[:, i, :].unsqueeze(1).to_broadcast([128, nh, 64])
        tv = [t4[:, i, 0:256].rearrange("p (h i) -> p h i", h=nh) for i in range(4)]
        self.tt(tv[0], x0, tb(0), ALU.mult, r=[xa, tabs], w=[t4])
        self.tt(tv[1], x1, tb(1), ALU.mult, r=[xa, tabs], w=[t4])
        self.tt(tv[2], x0, tb(2), ALU.mult, r=[xa, tabs], w=[t4])
        self.tt(tv[3], x1, tb(3), ALU.mult, r=[xa, tabs], w=[t4])
        self.tt(o0, tv[0], tv[1], ALU.subtract, r=[t4], w=[out_bf])
        self.tt(o1, tv[2], tv[3], ALU.add, r=[t4], w=[out_bf])


def rope_table(pos):
    pos = np.asarray(pos)
    row = (pos // 64).astype(np.float32)
    col = (pos % 64).astype(np.float32)
    freqs = (np.float32(10000.0) ** (-np.arange(32, dtype=np.float32) / np.float32(32))).astype(np.float32)
    ang = np.concatenate([row[:, None] * freqs, col[:, None] * freqs], axis=-1).astype(np.float32)
    return np.concatenate([np.cos(ang), np.sin(ang)], axis=-1).astype(np.float32)


def host_inputs(inp, Lp, S, ncores=NCORES):
    f = lambda a: np.ascontiguousarray(np.asarray(a, dtype=np.float32))
    Ls = 8 * S
    xs_all = f(inp["x_sample"])[0]
    shared = {}
    shared["w_in"] = f(inp["w_in"])[0]
    shared["w_glu"] = f(inp["w_glu"])[0]
    shared["w_mem_kv"] = f(inp["w_mem_kv"])[0]
    shared["w_pa"] = f(inp["w_proj_attn"])[0]
    shared["w_ps"] = f(inp["w_proj_ssm"])[0]
    shared["w_px"] = f(inp["w_proj_cross"])[0]
    shared["w_out"] = f(inp["w_out"])[0]
    shared["g_in"] = f(f(inp["norm_in"])[0].reshape(8, 128).T)
    shared["g_mem"] = f(f(inp["norm_mem"])[0].reshape(8, 128).T)
    qn, kn = f(inp["q_norm"])[0], f(inp["k_norm"])[0]
    shared["g_q"] = f(np.tile(np.concatenate([qn[0::2], qn[1::2]])[None, :], (128, 1)))
    shared["g_k"] = f(np.tile(np.concatenate([kn[0::2], kn[1::2]])[None, :], (128, 1)))
    shared["g_f"] = f(np.tile(f(inp["norm_final"])[None, :], (128, 1)))
    shared["s5d"] = f(f(inp["s5_d"])[0].reshape(4, 128).T)
    shared["bglu"] = f(f(inp["b_glu"])[0].reshape(4, 128).T)
    a_re, a_im = f(inp["s5_a_re"])[0], f(inp["s5_a_im"])[0]
    dup = lambda a: f(np.concatenate([a.reshape(64, 64).T, a.reshape(64, 64).T], axis=0))
    shared["are"] = dup(a_re)
    shared["aim"] = dup(a_im)
    shared["lst"] = f(np.tile(f(inp["s5_log_step"])[0].reshape(1, 64), (128, 1)))
    b_re, b_im = f(inp["s5_b_re"])[0], f(inp["s5_b_im"])[0]
    c_re, c_im = f(inp["s5_c_re"])[0], f(inp["s5_c_im"])[0]
    B1 = np.zeros((64, 128, 128), np.float32)
    B2 = np.zeros((64, 128, 128), np.float32)
    C1 = np.zeros((64, 128, 16), np.float32)
    C2 = np.zeros((64, 128, 16), np.float32)
    for d in range(2):
        for g in range(32):
            col = d * 32 + g
            r0 = (g % 8) * 16
            B1[col, r0:r0 + 16, 0:64] = b_re[d, g].T
            B1[col, r0:r0 + 16, 64:128] = b_im[d, g].T
            B2[col, r0:r0 + 16, 0:64] = b_im[d, g].T
            B2[col, r0:r0 + 16, 64:128] = b_re[d, g].T
            C1[col, 0:64, :] = c_re[d, g].T
            C1[col, 64:128, :] = c_im[d, g].T
            C2[col, 0:64, :] = c_im[d, g].T
            C2[col, 64:128, :] = c_re[d, g].T
    shared.update(B1=B1, B2=B2, C1=C1, C2=C2)
    shared["ident"] = np.eye(128, dtype=np.float32)
    sw = np.zeros((128, 128), np.float32)
    sw[np.arange(128), (np.arange(128) + 64) % 128] = 1.0
    shared["swap"] = sw
    shared["iota1"] = f(np.tile(np.arange(1, PIECE + 1, dtype=np.float32)[None, :], (128, 1)))
    sg = np.ones((128, 2), np.float32)
    sg[0:64, 0] = -1.0
    sg[64:128, 1] = -1.0
    shared["sgn"] = sg
    shared["csp"] = rope_table(np.arange(Lp))
    maps = []
    for c in range(ncores):
        m = dict(shared)
        m["xp"] = f(inp["x_prompt"])[c]
        order = np.concatenate([np.arange((c + 1) * S, Ls), np.arange(0, c * S), np.arange(c * S, (c + 1) * S)])
        m["xs"] = np.ascontiguousarray(xs_all[order])
        m["css"] = rope_table(order)
        m["memp"] = f(inp["mem_prompt"])[c]
        m["mems"] = f(inp["mem_sample"])[0]
        mf = np.zeros((1, 7 * S), np.float32)
        mf[0, (7 - c) * S:] = 1.0
        m["mf"] = mf
        m["mb"] = (1.0 - mf).astype(np.float32)
        maps.append(m)
    return maps


_NC_CACHE = {}


def run(inp, Lp, S):
    key = (Lp, S)
    if key not in _NC_CACHE:
        _NC_CACHE[key] = Builder(Lp, S).build()
    nc = _NC_CACHE[key]
    maps = host_inputs(inp, Lp, S)
    res = run_bass_kernel_spmd(nc, maps, core_ids=list(range(NCORES)))
    ys = [np.asarray(r["y"]) for r in res.results]
    y_prompt = np.stack([y[:Lp] for y in ys], axis=0).astype(np.float32)
    y_sample = np.concatenate([y[Lp:Lp + S] for y in ys], axis=0)[None].astype(np.float32)
    return y_prompt, y_sample


def kernel(**inputs):
    Lp = int(np.asarray(inputs["x_prompt"]).shape[1])
    Ls = int(np.asarray(inputs["x_sample"]).shape[1])
    return run(inputs, Lp, Ls // 8)
```

```python
import math
from contextlib import ExitStack

import numpy as np
import concourse.bass as bass
import concourse.mybir as mybir
from concourse.bass_utils import run_bass_kernel_spmd

F32 = mybir.dt.float32
BF16 = mybir.dt.bfloat16
AF = mybir.ActivationFunctionType
ALU = mybir.AluOpType
AX = mybir.AxisListType

D = 1024
IN_W = 7680
C_Q, C_K, C_V, C_GA, C_U, C_GS, C_QX, C_GX, C_MG = 0, 1024, 1280, 1536, 2560, 3072, 3584, 4096, 4608
EPS = 1e-6
NCORES = 8
TS = 256
NPC = 4
PIECE = NPC * TS
MAGIC = 12582912.0
TWO_PI = 2.0 * math.pi
CW1 = 6.28125
CW2 = TWO_PI - CW1
SEM_CH = 20000
N_DMA_SEMS = 32


class Buf:
    __slots__ = ("lw", "rd", "ex")

    def __init__(self):
        self.lw = None
        self.rd = []
        self.ex = False


class T:
    def __init__(self, h, b=None):
        self.h = h
        self.b = b if b is not None else Buf()

    def __getitem__(self, k):
        return self.h[k]


def _bufs(xs):
    out = []
    for x in xs:
        if x is None:
            continue
        out.append(x.b if isinstance(x, T) else x)
    return out


class Prog:
    ENGS = ("pe", "act", "dve", "pool", "sp")

    def __init__(self, nc, stack):
        self.nc = nc
        self.stack = stack
        self.q = {e: [] for e in self.ENGS}
        self.cnt = {e: 0 for e in self.ENGS}
        self.sems = {e: [] for e in self.ENGS}
        self.dpool = {e: {"sems": [], "cnt": [], "rr": 0} for e in self.ENGS}
        self.n_dma = 0
        self.pend = {e: [] for e in self.ENGS}
        self.waited = {e: {} for e in self.ENGS}

    def _sem(self, e, idx):
        k = idx // SEM_CH
        while len(self.sems[e]) <= k:
            self.sems[e].append(self.stack.enter_context(
                self.nc.semaphore(f"s_{e}_{len(self.sems[e])}")))
        return self.sems[e][k], idx % SEM_CH + 1

    def op(self, e, fn, reads=(), writes=(), dma=False):
        reads = _bufs(reads)
        writes = _bufs(writes)
        exr = [b for b in reads if b.ex]
        if exr:
            reads = [b for b in reads if not b.ex]
            writes = writes + [b for b in exr if b not in writes]
        deps = {}

        def add(d):
            if d is None:
                return
            if d[0] not in deps or deps[d[0]][1] < d[1]:
                deps[d[0]] = d
        for b in reads:
            add(b.lw)
        for b in writes:
            add(b.lw)
            for r in b.rd:
                add(r)
        idx = self.cnt[e]
        if not dma:
            self.cnt[e] += 1
        waits = list(self.pend[e])
        self.pend[e] = []
        for key, d in deps.items():
            if key == "pe" and e == "pe":
                continue
            waits.append((d[2], d[3]))
        wd = self.waited[e]
        ww = []
        for (ws, wv) in waits:
            if wd.get(id(ws), 0) >= wv:
                continue
            wd[id(ws)] = wv
            ww.append((ws, wv))
        waits = ww
        if dma:
            dp = self.dpool[e]
            if len(dp["sems"]) < N_DMA_SEMS:
                dp["sems"].append(self.stack.enter_context(
                    self.nc.semaphore(f"s_dma_{e}_{len(dp['sems'])}")))
                dp["cnt"].append(0)
                i = len(dp["sems"]) - 1
            else:
                i = dp["rr"]
                dp["rr"] = (dp["rr"] + 1) % N_DMA_SEMS
            dsem = dp["sems"][i]
            if dp["cnt"][i] > 0 and wd.get(id(dsem), 0) < dp["cnt"][i]:
                wd[id(dsem)] = dp["cnt"][i]
                waits.append((dsem, dp["cnt"][i]))
            dp["cnt"][i] += 16
            me = ("dma%d" % self.n_dma, 0, dsem, dp["cnt"][i])
            self.n_dma += 1
            self.q[e].append((waits, fn, dsem, 16))
        else:
            s, v = self._sem(e, idx)
            me = (e, idx, s, v)
            self.q[e].append((waits, fn, s, 1))
        for b in reads:
            b.rd.append(me)
        for b in writes:
            b.lw = me
            b.rd = []
        return me

    def all_done_waits(self):
        final = []
        for dp in self.dpool.values():
            final += [(s, c) for s, c in zip(dp["sems"], dp["cnt"]) if c > 0]
        for e in self.ENGS:
            if self.cnt[e] > 0:
                final.append(self._sem(e, self.cnt[e] - 1))
        return final

    def barrier(self):
        w = self.all_done_waits()
        for e in self.ENGS:
            self.pend[e] = list(w)

    def emit(self, last=False):
        nc = self.nc
        prog = self
        final = self.all_done_waits() if last else []
        with nc.Block() as block:
            def run(eng, name):
                for waits, fn, s, inc in prog.q[name]:
                    for (ws, wv) in waits:
                        eng.wait_ge(ws, wv)
                    fn(eng).then_inc(s, inc)
                prog.q[name] = []

            @block.tensor
            def _(eng):
                run(eng, "pe")

            @block.scalar
            def _(eng):
                run(eng, "act")

            @block.vector
            def _(eng):
                run(eng, "dve")

            @block.gpsimd
            def _(eng):
                run(eng, "pool")

            @block.sync
            def _(eng):
                run(eng, "sp")
                for (ws, wv) in final:
                    eng.wait_ge(ws, wv)


def rev_ap(ap2d):
    a = ap2d.ap
    assert len(a) == 2, a
    n = a[1][1]
    st = a[1][0]
    return bass.AP(ap2d.tensor, ap2d.offset + st * (n - 1), [list(a[0]), [-st, n]])


class Builder:
    def __init__(self, Lp, S, dbg=False):
        self.Lp, self.S = Lp, S
        self.Ls = 8 * S
        self.Lo = Lp + S
        self.dbg = dbg
        self.nc = bass.Bass("TRN2", target_bir_lowering=False)

    def dram_in(self, name, shape, dt=F32):
        return self.nc.dram_tensor(name, list(shape), dt, kind="ExternalInput").ap()

    def dram_out(self, name, shape, dt=F32):
        return self.nc.dram_tensor(name, list(shape), dt, kind="ExternalOutput").ap()

    def dram_scr(self, name, shape, dt):
        kind = "ExternalOutput" if (self.dbg and name.split("_")[0] in str(self.dbg)) else "Internal"
        return T(self.nc.dram_tensor(name, list(shape), dt, kind=kind).ap())

    _uid = 0

    def sb(self, st, name, shape, dt):
        Builder._uid += 1
        return T(st.enter_context(self.nc.sbuf_tensor(f"sb{Builder._uid}_{name}", list(shape), dt)))

    def ps(self, st, name, shape, dt):
        Builder._uid += 1
        nbytes = int(np.prod(shape[1:])) * (4 if dt == F32 else 2)
        assert nbytes == 2048, (name, shape)
        t = T(st.enter_context(self.nc.psum_tensor(f"ps{Builder._uid}_{name}", list(shape), dt)))
        t.b.ex = True
        return t

    def load(self, out, in_, r=(), w=()):
        self.P.op("sp", lambda e: e.dma_start(out=out, in_=in_), r, w, dma=True)

    def store(self, out, in_, r=(), w=()):
        self.P.op("pool", lambda e: e.dma_start(out=out, in_=in_), r, w, dma=True)

    def mm(self, out, lhsT, rhs, start, stop, r=(), w=()):
        self.P.op("pe", lambda e: e.matmul(out, lhsT=lhsT, rhs=rhs, start=start, stop=stop), r, w)

    def tr(self, out, in_, ident, r=(), w=()):
        self.P.op("pe", lambda e: e.transpose(out, in_, ident), r, w)

    def act(self, out, in_, func, r=(), w=(), eng="act", **kw):
        self.P.op(eng, lambda e: e.activation(out=out, in_=in_, func=func, **kw), r, w)

    def tt(self, out, in0, in1, op, r=(), w=(), eng="dve"):
        self.P.op(eng, lambda e: e.tensor_tensor(out=out, in0=in0, in1=in1, op=op), r, w)

    def ts(self, out, in0, s1, s2, op0, op1=None, r=(), w=(), eng="dve"):
        if op1 is None:
            self.P.op(eng, lambda e: e.tensor_scalar(out=out, in0=in0, scalar1=s1, scalar2=None, op0=op0), r, w)
        else:
            self.P.op(eng, lambda e: e.tensor_scalar(out=out, in0=in0, scalar1=s1, scalar2=s2, op0=op0, op1=op1), r, w)

    def stt(self, out, in0, scalar, in1, op0, op1, r=(), w=()):
        self.P.op("dve", lambda e: e.scalar_tensor_tensor(out=out, in0=in0, scalar=scalar, in1=in1, op0=op0, op1=op1), r, w)

    def cp(self, out, in_, r=(), w=(), eng="dve"):
        if eng == "act":
            self.P.op("act", lambda e: e.activation(out=out, in_=in_, func=AF.Copy), r, w)
        elif eng == "dve":
            self.P.op("dve", lambda e: e.tensor_scalar(out=out, in0=in_, scalar1=1.0, scalar2=None, op0=ALU.mult), r, w)
        else:
            self.P.op(eng, lambda e: e.tensor_copy(out=out, in_=in_), r, w)

    def recip(self, out, in_, r=(), w=()):
        self.P.op("dve", lambda e: e.reciprocal(out=out, in_=in_), r, w)

    def memset(self, ap, val, r=(), w=(), eng="dve"):
        self.P.op(eng, lambda e: e.memset(ap, val), r, w)

    def rstd(self, v, n, inv_n, r=(), w=()):
        self.ts(v[:, 0:n], v[:, 0:n], inv_n, EPS, ALU.mult, ALU.add, r=list(r) + [v], w=[v])
        self.act(v[:, 0:n], v[:, 0:n], AF.Sqrt, r=[v], w=[v])
        self.recip(v[:, 0:n], v[:, 0:n], r=[v], w=list(w) + [v])

    def build(self):
        nc = self.nc
        Lp, S, Ls, Lo = self.Lp, self.S, self.Ls, self.Lo
        I = {}
        I["xp"] = self.dram_in("xp", [Lp, D])
        I["xs"] = self.dram_in("xs", [Ls, D])
        I["memp"] = self.dram_in("memp", [256, D])
        I["mems"] = self.dram_in("mems", [256, D])
        I["csp"] = self.dram_in("csp", [Lp, 128])
        I["css"] = self.dram_in("css", [Ls, 128])
        I["mf"] = self.dram_in("mf", [1, 7 * S])
        I["mb"] = self.dram_in("mb", [1, 7 * S])
        I["w_in"] = self.dram_in("w_in", [D, IN_W])
        I["w_glu"] = self.dram_in("w_glu", [512, 512])
        I["w_mem_kv"] = self.dram_in("w_mem_kv", [D, 1024])
        I["w_pa"] = self.dram_in("w_pa", [1024, D])
        I["w_ps"] = self.dram_in("w_ps", [512, D])
        I["w_px"] = self.dram_in("w_px", [512, D])
        I["w_out"] = self.dram_in("w_out", [D, D])
        I["g_in"] = self.dram_in("g_in", [128, 8])
        I["g_mem"] = self.dram_in("g_mem", [128, 8])
        I["g_q"] = self.dram_in("g_q", [128, 128])
        I["g_k"] = self.dram_in("g_k", [128, 128])
        I["g_f"] = self.dram_in("g_f", [128, D])
        I["s5d"] = self.dram_in("s5d", [128, 4])
        I["bglu"] = self.dram_in("bglu", [128, 4])
        I["are"] = self.dram_in("are", [128, 64])
        I["aim"] = self.dram_in("aim", [128, 64])
        I["lst"] = self.dram_in("lst", [128, 64])
        I["B1"] = self.dram_in("B1", [64, 128, 128])
        I["B2"] = self.dram_in("B2", [64, 128, 128])
        I["C1"] = self.dram_in("C1", [64, 128, 16])
        I["C2"] = self.dram_in("C2", [64, 128, 16])
        I["ident"] = self.dram_in("ident", [128, 128])
        I["swap"] = self.dram_in("swap", [128, 128])
        I["iota1"] = self.dram_in("iota1", [128, PIECE])
        I["sgn"] = self.dram_in("sgn", [128, 2])
        self.I = I
        self.y_out = self.dram_out("y", [Lo, D])

        self.W = {
            "w_in": self.dram_scr("wb_in", [D, IN_W], BF16),
            "w_glu": self.dram_scr("wb_glu", [512, 512], BF16),
            "w_mem_kv": self.dram_scr("wb_mkv", [D, 1024], BF16),
            "w_pa": self.dram_scr("wb_pa", [1024, D], BF16),
            "w_ps": self.dram_scr("wb_ps", [512, D], BF16),
            "w_px": self.dram_scr("wb_px", [512, D], BF16),
            "w_out": self.dram_scr("wb_out", [D, D], BF16),
        }
        self.KT = {"p": self.dram_scr("KT_p", [2, 128, Lp], BF16),
                   "s": self.dram_scr("KT_s", [2, 128, Ls], BF16)}
        self.VA = {"p": self.dram_scr("VA_p", [2, 128, Lp // 128, 129], BF16),
                   "s": self.dram_scr("VA_s", [2, 128, Ls // 128, 129], BF16)}
        self.UT = {"p": self.dram_scr("UT_p", [512, Lp], BF16),
                   "sf": self.dram_scr("UT_sf", [512, Ls], BF16),
                   "sb": self.dram_scr("UT_sb", [512, Ls], BF16)}
        self.YS = [self.dram_scr("YS_f", [512, Lo], F32), self.dram_scr("YS_b", [512, Lo], F32)]
        self.YAs = self.dram_scr("YA_s", [1024, Lo], BF16)

        with ExitStack() as gst:
            self.P = Prog(nc, gst)
            self.gst = gst
            self.consts(gst)
            phases = [self.phase0, self.phase1, self.phase2, self.phase3]
            stop = getattr(self, "stop", 3)
            for i, ph in enumerate(phases):
                with ExitStack() as st:
                    ph(st)
                    self.P.emit(last=(i == stop))
                if i == stop:
                    break
                self.P.barrier()
        return nc

    def consts(self, st):
        I = self.I
        self.ident_f = self.sb(st, "ident_f", [128, 128], F32)
        self.ident_b = self.sb(st, "ident_b", [128, 128], BF16)
        self.swap_f = self.sb(st, "swap_f", [128, 128], F32)
        self.ones_b = self.sb(st, "ones_b", [128, 128], BF16)
        self.g_in = self.sb(st, "g_in", [128, 8], F32)
        self.g_mem = self.sb(st, "g_mem", [128, 8], F32)
        self.g_q = self.sb(st, "g_q", [128, 128], F32)
        self.g_k = self.sb(st, "g_k", [128, 128], F32)
        self.s5d = self.sb(st, "s5d", [128, 4], F32)
        self.bglu = self.sb(st, "bglu", [128, 4], F32)
        self.sgn = self.sb(st, "sgn", [128, 2], F32)
        self.halfpi = self.sb(st, "halfpi", [128, 1], F32)
        self.KmT = {k: self.sb(st, "KmT" + k, [128, 4, 256], BF16) for k in "ps"}
        self.Vm = {k: self.sb(st, "Vm" + k, [128, 2, 512], BF16) for k in "ps"}
        for t, n in ((self.ident_f, "ident"), (self.swap_f, "swap"), (self.g_in, "g_in"),
                     (self.g_mem, "g_mem"), (self.g_q, "g_q"), (self.g_k, "g_k"),
                     (self.s5d, "s5d"), (self.bglu, "bglu"), (self.sgn, "sgn")):
            self.load(t[:], I[n][:, :], w=[t])
        self.cp(self.ident_b[:], self.ident_f[:], r=[self.ident_f], w=[self.ident_b])
        self.memset(self.ones_b[:], 1.0, w=[self.ones_b])
        self.memset(self.halfpi[:], math.pi / 2.0, w=[self.halfpi])

    def make_hT(self, x_ap_rows, xt, ss, xn, ptr, hT, gain, nblk=4, mask=None):
        self.load(xt[:, 0:nblk, :], x_ap_rows.rearrange("(b p) d -> p b d", p=128), w=[xt])
        for b in range(nblk):
            self.act(xn[:, b, :], xt[:, b, :], AF.Square, r=[xt], w=[xn, ss],
                     accum_out=ss[:, b:b + 1])
        import os
        if os.environ.get("DBG_H") == "1":
            return
        self.rstd(ss, nblk, 1.0 / D)
        if os.environ.get("DBG_H") == "2":
            return
        for b in range(nblk):
            if b % 2 == 0:
                self.act(xn[:, b, :], xt[:, b, :], AF.Copy, r=[xt, ss], w=[xn], scale=ss[:, b:b + 1])
            else:
                self.ts(xn[:, b, :], xt[:, b, :], ss[:, b:b + 1], None, ALU.mult, r=[xt, ss], w=[xn])
        if os.environ.get("DBG_H") == "3":
            return
        for j in range(8):
            pt = ptr[j % len(ptr)]
            for b in range(nblk):
                self.tr(pt[:, b * 128:(b + 1) * 128], xn[:, b, j * 128:(j + 1) * 128], self.ident_b[:],
                        r=[xn, self.ident_b], w=[pt])
            if j % 2 == 0:
                self.ts(hT[:, j, 0:nblk * 128], pt[:, 0:nblk * 128], gain[:, j:j + 1], None, ALU.mult,
                        r=[pt, gain], w=[hT])
            else:
                self.act(hT[:, j, 0:nblk * 128], pt[:, 0:nblk * 128], AF.Copy, r=[pt, gain], w=[hT],
                         scale=gain[:, j:j + 1])

    def phase0(self, st):
        I = self.I
        import os
        if os.environ.get("DBG_P0") == "none":
            return
        stg = [self.sb(st, f"wstg{i}", [128, 2048], F32) for i in range(2)]
        stb = [self.sb(st, f"wstb{i}", [128, 2048], BF16) for i in range(2)]
        k = 0
        for name, rows, cols in (("w_in", D, IN_W), ("w_glu", 512, 512), ("w_mem_kv", D, 1024),
                                 ("w_pa", 1024, D), ("w_ps", 512, D), ("w_px", 512, D), ("w_out", D, D)):
            cw = 1920 if cols == IN_W else cols
            for r0 in range(0, rows, 128):
                for c0 in range(0, cols, cw):
                    a, b = stg[k % 2], stb[k % 2]
                    self.load(a[:, 0:cw], I[name][r0:r0 + 128, c0:c0 + cw], w=[a])
                    self.cp(b[:, 0:cw], a[:, 0:cw], r=[a], w=[b], eng="dve" if k % 2 == 0 else "act")
                    self.store(self.W[name][r0:r0 + 128, c0:c0 + cw], b[:, 0:cw], r=[b], w=[self.W[name]])
                    k += 1
        import os
        if os.environ.get("DBG_P0") == "a":
            return
        wm = self.sb(st, "wm", [128, 8, 1024], BF16)
        self.load(wm[:], self.W["w_mem_kv"].h.rearrange("(j p) c -> p j c", p=128), r=[self.W["w_mem_kv"]], w=[wm])
        xt = self.sb(st, "m_xt", [128, 2, D], F32)
        xn = self.sb(st, "m_xn", [128, 2, D], BF16)
        ss = self.sb(st, "m_ss", [128, 4], F32)
        hT = self.sb(st, "m_hT", [128, 8, 256], BF16)
        vtmp = self.sb(st, "m_v", [128, 512], BF16)
        ptr = [self.ps(st, f"m_ptr{i}", [128, 1024], BF16) for i in range(2)]
        pk = [self.ps(st, f"m_pk{i}", [128, 512], F32) for i in range(2)]
        for key, src in (("p", I["memp"]), ("s", I["mems"])):
            self.make_hT(src[:, :], xt, ss, xn, ptr, hT, self.g_mem, nblk=2)
            if os.environ.get("DBG_P0") == "b1":
                continue
            for hx in range(4):
                p = pk[hx % 2]
                for j in range(8):
                    self.mm(p[:, 0:256], wm[:, j, hx * 128:(hx + 1) * 128], hT[:, j, :], j == 0, j == 7,
                            r=[wm, hT], w=[p])
                if os.environ.get("DBG_P0") == "b2":
                    continue
                self.cp(self.KmT[key][:, hx, :], p[:, 0:256], r=[p], w=[self.KmT[key]], eng="act")
            if os.environ.get("DBG_P0") in ("b2", "b3"):
                continue
            for m in range(2):
                p = pk[m % 2]
                for j in range(8):
                    self.mm(p[:, :], hT[:, j, m * 128:(m + 1) * 128], wm[:, j, 512:1024], j == 0, j == 7,
                            r=[wm, hT], w=[p])
                self.cp(self.Vm[key][:, m, :], p[:, :], r=[p], w=[self.Vm[key]], eng=os.environ.get("DBG_VE", "dve"))

    def rope_tables(self, cs, b, gain, tabs, r_extra=()):
        c = cs[:, b, 0:64]
        s = cs[:, b, 64:128]
        g0 = gain[:, 0:64]
        g1 = gain[:, 64:128]
        rr = [cs, gain] + list(r_extra)
        self.tt(tabs[:, 0, :], c, g0, ALU.mult, r=rr, w=[tabs])
        self.tt(tabs[:, 1, :], s, g1, ALU.mult, r=rr, w=[tabs])
        self.tt(tabs[:, 2, :], s, g0, ALU.mult, r=rr, w=[tabs])
        self.tt(tabs[:, 3, :], c, g1, ALU.mult, r=rr, w=[tabs])

    def norm_rope(self, psrc, nh, sq, ssv, xa, t4, tabs, out_bf, rsrc):
        n = nh * 128
        self.act(sq[:, 0:n], psrc, AF.Square, r=rsrc, w=[sq])
        self.P.op("dve", lambda e: e.tensor_reduce(out=ssv[:, 0:nh], in_=sq[:, 0:n].rearrange("p (h d) -> p h d", h=nh),
                                                   axis=AX.X, op=ALU.add), _bufs([sq]), _bufs([ssv]))
        self.rstd(ssv, nh, 1.0 / 128.0)
        xa3 = xa[:, 0:n].rearrange("p (h d) -> p h d", h=nh)
        self.tt(xa3, psrc.rearrange("p (h d) -> p h d", h=nh),
                ssv[:, 0:nh].unsqueeze(2).to_broadcast([128, nh, 128]), ALU.mult, r=list(rsrc) + [ssv], w=[xa])
        x0 = xa[:, 0:n].rearrange("p (h i two) -> p h i two", h=nh, two=2)[:, :, :, 0]
        x1 = xa[:, 0:n].rearrange("p (h i two) -> p h i two", h=nh, two=2)[:, :, :, 1]
        o0 = out_bf[:, 0:nh, :].rearrange("p h (i two) -> p h i two", two=2)[:, :, :, 0]
        o1 = out_bf[:, 0:nh, :].rearrange("p h (i two) -> p h i two", two=2)[:, :, :, 1]

        def tb(i):
            return tabs[:, i, :].unsqueeze(1).to_broadcast([128, nh, 64])
        tv = [t4[:, i, 0:nh * 64].rearrange("p (h i) -> p h i", h=nh) for i in range(4)]
        self.tt(tv[0], x0, tb(0), ALU.mult, r=[xa, tabs], w=[t4])
        self.tt(tv[1], x1, tb(1), ALU.mult, r=[xa, tabs], w=[t4])
        self.tt(tv[2], x0, tb(2), ALU.mult, r=[xa, tabs], w=[t4])
        self.tt(tv[3], x1, tb(3), ALU.mult, r=[xa, tabs], w=[t4])
        self.tt(o0, tv[0], tv[1], ALU.subtract, r=[t4], w=[out_bf])
        self.tt(o1, tv[2], tv[3], ALU.add, r=[t4], w=[out_bf])

    def phase1(self, st):
        I = self.I
        Lp, S, Ls = self.Lp, self.S, self.Ls
        wkv = self.sb(st, "wkv", [128, 8, 512], BF16)
        wu = self.sb(st, "wu", [128, 8, 512], BF16)
        wv = self.W["w_in"].h.rearrange("(j p) c -> p j c", p=128)
        self.load(wkv[:], wv[:, :, C_K:C_K + 512], r=[self.W["w_in"]], w=[wkv])
        self.load(wu[:], wv[:, :, C_U:C_U + 512], r=[self.W["w_in"]], w=[wu])
        xt = [self.sb(st, f"xt{i}", [128, 4, D], F32) for i in range(2)]
        xn = [self.sb(st, f"xn{i}", [128, 4, D], BF16) for i in range(2)]
        ss = [self.sb(st, f"ss{i}", [128, 4], F32) for i in range(2)]
        hT = [self.sb(st, f"hT{i}", [128, 8, 512], BF16) for i in range(2)]
        cs = [self.sb(st, f"cs{i}", [128, 4, 128], F32) for i in range(2)]
        tabs = [self.sb(st, f"tabs{i}", [128, 4, 64], F32) for i in range(2)]
        sq = self.sb(st, "sq", [128, 256], F32)
        kss = [self.sb(st, f"kss{i}", [128, 2], F32) for i in range(2)]
        ka = self.sb(st, "ka", [128, 256], F32)
        t4 = self.sb(st, "t4", [128, 4, 128], F32)
        krot = [self.sb(st, f"krot{i}", [128, 2, 128], BF16) for i in range(2)]
        KTt = [self.sb(st, f"KTt{i}", [128, 2, 512], BF16) for i in range(2)]
        VAt = [self.sb(st, f"VAt{i}", [128, 2, 4, 129], BF16) for i in range(2)]
        UTt = [self.sb(st, f"UTt{i}", [128, 4, 512], BF16) for i in range(2)]
        UTb = [self.sb(st, f"UTb{i}", [128, 4, 512], BF16) for i in range(2)]
        mrow = [self.sb(st, f"mrow{i}", [128, 2, 512], F32) for i in range(2)]
        ptr = [self.ps(st, f"ptr{i}", [128, 1024], BF16) for i in range(2)]
        pkv = [self.ps(st, f"pkv{i}", [128, 512], F32) for i in range(2)]
        pkt = self.ps(st, "pkt", [128, 2, 512], BF16)
        pu = [self.ps(st, f"pu{i}", [128, 512], F32) for i in range(2)]
        for v in VAt:
            self.memset(v[:, :, :, 128:129], 1.0, w=[v])
        it = 0
        for key, xsrc, cssrc, L in (("p", I["xp"], I["csp"], Lp), ("s", I["xs"], I["css"], Ls)):
            for t in range(L // 512):
                t0 = t * 512
                sl = it % 2
                it += 1
                prefix = (key == "s" and t0 < 7 * S)
                self.make_hT(xsrc[t0:t0 + 512, :], xt[sl], ss[sl], xn[sl], ptr, hT[sl], self.g_in)
                self.load(cs[sl][:], cssrc[t0:t0 + 512, :].rearrange("(b p) c -> p b c", p=128), w=[cs[sl]])
                if prefix:
                    self.load(mrow[sl][:, 0, :], I["mf"][0:1, t0:t0 + 512].partition_broadcast(128), w=[mrow[sl]])
                    self.load(mrow[sl][:, 1, :], I["mb"][0:1, t0:t0 + 512].partition_broadcast(128), w=[mrow[sl]])
                for b in range(4):
                    p = pkv[b % 2]
                    for j in range(8):
                        self.mm(p[:, :], hT[sl][:, j, b * 128:(b + 1) * 128], wkv[:, j, :], j == 0, j == 7,
                                r=[hT[sl], wkv], w=[p])
                    self.cp(VAt[sl][:, :, b, 0:128], p[:, 256:512].rearrange("p (h d) -> p h d", h=2),
                            r=[p], w=[VAt[sl]], eng="act")
                    tb_ = tabs[b % 2]
                    self.rope_tables(cs[sl], b, self.g_k, tb_)
                    kr = krot[b % 2]
                    self.norm_rope(p[:, 0:256], 2, sq, kss[b % 2], ka, t4, tb_, kr, [p])
                    for h in range(2):
                        self.tr(pkt[:, h, b * 128:(b + 1) * 128], kr[:, h, :], self.ident_b[:],
                                r=[kr, self.ident_b], w=[pkt])
                self.cp(KTt[sl][:], pkt[:], r=[pkt], w=[KTt[sl]])
                self.store(self.KT[key].h[:, :, t0:t0 + 512].rearrange("h p l -> p h l"), KTt[sl][:],
                           r=[KTt[sl]], w=[self.KT[key]])
                self.store(self.VA[key].h[:, :, t0 // 128:t0 // 128 + 4, :].rearrange("h p b c -> p h b c"),
                           VAt[sl][:], r=[VAt[sl]], w=[self.VA[key]])
                for i in range(4):
                    p = pu[i % 2]
                    for j in range(8):
                        self.mm(p[:, :], wu[:, j, i * 128:(i + 1) * 128], hT[sl][:, j, :], j == 0, j == 7,
                                r=[hT[sl], wu], w=[p])
                    if prefix:
                        self.tt(UTt[sl][:, i, :], p[:, :], mrow[sl][:, 0, :], ALU.mult, r=[p, mrow[sl]], w=[UTt[sl]])
                        self.tt(UTb[sl][:, i, :], p[:, :], mrow[sl][:, 1, :], ALU.mult, r=[p, mrow[sl]], w=[UTb[sl]])
                    else:
                        self.cp(UTt[sl][:, i, :], p[:, :], r=[p], w=[UTt[sl]], eng="act" if i % 2 else "dve")
                if key == "p":
                    self.store(self.UT["p"].h[:, t0:t0 + 512].rearrange("(i p) l -> p i l", p=128), UTt[sl][:],
                               r=[UTt[sl]], w=[self.UT["p"]])
                else:
                    self.store(self.UT["sf"].h[:, t0:t0 + 512].rearrange("(i p) l -> p i l", p=128), UTt[sl][:],
                               r=[UTt[sl]], w=[self.UT["sf"]])
                    self.store(self.UT["sb"].h[:, t0:t0 + 512].rearrange("(i p) l -> p i l", p=128),
                               (UTb if prefix else UTt)[sl][:], r=[(UTb if prefix else UTt)[sl]], w=[self.UT["sb"]])

    def phase2(self, st):
        banks = [self.ps(st, f"cb{i}", [128, 512], F32) for i in range(8)]
        ga = self.gen_s5(st, banks[0:4])
        gb = self.gen_att(st, banks[4:8])
        ta = tb = 0.0
        da = db = False
        import os
        if os.environ.get("DBG_NOATT"):
            db = True
        while not (da and db):
            if not da and (db or ta <= tb):
                try:
                    ta += next(ga)
                except StopIteration:
                    da = True
            else:
                try:
                    tb += next(gb)
                except StopIteration:
                    db = True

    def gen_s5(self, st, banks):
        I = self.I
        Lp, S, Ls = self.Lp, self.S, self.Ls
        def gt(name):
            return self.sb(st, name, [128, 64], F32)
        are, aim, lst = gt("are"), gt("aim"), gt("lst")
        for t, n in ((are, "are"), (aim, "aim"), (lst, "lst")):
            self.load(t[:], I[n][:, :], w=[t])
        step, Rv, th, kk, thr, sn, cs_, ab = gt("step"), gt("Rv"), gt("th"), gt("kk"), gt("thr"), gt("sn"), gt("cs_"), gt("ab")
        nr, ni, den, CR, CI, SCI, NSCR, tmp = gt("nr"), gt("ni"), gt("den"), gt("CR"), gt("CI"), gt("SCI"), gt("NSCR"), gt("tmp")
        self.act(step[:], lst[:], AF.Exp, r=[lst], w=[step])
        self.ts(are[:], are[:], -1e-4, None, ALU.min, r=[are], w=[are])
        self.tt(Rv[:], are[:], step[:], ALU.mult, r=[are, step], w=[Rv])
        RL, RP, NCI, SCR = gt("RL"), gt("RP"), gt("NCI"), gt("SCR")
        self.cp(RL[:], Rv[:], r=[Rv], w=[RL])
        self.act(RP[:], Rv[:], AF.Exp, r=[Rv], w=[RP], scale=float(PIECE))
        self.act(Rv[:], Rv[:], AF.Exp, r=[Rv], w=[Rv])
        self.tt(th[:], aim[:], step[:], ALU.mult, r=[aim, step], w=[th])
        self.reduce_angle(th, kk, thr)
        self.sincos(thr, ab, sn, cs_)
        self.tt(nr[:], Rv[:], cs_[:], ALU.mult, r=[Rv, cs_], w=[nr])
        self.ts(nr[:], nr[:], -1.0, None, ALU.add, r=[nr], w=[nr])
        self.tt(ni[:], Rv[:], sn[:], ALU.mult, r=[Rv, sn], w=[ni])
        self.tt(den[:], are[:], are[:], ALU.mult, r=[are], w=[den])
        self.tt(tmp[:], aim[:], aim[:], ALU.mult, r=[aim], w=[tmp])
        self.tt(den[:], den[:], tmp[:], ALU.add, r=[den, tmp], w=[den])
        self.recip(den[:], den[:], r=[den], w=[den])
        self.tt(CR[:], nr[:], are[:], ALU.mult, r=[nr, are], w=[CR])
        self.tt(tmp[:], ni[:], aim[:], ALU.mult, r=[ni, aim], w=[tmp])
        self.tt(CR[:], CR[:], tmp[:], ALU.add, r=[CR, tmp], w=[CR])
        self.tt(CR[:], CR[:], den[:], ALU.mult, r=[CR, den], w=[CR])
        self.tt(CI[:], ni[:], are[:], ALU.mult, r=[ni, are], w=[CI])
        self.tt(tmp[:], nr[:], aim[:], ALU.mult, r=[nr, aim], w=[tmp])
        self.tt(CI[:], CI[:], tmp[:], ALU.subtract, r=[CI, tmp], w=[CI])
        self.tt(CI[:], CI[:], den[:], ALU.mult, r=[CI, den], w=[CI])
        self.ts(SCI[:], CI[:], self.sgn[:, 0:1], None, ALU.mult, r=[CI, self.sgn], w=[SCI])
        self.ts(NSCR[:], CR[:], self.sgn[:, 1:2], None, ALU.mult, r=[CR, self.sgn], w=[NSCR])
        self.ts(SCR[:], CR[:], self.sgn[:, 0:1], None, ALU.mult, r=[CR, self.sgn], w=[SCR])
        self.ts(NCI[:], CI[:], -1.0, None, ALU.mult, r=[CI], w=[NCI])
        self.G = dict(RL=RL, RP=RP, NCI=NCI, SCR=SCR, CR=CR, CI=CI, SCI=SCI)

        iota1 = self.sb(st, "iota1", [128, PIECE], F32)
        self.load(iota1[:], I["iota1"][:, :], w=[iota1])
        ones_f = self.sb(st, "ones_f", [128, TS], F32)
        self.memset(ones_f[:], 1.0, w=[ones_f])
        iotaR = self.sb(st, "iotaR", [128, PIECE], F32)
        self.ts(iotaR[:], iota1[:], -1.0, float(PIECE), ALU.mult, ALU.add, r=[iota1], w=[iotaR])
        self.G["iotaR"] = iotaR

        yield 20.0
        UCH = 1024

        class Stream:
            pass
        strs = []
        for d in range(2):
            s_ = Stream()
            n = f"s{d}_"
            if d == 0:
                tmp4 = [self.sb(st, "s5tmp%d" % i, [128, PIECE], F32) for i in range(4)]
            s_.phi, s_.k2, s_.sinp, s_.cosp = tmp4
            s_.TA = self.sb(st, n + "TA", [128, PIECE], F32)
            s_.TB = self.sb(st, n + "TB", [128, PIECE], F32)
            s_.RC = self.sb(st, n + "RC", [128, PIECE], F32)
            s_.RS = self.sb(st, n + "RS", [128, PIECE], F32)
            s_.Rd = self.sb(st, n + "Rd", [128, TS], F32)
            s_.rot = self.sb(st, n + "rot", [128, 128], F32)
            s_.rb = self.sb(st, n + "rb", [128, 1], F32)
            s_.Bf = self.sb(st, n + "Bf", [128, 2, 128], F32)
            s_.Bb = self.sb(st, n + "Bb", [128, 2, 128], BF16)
            s_.Cf = self.sb(st, n + "Cf", [128, 2, 16], F32)
            s_.Cb = self.sb(st, n + "Cb", [128, 2, 16], BF16)
            s_.t2 = [self.sb(st, n + f"t2{i}", [128, TS], F32) for i in range(2)]
            s_.dd = [self.sb(st, n + f"dd{i}", [128, TS], F32) for i in range(2)]
            s_.e1 = [self.sb(st, n + f"e1{i}", [128, TS], BF16) for i in range(3)]
            s_.e2 = [self.sb(st, n + f"e2{i}", [128, TS], BF16) for i in range(3)]
            s_.wl = self.sb(st, n + "wl", [128, 1], F32)
            s_.carry = self.sb(st, n + "carry", [128, 1], F32)
            s_.rotD = self.sb(st, n + "rotD", [128, 128], F32)
            s_.zacc = [self.sb(st, n + f"zacc{i}", [128, 2 * NPC], F32) for i in range(2)]
            s_.zq = [self.sb(st, n + f"zq{i}", [128, 1], F32) for i in range(2)]
            s_.X = self.sb(st, n + "X", [128, 1], F32)
            s_.ystg = [self.sb(st, n + f"ystg{i}", [16, 512], F32) for i in range(2)]
            s_.ub = [self.sb(st, n + f"ub{i}", [128, UCH], BF16) for i in range(2)]
            s_.ucnt = 0
            s_.ucur = None
            s_.uch = UCH
            s_.pp = [banks[0], banks[1]]
            s_.pw = banks[2]
            s_.py = banks[3]
            s_.prot = T(banks[3].h[:, TS:TS + 2], banks[3].b)
            s_.nchunk = 0
            s_.nstg = 0
            strs.append(s_)
        Lp, S, Ls = self.Lp, self.S, self.Ls
        nset = 0
        for j in range(4):
            for gl in range(8):
                g = 8 * j + gl
                for d in range(2):
                    s_ = strs[nset % 2]
                    nset += 1
                    s_.ucur = None
                    col = d * 32 + g
                    self.s5_prep(s_, col, iota1, ones_f, Rv, thr, CR, CI, SCI, NSCR)
                    yield 4.0
                    self.s5_prep_G(s_, col)
                    yield 8.0
                    npre = 7 * S // TS
                    pre = []
                    nfirst = npre % NPC if npre % NPC else NPC
                    for i in range(npre):
                        if d == 0:
                            src, t0, rev = self.UT["sf"], i * TS, False
                        else:
                            src, t0, rev = self.UT["sb"], 7 * S - (i + 1) * TS, True
                        if i < nfirst:
                            q, pos, cnt = 0, i, nfirst
                        else:
                            q, pos, cnt = 1 + (i - nfirst) // NPC, (i - nfirst) % NPC, NPC
                        pre.append(dict(s=s_, d=d, g=g, src=src, j=j, t0=t0, rev=rev, q=q,
                                        gcol=NPC - cnt + pos, pend=(pos == cnt - 1), c0=NPC - cnt))
                    self.s5_M(pre[0])
                    hq = []
                    hq_done = [False]
                    for c in range(npre):
                        if c + 1 < npre:
                            self.s5_M(pre[c + 1])
                        it_ = pre[c]
                        self.s5_PR(it_)
                        if it_["pend"]:
                            self.s5_piece_sum(s_, it_["q"], it_["c0"])
                            if hq:
                                self.s5_horner(s_, hq.pop(0), False if hq_done[0] else True)
                                hq_done[0] = True
                            hq.append(it_["q"])
                        yield 0.75
                    while hq:
                        self.s5_horner(s_, hq.pop(0), False if hq_done[0] else True)
                        hq_done[0] = True
                    self.s5_prep_T(s_, col, CR, CI, SCI, NSCR)
                    yield 6.0
                    items = []
                    pcnt = [0]

                    def add(src, t0, rev, ypos, first=False, last=False, fromX=False):
                        if first:
                            pcnt[0] = 0
                        items.append(dict(s=s_, d=d, g=g, src=src, j=j, t0=t0, rev=rev, ypos=ypos,
                                          first=first, last=last, p=pcnt[0] % NPC, fromX=fromX))
                        pcnt[0] += 1
                    npc, nown = Lp // TS, S // TS
                    for i in range(npc):
                        t0 = i * TS if d == 0 else Lp - (i + 1) * TS
                        add(self.UT["p"], t0, d == 1, t0, i == 0, i == npc - 1)
                    for i in range(nown):
                        t0 = 7 * S + i * TS if d == 0 else Ls - (i + 1) * TS
                        add(self.UT["sf"] if d == 0 else self.UT["sb"], t0, d == 1, Lp + t0 - 7 * S,
                            i == 0, i == nown - 1, fromX=(i == 0))
                    ni = len(items)
                    for c_, it_ in enumerate(items):
                        it_["h"] = c_ % 2
                        it_["k3"] = c_ % 3
                    self.s5_M(items[0])
                    self.s5_A(items[0])
                    for c in range(ni + 3):
                        if c + 1 < ni:
                            self.s5_M(items[c + 1])
                        if 0 <= c - 1 < ni:
                            self.s5_rot(items[c - 1])
                        if 0 <= c - 3 < ni:
                            self.s5_Yev(items[c - 3])
                        if 0 <= c - 2 < ni:
                            self.s5_Ymm(items[c - 2])
                        if c < ni:
                            self.s5_S(items[c], items[c - 1] if c > 0 else None)
                            self.s5_E(items[c])
                            if c + 1 < ni:
                                self.s5_A(items[c + 1])
                            yield 2.05 + (0.85 if items[c]["ypos"] is not None else 0.0)

    def gen_att(self, st, bk):
        I = self.I
        Lp, S, Ls = self.Lp, self.S, self.Ls
        SC = 128.0 ** -0.5
        wv = self.W["w_in"].h.rearrange("(j p) c -> p j c", p=128)
        ws = [self.sb(st, f"aws{i}", [128, 8, 512], BF16) for i in range(2)]
        wsi = [0]

        def wload(src_ap, rd_):
            t = ws[wsi[0] % 2]
            wsi[0] += 1
            self.load(t[:], src_ap, r=[rd_], w=[t])
            return t
        xt = [self.sb(st, f"axt{i}", [128, D], F32) for i in range(2)]
        xn = self.sb(st, "axn", [128, 4, D], BF16)
        ss = self.sb(st, "ass", [128, 4], F32)
        hT = self.sb(st, "ahT", [128, 8, 512], BF16)
        cs = self.sb(st, "acs", [128, 4, 128], F32)
        tabs = [self.sb(st, f"atabs{i}", [128, 4, 64], F32) for i in range(2)]
        sq = self.sb(st, "asq", [128, 512], F32)
        qss = [self.sb(st, f"aqss{i}", [128, 8], F32) for i in range(2)]
        qa = self.sb(st, "aqa", [128, 512], F32)
        t4 = self.sb(st, "at4", [128, 4, 256], F32)
        qrot = [self.sb(st, f"aqrot{i}", [128, 8, 128], BF16) for i in range(2)]
        QT = self.sb(st, "aQT", [128, 8, 512], BF16)
        GA = self.sb(st, "aGA", [128, 8, 512], BF16)
        YAh = [self.sb(st, f"aYAh{i}", [128, 512], BF16) for i in range(2)]
        KC = 512
        kts = [self.sb(st, f"akts{i}", [128, KC], BF16) for i in range(3)]
        vas = [self.sb(st, f"avas{i}", [128, KC // 128, 129], BF16) for i in range(3)]
        PT = [self.sb(st, f"aPT{i}", [128, 512], BF16) for i in range(3)]
        rd = self.sb(st, "ard", [128, 512], F32)
        yx = self.sb(st, "ayx", [128, 512], F32)

        def bf(b):
            return b[:].bitcast(BF16)
        seqs = [("p", I["xp"], I["csp"], 0, Lp, Lp, 0), ("s", I["xs"], I["css"], 7 * S, S, Ls, Lp)]
        nslot = 0
        for key, xsrc, cssrc, xoff, nown, Lk, yoff in seqs:
            for t in range(nown // 512):
                t0 = xoff + t * 512
                yo = yoff + t * 512
                self.make_hT_bank(xsrc[t0:t0 + 512, :], xt, ss, xn, [bk[2], bk[3]], hT, self.g_in)
                self.load(cs[:], cssrc[t0:t0 + 512, :].rearrange("(b p) c -> p b c", p=128), w=[cs])
                yield 12.0
                wq0 = wload(wv[:, :, C_Q:C_Q + 512], self.W["w_in"])
                wq1 = wload(wv[:, :, C_Q + 512:C_Q + 1024], self.W["w_in"])
                for b in range(4):
                    pa, pb, pT = bk[0], bk[1], bk[2 + b % 2]
                    for j in range(8):
                        self.mm(pa[:, :], hT[:, j, b * 128:(b + 1) * 128], wq0[:, j, :], j == 0, j == 7, r=[hT, wq0], w=[pa])
                    for j in range(8):
                        self.mm(pb[:, :], hT[:, j, b * 128:(b + 1) * 128], wq1[:, j, :], j == 0, j == 7, r=[hT, wq1], w=[pb])
                    tb_ = tabs[b % 2]
                    self.rope_tables(cs, b, self.g_q, tb_)
                    qr = qrot[b % 2]
                    for half, pbank in ((0, pa), (1, pb)):
                        self.norm_rope_q(pbank, half, sq, qss[b % 2], qa, t4, tb_, qr)
                    for h in range(8):
                        self.tr(bf(pT)[:, h * 128:(h + 1) * 128], qr[:, h, :], self.ident_b[:],
                                r=[qr, self.ident_b], w=[pT])
                    self.cp(QT[:, :, b * 128:(b + 1) * 128], bf(pT).rearrange("p (h q) -> p h q", h=8),
                            r=[pT], w=[QT], eng="act")
                    yield 5.0
                for half in range(2):
                    wg = wload(wv[:, :, C_GA + half * 512:C_GA + (half + 1) * 512], self.W["w_in"])
                    for o in range(4):
                        p = bk[o % 2]
                        for j in range(8):
                            self.mm(p[:, :], wg[:, j, o * 128:(o + 1) * 128], hT[:, j, :], j == 0, j == 7, r=[wg, hT], w=[p])
                        self.act(GA[:, half * 4 + o, :], p[:, :], AF.Silu, r=[p], w=[GA])
                    yield 8.0
                nkc = Lk // KC
                nk = Lk // 128
                kpc = KC // 128
                for h in range(8):
                    hk = h // 4
                    pO, pD = bk[2], bk[3]
                    cur = {}
                    for idx in range(nk + 1):
                        if idx < nk:
                            c, kk = idx // kpc, idx % kpc
                            if kk == 0:
                                sl = nslot % 3
                                nslot += 1
                                kt_, va_ = kts[sl], vas[sl]
                                self.load(kt_[:], self.KT[key].h[hk, :, c * KC:(c + 1) * KC], r=[self.KT[key]], w=[kt_])
                                self.load(va_[:], self.VA[key].h[hk, :, c * kpc:(c + 1) * kpc, :],
                                          r=[self.VA[key]], w=[va_])
                                cur[c] = (kt_, va_)
                            kt_, va_ = cur[c]
                            psb = bk[idx % 2]
                            pt = PT[idx % 3]
                            self.mm(psb[:, :], kt_[:, kk * 128:(kk + 1) * 128], QT[:, h, :], True, True,
                                    r=[kt_, QT], w=[psb])
                            self.act(pt[:], psb[:, :], AF.Exp, r=[psb], w=[pt], scale=SC)
                        if idx >= 1:
                            jx = idx - 1
                            c, kk = jx // kpc, jx % kpc
                            kt_, va_ = cur[c]
                            pt = PT[jx % 3]
                            self.mm(pO[:, :], va_[:, kk, 0:128], pt[:], jx == 0, jx == nk - 1, r=[va_, pt], w=[pO])
                            self.mm(pD[:, :], self.ones_b[:], pt[:], jx == 0, jx == nk - 1, r=[self.ones_b, pt], w=[pD])
                        yield 0.88
                    self.act(rd[:], pD[:, :], AF.Ln, r=[pD], w=[rd])
                    self.act(rd[:], rd[:], AF.Exp, r=[rd], w=[rd], scale=-1.0)
                    self.tt(yx[:], pO[:, :], rd[:], ALU.mult, r=[pO, rd], w=[yx])
                    ya = YAh[h % 2]
                    self.tt(ya[:], yx[:], GA[:, h, :], ALU.mult, r=[yx, GA], w=[ya])
                    self.store(self.YAs.h[h * 128:(h + 1) * 128, yo:yo + 512], ya[:], r=[ya], w=[self.YAs])
                    yield 1.0

    def reduce_angle(self, th, kk, thr):
        self.ts(kk[:], th[:], 1.0 / TWO_PI, None, ALU.mult, r=[th], w=[kk])
        self.ts(kk[:], kk[:], MAGIC, None, ALU.add, r=[kk], w=[kk])
        self.ts(kk[:], kk[:], -MAGIC, None, ALU.add, r=[kk], w=[kk])
        self.stt(thr[:], kk[:], -CW1, th[:], ALU.mult, ALU.add, r=[kk, th], w=[thr])
        self.stt(thr[:], kk[:], -CW2, thr[:], ALU.mult, ALU.add, r=[kk, thr], w=[thr])
        self.ts(thr[:], thr[:], math.pi, -math.pi, ALU.min, ALU.max, r=[thr], w=[thr])

    def sincos(self, thr, ab, sn, cs_):
        self.act(sn[:], thr[:], AF.Sin, r=[thr], w=[sn])
        self.act(ab[:], thr[:], AF.Sin, r=[thr], w=[ab], scale=0.5)
        self.tt(cs_[:], ab[:], ab[:], ALU.mult, r=[ab], w=[cs_])
        self.ts(cs_[:], cs_[:], -2.0, 1.0, ALU.mult, ALU.add, r=[cs_], w=[cs_])

    def s5_prep(self, s_, col, iota1, ones_f, Rv, thr, CR, CI, SCI, NSCR):
        I = self.I
        c1 = slice(col, col + 1)
        self.load(s_.Bf[:, 0, :], I["B1"][col, :, :], w=[s_.Bf])
        self.load(s_.Bf[:, 1, :], I["B2"][col, :, :], w=[s_.Bf])
        self.load(s_.Cf[:, 0, :], I["C1"][col, :, :], w=[s_.Cf])
        self.load(s_.Cf[:, 1, :], I["C2"][col, :, :], w=[s_.Cf])
        self.cp(s_.Bb[:], s_.Bf[:], r=[s_.Bf], w=[s_.Bb], eng="act")
        self.cp(s_.Cb[:], s_.Cf[:], r=[s_.Cf], w=[s_.Cb], eng="act")
        self.ts(s_.phi[:], iota1[:], thr[:, c1], None, ALU.mult, r=[iota1, thr], w=[s_.phi])
        self.reduce_angle(s_.phi, s_.k2, s_.phi)
        self.sincos(s_.phi, s_.k2, s_.sinp, s_.cosp)
        self.ts(s_.rb[:], s_.sinp[:, PIECE - 1:PIECE], self.sgn[:, 1:2], None, ALU.mult, r=[s_.sinp, self.sgn], w=[s_.rb])
        self.ts(s_.rot[:], self.ident_f[:], s_.cosp[:, PIECE - 1:PIECE], None, ALU.mult, r=[self.ident_f, s_.cosp], w=[s_.rot])
        self.stt(s_.rot[:], self.swap_f[:], s_.rb[:, 0:1], s_.rot[:], ALU.mult, ALU.add,
                 r=[self.swap_f, s_.rb, s_.rot], w=[s_.rot])
        self.act(s_.Rd[:], ones_f[:], AF.Copy, r=[ones_f, Rv], w=[s_.Rd], scale=Rv[:, c1])

    def s5_prep_G(self, s_, col):
        G = self.G
        c1 = slice(col, col + 1)
        n1 = PIECE - 1
        mag, GA, GB = s_.k2, s_.RC, s_.RS
        crv = rev_ap(s_.cosp[:, 0:n1])
        srv = rev_ap(s_.sinp[:, 0:n1])
        self.act(mag[:], G["iotaR"][:], AF.Exp, r=[G["iotaR"], G["RL"]], w=[mag], scale=G["RL"][:, c1])
        self.ts(GA[:, 0:n1], crv, G["CR"][:, c1], None, ALU.mult, r=[s_.cosp, G["CR"]], w=[GA])
        self.stt(GA[:, 0:n1], srv, G["NCI"][:, c1], GA[:, 0:n1], ALU.mult, ALU.add, r=[s_.sinp, G["NCI"], GA], w=[GA])
        self.ts(GA[:, n1:PIECE], G["CR"][:, c1], 1.0, None, ALU.mult, r=[G["CR"]], w=[GA])
        self.tt(GA[:], GA[:], mag[:], ALU.mult, r=[GA, mag], w=[GA])
        self.ts(GB[:, 0:n1], crv, G["SCI"][:, c1], None, ALU.mult, r=[s_.cosp, G["SCI"]], w=[GB])
        self.stt(GB[:, 0:n1], srv, G["SCR"][:, c1], GB[:, 0:n1], ALU.mult, ALU.add, r=[s_.sinp, G["SCR"], GB], w=[GB])
        self.ts(GB[:, n1:PIECE], G["SCI"][:, c1], 1.0, None, ALU.mult, r=[G["SCI"]], w=[GB])
        self.tt(GB[:], GB[:], mag[:], ALU.mult, r=[GB, mag], w=[GB])
        self.ts(s_.rotD[:], s_.rot[:], G["RP"][:, c1], None, ALU.mult, r=[s_.rot, G["RP"]], w=[s_.rotD])

    def s5_prep_T(self, s_, col, CR, CI, SCI, NSCR):
        c1 = slice(col, col + 1)
        self.ts(s_.TA[:], s_.cosp[:], CR[:, c1], None, ALU.mult, r=[s_.cosp, CR], w=[s_.TA])
        self.stt(s_.TA[:], s_.sinp[:], CI[:, c1], s_.TA[:], ALU.mult, ALU.add, r=[s_.sinp, CI, s_.TA], w=[s_.TA])
        self.ts(s_.TB[:], s_.cosp[:], SCI[:, c1], None, ALU.mult, r=[s_.cosp, SCI], w=[s_.TB])
        self.stt(s_.TB[:], s_.sinp[:], NSCR[:, c1], s_.TB[:], ALU.mult, ALU.add, r=[s_.sinp, NSCR, s_.TB], w=[s_.TB])
        self.ts(s_.RC[:], s_.cosp[:], self.sgn[:, 1:2], None, ALU.mult, r=[s_.cosp, self.sgn], w=[s_.RC])
        self.act(s_.RS[:], s_.sinp[:], AF.Copy, r=[s_.sinp], w=[s_.RS], scale=-1.0)

    def s5_M(self, it):
        s_ = it["s"]
        k = s_.nchunk % 2
        s_.nchunk += 1
        it["k"] = k
        pp = s_.pp[k]
        src, jj, t0 = it["src"], it["j"], it["t0"]
        piece = t0 // s_.uch
        ukey = (id(src), jj, piece)
        if s_.ucur != ukey:
            ub = s_.ub[s_.ucnt % 2]
            s_.ucnt += 1
            self.load(ub[:], src.h[jj * 128:(jj + 1) * 128, piece * s_.uch:(piece + 1) * s_.uch], r=[src], w=[ub])
            s_.ucur = ukey
            s_.ubcur = ub
        ut = s_.ubcur
        off = t0 - piece * s_.uch
        rhs = ut[:, off:off + TS]
        if it["rev"]:
            rhs = rev_ap(rhs)
        self.mm(pp[:, 0:TS], s_.Bb[:, 0, :], rhs, True, True, r=[s_.Bb, ut], w=[pp])
        self.mm(pp[:, TS:2 * TS], s_.Bb[:, 1, :], rhs, True, True, r=[s_.Bb, ut], w=[pp])

    def s5_PR(self, it):
        s_ = it["s"]
        k = it["k"]
        pp, junk = s_.pp[k], s_.t2[k]
        ps_ = slice(it["gcol"] * TS, (it["gcol"] + 1) * TS)
        za = s_.zacc[it["q"] % 2]
        j = it["gcol"]
        self.P.op("dve", lambda e: e.scalar_tensor_tensor(out=junk[:], in0=pp[:, 0:TS], scalar=1.0, in1=s_.RC[:, ps_],
                                                         op0=ALU.mult, op1=ALU.mult,
                                                         accum_out=za[:, 2 * j:2 * j + 1]),
                  _bufs([pp, s_.RC]), _bufs([junk, za]))
        self.P.op("dve", lambda e: e.scalar_tensor_tensor(out=junk[:], in0=pp[:, TS:2 * TS], scalar=1.0, in1=s_.RS[:, ps_],
                                                         op0=ALU.mult, op1=ALU.mult,
                                                         accum_out=za[:, 2 * j + 1:2 * j + 2]),
                  _bufs([pp, s_.RS]), _bufs([junk, za]))

    def s5_piece_sum(self, s_, q, c0):
        za, zq = s_.zacc[q % 2], s_.zq[q % 2]
        self.P.op("dve", lambda e: e.tensor_reduce(out=zq[:], in_=za[:, 2 * c0:2 * NPC], axis=AX.X, op=ALU.add),
                  _bufs([za]), _bufs([zq]))

    def s5_horner(self, s_, q, first):
        zq = s_.zq[q % 2]
        if first:
            self.ts(s_.X[:], zq[:], 1.0, None, ALU.mult, r=[zq], w=[s_.X])
        else:
            self.mm(s_.prot[:, 0:1], s_.rotD[:], s_.X[:], True, True, r=[s_.rotD, s_.X], w=[s_.py])
            self.tt(s_.X[:], s_.prot[:, 0:1], zq[:], ALU.add, r=[s_.py, zq], w=[s_.X])

    def s5_A(self, it):
        s_ = it["s"]
        k = it["k"]
        pp, t2, dd = s_.pp[k], s_.t2[k], s_.dd[k]
        ps_ = slice(it["p"] * TS, (it["p"] + 1) * TS)
        self.tt(pp[:, 0:TS], pp[:, 0:TS], s_.TA[:, ps_], ALU.mult, r=[pp, s_.TA], w=[pp])
        self.tt(t2[:], pp[:, TS:2 * TS], s_.TB[:, ps_], ALU.mult, r=[pp, s_.TB], w=[t2])
        self.tt(dd[:], pp[:, 0:TS], t2[:], ALU.add, r=[pp, t2], w=[dd])

    def s5_S(self, it, prev):
        s_ = it["s"]
        k = it["k"]
        dd = s_.dd[k]
        h = it["h"]
        out = s_.pw[:, h * TS:(h + 1) * TS]
        rds = [s_.Rd, dd]
        if it["first"] and it.get("fromX"):
            init = s_.X[:, 0:1]
            rds.append(s_.X)
        elif it["first"]:
            init = 0.0
        elif prev["p"] == NPC - 1:
            init = s_.prot[:, 0:1]
            rds.append(s_.py)
        else:
            ph = prev["h"]
            init = s_.pw[:, ph * TS + TS - 1:ph * TS + TS]
        self.P.op("dve", lambda e: e.tensor_tensor_scan(out=out, data0=s_.Rd[:], data1=dd[:],
                                                        initial=init, op0=ALU.mult, op1=ALU.add),
                  _bufs(rds), _bufs([s_.pw]))
        if not it["last"] and it["p"] == NPC - 1:
            self.ts(s_.wl[:], s_.pw[:, h * TS + TS - 1:h * TS + TS], 1.0, None, ALU.mult, r=[s_.pw], w=[s_.wl])

    def s5_rot(self, it):
        s_ = it["s"]
        if not it["last"] and it["p"] == NPC - 1:
            self.mm(s_.prot[:, 0:1], s_.rot[:], s_.wl[:], True, True, r=[s_.rot, s_.wl], w=[s_.py])

    def s5_E(self, it):
        s_ = it["s"]
        k = it["k"]
        if it["ypos"] is None:
            return
        h = it["h"]
        e1, e2 = s_.e1[it["k3"]], s_.e2[it["k3"]]
        ps_ = slice(it["p"] * TS, (it["p"] + 1) * TS)
        self.tt(e1[:], s_.pw[:, h * TS:(h + 1) * TS], s_.RC[:, ps_], ALU.mult, r=[s_.pw, s_.RC], w=[e1])
        self.tt(e2[:], s_.pw[:, h * TS:(h + 1) * TS], s_.RS[:, ps_], ALU.mult, r=[s_.pw, s_.RS], w=[e2])

    def s5_Ymm(self, it):
        s_ = it["s"]
        k = it["k"]
        if it["ypos"] is None:
            return
        e1, e2 = s_.e1[it["k3"]], s_.e2[it["k3"]]
        py = s_.py[0:16, 0:TS]
        self.mm(py, s_.Cb[:, 0, :], e1[:], True, False, r=[s_.Cb, e1], w=[s_.py])
        self.mm(py, s_.Cb[:, 1, :], e2[:], False, True, r=[s_.Cb, e2], w=[s_.py])

    def s5_Yev(self, it):
        s_ = it["s"]
        ypos, rev, d, g = it["ypos"], it["rev"], it["d"], it["g"]
        if ypos is None:
            return
        py = s_.py[0:16, 0:TS]
        nper = 512 // TS
        sidx = s_.nstg // nper
        stg = s_.ystg[sidx % 2]
        q = s_.nstg % nper
        s_.nstg += 1
        base = (ypos // 512) * 512
        off = ypos - base
        dst = stg[0:16, off:off + TS]
        if rev:
            dst = rev_ap(dst)
        self.cp(dst, py, r=[s_.py], w=[stg], eng="act")
        if q == nper - 1:
            self.store(self.YS[d].h[g * 16:(g + 1) * 16, base:base + 512], stg[0:16, :], r=[stg], w=[self.YS[d]])

    def phase3(self, st):
        I = self.I
        Lp, S, Ls = self.Lp, self.S, self.Ls
        SC = 128.0 ** -0.5
        wv = self.W["w_in"].h.rearrange("(j p) c -> p j c", p=128)
        ws = [self.sb(st, f"ws{i}", [128, 8, 512], BF16) for i in range(3)]
        self.wsi = 0

        def wload(src_ap, rd):
            t = ws[self.wsi % 3]
            self.wsi += 1
            self.load(t[:], src_ap, r=[rd], w=[t])
            return t

        xt = [self.sb(st, f"xt{i}", [128, D], F32) for i in range(2)]
        xn = self.sb(st, "xn", [128, 4, D], BF16)
        ss = self.sb(st, "ss", [128, 4], F32)
        hT = self.sb(st, "hT", [128, 8, 512], BF16)
        cs = self.sb(st, "cs", [128, 4, 128], F32)
        tabs = [self.sb(st, f"tabs{i}", [128, 4, 64], F32) for i in range(2)]
        sq = self.sb(st, "sq", [128, 512], F32)
        qss = [self.sb(st, f"qss{i}", [128, 8], F32) for i in range(2)]
        qa = self.sb(st, "qa", [128, 512], F32)
        t4 = self.sb(st, "t4", [128, 4, 256], F32)
        qrot = [self.sb(st, f"qrot{i}", [128, 8, 128], BF16) for i in range(2)]
        QT = self.sb(st, "QT", [128, 8, 512], BF16)
        GA = self.sb(st, "GA", [128, 8, 512], BF16)
        YA = self.sb(st, "YA", [128, 8, 512], BF16)
        KC = 512
        kts = [self.sb(st, f"kts{i}", [128, KC], BF16) for i in range(3)]
        vas = [self.sb(st, f"vas{i}", [128, KC // 128, 129], BF16) for i in range(3)]
        PT = [self.sb(st, f"PT{i}", [128, 512], BF16) for i in range(3)]
        rcp = self.sb(st, "rcp", [128, 4], F32)
        yn = [self.sb(st, f"yn{i}", [128, 128], BF16) for i in range(2)]
        y0 = [self.sb(st, f"y0_{i}", [128, 512], F32) for i in range(1)] * 2
        y1 = [self.sb(st, f"y1_{i}", [128, 512], F32) for i in range(1)] * 2
        uu = [self.sb(st, f"uu{i}", [128, 512], BF16) for i in range(2)]
        gx = [self.sb(st, f"gx{i}", [128, 512], F32) for i in range(2)]
        g2 = [self.sb(st, f"g2{i}", [128, 512], F32) for i in range(2)]
        YG = self.sb(st, "YG", [128, 4, 512], F32)
        YGb = self.sb(st, "YGb", [128, 4, 512], BF16)
        GS = self.sb(st, "GS", [128, 4, 512], BF16)
        sgl = [self.sb(st, f"sgl{i}", [128, 512], F32) for i in range(2)]
        YSb = self.sb(st, "YSb", [128, 4, 512], BF16)
        wglu = self.sb(st, "wglu", [128, 4, 512], BF16)
        self.load(wglu[:], self.W["w_glu"].h.rearrange("(k p) c -> p k c", p=128), r=[self.W["w_glu"]], w=[wglu])
        QX = self.sb(st, "QX", [128, 4, 512], BF16)
        GX = self.sb(st, "GX", [128, 4, 512], BF16)
        PX = [self.sb(st, f"PX{i}", [128, 512], BF16) for i in range(2)]
        rd = self.sb(st, "rd", [128, 512], F32)
        yx = self.sb(st, "yx", [128, 512], F32)
        YX = self.sb(st, "YX", [128, 4, 512], BF16)
        G3 = self.sb(st, "G3", [128, 3, 4, 512], BF16)
        m = [self.sb(st, f"m{i}", [128, 512], F32) for i in range(3)]
        M = self.sb(st, "M", [128, 8, 512], BF16)
        yres = [self.sb(st, f"yres{i}", [128, D], F32) for i in range(1)] * 2
        fss = [self.sb(st, f"fss{i}", [128, 1], F32) for i in range(2)]
        gf = self.sb(st, "gf", [128, D], F32)
        self.load(gf[:], I["g_f"][:, :], w=[gf])
        bk = [self.ps(st, f"bk{i}", [128, 512], F32) for i in range(8)]

        def bf(b):
            return b[:].bitcast(BF16)

        seqs = [("p", I["xp"], I["csp"], 0, Lp, Lp, 0), ("s", I["xs"], I["css"], 7 * S, S, Ls, Lp)]
        for key, xsrc, cssrc, xoff, nown, Lk, yoff in seqs:
            for t in range(nown // 512):
                t0 = xoff + t * 512
                yo = yoff + t * 512
                self.make_hT_bank(xsrc[t0:t0 + 512, :], xt, ss, xn, [bk[6], bk[7]], hT, self.g_in)
                self.load(cs[:], cssrc[t0:t0 + 512, :].rearrange("(b p) c -> p b c", p=128), w=[cs])
                self.load(YA[:], self.YAs.h[:, yo:yo + 512].rearrange("(h p) l -> p h l", p=128), r=[self.YAs], w=[YA])
                wgs = wload(wv[:, :, C_GS:C_GS + 512], self.W["w_in"])
                for i in range(4):
                    k2 = i % 2
                    self.load(y0[k2][:], self.YS[0].h[i * 128:(i + 1) * 128, yo:yo + 512], r=[self.YS[0]], w=[y0[k2]])
                    self.load(y1[k2][:], self.YS[1].h[i * 128:(i + 1) * 128, yo:yo + 512], r=[self.YS[1]], w=[y1[k2]])
                    usrc = self.UT["p"] if key == "p" else self.UT["sf"]
                    self.load(uu[k2][:], usrc.h[i * 128:(i + 1) * 128, t0:t0 + 512], r=[usrc], w=[uu[k2]])
                    a, bq = gx[k2], g2[k2]
                    self.tt(a[:], y0[k2][:], y1[k2][:], ALU.add, r=[y0[k2], y1[k2]], w=[a])
                    self.stt(a[:], uu[k2][:], self.s5d[:, i:i + 1], a[:], ALU.mult, ALU.add, r=[uu[k2], self.s5d, a], w=[a])
                    self.tt(bq[:], a[:], a[:], ALU.mult, r=[a], w=[bq])
                    self.ts(bq[:], bq[:], 0.044715, 1.0, ALU.mult, ALU.add, r=[bq], w=[bq])
                    self.tt(bq[:], bq[:], a[:], ALU.mult, r=[bq, a], w=[bq])
                    self.act(bq[:], bq[:], AF.Sigmoid, r=[bq], w=[bq], scale=2.0 * math.sqrt(2.0 / math.pi))
                    self.tt(YG[:, i, :], a[:], bq[:], ALU.mult, r=[a, bq], w=[YG])
                    self.cp(YGb[:, i, :], YG[:, i, :], r=[YG], w=[YGb], eng="act")
                    p = bk[i % 2]
                    for j in range(8):
                        self.mm(p[:, :], wgs[:, j, i * 128:(i + 1) * 128], hT[:, j, :], j == 0, j == 7, r=[wgs, hT], w=[p])
                    self.act(GS[:, i, :], p[:, :], AF.Silu, r=[p], w=[GS])
                for o in range(4):
                    p = bk[2 + o % 2]
                    for k_ in range(4):
                        self.mm(p[:, :], wglu[:, k_, o * 128:(o + 1) * 128], YGb[:, k_, :], k_ == 0, k_ == 3,
                                r=[wglu, YGb], w=[p])
                    s_ = sgl[o % 2]
                    self.act(s_[:], p[:, :], AF.Sigmoid, r=[p, self.bglu], w=[s_], bias=self.bglu[:, o:o + 1])
                    self.tt(s_[:], s_[:], YG[:, o, :], ALU.mult, r=[s_, YG], w=[s_])
                    self.tt(YSb[:, o, :], s_[:], GS[:, o, :], ALU.mult, r=[s_, GS], w=[YSb])
                wqx = wload(wv[:, :, C_QX:C_QX + 512], self.W["w_in"])
                wgx = wload(wv[:, :, C_GX:C_GX + 512], self.W["w_in"])
                for o in range(4):
                    p = bk[o % 2]
                    for j in range(8):
                        self.mm(p[:, :], wqx[:, j, o * 128:(o + 1) * 128], hT[:, j, :], j == 0, j == 7, r=[wqx, hT], w=[p])
                    self.cp(QX[:, o, :], p[:, :], r=[p], w=[QX], eng="act")
                    p2 = bk[2 + o % 2]
                    for j in range(8):
                        self.mm(p2[:, :], wgx[:, j, o * 128:(o + 1) * 128], hT[:, j, :], j == 0, j == 7, r=[wgx, hT], w=[p2])
                    self.act(GX[:, o, :], p2[:, :], AF.Silu, r=[p2], w=[GX])
                KmT, Vm = self.KmT[key], self.Vm[key]
                for hx in range(4):
                    for mt in range(2):
                        p = bk[mt]
                        self.mm(p[:, :], KmT[:, hx, mt * 128:(mt + 1) * 128], QX[:, hx, :], True, True, r=[KmT, QX], w=[p])
                        self.act(PX[mt][:], p[:, :], AF.Exp, r=[p], w=[PX[mt]], scale=SC)
                    po_, pd_ = bk[4], bk[5]
                    for mt in range(2):
                        self.mm(po_[:, :], Vm[:, mt, hx * 128:(hx + 1) * 128], PX[mt][:], mt == 0, mt == 1, r=[Vm, PX[mt]], w=[po_])
                    for mt in range(2):
                        self.mm(pd_[:, :], self.ones_b[:], PX[mt][:], mt == 0, mt == 1, r=[self.ones_b, PX[mt]], w=[pd_])
                    self.recip(rd[:], pd_[:, :], r=[pd_], w=[rd])
                    self.tt(yx[:], po_[:, :], rd[:], ALU.mult, r=[po_, rd], w=[yx])
                    self.tt(YX[:, hx, :], yx[:], GX[:, hx, :], ALU.mult, r=[yx, GX], w=[YX])
                wpa = self.W["w_pa"].h.rearrange("(k p) c -> p k c", p=128)
                wps = self.W["w_ps"].h.rearrange("(k p) c -> p k c", p=128)
                wpx = self.W["w_px"].h.rearrange("(k p) c -> p k c", p=128)
                for og in range(2):
                    for br in range(3):
                        wm_ = wload(wv[:, :, C_MG + br * 1024 + og * 512:C_MG + br * 1024 + (og + 1) * 512], self.W["w_in"])
                        for o in range(4):
                            p = bk[o % 2]
                            for j in range(8):
                                self.mm(p[:, :], wm_[:, j, o * 128:(o + 1) * 128], hT[:, j, :], j == 0, j == 7, r=[wm_, hT], w=[p])
                            self.act(G3[:, br, o, :], p[:, :], AF.Sigmoid, r=[p], w=[G3])
                    wa = wload(wpa[:, :, og * 512:(og + 1) * 512], self.W["w_pa"])
                    wsx = ws[self.wsi % 3]
                    self.wsi += 1
                    self.load(wsx[:, 0:4, :], wps[:, :, og * 512:(og + 1) * 512], r=[self.W["w_ps"]], w=[wsx])
                    self.load(wsx[:, 4:8, :], wpx[:, :, og * 512:(og + 1) * 512], r=[self.W["w_px"]], w=[wsx])
                    for o in range(4):
                        pa_, ps_, px_ = bk[2 + (o % 2) * 3], bk[3 + (o % 2) * 3], bk[4 + (o % 2) * 3]
                        for k_ in range(8):
                            self.mm(pa_[:, :], wa[:, k_, o * 128:(o + 1) * 128], YA[:, k_, :], k_ == 0, k_ == 7, r=[wa, YA], w=[pa_])
                        for k_ in range(4):
                            self.mm(ps_[:, :], wsx[:, k_, o * 128:(o + 1) * 128], YSb[:, k_, :], k_ == 0, k_ == 3, r=[wsx, YSb], w=[ps_])
                        for k_ in range(4):
                            self.mm(px_[:, :], wsx[:, 4 + k_, o * 128:(o + 1) * 128], YX[:, k_, :], k_ == 0, k_ == 3, r=[wsx, YX], w=[px_])
                        self.tt(m[0][:], pa_[:, :], G3[:, 0, o, :], ALU.mult, r=[pa_, G3], w=[m[0]])
                        self.tt(m[1][:], ps_[:, :], G3[:, 1, o, :], ALU.mult, r=[ps_, G3], w=[m[1]])
                        self.tt(m[2][:], px_[:, :], G3[:, 2, o, :], ALU.mult, r=[px_, G3], w=[m[2]])
                        self.tt(m[0][:], m[0][:], m[1][:], ALU.add, r=[m[0], m[1]], w=[m[0]])
                        self.tt(M[:, og * 4 + o, :], m[0][:], m[2][:], ALU.add, r=[m[0], m[2]], w=[M])
                wo_ = self.W["w_out"].h.rearrange("(k p) c -> p k c", p=128)
                wo0 = wload(wo_[:, :, 0:512], self.W["w_out"])
                wo1 = wload(wo_[:, :, 512:1024], self.W["w_out"])
                for b in range(4):
                    pa_, pb_ = bk[(2 * b) % 4], bk[(2 * b + 1) % 4]
                    for k_ in range(8):
                        self.mm(pa_[:, :], M[:, k_, b * 128:(b + 1) * 128], wo0[:, k_, :], k_ == 0, k_ == 7, r=[M, wo0], w=[pa_])
                    for k_ in range(8):
                        self.mm(pb_[:, :], M[:, k_, b * 128:(b + 1) * 128], wo1[:, k_, :], k_ == 0, k_ == 7, r=[M, wo1], w=[pb_])
                    yr, fs = yres[b % 2], fss[b % 2]
                    xb = xt[b % 2]
                    self.load(xb[:], xsrc[t0 + b * 128:t0 + (b + 1) * 128, :], w=[xb])
                    self.tt(yr[:, 0:512], pa_[:, :], xb[:, 0:512], ALU.add, r=[pa_, xb], w=[yr])
                    self.tt(yr[:, 512:1024], pb_[:, :], xb[:, 512:1024], ALU.add, r=[pb_, xb], w=[yr])
                    self.act(xn[:, 0, :], yr[:], AF.Square, r=[yr], w=[xn, fs], accum_out=fs[:, 0:1])
                    self.rstd(fs, 1, 1.0 / D)
                    self.stt(yr[:], yr[:], fs[:, 0:1], gf[:], ALU.mult, ALU.mult, r=[yr, fs, gf], w=[yr])
                    self.store(self.y_out[yo + b * 128:yo + (b + 1) * 128, :], yr[:], r=[yr])

    def make_hT_bank(self, x_rows, xt, ss, xn, banks, hT, gain):
        for b in range(4):
            xb = xt[b % 2]
            sb_ = ss[b % 2] if isinstance(ss, list) else ss
            self.load(xb[:], x_rows[b * 128:(b + 1) * 128, :], w=[xb])
            self.act(xn[:, b, :], xb[:], AF.Square, r=[xb], w=[xn, ss], accum_out=ss[:, b:b + 1])
            v = ss[:, b:b + 1]
            self.ts(v, v, 1.0 / D, EPS, ALU.mult, ALU.add, r=[ss], w=[ss])
            self.act(v, v, AF.Sqrt, r=[ss], w=[ss])
            self.recip(v, v, r=[ss], w=[ss])
            if b % 2 == 0:
                self.act(xn[:, b, :], xb[:], AF.Copy, r=[xb, ss], w=[xn], scale=ss[:, b:b + 1])
            else:
                self.ts(xn[:, b, :], xb[:], ss[:, b:b + 1], None, ALU.mult, r=[xb, ss], w=[xn])
        for j in range(8):
            bank = banks[j % len(banks)]
            pv = bank[:].bitcast(BF16)
            for b in range(4):
                self.tr(pv[:, b * 128:(b + 1) * 128], xn[:, b, j * 128:(j + 1) * 128], self.ident_b[:],
                        r=[xn, self.ident_b], w=[bank])
            if j % 2 == 0:
                self.ts(hT[:, j, :], pv[:, 0:512], gain[:, j:j + 1], None, ALU.mult, r=[bank, gain], w=[hT])
            else:
                self.act(hT[:, j, :], pv[:, 0:512], AF.Copy, r=[bank, gain], w=[hT], scale=gain[:, j:j + 1])

    def norm_rope_q(self, pbank, half, sq, ssv, xa, t4, tabs, out_bf):
        nh = 4
        o = 0
        h0 = half * 4
        psrc = pbank[:, :]
        self.act(sq[:, o:o + 512], psrc, AF.Square, r=[pbank], w=[sq])
        self.P.op("dve", lambda e: e.tensor_reduce(out=ssv[:, h0:h0 + 4], in_=sq[:, o:o + 512].rearrange("p (h d) -> p h d", h=nh),
                                                   axis=AX.X, op=ALU.add), _bufs([sq]), _bufs([ssv]))
        v = ssv[:, h0:h0 + 4]
        self.ts(v, v, 1.0 / 128.0, EPS, ALU.mult, ALU.add, r=[ssv], w=[ssv])
        self.act(v, v, AF.Sqrt, r=[ssv], w=[ssv])
        self.recip(v, v, r=[ssv], w=[ssv])
        xa3 = xa[:, o:o + 512].rearrange("p (h d) -> p h d", h=nh)
        self.tt(xa3, psrc.rearrange("p (h d) -> p h d", h=nh), v.unsqueeze(2).to_broadcast([128, nh, 128]),
                ALU.mult, r=[pbank, ssv], w=[xa])
        x0 = xa[:, o:o + 512].rearrange("p (h i two) -> p h i two", h=nh, two=2)[:, :, :, 0]
        x1 = xa[:, o:o + 512].rearrange("p (h i two) -> p h i two", h=nh, two=2)[:, :, :, 1]
        ob = out_bf[:, h0:h0 + 4, :].rearrange("p h (i two) -> p h i two", two=2)
        o0, o1 = ob[:, :, :, 0], ob[:, :, :, 1]

        def tb(i):
            return tabs[:, i, :].unsqueeze(1).to_broadcast([128, nh, 64])
        tv = [t4[:, i, 0:256].rearrange("p (h i) -> p h i", h=nh) for i in range(4)]
        self.tt(tv[0], x0, tb(0), ALU.mult, r=[xa, tabs], w=[t4])
        self.tt(tv[1], x1, tb(1), ALU.mult, r=[xa, tabs], w=[t4])
        self.tt(tv[2], x0, tb(2), ALU.mult, r=[xa, tabs], w=[t4])
        self.tt(tv[3], x1, tb(3), ALU.mult, r=[xa, tabs], w=[t4])
        self.tt(o0, tv[0], tv[1], ALU.subtract, r=[t4], w=[out_bf])
        self.tt(o1, tv[2], tv[3], ALU.add, r=[t4], w=[out_bf])


def rope_table(pos):
    pos = np.asarray(pos)
    row = (pos // 64).astype(np.float32)
    col = (pos % 64).astype(np.float32)
    freqs = (np.float32(10000.0) ** (-np.arange(32, dtype=np.float32) / np.float32(32))).astype(np.float32)
    ang = np.concatenate([row[:, None] * freqs, col[:, None] * freqs], axis=-1).astype(np.float32)
    return np.concatenate([np.cos(ang), np.sin(ang)], axis=-1).astype(np.float32)


def host_inputs(inp, Lp, S, ncores=NCORES):
    f = lambda a: np.ascontiguousarray(np.asarray(a, dtype=np.float32))
    Ls = 8 * S
    xs_all = f(inp["x_sample"])[0]
    shared = {}
    shared["w_in"] = f(inp["w_in"])[0]
    shared["w_glu"] = f(inp["w_glu"])[0]
    shared["w_mem_kv"] = f(inp["w_mem_kv"])[0]
    shared["w_pa"] = f(inp["w_proj_attn"])[0]
    shared["w_ps"] = f(inp["w_proj_ssm"])[0]
    shared["w_px"] = f(inp["w_proj_cross"])[0]
    shared["w_out"] = f(inp["w_out"])[0]
    shared["g_in"] = f(f(inp["norm_in"])[0].reshape(8, 128).T)
    shared["g_mem"] = f(f(inp["norm_mem"])[0].reshape(8, 128).T)
    qn, kn = f(inp["q_norm"])[0], f(inp["k_norm"])[0]
    shared["g_q"] = f(np.tile(np.concatenate([qn[0::2], qn[1::2]])[None, :], (128, 1)))
    shared["g_k"] = f(np.tile(np.concatenate([kn[0::2], kn[1::2]])[None, :], (128, 1)))
    shared["g_f"] = f(np.tile(f(inp["norm_final"])[None, :], (128, 1)))
    shared["s5d"] = f(f(inp["s5_d"])[0].reshape(4, 128).T)
    shared["bglu"] = f(f(inp["b_glu"])[0].reshape(4, 128).T)
    a_re, a_im = f(inp["s5_a_re"])[0], f(inp["s5_a_im"])[0]
    dup = lambda a: f(np.concatenate([a.reshape(64, 64).T, a.reshape(64, 64).T], axis=0))
    shared["are"] = dup(a_re)
    shared["aim"] = dup(a_im)
    shared["lst"] = f(np.tile(f(inp["s5_log_step"])[0].reshape(1, 64), (128, 1)))
    b_re, b_im = f(inp["s5_b_re"])[0], f(inp["s5_b_im"])[0]
    c_re, c_im = f(inp["s5_c_re"])[0], f(inp["s5_c_im"])[0]
    B1 = np.zeros((64, 128, 128), np.float32)
    B2 = np.zeros((64, 128, 128), np.float32)
    C1 = np.zeros((64, 128, 16), np.float32)
    C2 = np.zeros((64, 128, 16), np.float32)
    for d in range(2):
        for g in range(32):
            col = d * 32 + g
            r0 = (g % 8) * 16
            B1[col, r0:r0 + 16, 0:64] = b_re[d, g].T
            B1[col, r0:r0 + 16, 64:128] = b_im[d, g].T
            B2[col, r0:r0 + 16, 0:64] = b_im[d, g].T
            B2[col, r0:r0 + 16, 64:128] = b_re[d, g].T
            C1[col, 0:64, :] = c_re[d, g].T
            C1[col, 64:128, :] = c_im[d, g].T
            C2[col, 0:64, :] = c_im[d, g].T
            C2[col, 64:128, :] = c_re[d, g].T
    shared.update(B1=B1, B2=B2, C1=C1, C2=C2)
    shared["ident"] = np.eye(128, dtype=np.float32)
    sw = np.zeros((128, 128), np.float32)
    sw[np.arange(128), (np.arange(128) + 64) % 128] = 1.0
    shared["swap"] = sw
    shared["iota1"] = f(np.tile(np.arange(1, PIECE + 1, dtype=np.float32)[None, :], (128, 1)))
    sg = np.ones((128, 2), np.float32)
    sg[0:64, 0] = -1.0
    sg[64:128, 1] = -1.0
    shared["sgn"] = sg
    shared["csp"] = rope_table(np.arange(Lp))
    maps = []
    for c in range(ncores):
        m = dict(shared)
        m["xp"] = f(inp["x_prompt"])[c]
        order = np.concatenate([np.arange((c + 1) * S, Ls), np.arange(0, c * S), np.arange(c * S, (c + 1) * S)])
        m["xs"] = np.ascontiguousarray(xs_all[order])
        m["css"] = rope_table(order)
        m["memp"] = f(inp["mem_prompt"])[c]
        m["mems"] = f(inp["mem_sample"])[0]
        mf = np.zeros((1, 7 * S), np.float32)
        mf[0, (7 - c) * S:] = 1.0
        m["mf"] = mf
        m["mb"] = (1.0 - mf).astype(np.float32)
        maps.append(m)
    return maps


_NC_CACHE = {}


def run(inp, Lp, S):
    key = (Lp, S)
    if key not in _NC_CACHE:
        _NC_CACHE[key] = Builder(Lp, S).build()
    nc = _NC_CACHE[key]
    maps = host_inputs(inp, Lp, S)
    res = run_bass_kernel_spmd(nc, maps, core_ids=list(range(NCORES)))
    ys = [np.asarray(r["y"]) for r in res.results]
    y_prompt = np.stack([y[:Lp] for y in ys], axis=0).astype(np.float32)
    y_sample = np.concatenate([y[Lp:Lp + S] for y in ys], axis=0)[None].astype(np.float32)
    return y_prompt, y_sample


def kernel(**inputs):
    Lp = int(np.asarray(inputs["x_prompt"]).shape[1])
    Ls = int(np.asarray(inputs["x_sample"]).shape[1])
    return run(inputs, Lp, Ls // 8)
```
